# Optimizing a Trainium2 kernel written in Bass

```python
import jax, jax.numpy as jnp
from jax import lax
import numpy as np

D_MODEL = 1024
BATCH = 2
SEQ = 8192
DEPTH = 2
DEC_BATCH = 32
DEC_SEQ = 2048
PAST_LEN = 128

N_HEADS = 8
QK_NOPE = 64
QK_ROPE = 32
QK_HEAD = QK_NOPE + QK_ROPE
V_HEAD = 64
Q_LORA = 256
KV_LORA = 128
ROPE_THETA = 10000.0
Q_BLOCK = 128
CONV_CH = 512
CONV_WIDTH = 31
SG_CH = 512
SG_GROUPS = 4
SG_GROUP_CH = SG_CH // SG_GROUPS
CHUNK = 128
D_FF = 2816
FFN_CONV_WIDTH = 3
N_BRANCH = 3
EPS = 1e-6
IN_SIZES = (Q_LORA, KV_LORA, QK_ROPE, 2 * CONV_CH, 2 * SG_CH, N_BRANCH * D_MODEL)
D_IN = Q_LORA + KV_LORA + QK_ROPE + 2 * CONV_CH + 2 * SG_CH + N_BRANCH * D_MODEL

kernel_name = 'hybrid_mla_conformer_sgu_encoder'


def split_cols(z, sizes):
    outs, start = [], 0
    for n in sizes:
        outs.append(z[..., start:start + n])
        start += n
    return outs


def rms_norm(x, g):
    xf = x.astype(jnp.float32)
    y = xf * lax.rsqrt(jnp.mean(xf * xf, axis=-1, keepdims=True) + EPS)
    return (y * g.astype(jnp.float32)).astype(x.dtype)


def layer_norm(x, g, b):
    xf = x.astype(jnp.float32)
    mu = jnp.mean(xf, axis=-1, keepdims=True)
    var = jnp.mean(jnp.square(xf - mu), axis=-1, keepdims=True)
    y = (xf - mu) * lax.rsqrt(var + EPS)
    return (y * g.astype(jnp.float32) + b.astype(jnp.float32)).astype(x.dtype)


def rope_tables(seq_len):
    inv_freq = jnp.power(ROPE_THETA, -jnp.arange(0, QK_ROPE, 2, dtype=jnp.float32) / QK_ROPE)
    ang = jnp.arange(seq_len, dtype=jnp.float32)[:, None] * inv_freq[None, :]
    return jnp.cos(ang), jnp.sin(ang)


def apply_rope(x, cos, sin):
    x1, x2 = jnp.split(x.astype(jnp.float32), 2, axis=-1)
    c = cos[None, :, None, :]
    s = sin[None, :, None, :]
    return jnp.concatenate([x1 * c - x2 * s, x1 * s + x2 * c], axis=-1).astype(x.dtype)


def depthwise_conv(x, w, b):
    y = lax.conv_general_dilated(x, w[:, None, :].astype(x.dtype), window_strides=(1,), padding='SAME',
                                 dimension_numbers=('NWC', 'WIO', 'NWC'), feature_group_count=x.shape[-1])
    return y + b.astype(x.dtype)


def modulate(h, shift, scale):
    return h * (1 + scale[:, None, :]) + shift[:, None, :]


def block_attention(q, k, v):
    bsz, seq, _, _ = q.shape
    nb = seq // Q_BLOCK
    qb = q.reshape(bsz, nb, Q_BLOCK, N_HEADS, QK_HEAD).transpose(1, 0, 2, 3, 4)
    scale = QK_HEAD ** -0.5

    def attend(q_blk):
        s = jnp.einsum('bqhd,bkhd->bhqk', q_blk, k, preferred_element_type=jnp.float32) * scale
        p = jax.nn.softmax(s, axis=-1)
        return jnp.einsum('bhqk,bkhd->bqhd', p.astype(v.dtype), v)

    o = lax.map(attend, qb)
    return o.transpose(1, 0, 2, 3, 4).reshape(bsz, seq, N_HEADS * V_HEAD)


def mla_branch(c_q, c_kv, k_r, cos, sin, q_a_norm, w_uq, kv_a_norm, w_ukv, q_head_norm, k_head_norm, w_o):
    bsz, seq, _ = c_q.shape
    q = (rms_norm(c_q, q_a_norm) @ w_uq).reshape(bsz, seq, N_HEADS, QK_HEAD)
    kv = (rms_norm(c_kv, kv_a_norm) @ w_ukv).reshape(bsz, seq, N_HEADS, QK_NOPE + V_HEAD)
    k_nope, v = kv[..., :QK_NOPE], kv[..., QK_NOPE:]
    k_rope = jnp.broadcast_to(k_r[:, :, None, :], (bsz, seq, N_HEADS, QK_ROPE))
    k = jnp.concatenate([k_nope, k_rope], axis=-1)
    q = rms_norm(q, q_head_norm)
    k = rms_norm(k, k_head_norm)
    q = jnp.concatenate([q[..., :QK_NOPE], apply_rope(q[..., QK_NOPE:], cos, sin)], axis=-1)
    k = jnp.concatenate([k[..., :QK_NOPE], apply_rope(k[..., QK_NOPE:], cos, sin)], axis=-1)
    return block_attention(q, k, v) @ w_o


def conv_branch(conv_in, w_dw, b_dw, ln_g, ln_b, w_o):
    a, g = jnp.split(conv_in, 2, axis=-1)
    h = a * jax.nn.sigmoid(g)
    h = depthwise_conv(h, w_dw, b_dw)
    h = jax.nn.silu(layer_norm(h, ln_g, ln_b))
    return h @ w_o


def sgu_branch(sg_in, ln_g, ln_b, w_s, b_s, w_o):
    bsz, seq, _ = sg_in.shape
    z = jax.nn.gelu(sg_in)
    u, v = jnp.split(z, 2, axis=-1)
    v = layer_norm(v, ln_g, ln_b)
    vc = v.reshape(bsz, seq // CHUNK, CHUNK, SG_GROUPS, SG_GROUP_CH)
    s = jnp.einsum('gpq,bnqgc->bnpgc', w_s.astype(v.dtype), vc) + b_s.T[:, :, None].astype(v.dtype)
    return (u * s.reshape(bsz, seq, SG_CH)) @ w_o


def conv_ffn(h, w_up, w_dw, b_dw, w_down):
    z = depthwise_conv(h @ w_up, w_dw, b_dw)
    g, v = jnp.split(z, 2, axis=-1)
    return (jax.nn.silu(g) * v) @ w_down


def setup_inputs(seed: int = 0) -> dict:
    key = jax.random.key(seed)
    ks = jax.random.split(key, 31)
    L, D = DEPTH, D_MODEL

    def nrm(k, shape, scale):
        return jax.random.normal(k, shape, jnp.float32) * scale

    return {
        'x_prompt': nrm(ks[0], (BATCH, SEQ, D), 1.0),
        'x_sample': nrm(ks[1], (DEC_BATCH, DEC_SEQ, D), 1.0),
        'c_prompt': nrm(ks[2], (BATCH, D), 1.0),
        'c_sample': nrm(ks[3], (DEC_BATCH, D), 1.0),
        'w_ada': nrm(ks[4], (L, D, 6 * D), 0.5 * D ** -0.5),
        'b_ada': nrm(ks[5], (L, 6 * D), 0.01),
        'norm1': 1.0 + nrm(ks[6], (L, D), 0.01),
        'w_in': nrm(ks[7], (L, D, D_IN), D ** -0.5),
        'q_a_norm': 1.0 + nrm(ks[8], (L, Q_LORA), 0.01),
        'w_uq': nrm(ks[9], (L, Q_LORA, N_HEADS * QK_HEAD), Q_LORA ** -0.5),
        'kv_a_norm': 1.0 + nrm(ks[10], (L, KV_LORA), 0.01),
        'w_ukv': nrm(ks[11], (L, KV_LORA, N_HEADS * (QK_NOPE + V_HEAD)), KV_LORA ** -0.5),
        'q_head_norm': 1.0 + nrm(ks[12], (L, QK_HEAD), 0.01),
        'k_head_norm': 1.0 + nrm(ks[13], (L, QK_HEAD), 0.01),
        'w_attn_o': nrm(ks[14], (L, N_HEADS * V_HEAD, D), (N_HEADS * V_HEAD) ** -0.5),
        'conv_dw': nrm(ks[15], (L, CONV_WIDTH, CONV_CH), CONV_WIDTH ** -0.5),
        'conv_dw_b': nrm(ks[16], (L, CONV_CH), 0.01),
        'conv_ln_g': 1.0 + nrm(ks[17], (L, CONV_CH), 0.01),
        'conv_ln_b': nrm(ks[18], (L, CONV_CH), 0.01),
        'w_conv_o': nrm(ks[19], (L, CONV_CH, D), CONV_CH ** -0.5),
        'sg_ln_g': 1.0 + nrm(ks[20], (L, SG_CH), 0.01),
        'sg_ln_b': nrm(ks[21], (L, SG_CH), 0.01),
        'sg_w': nrm(ks[22], (L, SG_GROUPS, CHUNK, CHUNK), CHUNK ** -0.5),
        'sg_b': 1.0 + nrm(ks[23], (L, SG_GROUPS, CHUNK), 0.01),
        'w_sg_o': nrm(ks[24], (L, SG_CH, D), SG_CH ** -0.5),
        'w_out': nrm(ks[25], (L, D, D), D ** -0.5),
        'norm2': 1.0 + nrm(ks[26], (L, D), 0.01),
        'w_up': nrm(ks[27], (L, D, 2 * D_FF), D ** -0.5),
        'ffn_dw': nrm(ks[28], (L, FFN_CONV_WIDTH, 2 * D_FF), FFN_CONV_WIDTH ** -0.5),
        'ffn_dw_b': nrm(ks[29], (L, 2 * D_FF), 0.01),
        'w_down': nrm(ks[30], (L, D_FF, D), D_FF ** -0.5),
    }


def reference(x_prompt, x_sample, c_prompt, c_sample, w_ada, b_ada, norm1, w_in, q_a_norm, w_uq, kv_a_norm, w_ukv,
              q_head_norm, k_head_norm, w_attn_o, conv_dw, conv_dw_b, conv_ln_g, conv_ln_b, w_conv_o,
              sg_ln_g, sg_ln_b, sg_w, sg_b, w_sg_o, w_out, norm2, w_up, ffn_dw, ffn_dw_b, w_down):
    def encode(x, c):
        cos, sin = rope_tables(x.shape[1])
        for l in range(DEPTH):
            mod = jax.nn.silu(c) @ w_ada[l] + b_ada[l]
            sh1, sc1, g1, sh2, sc2, g2 = jnp.split(mod, 6, axis=-1)
            h = modulate(rms_norm(x, norm1[l]), sh1, sc1)
            c_q, c_kv, k_r, conv_in, sg_in, gate_in = split_cols(h @ w_in[l], IN_SIZES)
            y_attn = mla_branch(c_q, c_kv, k_r, cos, sin, q_a_norm[l], w_uq[l], kv_a_norm[l], w_ukv[l],
                                q_head_norm[l], k_head_norm[l], w_attn_o[l])
            y_conv = conv_branch(conv_in, conv_dw[l], conv_dw_b[l], conv_ln_g[l], conv_ln_b[l], w_conv_o[l])
            y_sg = sgu_branch(sg_in, sg_ln_g[l], sg_ln_b[l], sg_w[l], sg_b[l], w_sg_o[l])
            ga, gc, gs = jnp.split(jax.nn.sigmoid(gate_in), N_BRANCH, axis=-1)
            merged = ga * y_attn + gc * y_conv + gs * y_sg
            x = x + g1[:, None, :] * (merged @ w_out[l])
            h2 = modulate(rms_norm(x, norm2[l]), sh2, sc2)
            x = x + g2[:, None, :] * conv_ffn(h2, w_up[l], ffn_dw[l], ffn_dw_b[l], w_down[l])
        return x

    y_prompt = encode(x_prompt, c_prompt)
    y_sample = encode(x_sample, c_sample)
    return (y_prompt, y_sample)
```

```python
import numpy as np
from contextlib import ExitStack
import concourse.bass as bass
import concourse.mybir as mybir
from concourse.bass_utils import run_bass_kernel_spmd

F32 = mybir.dt.float32
BF16 = mybir.dt.bfloat16
AF = mybir.ActivationFunctionType
ALU = mybir.AluOpType
AX = mybir.AxisListType

D = 1024
KT = 8
NH = 8
DQK = 96
EPS = 1e-6
D_IN = 5536
D_FF = 2816
NFT = 22
HALO = 128

CFG = dict(NS=4, SS=2048, PCH=2048, PS=8192)


class Trk:
    __slots__ = ("w", "r", "dsem", "dcnt")

    def __init__(self):
        self.w = {}
        self.r = {}
        self.dsem = None
        self.dcnt = 0


class KB:
    def __init__(self, nc, es):
        self.nc = nc
        self.es = es
        self.eng = {"pe": nc.tensor, "act": nc.scalar, "dve": nc.vector, "pool": nc.gpsimd, "sp": nc.sync}
        self.sem = {k: es.enter_context(nc.semaphore("s_" + k)) for k in ["pe", "act", "dve", "pool"]}
        self.cnt = {k: 0 for k in self.sem}
        self.seen = {k: {} for k in self.eng}
        self.pend = {k: [] for k in self.eng}
        self.dma_trks = []
        self.nops = 0
        self.sem_free = []
        self.phase_trks = []
        self.nsem = 0

    def get_dsem(self):
        if self.sem_free:
            return self.sem_free.pop()
        self.nsem += 1
        return (self.es.enter_context(self.nc.semaphore("dq%d" % self.nsem)), 0)

    def release(self, trks):
        for k in trks:
            if k.dsem is not None:
                self.sem_free.append((k.dsem, k.dcnt))
                if k in self.dma_trks:
                    self.dma_trks.remove(k)

    def _wait(self, e, sem, val):
        d = self.seen[e]
        if d.get(sem, 0) >= val:
            return
        self.eng[e].wait_ge(sem, val)
        d[sem] = val

    def _deps(self, e, reads, writes):
        for t in reads:
            for s, (v, ek) in t.w.items():
                self._wait(e, s, v)
        for t in writes:
            for s, (v, ek) in t.w.items():
                if ek != e:
                    self._wait(e, s, v)
            for ek, (s, v) in t.r.items():
                if ek != e:
                    self._wait(e, s, v)

    def op(self, e, fn, reads=(), writes=(), inc=True):
        if getattr(self, 'skip', False):
            return
        self._deps(e, reads, writes)
        ins = fn(self.eng[e])
        self.nops += 1
        if not inc:
            self.pend[e].append((reads, writes))
            return
        self.cnt[e] += 1
        c = self.cnt[e]
        s = self.sem[e]
        ins.then_inc(s, 1)
        self.pend[e].append((reads, writes))
        for rd, wr in self.pend[e]:
            for t in wr:
                t.w = {s: (c, e)}
                t.r = {}
        for rd, wr in self.pend[e]:
            for t in rd:
                t.r[e] = (s, c)
        self.pend[e] = []

    def dma(self, q, out, in_, own, reads=(), writes=(), acc=(), dram_reads=(), slow=False):
        if getattr(self, 'skip', False):
            return
        self._deps(q, list(reads) + list(dram_reads), writes)
        for t in acc:
            for ek, (s, v) in t.r.items():
                self._wait(q, s, v)
        own.dcnt += 16
        if slow:
            self.eng[q].dma_start(out=out, in_=in_, allow_slow_non_contiguous=True).then_inc(own.dsem, 16)
        else:
            self.eng[q].dma_start(out=out, in_=in_).then_inc(own.dsem, 16)
        self.nops += 1
        key = ("dma", own.dsem)
        for t in writes:
            t.w = {own.dsem: (own.dcnt, key)}
            t.r = {}
        for t in acc:
            t.w[own.dsem] = (own.dcnt, key)
        for t in reads:
            t.r[key] = (own.dsem, own.dcnt)

    def end_phase(self):
        self.barrier()
        self.release(list(self.phase_trks))
        self.phase_trks = []

    def barrier(self):
        for e in self.eng:
            for k in self.sem:
                if k != e and self.cnt[k] > 0:
                    self._wait(e, self.sem[k], self.cnt[k])
            for t in self.dma_trks:
                if t.dcnt > 0:
                    self._wait(e, t.dsem, t.dcnt)

    def sb(self, es, name, shape, dt, dma=False):
        self.nalloc = getattr(self, "nalloc", 0) + 1
        t = es.enter_context(self.nc.sbuf_tensor("%s_u%d" % (name, self.nalloc), list(shape), dt))
        k = Trk()
        if dma:
            k.dsem, k.dcnt = self.get_dsem()
            self.dma_trks.append(k)
            self.phase_trks.append(k)
        return t, k

    def ring(self, es, name, n, shape, dt, dma=False):
        return Ring([self.sb(es, "%s%d" % (name, i), shape, dt, dma) for i in range(n)])


class Ring:
    def __init__(self, items):
        self.items = items
        self.i = 0

    def next(self):
        it = self.items[self.i % len(self.items)]
        self.i += 1
        return it


def blocks_of(n, bs=512):
    out = []
    o = 0
    while o < n:
        b = min(bs, n - o)
        out.append((o, b))
        o += b
    return out


def build_program(cfg, n_layers=1):
    NS, SS, PCH, PS = cfg["NS"], cfg["SS"], cfg["PCH"], cfg["PS"]
    PSEG = PCH + 2 * HALO
    nc = bass.Bass("TRN2", target_bir_lowering=False)

    def din(name, shape):
        return nc.dram_tensor(name, list(shape), F32, kind="ExternalInput").ap()

    xs = din("xs", [NS * SS, D])
    psel = din("psel", [128, PS // PCH])
    xpctx = din("xpctx", [PS, D])
    cvT = din("cvT", [128, (NS + 1) * KT])
    pmask = din("pmask", [128, 2])
    rope_s = din("rope_s", [SS, 32])
    rope_pc = din("rope_pc", [PS, 32])
    rope_pq = din("rope_pq", [PSEG, 32])
    W = {}
    wshapes = dict(
        w_ada=[D, 6 * D], b_ada=[1, 6 * D], norm1=[1, D], norm2=[1, D], w_in=[D, D_IN],
        w_uq=[256, 768], w_ukv=[128, 1024], qan=[128, 2], kvan=[128, 1], qhn=[1, 96], khn=[1, 96],
        w_ao=[64, 8 * D], cdw=[128, 4 * 31], cdwb=[128, 4], clng=[128, 4], clnb=[128, 4],
        w_co=[512, D], sglng=[1, 512], sglnb=[1, 512], sgwT=[128, 4 * 128], sgb=[1, 512], w_so=[512, D],
        w_out=[D, D], w_up=[D, 2 * D_FF], fdw=[128, 44 * 3], fdwb=[128, 44], w_down=[D_FF, D])
    for l in range(n_layers):
        for k, shp in wshapes.items():
            W[(l, k)] = din("%s_%d" % (k, l), shp)
    ys = nc.dram_tensor("ys", [NS * SS, D], F32, kind="ExternalOutput").ap()
    yp = nc.dram_tensor("yp", [PCH, D], F32, kind="ExternalOutput").ap()

    x1s = nc.dram_tensor("x1s", [NS * SS, D], F32).ap()
    x1full = nc.dram_tensor("x1full", [PS, D], F32).ap()
    x1seg = nc.dram_tensor("x1seg", [PSEG, D], F32).ap()
    x1s_k = [Trk() for _ in range(NS)]
    x1full_k = Trk()
    x1seg_k = Trk()
    assert n_layers == 2

    def make_segs(l):
        segs = []
        for i in range(NS):
            if l == 0:
                segs.append(dict(n=SS, s=SS, x=xs[i * SS:(i + 1) * SS, :], x_k=Trk(), ctx=None, ctx_k=None, rq=rope_s, rk=rope_s,
                                 y=x1s[i * SS:(i + 1) * SS, :], y_k=x1s_k[i], prompt=False))
            else:
                segs.append(dict(n=SS, s=SS, x=x1s[i * SS:(i + 1) * SS, :], x_k=x1s_k[i], ctx=None, ctx_k=None, rq=rope_s, rk=rope_s,
                                 y=ys[i * SS:(i + 1) * SS, :], y_k=Trk(), prompt=False))
        if l == 0:
            segs.append(dict(n=PS, s=PS, x=xpctx, x_k=Trk(), ctx=None, ctx_k=None, rq=rope_pc, rk=rope_pc,
                             y=x1full, y_k=x1full_k, prompt=False))
        else:
            segs.append(dict(n=PSEG, s=PS, x=x1seg, x_k=x1seg_k, ctx=x1full, ctx_k=x1full_k, rq=rope_pq, rk=rope_pc,
                             y=yp, y_k=Trk(), prompt=True))
        for si, sg in enumerate(segs):
            n, s_ = sg["n"], sg["s"]

            def dsc(nm, shape, dt=BF16):
                return nc.dram_tensor("%s_%d_%d" % (nm, l, si), list(shape), dt).ap()
            sg["hT"] = dsc("hT", [128, KT, n]); sg["hT_k"] = Trk()
            sg["qT"] = dsc("qT", [DQK, NH, n]); sg["qT_k"] = Trk()
            sg["kT"] = dsc("kT", [NH, DQK, s_]); sg["kT_k"] = Trk()
            sg["v"] = dsc("v", [NH, 128, s_ // 128, 65]); sg["v_k"] = Trk()
            sg["gat"] = dsc("gat", [128, 24, n]); sg["gat_k"] = Trk()
            sg["glu"] = dsc("glu", [128, 4, n + 30]); sg["glu_k"] = Trk()
            sg["us"] = dsc("us", [128, 4, n]); sg["us_k"] = Trk()
            sg["ao"] = dsc("ao", [64, NH, n]); sg["ao_k"] = Trk()
            sg["xm"] = dsc("xm", [n, D], F32); sg["xm_k"] = Trk()
            sg["h2T"] = dsc("h2T", [128, KT, n + 2]); sg["h2T_k"] = Trk()
            sg["mod"] = dsc("mod", [128, 6 * D], F32); sg["mod_k"] = Trk()
        return segs

    all_segs = [make_segs(l) for l in range(n_layers)]

    with ExitStack() as es:
        K = KB(nc, es)
        ident, ident_k = K.sb(es, "ident", [128, 128], BF16)
        ones_bf, ones_k = K.sb(es, "ones_bf", [128, 128], BF16)
        ones_f, onesf_k = K.sb(es, "ones_f", [128, 64], F32)
        zt, zt_k = K.sb(es, "zt", [128, 64], BF16, dma=True)
        mk, mk_k = K.sb(es, "mk", [128, 2], F32, dma=True)
        K.op("dve", lambda e: e.memset(ident[:], 1.0), writes=[ident_k])
        K.op("pool", lambda e: e.affine_select(out=ident[:], in_=ident[:], pattern=[[-1, 128]],
                                               compare_op=ALU.is_equal, fill=0.0, base=0, channel_multiplier=1),
             reads=[ident_k], writes=[ident_k])
        K.op("dve", lambda e: e.memset(ones_bf[:], 1.0), writes=[ones_k])
        K.op("dve", lambda e: e.memset(ones_f[:], 1.0), writes=[onesf_k])
        K.op("dve", lambda e: e.memset(zt[:], 0.0), writes=[zt_k])
        K.dma("sp", mk[:], pmask, mk_k, writes=[mk_k])
        K.phase_trks = []
        PSB = []
        for i in range(8):
            t = es.enter_context(nc.psum_tensor("psb%d" % i, [128, 512], F32))
            PSB.append((t, Trk()))
        psr_state = [0]

        PSR_HI = [8]

        def psr(lo=0, hi=None):
            if hi is None:
                hi = PSR_HI[0]
            i = lo + psr_state[0] % (hi - lo)
            psr_state[0] += 1
            return PSB[i]

        for sg in [g for sl in all_segs for g in sl]:
            n = sg["n"]
            K.dma("sp", sg["glu"][:, :, 0:15], zt[:, 0:60].rearrange("p (a b) -> p a b", a=4), zt_k, reads=[zt_k], acc=[sg["glu_k"]])
            K.dma("sp", sg["glu"][:, :, n + 15:n + 30], zt[:, 0:60].rearrange("p (a b) -> p a b", a=4), zt_k, reads=[zt_k], acc=[sg["glu_k"]])
            K.dma("sp", sg["h2T"][:, :, 0:1], zt[:, 0:8].rearrange("p (a b) -> p a b", a=8), zt_k, reads=[zt_k], acc=[sg["h2T_k"]], slow=True)
            K.dma("sp", sg["h2T"][:, :, n + 1:n + 2], zt[:, 0:8].rearrange("p (a b) -> p a b", a=8), zt_k, reads=[zt_k], acc=[sg["h2T_k"]], slow=True)

        def wview(ap_, p=128):
            return ap_.rearrange("(kt p) n -> p kt n", p=p)

        def norm_tile(P, xt, xk, gam, sh, mask_col=None):
            junk, junk_k = P["junk"].next()
            ss, ss_k = P["ss"].next()
            K.op("dve", lambda e: e.memset(ss[:], 0.0), writes=[ss_k])
            K.op("act", lambda e: e.activation(out=junk[:], in_=xt[:], func=AF.Square, accum_out=ss[:, 0:1]),
                 reads=[xk, ss_k], writes=[junk_k, ss_k])
            K.op("act", lambda e: e.activation(out=ss[:, 1:2], in_=ss[:, 0:1], func=AF.Sqrt, bias=EPS, scale=1.0 / D),
                 reads=[ss_k], writes=[ss_k])
            K.op("dve", lambda e: e.reciprocal(out=ss[:, 2:3], in_=ss[:, 1:2]), reads=[ss_k], writes=[ss_k])
            K.op("dve", lambda e: e.scalar_tensor_tensor(out=junk[:], in0=xt[:], scalar=ss[:, 2:3], in1=gam[0][:],
                                                         op0=ALU.mult, op1=ALU.mult),
                 reads=[xk, ss_k, gam[1], junk_k], writes=[junk_k])
            hb, hb_k = P["hb"].next()
            K.op("pool", lambda e: e.tensor_tensor(out=hb[:], in0=junk[:], in1=sh[0][:], op=ALU.add),
                 reads=[junk_k, sh[1]], writes=[hb_k])
            if mask_col is not None:
                K.op("dve", lambda e: e.tensor_scalar(out=hb[:], in0=hb[:], scalar1=mk[:, mask_col:mask_col + 1],
                                                      scalar2=None, op0=ALU.mult),
                     reads=[hb_k, mk_k], writes=[hb_k])
            return hb, hb_k

        def transpose_to(P, hb, hb_k, dst, dst_k, ncols_src=D, rows=128, chunk=128):
            nchunk = ncols_src // chunk
            ps, ps_k = psr()
            psb = ps[:].bitcast(BF16)
            for i in range(nchunk):
                K.op("pe", lambda e, i=i: e.transpose(psb[0:chunk, i * 128:(i + 1) * 128], hb[:, i * chunk:(i + 1) * chunk], ident[:]),
                     reads=[hb_k, ident_k], writes=[ps_k], inc=(i == nchunk - 1))
            K.op("act", lambda e: e.activation(out=dst, in_=psb[0:chunk, 0:nchunk * 128].rearrange("p (a b) -> p a b", a=nchunk),
                                               func=AF.Copy),
                 reads=[ps_k], writes=[dst_k])

        def headnorm_rope(P, f, f_k, gain, rp, rp_k, out, out_k):
            f3 = f[:].rearrange("p (h d) -> p h d", h=NH)
            o3 = out[:].rearrange("p (h d) -> p h d", h=NH)
            sq, sq_k = P["sq"].next()
            st, st_k = P["st"].next()
            K.op("dve", lambda e: e.tensor_tensor(out=sq[:], in0=f[:], in1=f[:], op=ALU.mult), reads=[f_k], writes=[sq_k])
            K.op("dve", lambda e: e.reduce_sum(out=st[:, 0:8], in_=sq[:].rearrange("p (h d) -> p h d", h=NH), axis=AX.X),
                 reads=[sq_k], writes=[st_k])
            K.op("act", lambda e: e.activation(out=st[:, 8:16], in_=st[:, 0:8], func=AF.Sqrt, bias=EPS, scale=1.0 / DQK),
                 reads=[st_k], writes=[st_k])
            K.op("dve", lambda e: e.reciprocal(out=st[:, 16:24], in_=st[:, 8:16]), reads=[st_k], writes=[st_k])
            K.op("dve", lambda e: e.tensor_tensor(out=f3, in0=f3, in1=st[:, 16:24].unsqueeze(2).to_broadcast([128, NH, DQK]), op=ALU.mult),
                 reads=[f_k, st_k], writes=[f_k])
            K.op("dve", lambda e: e.tensor_tensor(out=f3, in0=f3, in1=gain[0][:].unsqueeze(1).to_broadcast([128, NH, DQK]), op=ALU.mult),
                 reads=[f_k, gain[1]], writes=[f_k])
            tt, tt_k = P["tt"].next()
            t4 = tt[:].rearrange("p (a h d) -> p a h d", a=4, h=NH)
            x1 = f3[:, :, 64:80]
            x2 = f3[:, :, 80:96]
            cs = rp[:, 0:16].unsqueeze(1).to_broadcast([128, NH, 16])
            sn = rp[:, 16:32].unsqueeze(1).to_broadcast([128, NH, 16])
            K.op("dve", lambda e: e.tensor_tensor(out=t4[:, 0], in0=x1, in1=cs, op=ALU.mult), reads=[f_k, rp_k], writes=[tt_k])
            K.op("dve", lambda e: e.tensor_tensor(out=t4[:, 1], in0=x2, in1=sn, op=ALU.mult), reads=[f_k, rp_k], writes=[tt_k])
            K.op("dve", lambda e: e.tensor_tensor(out=t4[:, 2], in0=x1, in1=sn, op=ALU.mult), reads=[f_k, rp_k], writes=[tt_k])
            K.op("dve", lambda e: e.tensor_tensor(out=t4[:, 3], in0=x2, in1=cs, op=ALU.mult), reads=[f_k, rp_k], writes=[tt_k])
            K.op("dve", lambda e: e.tensor_tensor(out=o3[:, :, 64:80], in0=t4[:, 0], in1=t4[:, 1], op=ALU.subtract), reads=[tt_k], writes=[out_k])
            K.op("dve", lambda e: e.tensor_tensor(out=o3[:, :, 80:96], in0=t4[:, 2], in1=t4[:, 3], op=ALU.add), reads=[tt_k], writes=[out_k])
            K.op("pool", lambda e: e.tensor_copy(out=o3[:, :, 0:64], in_=f3[:, :, 0:64]), reads=[f_k], writes=[out_k])

        import os as _os
        _stop = _os.environ.get("KSTOP", "")
        _phc = [0]

        class _Stop(Exception):
            pass

        def chk(l):
            _phc[0] += 1
            if _stop and _stop == "%d,%d" % (l, _phc[0]):
                K.skip = True
                print('STOPPED at', _stop)

        try:
          for l in range(n_layers):
              _phc[0] = 0
              Wl = {k: W[(l, k)] for k in wshapes}
              segs = all_segs[l]
              if l == 1:
                  with ExitStack() as ph:
                      selt, selt_k = K.sb(ph, "selt", [128, PS // PCH], F32, dma=True)
                      K.dma("sp", selt[:], psel, selt_k, writes=[selt_k])
                      accr_ = K.ring(ph, "xacc", 2, [128, D], F32, dma=True)
                      ldr_ = K.ring(ph, "xld", 4, [128, D], F32, dma=True)
                      for j in range(PSEG // 128):
                          ac, ac_k = accr_.next()
                          K.op("dve", lambda e: e.memset(ac[:], 0.0), writes=[ac_k])
                          for r_ in range(PS // PCH):
                              row = r_ * PCH - HALO + j * 128
                              if row < 0 or row + 128 > PS:
                                  continue
                              ld, ld_k = ldr_.next()
                              K.dma("sp", ld[:], x1full[row:row + 128, :], ld_k, writes=[ld_k], dram_reads=[x1full_k])
                              K.op("dve", lambda e, r_=r_: e.scalar_tensor_tensor(out=ac[:], in0=ld[:], scalar=selt[:, r_:r_ + 1], in1=ac[:],
                                                                                  op0=ALU.mult, op1=ALU.add),
                                   reads=[ld_k, selt_k, ac_k], writes=[ac_k])
                          K.dma("sp", x1seg[j * 128:(j + 1) * 128, :], ac[:], ac_k, reads=[ac_k], acc=[x1seg_k])
                      K.end_phase()
                      chk(l)
              with ExitStack() as ph:
                  PSR_HI[0] = 8
                  nseg = len(segs)
                  cT, cT_k = K.sb(ph, "cT", [128, nseg * KT], F32, dma=True)
                  crep, crep_k = K.sb(ph, "crep", [128, nseg * KT, 128], BF16)
                  bb, bb_k = K.sb(ph, "bb", [1, 6 * D], BF16, dma=True)
                  n1b, n1b_k = K.sb(ph, "n1b", [128, D], F32, dma=True)
                  n2b, n2b_k = K.sb(ph, "n2b", [128, D], F32, dma=True)
                  wr = K.ring(ph, "wada", 2, [128, KT, 512], BF16, dma=True)
                  modt = [K.sb(ph, "modt%d" % s, [128, 6 * D], F32, dma=True) for s in range(nseg)]
                  K.dma("sp", cT[:], cvT, cT_k, writes=[cT_k])
                  K.op("act", lambda e: e.activation(out=cT[:], in_=cT[:], func=AF.Silu), reads=[cT_k], writes=[cT_k])
                  K.op("dve", lambda e: e.tensor_copy(out=crep[:], in_=cT[:].unsqueeze(2).to_broadcast([128, nseg * KT, 128])),
                       reads=[cT_k], writes=[crep_k])
                  K.dma("pool", bb[:], Wl["b_ada"], bb_k, writes=[bb_k])
                  K.dma("sp", n1b[:], Wl["norm1"][0, :].partition_broadcast(128), n1b_k, writes=[n1b_k])
                  K.dma("sp", n2b[:], Wl["norm2"][0, :].partition_broadcast(128), n2b_k, writes=[n2b_k])
                  wav = wview(Wl["w_ada"])
                  for c in range(12):
                      wt, wt_k = wr.next()
                      K.dma("pool", wt[:], wav[:, :, c * 512:(c + 1) * 512], wt_k, writes=[wt_k])
                      for s in range(nseg):
                          ps, ps_k = psr()
                          for kt in range(KT):
                              K.op("pe", lambda e, kt=kt: e.matmul(ps[:], lhsT=crep[:, s * KT + kt, :], rhs=wt[:, kt, :],
                                                                   start=(kt == 0), stop=False),
                                   reads=[crep_k, wt_k], writes=[ps_k], inc=False)
                          K.op("pe", lambda e: e.matmul(ps[:], lhsT=ones_bf[0:1, :], rhs=bb[0:1, c * 512:(c + 1) * 512],
                                                        start=False, stop=True),
                               reads=[ones_k, bb_k], writes=[ps_k])
                          mt, mt_k = modt[s]
                          K.op("act", lambda e: e.activation(out=mt[:, c * 512:(c + 1) * 512], in_=ps[:], func=AF.Copy),
                               reads=[ps_k], writes=[mt_k])
                  for s in range(nseg):
                      mt, mt_k = modt[s]
                      K.op("dve", lambda e: e.scalar_tensor_tensor(out=mt[:, D:2 * D], in0=mt[:, D:2 * D], scalar=1.0, in1=n1b[:],
                                                                   op0=ALU.add, op1=ALU.mult),
                           reads=[mt_k, n1b_k], writes=[mt_k])
                      K.op("dve", lambda e: e.scalar_tensor_tensor(out=mt[:, 4 * D:5 * D], in0=mt[:, 4 * D:5 * D], scalar=1.0, in1=n2b[:],
                                                                   op0=ALU.add, op1=ALU.mult),
                           reads=[mt_k, n2b_k], writes=[mt_k])
                      K.dma("sp", segs[s]["mod"], mt[:], mt_k, reads=[mt_k], acc=[segs[s]["mod_k"]])
                  K.end_phase()
                  chk(l)

              with ExitStack() as ph:
                  wlat, wlat_k = K.sb(ph, "wlat", [128, KT, 416], BF16, dma=True)
                  wuq, wuq_k = K.sb(ph, "wuq", [128, 2, 768], BF16, dma=True)
                  wukv, wukv_k = K.sb(ph, "wukv", [128, 1024], BF16, dma=True)
                  qan, qan_k = K.sb(ph, "qan", [128, 2], F32, dma=True)
                  kvan, kvan_k = K.sb(ph, "kvan", [128, 1], F32, dma=True)
                  gq, gq_k = K.sb(ph, "gq", [128, DQK], F32, dma=True)
                  gk, gk_k = K.sb(ph, "gk", [128, DQK], F32, dma=True)
                  K.dma("pool", wlat[:], wview(Wl["w_in"])[:, :, 0:416], wlat_k, writes=[wlat_k])
                  K.dma("pool", wuq[:], wview(Wl["w_uq"]), wuq_k, writes=[wuq_k])
                  K.dma("pool", wukv[:], Wl["w_ukv"], wukv_k, writes=[wukv_k])
                  K.dma("sp", qan[:], Wl["qan"], qan_k, writes=[qan_k])
                  K.dma("sp", kvan[:], Wl["kvan"], kvan_k, writes=[kvan_k])
                  K.dma("sp", gq[:], Wl["qhn"][0, :].partition_broadcast(128), gq_k, writes=[gq_k])
                  K.dma("sp", gk[:], Wl["khn"][0, :].partition_broadcast(128), gk_k, writes=[gk_k])
                  P = dict(
                      junk=K.ring(ph, "junk", 2, [128, D], F32), ss=K.ring(ph, "ss", 3, [128, 4], F32),
                      hb=K.ring(ph, "hb", 2, [128, D], BF16), sq=K.ring(ph, "sq", 1, [128, 768], F32),
                      st=K.ring(ph, "st", 2, [128, 24], F32), tt=K.ring(ph, "tt", 1, [128, 4 * NH * 16], F32))
                  xr = K.ring(ph, "xr", 3, [128, D], F32, dma=True)
                  rpr = K.ring(ph, "rpr", 4, [128, 32], F32, dma=True)
                  hst = K.ring(ph, "hst", 2, [128, KT, 512], BF16, dma=True)
                  qst = K.ring(ph, "qst", 2, [128, NH, 512], BF16, dma=True)
                  kst = K.ring(ph, "kst", 2, [128, NH, 512], BF16, dma=True)
                  vst = K.ring(ph, "vst", 2, [128, NH, 4, 65], BF16, dma=True)
                  for vt, vk in vst.items:
                      K.op("dve", lambda e, vt=vt: e.memset(vt[:], 1.0), writes=[vk])
                  gamt, gam_k = K.sb(ph, "gam1", [128, D], F32, dma=True)
                  sht, sh_k = K.sb(ph, "sh1", [128, D], F32, dma=True)
                  cqn_r = K.ring(ph, "cqn", 2, [128, 256], BF16)
                  cqT_r = K.ring(ph, "cqT", 2, [128, 2, 128], BF16)
                  ckn_r = K.ring(ph, "ckn", 2, [128, 128], BF16)
                  ckT_r = K.ring(ph, "ckT", 2, [128, 128], BF16)
                  qf_r = K.ring(ph, "qf", 2, [128, 768], F32)
                  qb_r = K.ring(ph, "qb", 2, [128, 768], BF16)
                  ssl = K.ring(ph, "ssl", 3, [128, 4], F32)
                  lat_r = K.ring(ph, "lat", 2, [128, 416], F32)

                  def small_rms(src_ap, src_k, n, ncols_scale, dst, dst_k):
                      s4, s4_k = ssl.next()
                      jk, jk_k = P["sq"].next()
                      K.op("dve", lambda e: e.memset(s4[:], 0.0), writes=[s4_k])
                      K.op("act", lambda e: e.activation(out=jk[:, 0:n], in_=src_ap, func=AF.Square, accum_out=s4[:, 0:1]),
                           reads=[src_k, s4_k], writes=[jk_k, s4_k])
                      K.op("act", lambda e: e.activation(out=s4[:, 1:2], in_=s4[:, 0:1], func=AF.Sqrt, bias=EPS, scale=1.0 / n),
                           reads=[s4_k], writes=[s4_k])
                      K.op("dve", lambda e: e.reciprocal(out=s4[:, 2:3], in_=s4[:, 1:2]), reads=[s4_k], writes=[s4_k])
                      K.op("dve", lambda e: e.tensor_scalar(out=dst, in0=src_ap, scalar1=s4[:, 2:3], scalar2=None, op0=ALU.mult),
                           reads=[src_k, s4_k], writes=[dst_k])

                  for sg in segs:
                      K.dma("sp", sht[:], sg["mod"][:, 0:D], sh_k, writes=[sh_k], dram_reads=[sg["mod_k"]])
                      K.dma("sp", gamt[:], sg["mod"][:, D:2 * D], gam_k, writes=[gam_k], dram_reads=[sg["mod_k"]])
                      passes = []
                      if sg["prompt"]:
                          passes.append((sg["ctx"], sg["s"], False, True, sg["rk"], sg["ctx_k"]))
                          passes.append((sg["x"], sg["n"], True, False, sg["rq"], sg["x_k"]))
                      else:
                          passes.append((sg["x"], sg["n"], True, True, sg["rq"], sg["x_k"]))
                      for (xsrc, ntok, want_q, want_k, rtab, xsrc_k) in passes:
                          for (b0, bw) in blocks_of(ntok):
                              nj = bw // 128
                              hs, hs_k = hst.next()
                              if want_q:
                                  qs, qs_k = qst.next()
                              if want_k:
                                  ks, ks_k = kst.next()
                                  vs, vs_k = vst.next()
                              for j in range(nj):
                                  t0 = b0 + j * 128
                                  xt, xk = xr.next()
                                  K.dma("sp", xt[:], xsrc[t0:t0 + 128, :], xk, writes=[xk], dram_reads=[xsrc_k])
                                  rp, rp_k = rpr.next()
                                  K.dma("sp", rp[:], rtab[t0:t0 + 128, :], rp_k, writes=[rp_k])
                                  hb, hb_k = norm_tile(P, xt, xk, (gamt, gam_k), (sht, sh_k))
                                  transpose_to(P, hb, hb_k, hs[:, :, j * 128:(j + 1) * 128], hs_k)
                                  pl, pl_k = psr()
                                  for kt in range(KT):
                                      K.op("pe", lambda e, kt=kt: e.matmul(pl[:, 0:416], lhsT=hs[:, kt, j * 128:(j + 1) * 128], rhs=wlat[:, kt, :],
                                                                           start=(kt == 0), stop=(kt == KT - 1)),
                                           reads=[hs_k, wlat_k], writes=[pl_k], inc=(kt == KT - 1))
                                  lat, lat_k = lat_r.next()
                                  K.op("act", lambda e: e.activation(out=lat[:], in_=pl[:, 0:416], func=AF.Copy), reads=[pl_k], writes=[lat_k])
                                  pl, pl_k = lat, lat_k
                                  if want_q:
                                      cqn, cqn_k = cqn_r.next()
                                      small_rms(pl[:, 0:256], pl_k, 256, None, cqn[:], cqn_k)
                                      cqT, cqT_k = cqT_r.next()
                                      p2, p2_k = psr()
                                      p2b = p2[:].bitcast(BF16)
                                      for i in range(2):
                                          K.op("pe", lambda e, i=i: e.transpose(p2b[:, i * 128:(i + 1) * 128], cqn[:, i * 128:(i + 1) * 128], ident[:]),
                                               reads=[cqn_k, ident_k], writes=[p2_k], inc=(i == 1))
                                      for i in range(2):
                                          K.op("act", lambda e, i=i: e.activation(out=cqT[:, i, :], in_=p2b[:, i * 128:(i + 1) * 128], func=AF.Copy,
                                                                                  scale=qan[:, i:i + 1]),
                                               reads=[p2_k, qan_k], writes=[cqT_k])
                                      pq0, pq0_k = psr()
                                      pq1, pq1_k = psr()
                                      for i in range(2):
                                          K.op("pe", lambda e, i=i: e.matmul(pq0[:, 0:480], lhsT=cqT[:, i, :], rhs=wuq[:, i, 0:480], start=(i == 0), stop=(i == 1)),
                                               reads=[cqT_k, wuq_k], writes=[pq0_k], inc=(i == 1))
                                      for i in range(2):
                                          K.op("pe", lambda e, i=i: e.matmul(pq1[:, 0:288], lhsT=cqT[:, i, :], rhs=wuq[:, i, 480:768], start=(i == 0), stop=(i == 1)),
                                               reads=[cqT_k, wuq_k], writes=[pq1_k], inc=(i == 1))
                                      qf, qf_k = qf_r.next()
                                      K.op("act", lambda e: e.activation(out=qf[:, 0:480], in_=pq0[:, 0:480], func=AF.Copy), reads=[pq0_k], writes=[qf_k])
                                      K.op("dve", lambda e: e.tensor_copy(out=qf[:, 480:768], in_=pq1[:, 0:288]), reads=[pq1_k], writes=[qf_k])
                                      qb, qb_k = qb_r.next()
                                      headnorm_rope(P, qf, qf_k, (gq, gq_k), rp, rp_k, qb, qb_k)
                                      transpose_to(P, qb, qb_k, qs[0:DQK, :, j * 128:(j + 1) * 128], qs_k, ncols_src=768, chunk=DQK)
                                  if want_k:
                                      ckn, ckn_k = ckn_r.next()
                                      small_rms(pl[:, 256:384], pl_k, 128, None, ckn[:], ckn_k)
                                      ckT, ckT_k = ckT_r.next()
                                      p3, p3_k = psr()
                                      p3b = p3[:].bitcast(BF16)
                                      K.op("pe", lambda e: e.transpose(p3b[:, 0:128], ckn[:], ident[:]), reads=[ckn_k, ident_k], writes=[p3_k])
                                      K.op("act", lambda e: e.activation(out=ckT[:], in_=p3b[:, 0:128], func=AF.Copy, scale=kvan[:, 0:1]),
                                           reads=[p3_k, kvan_k], writes=[ckT_k])
                                      pk, pk_k = psr()
                                      pv, pv_k = psr()
                                      K.op("pe", lambda e: e.matmul(pk[:], lhsT=ckT[:], rhs=wukv[:, 0:512], start=True, stop=True),
                                           reads=[ckT_k, wukv_k], writes=[pk_k])
                                      K.op("pe", lambda e: e.matmul(pv[:], lhsT=ckT[:], rhs=wukv[:, 512:1024], start=True, stop=True),
                                           reads=[ckT_k, wukv_k], writes=[pv_k])
                                      kf, kf_k = qf_r.next()
                                      kf3 = kf[:].rearrange("p (h d) -> p h d", h=NH)
                                      K.op("act", lambda e: e.activation(out=kf3[:, :, 0:64], in_=pk[:].rearrange("p (h d) -> p h d", h=NH), func=AF.Copy),
                                           reads=[pk_k], writes=[kf_k])
                                      K.op("dve", lambda e: e.tensor_copy(out=kf3[:, :, 64:96], in_=pl[:, 384:416].unsqueeze(1).to_broadcast([128, NH, 32])),
                                           reads=[pl_k], writes=[kf_k])
                                      kb, kb_k = qb_r.next()
                                      headnorm_rope(P, kf, kf_k, (gk, gk_k), rp, rp_k, kb, kb_k)
                                      transpose_to(P, kb, kb_k, ks[0:DQK, :, j * 128:(j + 1) * 128], ks_k, ncols_src=768, chunk=DQK)
                                      K.op("act", lambda e: e.activation(out=vs[:, :, j, 0:64], in_=pv[:].rearrange("p (h d) -> p h d", h=NH), func=AF.Copy),
                                           reads=[pv_k], writes=[vs_k])
                              if want_q:
                                  K.dma("sp", sg["hT"][:, :, b0:b0 + bw], hs[:, :, 0:bw], hs_k, reads=[hs_k], acc=[sg["hT_k"]])
                                  K.dma("sp", sg["qT"][:, :, b0:b0 + bw], qs[0:DQK, :, 0:bw], qs_k, reads=[qs_k], acc=[sg["qT_k"]])
                              if want_k:
                                  K.dma("sp", sg["kT"].rearrange("h d s -> d h s")[:, :, b0:b0 + bw], ks[0:DQK, :, 0:bw], ks_k, reads=[ks_k], acc=[sg["kT_k"]])
                                  K.dma("sp", sg["v"].rearrange("h p k c -> p h k c")[:, :, b0 // 128:b0 // 128 + nj, :], vs[:, :, 0:nj, :], vs_k,
                                        reads=[vs_k], acc=[sg["v_k"]])
                  K.end_phase()
                  chk(l)

              with ExitStack() as ph:
                  PSR_HI[0] = 4
                  NW = D_IN - 416
                  wbig, wbig_k = K.sb(ph, "wbig", [128, KT, NW], BF16, dma=True)
                  wv = wview(Wl["w_in"])
                  for c in range(0, NW, 640):
                      K.dma("pool", wbig[:, :, c:c + 640], wv[:, :, 416 + c:416 + c + 640], wbig_k, acc=[wbig_k])
                  wsT, wsT_k = K.sb(ph, "wsT", [128, 4, 128], BF16, dma=True)
                  K.dma("pool", wsT[:], Wl["sgwT"].rearrange("p (g q) -> p g q", g=4), wsT_k, writes=[wsT_k])
                  lng, lng_k = K.sb(ph, "lng", [128, 512], F32, dma=True)
                  lnb, lnb_k = K.sb(ph, "lnb", [128, 512], F32, dma=True)
                  bsb, bsb_k = K.sb(ph, "bsb", [128, 4, 4, 128], F32, dma=True)
                  K.dma("sp", lng[:], Wl["sglng"][0, :].partition_broadcast(128), lng_k, writes=[lng_k])
                  K.dma("sp", lnb[:], Wl["sglnb"][0, :].partition_broadcast(128), lnb_k, writes=[lnb_k])
                  for j in range(4):
                      K.dma("sp", bsb[:, :, j, :], Wl["sgb"][0, :].partition_broadcast(128).rearrange("p (g q) -> p g q", g=4), bsb_k, acc=[bsb_k])
                  hbr = K.ring(ph, "hTb", 2, [128, KT, 512], BF16, dma=True)
                  sgt_r = K.ring(ph, "sgt", 2, [128, 512], F32)
                  glub_r = K.ring(ph, "glub", 2, [128, 4, 512], BF16, dma=True)
                  ug_r = K.ring(ph, "ug", 2, [128, 4, 512], BF16)
                  vg_r = K.ring(ph, "vg", 2, [128, 512], F32)
                  jk_r = K.ring(ph, "jk2", 1, [128, 512], F32)
                  s8_r = K.ring(ph, "s8", 3, [128, 8], F32)
                  vnb_r = K.ring(ph, "vnb", 2, [128, 512], BF16)
                  tq_r = K.ring(ph, "tq", 2, [128, 512], F32)
                  usb_r = K.ring(ph, "usb", 2, [128, 4, 512], BF16, dma=True)
                  gst_r = K.ring(ph, "gst", 2, [128, 24, 512], BF16, dma=True)
                  for sg in segs:
                      n = sg["n"]
                      blks = blocks_of(n)
                      for bi, (b0, bw) in enumerate(blks):
                          nj = bw // 128
                          hT, hT_k = hbr.next()
                          K.dma("sp", hT[:, :, 0:bw], sg["hT"][:, :, b0:b0 + bw], hT_k, writes=[hT_k], dram_reads=[sg["hT_k"]])

                          def fm(col0, ps, ps_k):
                              for kt in range(KT):
                                  K.op("pe", lambda e, kt=kt: e.matmul(ps[:, 0:bw], lhsT=wbig[:, kt, col0:col0 + 128], rhs=hT[:, kt, 0:bw],
                                                                       start=(kt == 0), stop=(kt == KT - 1)),
                                       reads=[wbig_k, hT_k], writes=[ps_k], inc=(kt == KT - 1))
                          glub, glub_k = glub_r.next()
                          for c in range(4):
                              pa, pa_k = psr()
                              pg, pg_k = psr()
                              fm(c * 128, pa, pa_k)
                              fm(512 + c * 128, pg, pg_k)
                              sgt, sgt_k = sgt_r.next()
                              K.op("act", lambda e: e.activation(out=sgt[:, 0:bw], in_=pg[:, 0:bw], func=AF.Sigmoid), reads=[pg_k], writes=[sgt_k])
                              K.op("dve", lambda e, c=c: e.tensor_tensor(out=glub[:, c, 0:bw], in0=pa[:, 0:bw], in1=sgt[:, 0:bw], op=ALU.mult),
                                   reads=[pa_k, sgt_k], writes=[glub_k])
                          if sg["prompt"]:
                              if bi == 0:
                                  K.op("dve", lambda e: e.tensor_scalar(out=glub[:, :, 0:128], in0=glub[:, :, 0:128], scalar1=mk[:, 0:1], scalar2=None, op0=ALU.mult),
                                       reads=[glub_k, mk_k], writes=[glub_k])
                              if bi == len(blks) - 1:
                                  K.op("dve", lambda e: e.tensor_scalar(out=glub[:, :, bw - 128:bw], in0=glub[:, :, bw - 128:bw], scalar1=mk[:, 1:2], scalar2=None, op0=ALU.mult),
                                       reads=[glub_k, mk_k], writes=[glub_k])
                          K.dma("sp", sg["glu"][:, :, 15 + b0:15 + b0 + bw], glub[:, :, 0:bw], glub_k, reads=[glub_k], acc=[sg["glu_k"]])
                          ug, ug_k = ug_r.next()
                          for g in range(4):
                              pu, pu_k = psr()
                              fm(1024 + g * 128, pu, pu_k)
                              K.op("act", lambda e, g=g: e.activation(out=ug[:, g, 0:bw], in_=pu[:, 0:bw], func=AF.Gelu_apprx_tanh), reads=[pu_k], writes=[ug_k])
                          pss = [PSB[4 + g] for g in range(4)]
                          for j in range(nj):
                              pv, pv_k = psr()
                              for kt in range(KT):
                                  K.op("pe", lambda e, kt=kt: e.matmul(pv[:], lhsT=hT[:, kt, j * 128:(j + 1) * 128], rhs=wbig[:, kt, 1536:2048],
                                                                       start=(kt == 0), stop=(kt == KT - 1)),
                                       reads=[hT_k, wbig_k], writes=[pv_k], inc=(kt == KT - 1))
                              vg, vg_k = vg_r.next()
                              K.op("act", lambda e: e.activation(out=vg[:], in_=pv[:], func=AF.Gelu_apprx_tanh), reads=[pv_k], writes=[vg_k])
                              s8, s8_k = s8_r.next()
                              jk, jk_k = jk_r.next()
                              K.op("dve", lambda e: e.memset(s8[:], 0.0), writes=[s8_k])
                              K.op("act", lambda e: e.activation(out=jk[:], in_=vg[:], func=AF.Square, accum_out=s8[:, 1:2]), reads=[vg_k, s8_k], writes=[jk_k, s8_k])
                              K.op("dve", lambda e: e.reduce_sum(out=s8[:, 0:1], in_=vg[:], axis=AX.X), reads=[vg_k, s8_k], writes=[s8_k])
                              K.op("dve", lambda e: e.tensor_scalar(out=s8[:, 2:3], in0=s8[:, 0:1], scalar1=1.0 / 512, scalar2=None, op0=ALU.mult), reads=[s8_k], writes=[s8_k])
                              K.op("dve", lambda e: e.tensor_tensor(out=s8[:, 3:4], in0=s8[:, 2:3], in1=s8[:, 2:3], op=ALU.mult), reads=[s8_k], writes=[s8_k])
                              K.op("dve", lambda e: e.scalar_tensor_tensor(out=s8[:, 4:5], in0=s8[:, 1:2], scalar=1.0 / 512, in1=s8[:, 3:4], op0=ALU.mult, op1=ALU.subtract),
                                   reads=[s8_k], writes=[s8_k])
                              K.op("act", lambda e: e.activation(out=s8[:, 5:6], in_=s8[:, 4:5], func=AF.Sqrt, bias=EPS, scale=1.0), reads=[s8_k], writes=[s8_k])
                              K.op("dve", lambda e: e.reciprocal(out=s8[:, 6:7], in_=s8[:, 5:6]), reads=[s8_k], writes=[s8_k])
                              K.op("dve", lambda e: e.tensor_scalar(out=vg[:], in0=vg[:], scalar1=s8[:, 2:3], scalar2=s8[:, 6:7], op0=ALU.subtract, op1=ALU.mult),
                                   reads=[vg_k, s8_k], writes=[vg_k])
                              K.op("dve", lambda e: e.tensor_tensor(out=vg[:], in0=vg[:], in1=lng[:], op=ALU.mult), reads=[vg_k, lng_k], writes=[vg_k])
                              vnb, vnb_k = vnb_r.next()
                              K.op("pool", lambda e: e.tensor_tensor(out=vnb[:], in0=vg[:], in1=lnb[:], op=ALU.add), reads=[vg_k, lnb_k], writes=[vnb_k])
                              for g in range(4):
                                  K.op("pe", lambda e, g=g: e.matmul(pss[g][0][:, j * 128:(j + 1) * 128], lhsT=vnb[:, g * 128:(g + 1) * 128], rhs=wsT[:, g, :],
                                                                     start=True, stop=True),
                                       reads=[vnb_k, wsT_k], writes=[pss[g][1]], inc=(g == 3))
                          usb, usb_k = usb_r.next()
                          for g in range(4):
                              tq, tq_k = tq_r.next()
                              K.op("dve", lambda e, g=g: e.tensor_tensor(out=tq[:, 0:bw], in0=pss[g][0][:, 0:bw],
                                                                         in1=bsb[:, g, :, :].rearrange("p j q -> p (j q)")[:, 0:bw], op=ALU.add),
                                   reads=[pss[g][1], bsb_k], writes=[tq_k])
                              K.op("dve", lambda e, g=g: e.tensor_tensor(out=usb[:, g, 0:bw], in0=tq[:, 0:bw], in1=ug[:, g, 0:bw], op=ALU.mult),
                                   reads=[tq_k, ug_k], writes=[usb_k])
                          K.dma("sp", sg["us"][:, :, b0:b0 + bw], usb[:, :, 0:bw], usb_k, reads=[usb_k], acc=[sg["us_k"]])
                          gst, gst_k = gst_r.next()
                          for m in range(24):
                              pg, pg_k = psr()
                              fm(2048 + m * 128, pg, pg_k)
                              K.op("act", lambda e, m=m: e.activation(out=gst[:, m, 0:bw], in_=pg[:, 0:bw], func=AF.Sigmoid), reads=[pg_k], writes=[gst_k])
                          K.dma("sp", sg["gat"][:, :, b0:b0 + bw], gst[:, :, 0:bw], gst_k, reads=[gst_k], acc=[sg["gat_k"]])
                  K.end_phase()
                  chk(l)

              with ExitStack() as ph:
                  PSR_HI[0] = 4
                  KSB = 4096
                  qtr = K.ring(ph, "qtb", 2, [128, NH, 512], BF16, dma=True)
                  ktr = K.ring(ph, "ktb", 2, [128, KSB], BF16, dma=True)
                  vtr = K.ring(ph, "vtb", 2, [128, KSB // 128, 65], BF16, dma=True)
                  ptr_ = K.ring(ph, "ptb", 4, [128, 512], BF16)
                  aor = K.ring(ph, "aob", 2, [64, NH, 512], BF16, dma=True)
                  rsr = K.ring(ph, "rsb", 2, [128, 512], F32)
                  rir = K.ring(ph, "rib", 2, [64, 512], F32)
                  scale = float(DQK) ** -0.5
                  for sg in segs:
                      n, S = sg["n"], sg["s"]
                      for (b0, bw) in blocks_of(n):
                          qt, qt_k = qtr.next()
                          K.dma("sp", qt[0:DQK, :, 0:bw], sg["qT"][:, :, b0:b0 + bw], qt_k, writes=[qt_k], dram_reads=[sg["qT_k"]])
                          ao, ao_k = aor.next()
                          for h in range(NH):
                              po, po_k = PSB[4 + (h % 2)]
                              first = True
                              for (s0, sw) in blocks_of(S, KSB):
                                  kt_, kt_k = ktr.next()
                                  vt_, vt_k = vtr.next()
                                  K.dma("sp", kt_[0:DQK, 0:sw], sg["kT"][h, :, s0:s0 + sw], kt_k, writes=[kt_k], dram_reads=[sg["kT_k"]])
                                  K.dma("sp", vt_[:, 0:sw // 128, :], sg["v"][h, :, s0 // 128:(s0 + sw) // 128, :], vt_k, writes=[vt_k], dram_reads=[sg["v_k"]])
                                  nk = sw // 128
                                  pend = None
                                  for ki in range(nk + 1):
                                      if ki < nk:
                                          ps, ps_k = psr(0, 4)
                                          K.op("pe", lambda e, ki=ki: e.matmul(ps[:, 0:bw], lhsT=kt_[0:DQK, ki * 128:(ki + 1) * 128], rhs=qt[0:DQK, h, 0:bw],
                                                                               start=True, stop=True),
                                               reads=[kt_k, qt_k], writes=[ps_k])
                                          pt, pt_k = ptr_.next()
                                          K.op("act", lambda e: e.activation(out=pt[:, 0:bw], in_=ps[:, 0:bw], func=AF.Exp, scale=scale), reads=[ps_k], writes=[pt_k])
                                          cur = (pt, pt_k, ki)
                                      else:
                                          cur = None
                                      if pend is not None:
                                          ppt, ppt_k, pki = pend
                                          last = (s0 + sw >= S) and (pki == nk - 1)
                                          K.op("pe", lambda e, pki=pki, ppt=ppt, f=first, last=last: e.matmul(po[0:65, 0:bw], lhsT=vt_[:, pki, 0:65], rhs=ppt[:, 0:bw],
                                                                                                            start=f, stop=last),
                                               reads=[vt_k, ppt_k], writes=[po_k])
                                          first = False
                                      pend = cur
                              rs, rs_k = rsr.next()
                              K.op("dve", lambda e: e.tensor_copy(out=rs[64:65, 0:bw], in_=po[64:65, 0:bw]), reads=[po_k], writes=[rs_k])
                              pb, pb_k = PSB[6 + (h % 2)]
                              K.op("pe", lambda e: e.matmul(pb[0:64, 0:bw], lhsT=ones_f[64:65, 0:64], rhs=rs[64:65, 0:bw], start=True, stop=True),
                                   reads=[onesf_k, rs_k], writes=[pb_k])
                              ri, ri_k = rir.next()
                              K.op("dve", lambda e: e.reciprocal(out=ri[:, 0:bw], in_=pb[0:64, 0:bw]), reads=[pb_k], writes=[ri_k])
                              K.op("dve", lambda e, h=h: e.tensor_tensor(out=ao[:, h, 0:bw], in0=po[0:64, 0:bw], in1=ri[:, 0:bw], op=ALU.mult),
                                   reads=[po_k, ri_k], writes=[ao_k])
                          K.dma("sp", sg["ao"][:, :, b0:b0 + bw], ao[:, :, 0:bw], ao_k, reads=[ao_k], acc=[sg["ao_k"]])
                  K.end_phase()
                  chk(l)

              with ExitStack() as ph:
                  PSR_HI[0] = 8
                  wao, wao_k = K.sb(ph, "wao", [64, NH, D], BF16, dma=True)
                  wco, wco_k = K.sb(ph, "wco", [128, 4, D], BF16, dma=True)
                  wso, wso_k = K.sb(ph, "wso", [128, 4, D], BF16, dma=True)
                  wout, wout_k = K.sb(ph, "wout", [128, KT, D], BF16, dma=True)
                  K.dma("pool", wao[:], Wl["w_ao"].rearrange("p (h n) -> p h n", h=NH), wao_k, writes=[wao_k])
                  K.dma("pool", wco[:], wview(Wl["w_co"]), wco_k, writes=[wco_k])
                  K.dma("pool", wso[:], wview(Wl["w_so"]), wso_k, writes=[wso_k])
                  K.dma("pool", wout[:], wview(Wl["w_out"]), wout_k, writes=[wout_k])
                  cdw, cdw_k = K.sb(ph, "cdw", [128, 4, 31], F32, dma=True)
                  cdwb, cdwb_k = K.sb(ph, "cdwb", [128, 4], F32, dma=True)
                  clng, clng_k = K.sb(ph, "clng", [128, 4], F32, dma=True)
                  clnb, clnb_k = K.sb(ph, "clnb", [128, 4], F32, dma=True)
                  K.dma("sp", cdw[:], Wl["cdw"].rearrange("p (c k) -> p c k", c=4), cdw_k, writes=[cdw_k])
                  K.dma("sp", cdwb[:], Wl["cdwb"], cdwb_k, writes=[cdwb_k])
                  K.dma("sp", clng[:], Wl["clng"], clng_k, writes=[clng_k])
                  K.dma("sp", clnb[:], Wl["clnb"], clnb_k, writes=[clnb_k])
                  dgt, dgt_k = K.sb(ph, "dgt", [128, 4, 31, 128], BF16)
                  for c in range(4):
                      for k in range(31):
                          K.op("dve", lambda e, c=c, k=k: e.tensor_scalar(out=dgt[:, c, k, :], in0=ident[:], scalar1=cdw[:, c, k:k + 1], scalar2=None, op0=ALU.mult),
                               reads=[ident_k, cdw_k], writes=[dgt_k])
                  g1t, g1_k = K.sb(ph, "g1t", [128, D], F32, dma=True)
                  glr = K.ring(ph, "glt", 2, [128, 4, 512 + 30], BF16, dma=True)
                  usr = K.ring(ph, "ust", 2, [128, 4, 512], BF16, dma=True)
                  gtr = K.ring(ph, "gtt", 1, [128, 24, 512], BF16, dma=True)
                  aor = K.ring(ph, "aot", 2, [64, NH, 512], BF16, dma=True)
                  hc, hc_k = K.sb(ph, "hc", [128, 4, 512], F32)
                  hcb, hcb_k = K.sb(ph, "hcb", [128, 4, 512], BF16)
                  sqb, sqb_k = K.sb(ph, "sqb", [128, 4, 512], BF16)
                  mean, mean_k = K.sb(ph, "mean", [128, 512], F32)
                  rstd, rstd_k = K.sb(ph, "rstd", [128, 512], F32)
                  cvn, cvn_k = K.sb(ph, "cvn", [128, 4, 512], BF16)
                  mrg, mrg_k = K.sb(ph, "mrg", [128, KT, 512], BF16)
                  tmr = K.ring(ph, "tm", 4, [128, 512], F32)
                  xr = K.ring(ph, "xr2", 2, [128, D], F32, dma=True)
                  for sg in segs:
                      n = sg["n"]
                      K.dma("sp", g1t[:], sg["mod"][:, 2 * D:3 * D], g1_k, writes=[g1_k], dram_reads=[sg["mod_k"]])
                      for (b0, bw) in blocks_of(n):
                          nj = bw // 128
                          glt, glt_k = glr.next()
                          K.dma("sp", glt[:, :, 0:bw + 30], sg["glu"][:, :, b0:b0 + bw + 30], glt_k, writes=[glt_k], dram_reads=[sg["glu_k"]])
                          ust, ust_k = usr.next()
                          K.dma("sp", ust[:, :, 0:bw], sg["us"][:, :, b0:b0 + bw], ust_k, writes=[ust_k], dram_reads=[sg["us_k"]])
                          gtt, gtt_k = gtr.next()
                          K.dma("sp", gtt[:, :, 0:bw], sg["gat"][:, :, b0:b0 + bw], gtt_k, writes=[gtt_k], dram_reads=[sg["gat_k"]])
                          aot, aot_k = aor.next()
                          K.dma("sp", aot[:, :, 0:bw], sg["ao"][:, :, b0:b0 + bw], aot_k, writes=[aot_k], dram_reads=[sg["ao_k"]])
                          for c in range(4):
                              ps, ps_k = psr()
                              for k in range(31):
                                  K.op("pe", lambda e, c=c, k=k: e.matmul(ps[:, 0:bw], lhsT=dgt[:, c, k, :], rhs=glt[:, c, k:k + bw], start=(k == 0), stop=(k == 30)),
                                       reads=[dgt_k, glt_k], writes=[ps_k], inc=(k == 30))
                              K.op("act", lambda e, c=c: e.activation(out=hc[:, c, 0:bw], in_=ps[:, 0:bw], func=AF.Identity, bias=cdwb[:, c:c + 1], scale=1.0),
                                   reads=[ps_k, cdwb_k], writes=[hc_k])
                          K.op("pool", lambda e: e.tensor_copy(out=hcb[:, :, 0:bw], in_=hc[:, :, 0:bw]), reads=[hc_k], writes=[hcb_k])
                          K.op("dve", lambda e: e.tensor_tensor(out=sqb[:, :, 0:bw], in0=hc[:, :, 0:bw], in1=hc[:, :, 0:bw], op=ALU.mult), reads=[hc_k], writes=[sqb_k])
                          p1, p1_k = psr()
                          p2, p2_k = psr()
                          for c in range(4):
                              K.op("pe", lambda e, c=c: e.matmul(p1[:, 0:bw], lhsT=ones_bf[:], rhs=hcb[:, c, 0:bw], start=(c == 0), stop=(c == 3)),
                                   reads=[ones_k, hcb_k], writes=[p1_k], inc=(c == 3))
                          for c in range(4):
                              K.op("pe", lambda e, c=c: e.matmul(p2[:, 0:bw], lhsT=ones_bf[:], rhs=sqb[:, c, 0:bw], start=(c == 0), stop=(c == 3)),
                                   reads=[ones_k, sqb_k], writes=[p2_k], inc=(c == 3))
                          K.op("dve", lambda e: e.tensor_scalar(out=mean[:, 0:bw], in0=p1[:, 0:bw], scalar1=1.0 / 512, scalar2=None, op0=ALU.mult), reads=[p1_k], writes=[mean_k])
                          tm, tm_k = tmr.next()
                          K.op("dve", lambda e: e.tensor_tensor(out=tm[:, 0:bw], in0=mean[:, 0:bw], in1=mean[:, 0:bw], op=ALU.mult), reads=[mean_k], writes=[tm_k])
                          K.op("dve", lambda e: e.scalar_tensor_tensor(out=tm[:, 0:bw], in0=p2[:, 0:bw], scalar=1.0 / 512, in1=tm[:, 0:bw], op0=ALU.mult, op1=ALU.subtract),
                               reads=[p2_k, tm_k], writes=[tm_k])
                          K.op("act", lambda e: e.activation(out=tm[:, 0:bw], in_=tm[:, 0:bw], func=AF.Sqrt, bias=EPS, scale=1.0), reads=[tm_k], writes=[tm_k])
                          K.op("dve", lambda e: e.reciprocal(out=rstd[:, 0:bw], in_=tm[:, 0:bw]), reads=[tm_k], writes=[rstd_k])
                          for c in range(4):
                              t2, t2_k = tmr.next()
                              K.op("dve", lambda e, c=c: e.tensor_tensor(out=t2[:, 0:bw], in0=hc[:, c, 0:bw], in1=mean[:, 0:bw], op=ALU.subtract), reads=[hc_k, mean_k], writes=[t2_k])
                              K.op("dve", lambda e: e.tensor_tensor(out=t2[:, 0:bw], in0=t2[:, 0:bw], in1=rstd[:, 0:bw], op=ALU.mult), reads=[t2_k, rstd_k], writes=[t2_k])
                              K.op("act", lambda e, c=c: e.activation(out=cvn[:, c, 0:bw], in_=t2[:, 0:bw], func=AF.Silu, bias=clnb[:, c:c + 1], scale=clng[:, c:c + 1]),
                                   reads=[t2_k, clnb_k, clng_k], writes=[cvn_k])
                          for j in range(8):
                              pa, pa_k = psr()
                              pc, pc_k = psr()
                              pss_, pss_k = psr()
                              for h in range(NH):
                                  K.op("pe", lambda e, h=h: e.matmul(pa[:, 0:bw], lhsT=wao[:, h, j * 128:(j + 1) * 128], rhs=aot[:, h, 0:bw], start=(h == 0), stop=(h == NH - 1)),
                                       reads=[wao_k, aot_k], writes=[pa_k], inc=(h == NH - 1))
                              for c in range(4):
                                  K.op("pe", lambda e, c=c: e.matmul(pc[:, 0:bw], lhsT=wco[:, c, j * 128:(j + 1) * 128], rhs=cvn[:, c, 0:bw], start=(c == 0), stop=(c == 3)),
                                       reads=[wco_k, cvn_k], writes=[pc_k], inc=(c == 3))
                              for c in range(4):
                                  K.op("pe", lambda e, c=c: e.matmul(pss_[:, 0:bw], lhsT=wso[:, c, j * 128:(j + 1) * 128], rhs=ust[:, c, 0:bw], start=(c == 0), stop=(c == 3)),
                                       reads=[wso_k, ust_k], writes=[pss_k], inc=(c == 3))
                              m1, m1_k = tmr.next()
                              m2, m2_k = tmr.next()
                              m3, m3_k = tmr.next()
                              K.op("dve", lambda e: e.tensor_tensor(out=m1[:, 0:bw], in0=pa[:, 0:bw], in1=gtt[:, j, 0:bw], op=ALU.mult), reads=[pa_k, gtt_k], writes=[m1_k])
                              K.op("dve", lambda e: e.tensor_tensor(out=m2[:, 0:bw], in0=pc[:, 0:bw], in1=gtt[:, 8 + j, 0:bw], op=ALU.mult), reads=[pc_k, gtt_k], writes=[m2_k])
                              K.op("dve", lambda e: e.tensor_tensor(out=m3[:, 0:bw], in0=pss_[:, 0:bw], in1=gtt[:, 16 + j, 0:bw], op=ALU.mult), reads=[pss_k, gtt_k], writes=[m3_k])
                              K.op("pool", lambda e: e.tensor_tensor(out=m1[:, 0:bw], in0=m1[:, 0:bw], in1=m2[:, 0:bw], op=ALU.add), reads=[m1_k, m2_k], writes=[m1_k])
                              K.op("pool", lambda e, j=j: e.tensor_tensor(out=mrg[:, j, 0:bw], in0=m1[:, 0:bw], in1=m3[:, 0:bw], op=ALU.add), reads=[m1_k, m3_k], writes=[mrg_k])
                          for t in range(nj):
                              t0 = b0 + t * 128
                              xt, xk = xr.next()
                              K.dma("sp", xt[:], sg["x"][t0:t0 + 128, :], xk, writes=[xk], dram_reads=[sg["x_k"]])
                              for half in range(2):
                                  po, po_k = psr()
                                  for j in range(8):
                                      K.op("pe", lambda e, j=j: e.matmul(po[:], lhsT=mrg[:, j, t * 128:(t + 1) * 128], rhs=wout[:, j, half * 512:(half + 1) * 512],
                                                                         start=(j == 0), stop=(j == 7)),
                                           reads=[mrg_k, wout_k], writes=[po_k], inc=(j == 7))
                                  tm2, tm2_k = tmr.next()
                                  K.op("dve", lambda e: e.tensor_tensor(out=tm2[:], in0=po[:], in1=g1t[:, half * 512:(half + 1) * 512], op=ALU.mult), reads=[po_k, g1_k], writes=[tm2_k])
                                  K.op("pool", lambda e: e.tensor_tensor(out=xt[:, half * 512:(half + 1) * 512], in0=xt[:, half * 512:(half + 1) * 512], in1=tm2[:], op=ALU.add),
                                       reads=[xk, tm2_k], writes=[xk])
                              K.dma("sp", sg["xm"][t0:t0 + 128, :], xt[:], xk, reads=[xk], acc=[sg["xm_k"]])
                  K.end_phase()
                  chk(l)

              with ExitStack() as ph:
                  P = dict(junk=K.ring(ph, "junkc", 2, [128, D], F32), ss=K.ring(ph, "ssc", 3, [128, 4], F32),
                           hb=K.ring(ph, "hbc", 2, [128, D], BF16))
                  xr = K.ring(ph, "xr3", 3, [128, D], F32, dma=True)
                  hst = K.ring(ph, "hst2", 2, [128, KT, 512], BF16, dma=True)
                  gamt, gam_k = K.sb(ph, "gam2", [128, D], F32, dma=True)
                  sht, sh_k = K.sb(ph, "sh2", [128, D], F32, dma=True)
                  for sg in segs:
                      n = sg["n"]
                      K.dma("sp", sht[:], sg["mod"][:, 3 * D:4 * D], sh_k, writes=[sh_k], dram_reads=[sg["mod_k"]])
                      K.dma("sp", gamt[:], sg["mod"][:, 4 * D:5 * D], gam_k, writes=[gam_k], dram_reads=[sg["mod_k"]])
                      for (b0, bw) in blocks_of(n):
                          nj = bw // 128
                          hs, hs_k = hst.next()
                          for j in range(nj):
                              t0 = b0 + j * 128
                              xt, xk = xr.next()
                              K.dma("sp", xt[:], sg["xm"][t0:t0 + 128, :], xk, writes=[xk], dram_reads=[sg["xm_k"]])
                              mc = None
                              if sg["prompt"] and t0 == 0:
                                  mc = 0
                              if sg["prompt"] and t0 == n - 128:
                                  mc = 1
                              hb, hb_k = norm_tile(P, xt, xk, (gamt, gam_k), (sht, sh_k), mask_col=mc)
                              transpose_to(P, hb, hb_k, hs[:, :, j * 128:(j + 1) * 128], hs_k)
                          K.dma("sp", sg["h2T"][:, :, 1 + b0:1 + b0 + bw], hs[:, :, 0:bw], hs_k, reads=[hs_k], acc=[sg["h2T_k"]])
                  K.end_phase()
                  chk(l)

              with ExitStack() as ph:
                  wup, wup_k = K.sb(ph, "wup", [128, KT, 2 * D_FF], BF16, dma=True)
                  wdn, wdn_k = K.sb(ph, "wdn", [128, NFT, D], BF16, dma=True)
                  wuv = wview(Wl["w_up"])
                  for c in range(0, 2 * D_FF, 704):
                      K.dma("pool", wup[:, :, c:c + 704], wuv[:, :, c:c + 704], wup_k, acc=[wup_k])
                  K.dma("pool", wdn[:], wview(Wl["w_down"]), wdn_k, writes=[wdn_k])
                  fdw, fdw_k = K.sb(ph, "fdw", [128, 44, 3], F32, dma=True)
                  fdwb, fdwb_k = K.sb(ph, "fdwb", [128, 44], F32, dma=True)
                  K.dma("sp", fdw[:], Wl["fdw"].rearrange("p (c k) -> p c k", c=44), fdw_k, writes=[fdw_k])
                  K.dma("sp", fdwb[:], Wl["fdwb"], fdwb_k, writes=[fdwb_k])
                  g2t, g2_k = K.sb(ph, "g2t", [128, D], F32, dma=True)
                  h2r = K.ring(ph, "h2b", 2, [128, KT, 514], BF16, dma=True)
                  zr = K.ring(ph, "zt_", 3, [128, 514], F32)
                  accr = K.ring(ph, "acc", 4, [128, 512], F32)
                  sgr = K.ring(ph, "sgf", 2, [128, 512], F32)
                  uT, uT_k = K.sb(ph, "uT", [128, NFT, 512], BF16)
                  xr = K.ring(ph, "xr4", 2, [128, D], F32, dma=True)
                  tmr = K.ring(ph, "tm4", 2, [128, 512], F32)
                  for sg in segs:
                      n = sg["n"]
                      K.dma("sp", g2t[:], sg["mod"][:, 5 * D:6 * D], g2_k, writes=[g2_k], dram_reads=[sg["mod_k"]])
                      for (b0, bw) in blocks_of(n):
                          nj = bw // 128
                          h2, h2_k = h2r.next()
                          K.dma("sp", h2[:, :, 0:bw + 2], sg["h2T"][:, :, b0:b0 + bw + 2], h2_k, writes=[h2_k], dram_reads=[sg["h2T_k"]])
                          half = (bw + 2) // 2
                          for i in range(NFT):
                              accs = []
                              for which in range(2):
                                  ci = which * NFT + i
                                  col0 = ci * 128
                                  z, z_k = zr.next()
                                  for (c0, c1) in ((0, half), (half, bw + 2)):
                                      ps, ps_k = psr(0, 8)
                                      for kt in range(KT):
                                          K.op("pe", lambda e, kt=kt: e.matmul(ps[:, 0:c1 - c0], lhsT=wup[:, kt, col0:col0 + 128], rhs=h2[:, kt, c0:c1],
                                                                               start=(kt == 0), stop=(kt == KT - 1)),
                                               reads=[wup_k, h2_k], writes=[ps_k], inc=(kt == KT - 1))
                                      K.op("act", lambda e: e.activation(out=z[:, c0:c1], in_=ps[:, 0:c1 - c0], func=AF.Copy), reads=[ps_k], writes=[z_k])
                                  acc, acc_k = accr.next()
                                  K.op("dve", lambda e: e.tensor_scalar(out=acc[:, 0:bw], in0=z[:, 0:bw], scalar1=fdw[:, ci, 0:1], scalar2=fdwb[:, ci:ci + 1],
                                                                        op0=ALU.mult, op1=ALU.add),
                                       reads=[z_k, fdw_k, fdwb_k], writes=[acc_k])
                                  K.op("dve", lambda e: e.scalar_tensor_tensor(out=acc[:, 0:bw], in0=z[:, 1:bw + 1], scalar=fdw[:, ci, 1:2], in1=acc[:, 0:bw],
                                                                               op0=ALU.mult, op1=ALU.add),
                                       reads=[z_k, fdw_k, acc_k], writes=[acc_k])
                                  K.op("dve", lambda e: e.scalar_tensor_tensor(out=acc[:, 0:bw], in0=z[:, 2:bw + 2], scalar=fdw[:, ci, 2:3], in1=acc[:, 0:bw],
                                                                               op0=ALU.mult, op1=ALU.add),
                                       reads=[z_k, fdw_k, acc_k], writes=[acc_k])
                                  accs.append((acc, acc_k))
                              sgf, sgf_k = sgr.next()
                              K.op("act", lambda e: e.activation(out=sgf[:, 0:bw], in_=accs[0][0][:, 0:bw], func=AF.Silu), reads=[accs[0][1]], writes=[sgf_k])
                              K.op("pool", lambda e, i=i: e.tensor_tensor(out=uT[:, i, 0:bw], in0=sgf[:, 0:bw], in1=accs[1][0][:, 0:bw], op=ALU.mult),
                                   reads=[sgf_k, accs[1][1]], writes=[uT_k])
                          for t in range(nj):
                              t0 = b0 + t * 128
                              if sg["prompt"] and (t0 < HALO or t0 >= n - HALO):
                                  continue
                              xt, xk = xr.next()
                              K.dma("sp", xt[:], sg["xm"][t0:t0 + 128, :], xk, writes=[xk], dram_reads=[sg["xm_k"]])
                              for hf in range(2):
                                  po, po_k = psr(0, 8)
                                  for i in range(NFT):
                                      K.op("pe", lambda e, i=i: e.matmul(po[:], lhsT=uT[:, i, t * 128:(t + 1) * 128], rhs=wdn[:, i, hf * 512:(hf + 1) * 512],
                                                                         start=(i == 0), stop=(i == NFT - 1)),
                                           reads=[uT_k, wdn_k], writes=[po_k], inc=(i == NFT - 1))
                                  tm2, tm2_k = tmr.next()
                                  K.op("dve", lambda e: e.tensor_tensor(out=tm2[:], in0=po[:], in1=g2t[:, hf * 512:(hf + 1) * 512], op=ALU.mult), reads=[po_k, g2_k], writes=[tm2_k])
                                  K.op("pool", lambda e: e.tensor_tensor(out=xt[:, hf * 512:(hf + 1) * 512], in0=xt[:, hf * 512:(hf + 1) * 512], in1=tm2[:], op=ALU.add),
                                       reads=[xk, tm2_k], writes=[xk])
                              yoff = t0 - HALO if sg["prompt"] else t0
                              K.dma("sp", sg["y"][yoff:yoff + 128, :], xt[:], xk, reads=[xk], acc=[sg["y_k"]])
                  K.end_phase()
                  chk(l)
        except _Stop:
            print('STOPPED at', _stop)
        K.barrier()
        print("instructions emitted:", K.nops)
    return nc


def rope_table(pos):
    inv = np.power(np.float32(10000.0), -np.arange(0, 32, 2, dtype=np.float32) / np.float32(32)).astype(np.float32)
    ang = pos.astype(np.float32)[:, None] * inv[None, :]
    return np.concatenate([np.cos(ang), np.sin(ang)], axis=1).astype(np.float32)


def layer_weights(inp, l):
    f = lambda a: np.ascontiguousarray(a, dtype=np.float32)
    ukv = inp["w_ukv"][l].reshape(128, NH, 128)
    w = dict(
        w_ada=f(inp["w_ada"][l]), b_ada=f(inp["b_ada"][l][None, :]), norm1=f(inp["norm1"][l][None, :]), norm2=f(inp["norm2"][l][None, :]),
        w_in=f(inp["w_in"][l]), w_uq=f(inp["w_uq"][l]),
        w_ukv=f(np.concatenate([ukv[:, :, 0:64].reshape(128, 512), ukv[:, :, 64:128].reshape(128, 512)], axis=1)),
        qan=f(inp["q_a_norm"][l].reshape(2, 128).T), kvan=f(inp["kv_a_norm"][l].reshape(128, 1)),
        qhn=f(inp["q_head_norm"][l][None, :]), khn=f(inp["k_head_norm"][l][None, :]),
        w_ao=f(inp["w_attn_o"][l].reshape(NH, 64, D).transpose(1, 0, 2).reshape(64, NH * D)),
        cdw=f(inp["conv_dw"][l].T.reshape(4, 128, 31).transpose(1, 0, 2).reshape(128, 4 * 31)),
        cdwb=f(inp["conv_dw_b"][l].reshape(4, 128).T), clng=f(inp["conv_ln_g"][l].reshape(4, 128).T), clnb=f(inp["conv_ln_b"][l].reshape(4, 128).T),
        w_co=f(inp["w_conv_o"][l]), sglng=f(inp["sg_ln_g"][l][None, :]), sglnb=f(inp["sg_ln_b"][l][None, :]),
        sgwT=f(inp["sg_w"][l].transpose(2, 0, 1).reshape(128, 4 * 128)),
        sgb=f(inp["sg_b"][l].reshape(1, 512)), w_so=f(inp["w_sg_o"][l]), w_out=f(inp["w_out"][l]), w_up=f(inp["w_up"][l]),
        fdw=f(inp["ffn_dw"][l].T.reshape(44, 128, 3).transpose(1, 0, 2).reshape(128, 44 * 3)),
        fdwb=f(inp["ffn_dw_b"][l].reshape(44, 128).T), w_down=f(inp["w_down"][l]))
    return w


_PROG = {}


def run_model(inp, cfg, n_cores=8):
    NS, SS, PCH, PS = cfg["NS"], cfg["SS"], cfg["PCH"], cfg["PS"]
    PSEG = PCH + 2 * HALO
    key = (NS, SS, PCH, PS)
    L = inp["w_ada"].shape[0]
    assert L == 2
    if key not in _PROG:
        _PROG[key] = build_program(cfg, L)
    nc = _PROG[key]
    xp = np.asarray(inp["x_prompt"], dtype=np.float32)
    xs = np.asarray(inp["x_sample"], dtype=np.float32)
    cp = np.asarray(inp["c_prompt"], dtype=np.float32)
    cs = np.asarray(inp["c_sample"], dtype=np.float32)
    nchunk = PS // PCH
    rope_s = rope_table(np.arange(SS))
    rope_pc = rope_table(np.arange(PS))
    wls = [layer_weights(inp, l) for l in range(L)]
    in_maps = []
    for c in range(n_cores):
        b = c // nchunk
        r = c % nchunk
        lo = r * PCH - HALO
        pm = np.zeros((128, 2), np.float32)
        pm[:, 0] = 1.0 if r > 0 else 0.0
        pm[:, 1] = 1.0 if r < nchunk - 1 else 0.0
        sel = np.zeros((128, nchunk), np.float32)
        sel[:, r] = 1.0
        m = dict(xs=np.ascontiguousarray(xs[c * NS:(c + 1) * NS].reshape(NS * SS, D)),
                 xpctx=np.ascontiguousarray(xp[b]),
                 cvT=np.ascontiguousarray(np.concatenate([cs[c * NS:(c + 1) * NS], cp[b:b + 1]], axis=0).reshape((NS + 1) * KT, 128).T),
                 pmask=pm, psel=sel, rope_s=rope_s, rope_pc=rope_pc, rope_pq=rope_table(np.arange(lo, lo + PSEG)))
        for l in range(L):
            for k, v in wls[l].items():
                m["%s_%d" % (k, l)] = v
        in_maps.append(m)
    res = run_bass_kernel_spmd(nc, in_maps, core_ids=list(range(n_cores)))
    ys = np.stack([np.asarray(r_["ys"]).reshape(NS, SS, D) for r_ in res.results], axis=0).reshape(n_cores * NS, SS, D)
    ypo = np.stack([np.asarray(r_["yp"]) for r_ in res.results], axis=0).reshape(n_cores // nchunk, PS, D)
    return ypo.astype(np.float32), ys.astype(np.float32)


def kernel(**inputs):
    yp, ys = run_model(inputs, CFG, 8)
    return (yp, ys)
```

```python
import numpy as np
from contextlib import ExitStack
import concourse.bass as bass
import concourse.mybir as mybir
from concourse.bass_utils import run_bass_kernel_spmd

F32 = mybir.dt.float32
BF16 = mybir.dt.bfloat16
AF = mybir.ActivationFunctionType
ALU = mybir.AluOpType
AX = mybir.AxisListType

D = 1024
KT = 8
NH = 8
DQK = 96
EPS = 1e-6
D_IN = 5536
D_FF = 2816
NFT = 22
HALO = 128

CFG = dict(NS=4, SS=2048, PCH=2048, PS=8192)


class Trk:
    __slots__ = ("w", "r", "dsem", "dcnt")

    def __init__(self):
        self.w = {}
        self.r = {}
        self.dsem = None
        self.dcnt = 0


class KB:
    def __init__(self, nc, es):
        self.nc = nc
        self.es = es
        self.eng = {"pe": nc.tensor, "act": nc.scalar, "dve": nc.vector, "pool": nc.gpsimd, "sp": nc.sync}
        self.sem = {k: es.enter_context(nc.semaphore("s_" + k)) for k in ["pe", "act", "dve", "pool"]}
        self.cnt = {k: 0 for k in self.sem}
        self.seen = {k: {} for k in self.eng}
        self.pend = {k: [] for k in self.eng}
        self.dma_trks = []
        self.nops = 0
        self.sem_free = []
        self.phase_trks = []
        self.nsem = 0

    def get_dsem(self):
        if self.sem_free:
            return self.sem_free.pop()
        self.nsem += 1
        return (self.es.enter_context(self.nc.semaphore("dq%d" % self.nsem)), 0)

    def release(self, trks):
        for k in trks:
            if k.dsem is not None:
                self.sem_free.append((k.dsem, k.dcnt))
                if k in self.dma_trks:
                    self.dma_trks.remove(k)

    def _wait(self, e, sem, val):
        d = self.seen[e]
        if d.get(sem, 0) >= val:
            return
        self.eng[e].wait_ge(sem, val)
        d[sem] = val

    def _deps(self, e, reads, writes):
        for t in reads:
            for s, (v, ek) in t.w.items():
                self._wait(e, s, v)
        for t in writes:
            for s, (v, ek) in t.w.items():
                if ek != e:
                    self._wait(e, s, v)
            for ek, (s, v) in t.r.items():
                if ek != e:
                    self._wait(e, s, v)

    def op(self, e, fn, reads=(), writes=(), inc=True):
        if getattr(self, 'skip', False):
            return
        self._deps(e, reads, writes)
        ins = fn(self.eng[e])
        self.nops += 1
        if not inc:
            self.pend[e].append((reads, writes))
            return
        self.cnt[e] += 1
        c = self.cnt[e]
        s = self.sem[e]
        ins.then_inc(s, 1)
        self.pend[e].append((reads, writes))
        for rd, wr in self.pend[e]:
            for t in wr:
                t.w = {s: (c, e)}
                t.r = {}
        for rd, wr in self.pend[e]:
            for t in rd:
                t.r[e] = (s, c)
        self.pend[e] = []

    def dma(self, q, out, in_, own, reads=(), writes=(), acc=(), dram_reads=(), slow=False):
        if getattr(self, 'skip', False):
            return
        self._deps(q, list(reads) + list(dram_reads), writes)
        for t in acc:
            for ek, (s, v) in t.r.items():
                self._wait(q, s, v)
        own.dcnt += 16
        if slow:
            self.eng[q].dma_start(out=out, in_=in_, allow_slow_non_contiguous=True).then_inc(own.dsem, 16)
        else:
            self.eng[q].dma_start(out=out, in_=in_).then_inc(own.dsem, 16)
        self.nops += 1
        key = ("dma", own.dsem)
        for t in writes:
            t.w = {own.dsem: (own.dcnt, key)}
            t.r = {}
        for t in acc:
            t.w[own.dsem] = (own.dcnt, key)
        for t in reads:
            t.r[key] = (own.dsem, own.dcnt)

    def end_phase(self):
        self.barrier()
        self.release(list(self.phase_trks))
        self.phase_trks = []

    def barrier(self):
        for e in self.eng:
            for k in self.sem:
                if k != e and self.cnt[k] > 0:
                    self._wait(e, self.sem[k], self.cnt[k])
            for t in self.dma_trks:
                if t.dcnt > 0:
                    self._wait(e, t.dsem, t.dcnt)

    def sb(self, es, name, shape, dt, dma=False):
        self.nalloc = getattr(self, "nalloc", 0) + 1
        t = es.enter_context(self.nc.sbuf_tensor("%s_u%d" % (name, self.nalloc), list(shape), dt))
        k = Trk()
        if dma:
            k.dsem, k.dcnt = self.get_dsem()
            self.dma_trks.append(k)
            self.phase_trks.append(k)
        return t, k

    def ring(self, es, name, n, shape, dt, dma=False):
        return Ring([self.sb(es, "%s%d" % (name, i), shape, dt, dma) for i in range(n)])


class Ring:
    def __init__(self, items):
        self.items = items
        self.i = 0

    def next(self):
        it = self.items[self.i % len(self.items)]
        self.i += 1
        return it


def blocks_of(n, bs=512):
    out = []
    o = 0
    while o < n:
        b = min(bs, n - o)
        out.append((o, b))
        o += b
    return out


def build_program(cfg, n_layers=1):
    NS, SS, PCH, PS = cfg["NS"], cfg["SS"], cfg["PCH"], cfg["PS"]
    PSEG = PCH + 2 * HALO
    nc = bass.Bass("TRN2", target_bir_lowering=False)

    def din(name, shape):
        return nc.dram_tensor(name, list(shape), F32, kind="ExternalInput").ap()

    xs = din("xs", [NS * SS, D])
    psel = din("psel", [128, PS // PCH])
    xpctx = din("xpctx", [PS, D])
    cvT = din("cvT", [128, (NS + 1) * KT])
    pmask = din("pmask", [128, 2])
    rope_s = din("rope_s", [SS, 32])
    rope_pc = din("rope_pc", [PS, 32])
    rope_pq = din("rope_pq", [PSEG, 32])
    W = {}
    wshapes = dict(
        w_ada=[D, 6 * D], b_ada=[1, 6 * D], norm1=[1, D], norm2=[1, D], w_in=[D, D_IN],
        w_uq=[256, 768], w_ukv=[128, 1024], qan=[128, 2], kvan=[128, 1], qhn=[1, 96], khn=[1, 96],
        w_ao=[64, 8 * D], cdw=[128, 4 * 31], cdwb=[128, 4], clng=[128, 4], clnb=[128, 4],
        w_co=[512, D], sglng=[1, 512], sglnb=[1, 512], sgwT=[128, 4 * 128], sgb=[1, 512], w_so=[512, D],
        w_out=[D, D], w_up=[D, 2 * D_FF], fdw=[128, 44 * 3], fdwb=[128, 44], w_down=[D_FF, D])
    for l in range(n_layers):
        for k, shp in wshapes.items():
            W[(l, k)] = din("%s_%d" % (k, l), shp)
    ys = nc.dram_tensor("ys", [NS * SS, D], F32, kind="ExternalOutput").ap()
    yp = nc.dram_tensor("yp", [PCH, D], F32, kind="ExternalOutput").ap()

    x1s = nc.dram_tensor("x1s", [NS * SS, D], F32).ap()
    x1full = nc.dram_tensor("x1full", [PS, D], F32).ap()
    x1seg = nc.dram_tensor("x1seg", [PSEG, D], F32).ap()
    x1s_k = [Trk() for _ in range(NS)]
    x1full_k = Trk()
    x1seg_k = Trk()
    assert n_layers == 2

    def make_segs(l):
        segs = []
        for i in range(NS):
            if l == 0:
                segs.append(dict(n=SS, s=SS, x=xs[i * SS:(i + 1) * SS, :], x_k=Trk(), ctx=None, ctx_k=None, rq=rope_s, rk=rope_s,
                                 y=x1s[i * SS:(i + 1) * SS, :], y_k=x1s_k[i], prompt=False))
            else:
                segs.append(dict(n=SS, s=SS, x=x1s[i * SS:(i + 1) * SS, :], x_k=x1s_k[i], ctx=None, ctx_k=None, rq=rope_s, rk=rope_s,
                                 y=ys[i * SS:(i + 1) * SS, :], y_k=Trk(), prompt=False))
        if l == 0:
            segs.append(dict(n=PS, s=PS, x=xpctx, x_k=Trk(), ctx=None, ctx_k=None, rq=rope_pc, rk=rope_pc,
                             y=x1full, y_k=x1full_k, prompt=False))
        else:
            segs.append(dict(n=PSEG, s=PS, x=x1seg, x_k=x1seg_k, ctx=x1full, ctx_k=x1full_k, rq=rope_pq, rk=rope_pc,
                             y=yp, y_k=Trk(), prompt=True))
        for si, sg in enumerate(segs):
            n, s_ = sg["n"], sg["s"]

            def dsc(nm, shape, dt=BF16):
                return nc.dram_tensor("%s_%d_%d" % (nm, l, si), list(shape), dt).ap()
            sg["hT"] = dsc("hT", [128, KT, n]); sg["hT_k"] = Trk()
            sg["qT"] = dsc("qT", [DQK, NH, n]); sg["qT_k"] = Trk()
            sg["kT"] = dsc("kT", [NH, DQK, s_]); sg["kT_k"] = Trk()
            sg["v"] = dsc("v", [NH, 128, s_ // 128, 65]); sg["v_k"] = Trk()
            sg["gat"] = dsc("gat", [128, 24, n]); sg["gat_k"] = Trk()
            sg["glu"] = dsc("glu", [128, 4, n + 30]); sg["glu_k"] = Trk()
            sg["us"] = dsc("us", [128, 4, n]); sg["us_k"] = Trk()
            sg["ao"] = dsc("ao", [64, NH, n]); sg["ao_k"] = Trk()
            sg["xm"] = dsc("xm", [n, D], F32); sg["xm_k"] = Trk()
            sg["h2T"] = dsc("h2T", [128, KT, n + 2]); sg["h2T_k"] = Trk()
            sg["mod"] = dsc("mod", [128, 6 * D], F32); sg["mod_k"] = Trk()
        return segs

    all_segs = [make_segs(l) for l in range(n_layers)]

    with ExitStack() as es:
        K = KB(nc, es)
        ident, ident_k = K.sb(es, "ident", [128, 128], BF16)
        ones_bf, ones_k = K.sb(es, "ones_bf", [128, 128], BF16)
        ones_f, onesf_k = K.sb(es, "ones_f", [128, 64], F32)
        zt, zt_k = K.sb(es, "zt", [128, 64], BF16, dma=True)
        mk, mk_k = K.sb(es, "mk", [128, 2], F32, dma=True)
        K.op("dve", lambda e: e.memset(ident[:], 1.0), writes=[ident_k])
        K.op("pool", lambda e: e.affine_select(out=ident[:], in_=ident[:], pattern=[[-1, 128]],
                                               compare_op=ALU.is_equal, fill=0.0, base=0, channel_multiplier=1),
             reads=[ident_k], writes=[ident_k])
        K.op("dve", lambda e: e.memset(ones_bf[:], 1.0), writes=[ones_k])
        K.op("dve", lambda e: e.memset(ones_f[:], 1.0), writes=[onesf_k])
        K.op("dve", lambda e: e.memset(zt[:], 0.0), writes=[zt_k])
        K.dma("sp", mk[:], pmask, mk_k, writes=[mk_k])
        K.phase_trks = []
        PSB = []
        for i in range(8):
            t = es.enter_context(nc.psum_tensor("psb%d" % i, [128, 512], F32))
            PSB.append((t, Trk()))
        psr_state = [0]

        PSR_HI = [8]

        def psr(lo=0, hi=None):
            if hi is None:
                hi = PSR_HI[0]
            i = lo + psr_state[0] % (hi - lo)
            psr_state[0] += 1
            return PSB[i]

        for sg in [g for sl in all_segs for g in sl]:
            n = sg["n"]
            K.dma("sp", sg["glu"][:, :, 0:15], zt[:, 0:60].rearrange("p (a b) -> p a b", a=4), zt_k, reads=[zt_k], acc=[sg["glu_k"]])
            K.dma("sp", sg["glu"][:, :, n + 15:n + 30], zt[:, 0:60].rearrange("p (a b) -> p a b", a=4), zt_k, reads=[zt_k], acc=[sg["glu_k"]])
            K.dma("sp", sg["h2T"][:, :, 0:1], zt[:, 0:8].rearrange("p (a b) -> p a b", a=8), zt_k, reads=[zt_k], acc=[sg["h2T_k"]], slow=True)
            K.dma("sp", sg["h2T"][:, :, n + 1:n + 2], zt[:, 0:8].rearrange("p (a b) -> p a b", a=8), zt_k, reads=[zt_k], acc=[sg["h2T_k"]], slow=True)

        def wview(ap_, p=128):
            return ap_.rearrange("(kt p) n -> p kt n", p=p)

        def norm_tile(P, xt, xk, gam, sh, mask_col=None):
            junk, junk_k = P["junk"].next()
            ss, ss_k = P["ss"].next()
            K.op("dve", lambda e: e.memset(ss[:], 0.0), writes=[ss_k])
            K.op("act", lambda e: e.activation(out=junk[:], in_=xt[:], func=AF.Square, accum_out=ss[:, 0:1]),
                 reads=[xk, ss_k], writes=[junk_k, ss_k])
            K.op("act", lambda e: e.activation(out=ss[:, 1:2], in_=ss[:, 0:1], func=AF.Sqrt, bias=EPS, scale=1.0 / D),
                 reads=[ss_k], writes=[ss_k])
            K.op("dve", lambda e: e.reciprocal(out=ss[:, 2:3], in_=ss[:, 1:2]), reads=[ss_k], writes=[ss_k])
            K.op("dve", lambda e: e.scalar_tensor_tensor(out=junk[:], in0=xt[:], scalar=ss[:, 2:3], in1=gam[0][:],
                                                         op0=ALU.mult, op1=ALU.mult),
                 reads=[xk, ss_k, gam[1], junk_k], writes=[junk_k])
            hb, hb_k = P["hb"].next()
            K.op("pool", lambda e: e.tensor_tensor(out=hb[:], in0=junk[:], in1=sh[0][:], op=ALU.add),
                 reads=[junk_k, sh[1]], writes=[hb_k])
            if mask_col is not None:
                K.op("dve", lambda e: e.tensor_scalar(out=hb[:], in0=hb[:], scalar1=mk[:, mask_col:mask_col + 1],
                                                      scalar2=None, op0=ALU.mult),
                     reads=[hb_k, mk_k], writes=[hb_k])
            return hb, hb_k

        def transpose_to(P, hb, hb_k, dst, dst_k, ncols_src=D, rows=128, chunk=128):
            nchunk = ncols_src // chunk
            ps, ps_k = psr()
            psb = ps[:].bitcast(BF16)
            for i in range(nchunk):
                K.op("pe", lambda e, i=i: e.transpose(psb[0:chunk, i * 128:(i + 1) * 128], hb[:, i * chunk:(i + 1) * chunk], ident[:]),
                     reads=[hb_k, ident_k], writes=[ps_k], inc=(i == nchunk - 1))
            K.op("act", lambda e: e.activation(out=dst, in_=psb[0:chunk, 0:nchunk * 128].rearrange("p (a b) -> p a b", a=nchunk),
                                               func=AF.Copy),
                 reads=[ps_k], writes=[dst_k])

        def headnorm_rope(P, f, f_k, gain, rp, rp_k, out, out_k):
            f3 = f[:].rearrange("p (h d) -> p h d", h=NH)
            o3 = out[:].rearrange("p (h d) -> p h d", h=NH)
            sq, sq_k = P["sq"].next()
            st, st_k = P["st"].next()
            K.op("dve", lambda e: e.tensor_tensor(out=sq[:], in0=f[:], in1=f[:], op=ALU.mult), reads=[f_k], writes=[sq_k])
            K.op("dve", lambda e: e.reduce_sum(out=st[:, 0:8], in_=sq[:].rearrange("p (h d) -> p h d", h=NH), axis=AX.X),
                 reads=[sq_k], writes=[st_k])
            K.op("act", lambda e: e.activation(out=st[:, 8:16], in_=st[:, 0:8], func=AF.Sqrt, bias=EPS, scale=1.0 / DQK),
                 reads=[st_k], writes=[st_k])
            K.op("dve", lambda e: e.reciprocal(out=st[:, 16:24], in_=st[:, 8:16]), reads=[st_k], writes=[st_k])
            K.op("dve", lambda e: e.tensor_tensor(out=f3, in0=f3, in1=st[:, 16:24].unsqueeze(2).to_broadcast([128, NH, DQK]), op=ALU.mult),
                 reads=[f_k, st_k], writes=[f_k])
            K.op("dve", lambda e: e.tensor_tensor(out=f3, in0=f3, in1=gain[0][:].unsqueeze(1).to_broadcast([128, NH, DQK]), op=ALU.mult),
                 reads=[f_k, gain[1]], writes=[f_k])
            tt, tt_k = P["tt"].next()
            t4 = tt[:].rearrange("p (a h d) -> p a h d", a=4, h=NH)
            x1 = f3[:, :, 64:80]
            x2 = f3[:, :, 80:96]
            cs = rp[:, 0:16].unsqueeze(1).to_broadcast([128, NH, 16])
            sn = rp[:, 16:32].unsqueeze(1).to_broadcast([128, NH, 16])
            K.op("dve", lambda e: e.tensor_tensor(out=t4[:, 0], in0=x1, in1=cs, op=ALU.mult), reads=[f_k, rp_k], writes=[tt_k])
            K.op("dve", lambda e: e.tensor_tensor(out=t4[:, 1], in0=x2, in1=sn, op=ALU.mult), reads=[f_k, rp_k], writes=[tt_k])
            K.op("dve", lambda e: e.tensor_tensor(out=t4[:, 2], in0=x1, in1=sn, op=ALU.mult), reads=[f_k, rp_k], writes=[tt_k])
            K.op("dve", lambda e: e.tensor_tensor(out=t4[:, 3], in0=x2, in1=cs, op=ALU.mult), reads=[f_k, rp_k], writes=[tt_k])
            K.op("dve", lambda e: e.tensor_tensor(out=o3[:, :, 64:80], in0=t4[:, 0], in1=t4[:, 1], op=ALU.subtract), reads=[tt_k], writes=[out_k])
            K.op("dve", lambda e: e.tensor_tensor(out=o3[:, :, 80:96], in0=t4[:, 2], in1=t4[:, 3], op=ALU.add), reads=[tt_k], writes=[out_k])
            K.op("pool", lambda e: e.tensor_copy(out=o3[:, :, 0:64], in_=f3[:, :, 0:64]), reads=[f_k], writes=[out_k])

        import os as _os
        _stop = _os.environ.get("KSTOP", "")
        _phc = [0]

        class _Stop(Exception):
            pass

        def chk(l):
            _phc[0] += 1
            if _stop and _stop == "%d,%d" % (l, _phc[0]):
                K.skip = True
                print('STOPPED at', _stop)

        try:
          for l in range(n_layers):
              _phc[0] = 0
              Wl = {k: W[(l, k)] for k in wshapes}
              segs = all_segs[l]
              if l == 1:
                  with ExitStack() as ph:
                      selt, selt_k = K.sb(ph, "selt", [128, PS // PCH], F32, dma=True)
                      K.dma("sp", selt[:], psel, selt_k, writes=[selt_k])
                      accr_ = K.ring(ph, "xacc", 2, [128, D], F32, dma=True)
                      ldr_ = K.ring(ph, "xld", 4, [128, D], F32, dma=True)
                      for j in range(PSEG // 128):
                          ac, ac_k = accr_.next()
                          K.op("dve", lambda e: e.memset(ac[:], 0.0), writes=[ac_k])
                          for r_ in range(PS // PCH):
                              row = r_ * PCH - HALO + j * 128
                              if row < 0 or row + 128 > PS:
                                  continue
                              ld, ld_k = ldr_.next()
                              K.dma("sp", ld[:], x1full[row:row + 128, :], ld_k, writes=[ld_k], dram_reads=[x1full_k])
                              K.op("dve", lambda e, r_=r_: e.scalar_tensor_tensor(out=ac[:], in0=ld[:], scalar=selt[:, r_:r_ + 1], in1=ac[:],
                                                                                  op0=ALU.mult, op1=ALU.add),
                                   reads=[ld_k, selt_k, ac_k], writes=[ac_k])
                          K.dma("sp", x1seg[j * 128:(j + 1) * 128, :], ac[:], ac_k, reads=[ac_k], acc=[x1seg_k])
                      K.end_phase()
                      chk(l)
              with ExitStack() as ph:
                  PSR_HI[0] = 8
                  nseg = len(segs)
                  cT, cT_k = K.sb(ph, "cT", [128, nseg * KT], F32, dma=True)
                  crep, crep_k = K.sb(ph, "crep", [128, nseg * KT, 128], BF16)
                  bb, bb_k = K.sb(ph, "bb", [1, 6 * D], BF16, dma=True)
                  n1b, n1b_k = K.sb(ph, "n1b", [128, D], F32, dma=True)
                  n2b, n2b_k = K.sb(ph, "n2b", [128, D], F32, dma=True)
                  wr = K.ring(ph, "wada", 2, [128, KT, 512], BF16, dma=True)
                  modt = [K.sb(ph, "modt%d" % s, [128, 6 * D], F32, dma=True) for s in range(nseg)]
                  K.dma("sp", cT[:], cvT, cT_k, writes=[cT_k])
                  K.op("act", lambda e: e.activation(out=cT[:], in_=cT[:], func=AF.Silu), reads=[cT_k], writes=[cT_k])
                  K.op("dve", lambda e: e.tensor_copy(out=crep[:], in_=cT[:].unsqueeze(2).to_broadcast([128, nseg * KT, 128])),
                       reads=[cT_k], writes=[crep_k])
                  K.dma("pool", bb[:], Wl["b_ada"], bb_k, writes=[bb_k])
                  K.dma("sp", n1b[:], Wl["norm1"][0, :].partition_broadcast(128), n1b_k, writes=[n1b_k])
                  K.dma("sp", n2b[:], Wl["norm2"][0, :].partition_broadcast(128), n2b_k, writes=[n2b_k])
                  wav = wview(Wl["w_ada"])
                  for c in range(12):
                      wt, wt_k = wr.next()
                      K.dma("pool", wt[:], wav[:, :, c * 512:(c + 1) * 512], wt_k, writes=[wt_k])
                      for s in range(nseg):
                          ps, ps_k = psr()
                          for kt in range(KT):
                              K.op("pe", lambda e, kt=kt: e.matmul(ps[:], lhsT=crep[:, s * KT + kt, :], rhs=wt[:, kt, :],
                                                                   start=(kt == 0), stop=False),
                                   reads=[crep_k, wt_k], writes=[ps_k], inc=False)
                          K.op("pe", lambda e: e.matmul(ps[:], lhsT=ones_bf[0:1, :], rhs=bb[0:1, c * 512:(c + 1) * 512],
                                                        start=False, stop=True),
                               reads=[ones_k, bb_k], writes=[ps_k])
                          mt, mt_k = modt[s]
                          K.op("act", lambda e: e.activation(out=mt[:, c * 512:(c + 1) * 512], in_=ps[:], func=AF.Copy),
                               reads=[ps_k], writes=[mt_k])
                  for s in range(nseg):
                      mt, mt_k = modt[s]
                      K.op("dve", lambda e: e.scalar_tensor_tensor(out=mt[:, D:2 * D], in0=mt[:, D:2 * D], scalar=1.0, in1=n1b[:],
                                                                   op0=ALU.add, op1=ALU.mult),
                           reads=[mt_k, n1b_k], writes=[mt_k])
                      K.op("dve", lambda e: e.scalar_tensor_tensor(out=mt[:, 4 * D:5 * D], in0=mt[:, 4 * D:5 * D], scalar=1.0, in1=n2b[:],
                                                                   op0=ALU.add, op1=ALU.mult),
                           reads=[mt_k, n2b_k], writes=[mt_k])
                      K.dma("sp", segs[s]["mod"], mt[:], mt_k, reads=[mt_k], acc=[segs[s]["mod_k"]])
                  K.end_phase()
                  chk(l)

              with ExitStack() as ph:
                  wlat, wlat_k = K.sb(ph, "wlat", [128, KT, 416], BF16, dma=True)
                  wuq, wuq_k = K.sb(ph, "wuq", [128, 2, 768], BF16, dma=True)
                  wukv, wukv_k = K.sb(ph, "wukv", [128, 1024], BF16, dma=True)
                  qan, qan_k = K.sb(ph, "qan", [128, 2], F32, dma=True)
                  kvan, kvan_k = K.sb(ph, "kvan", [128, 1], F32, dma=True)
                  gq, gq_k = K.sb(ph, "gq", [128, DQK], F32, dma=True)
                  gk, gk_k = K.sb(ph, "gk", [128, DQK], F32, dma=True)
                  K.dma("pool", wlat[:], wview(Wl["w_in"])[:, :, 0:416], wlat_k, writes=[wlat_k])
                  K.dma("pool", wuq[:], wview(Wl["w_uq"]), wuq_k, writes=[wuq_k])
                  K.dma("pool", wukv[:], Wl["w_ukv"], wukv_k, writes=[wukv_k])
                  K.dma("sp", qan[:], Wl["qan"], qan_k, writes=[qan_k])
                  K.dma("sp", kvan[:], Wl["kvan"], kvan_k, writes=[kvan_k])
                  K.dma("sp", gq[:], Wl["qhn"][0, :].partition_broadcast(128), gq_k, writes=[gq_k])
                  K.dma("sp", gk[:], Wl["khn"][0, :].partition_broadcast(128), gk_k, writes=[gk_k])
                  P = dict(
                      junk=K.ring(ph, "junk", 2, [128, D], F32), ss=K.ring(ph, "ss", 3, [128, 4], F32),
                      hb=K.ring(ph, "hb", 2, [128, D], BF16), sq=K.ring(ph, "sq", 1, [128, 768], F32),
                      st=K.ring(ph, "st", 2, [128, 24], F32), tt=K.ring(ph, "tt", 1, [128, 4 * NH * 16], F32))
                  xr = K.ring(ph, "xr", 3, [128, D], F32, dma=True)
                  rpr = K.ring(ph, "rpr", 4, [128, 32], F32, dma=True)
                  hst = K.ring(ph, "hst", 2, [128, KT, 512], BF16, dma=True)
                  qst = K.ring(ph, "qst", 2, [128, NH, 512], BF16, dma=True)
                  kst = K.ring(ph, "kst", 2, [128, NH, 512], BF16, dma=True)
                  vst = K.ring(ph, "vst", 2, [128, NH, 4, 65], BF16, dma=True)
                  for vt, vk in vst.items:
                      K.op("dve", lambda e, vt=vt: e.memset(vt[:], 1.0), writes=[vk])
                  gamt, gam_k = K.sb(ph, "gam1", [128, D], F32, dma=True)
                  sht, sh_k = K.sb(ph, "sh1", [128, D], F32, dma=True)
                  cqn_r = K.ring(ph, "cqn", 2, [128, 256], BF16)
                  cqT_r = K.ring(ph, "cqT", 2, [128, 2, 128], BF16)
                  ckn_r = K.ring(ph, "ckn", 2, [128, 128], BF16)
                  ckT_r = K.ring(ph, "ckT", 2, [128, 128], BF16)
                  qf_r = K.ring(ph, "qf", 2, [128, 768], F32)
                  qb_r = K.ring(ph, "qb", 2, [128, 768], BF16)
                  ssl = K.ring(ph, "ssl", 3, [128, 4], F32)
                  lat_r = K.ring(ph, "lat", 2, [128, 416], F32)

                  def small_rms(src_ap, src_k, n, ncols_scale, dst, dst_k):
                      s4, s4_k = ssl.next()
                      jk, jk_k = P["sq"].next()
                      K.op("dve", lambda e: e.memset(s4[:], 0.0), writes=[s4_k])
                      K.op("act", lambda e: e.activation(out=jk[:, 0:n], in_=src_ap, func=AF.Square, accum_out=s4[:, 0:1]),
                           reads=[src_k, s4_k], writes=[jk_k, s4_k])
                      K.op("act", lambda e: e.activation(out=s4[:, 1:2], in_=s4[:, 0:1], func=AF.Sqrt, bias=EPS, scale=1.0 / n),
                           reads=[s4_k], writes=[s4_k])
                      K.op("dve", lambda e: e.reciprocal(out=s4[:, 2:3], in_=s4[:, 1:2]), reads=[s4_k], writes=[s4_k])
                      K.op("dve", lambda e: e.tensor_scalar(out=dst, in0=src_ap, scalar1=s4[:, 2:3], scalar2=None, op0=ALU.mult),
                           reads=[src_k, s4_k], writes=[dst_k])

                  for sg in segs:
                      K.dma("sp", sht[:], sg["mod"][:, 0:D], sh_k, writes=[sh_k], dram_reads=[sg["mod_k"]])
                      K.dma("sp", gamt[:], sg["mod"][:, D:2 * D], gam_k, writes=[gam_k], dram_reads=[sg["mod_k"]])
                      passes = []
                      if sg["prompt"]:
                          passes.append((sg["ctx"], sg["s"], False, True, sg["rk"], sg["ctx_k"]))
                          passes.append((sg["x"], sg["n"], True, False, sg["rq"], sg["x_k"]))
                      else:
                          passes.append((sg["x"], sg["n"], True, True, sg["rq"], sg["x_k"]))
                      for (xsrc, ntok, want_q, want_k, rtab, xsrc_k) in passes:
                          for (b0, bw) in blocks_of(ntok):
                              nj = bw // 128
                              hs, hs_k = hst.next()
                              if want_q:
                                  qs, qs_k = qst.next()
                              if want_k:
                                  ks, ks_k = kst.next()
                                  vs, vs_k = vst.next()
                              for j in range(nj):
                                  t0 = b0 + j * 128
                                  xt, xk = xr.next()
                                  K.dma("sp", xt[:], xsrc[t0:t0 + 128, :], xk, writes=[xk], dram_reads=[xsrc_k])
                                  rp, rp_k = rpr.next()
                                  K.dma("sp", rp[:], rtab[t0:t0 + 128, :], rp_k, writes=[rp_k])
                                  hb, hb_k = norm_tile(P, xt, xk, (gamt, gam_k), (sht, sh_k))
                                  transpose_to(P, hb, hb_k, hs[:, :, j * 128:(j + 1) * 128], hs_k)
                                  pl, pl_k = psr()
                                  for kt in range(KT):
                                      K.op("pe", lambda e, kt=kt: e.matmul(pl[:, 0:416], lhsT=hs[:, kt, j * 128:(j + 1) * 128], rhs=wlat[:, kt, :],
                                                                           start=(kt == 0), stop=(kt == KT - 1)),
                                           reads=[hs_k, wlat_k], writes=[pl_k], inc=(kt == KT - 1))
                                  lat, lat_k = lat_r.next()
                                  K.op("act", lambda e: e.activation(out=lat[:], in_=pl[:, 0:416], func=AF.Copy), reads=[pl_k], writes=[lat_k])
                                  pl, pl_k = lat, lat_k
                                  if want_q:
                                      cqn, cqn_k = cqn_r.next()
                                      small_rms(pl[:, 0:256], pl_k, 256, None, cqn[:], cqn_k)
                                      cqT, cqT_k = cqT_r.next()
                                      p2, p2_k = psr()
                                      p2b = p2[:].bitcast(BF16)
                                      for i in range(2):
                                          K.op("pe", lambda e, i=i: e.transpose(p2b[:, i * 128:(i + 1) * 128], cqn[:, i * 128:(i + 1) * 128], ident[:]),
                                               reads=[cqn_k, ident_k], writes=[p2_k], inc=(i == 1))
                                      for i in range(2):
                                          K.op("act", lambda e, i=i: e.activation(out=cqT[:, i, :], in_=p2b[:, i * 128:(i + 1) * 128], func=AF.Copy,
                                                                                  scale=qan[:, i:i + 1]),
                                               reads=[p2_k, qan_k], writes=[cqT_k])
                                      pq0, pq0_k = psr()
                                      pq1, pq1_k = psr()
                                      for i in range(2):
                                          K.op("pe", lambda e, i=i: e.matmul(pq0[:, 0:480], lhsT=cqT[:, i, :], rhs=wuq[:, i, 0:480], start=(i == 0), stop=(i == 1)),
                                               reads=[cqT_k, wuq_k], writes=[pq0_k], inc=(i == 1))
                                      for i in range(2):
                                          K.op("pe", lambda e, i=i: e.matmul(pq1[:, 0:288], lhsT=cqT[:, i, :], rhs=wuq[:, i, 480:768], start=(i == 0), stop=(i == 1)),
                                               reads=[cqT_k, wuq_k], writes=[pq1_k], inc=(i == 1))
                                      qf, qf_k = qf_r.next()
                                      K.op("act", lambda e: e.activation(out=qf[:, 0:480], in_=pq0[:, 0:480], func=AF.Copy), reads=[pq0_k], writes=[qf_k])
                                      K.op("dve", lambda e: e.tensor_copy(out=qf[:, 480:768], in_=pq1[:, 0:288]), reads=[pq1_k], writes=[qf_k])
                                      qb, qb_k = qb_r.next()
                                      headnorm_rope(P, qf, qf_k, (gq, gq_k), rp, rp_k, qb, qb_k)
                                      transpose_to(P, qb, qb_k, qs[0:DQK, :, j * 128:(j + 1) * 128], qs_k, ncols_src=768, chunk=DQK)
                                  if want_k:
                                      ckn, ckn_k = ckn_r.next()
                                      small_rms(pl[:, 256:384], pl_k, 128, None, ckn[:], ckn_k)
                                      ckT, ckT_k = ckT_r.next()
                                      p3, p3_k = psr()
                                      p3b = p3[:].bitcast(BF16)
                                      K.op("pe", lambda e: e.transpose(p3b[:, 0:128], ckn[:], ident[:]), reads=[ckn_k, ident_k], writes=[p3_k])
                                      K.op("act", lambda e: e.activation(out=ckT[:], in_=p3b[:, 0:128], func=AF.Copy, scale=kvan[:, 0:1]),
                                           reads=[p3_k, kvan_k], writes=[ckT_k])
                                      pk, pk_k = psr()
                                      pv, pv_k = psr()
                                      K.op("pe", lambda e: e.matmul(pk[:], lhsT=ckT[:], rhs=wukv[:, 0:512], start=True, stop=True),
                                           reads=[ckT_k, wukv_k], writes=[pk_k])
                                      K.op("pe", lambda e: e.matmul(pv[:], lhsT=ckT[:], rhs=wukv[:, 512:1024], start=True, stop=True),
                                           reads=[ckT_k, wukv_k], writes=[pv_k])
                                      kf, kf_k = qf_r.next()
                                      kf3 = kf[:].rearrange("p (h d) -> p h d", h=NH)
                                      K.op("act", lambda e: e.activation(out=kf3[:, :, 0:64], in_=pk[:].rearrange("p (h d) -> p h d", h=NH), func=AF.Copy),
                                           reads=[pk_k], writes=[kf_k])
                                      K.op("dve", lambda e: e.tensor_copy(out=kf3[:, :, 64:96], in_=pl[:, 384:416].unsqueeze(1).to_broadcast([128, NH, 32])),
                                           reads=[pl_k], writes=[kf_k])
                                      kb, kb_k = qb_r.next()
                                      headnorm_rope(P, kf, kf_k, (gk, gk_k), rp, rp_k, kb, kb_k)
                                      transpose_to(P, kb, kb_k, ks[0:DQK, :, j * 128:(j + 1) * 128], ks_k, ncols_src=768, chunk=DQK)
                                      K.op("act", lambda e: e.activation(out=vs[:, :, j, 0:64], in_=pv[:].rearrange("p (h d) -> p h d", h=NH), func=AF.Copy),
                                           reads=[pv_k], writes=[vs_k])
                              if want_q:
                                  K.dma("sp", sg["hT"][:, :, b0:b0 + bw], hs[:, :, 0:bw], hs_k, reads=[hs_k], acc=[sg["hT_k"]])
                                  K.dma("sp", sg["qT"][:, :, b0:b0 + bw], qs[0:DQK, :, 0:bw], qs_k, reads=[qs_k], acc=[sg["qT_k"]])
                              if want_k:
                                  K.dma("sp", sg["kT"].rearrange("h d s -> d h s")[:, :, b0:b0 + bw], ks[0:DQK, :, 0:bw], ks_k, reads=[ks_k], acc=[sg["kT_k"]])
                                  K.dma("sp", sg["v"].rearrange("h p k c -> p h k c")[:, :, b0 // 128:b0 // 128 + nj, :], vs[:, :, 0:nj, :], vs_k,
                                        reads=[vs_k], acc=[sg["v_k"]])
                  K.end_phase()
                  chk(l)

              with ExitStack() as ph:
                  PSR_HI[0] = 4
                  NW = D_IN - 416
                  wbig, wbig_k = K.sb(ph, "wbig", [128, KT, NW], BF16, dma=True)
                  wv = wview(Wl["w_in"])
                  for c in range(0, NW, 640):
                      K.dma("pool", wbig[:, :, c:c + 640], wv[:, :, 416 + c:416 + c + 640], wbig_k, acc=[wbig_k])
                  wsT, wsT_k = K.sb(ph, "wsT", [128, 4, 128], BF16, dma=True)
                  K.dma("pool", wsT[:], Wl["sgwT"].rearrange("p (g q) -> p g q", g=4), wsT_k, writes=[wsT_k])
                  lng, lng_k = K.sb(ph, "lng", [128, 512], F32, dma=True)
                  lnb, lnb_k = K.sb(ph, "lnb", [128, 512], F32, dma=True)
                  bsb, bsb_k = K.sb(ph, "bsb", [128, 4, 4, 128], F32, dma=True)
                  K.dma("sp", lng[:], Wl["sglng"][0, :].partition_broadcast(128), lng_k, writes=[lng_k])
                  K.dma("sp", lnb[:], Wl["sglnb"][0, :].partition_broadcast(128), lnb_k, writes=[lnb_k])
                  for j in range(4):
                      K.dma("sp", bsb[:, :, j, :], Wl["sgb"][0, :].partition_broadcast(128).rearrange("p (g q) -> p g q", g=4), bsb_k, acc=[bsb_k])
                  hbr = K.ring(ph, "hTb", 2, [128, KT, 512], BF16, dma=True)
                  sgt_r = K.ring(ph, "sgt", 2, [128, 512], F32)
                  glub_r = K.ring(ph, "glub", 2, [128, 4, 512], BF16, dma=True)
                  ug_r = K.ring(ph, "ug", 2, [128, 4, 512], BF16)
                  vg_r = K.ring(ph, "vg", 2, [128, 512], F32)
                  jk_r = K.ring(ph, "jk2", 1, [128, 512], F32)
                  s8_r = K.ring(ph, "s8", 3, [128, 8], F32)
                  vnb_r = K.ring(ph, "vnb", 4, [128, 512], BF16)
                  tq_r = K.ring(ph, "tq", 2, [128, 512], F32)
                  usb_r = K.ring(ph, "usb", 2, [128, 4, 512], BF16, dma=True)
                  gst_r = K.ring(ph, "gst", 2, [128, 24, 512], BF16, dma=True)
                  for sg in segs:
                      n = sg["n"]
                      blks = blocks_of(n)
                      for bi, (b0, bw) in enumerate(blks):
                          nj = bw // 128
                          hT, hT_k = hbr.next()
                          K.dma("sp", hT[:, :, 0:bw], sg["hT"][:, :, b0:b0 + bw], hT_k, writes=[hT_k], dram_reads=[sg["hT_k"]])

                          def fm(col0, ps, ps_k):
                              for kt in range(KT):
                                  K.op("pe", lambda e, kt=kt: e.matmul(ps[:, 0:bw], lhsT=wbig[:, kt, col0:col0 + 128], rhs=hT[:, kt, 0:bw],
                                                                       start=(kt == 0), stop=(kt == KT - 1)),
                                       reads=[wbig_k, hT_k], writes=[ps_k], inc=(kt == KT - 1))
                          glub, glub_k = glub_r.next()
                          for c in range(4):
                              pa, pa_k = psr()
                              pg, pg_k = psr()
                              fm(c * 128, pa, pa_k)
                              fm(512 + c * 128, pg, pg_k)
                              sgt, sgt_k = sgt_r.next()
                              K.op("act", lambda e: e.activation(out=sgt[:, 0:bw], in_=pg[:, 0:bw], func=AF.Sigmoid), reads=[pg_k], writes=[sgt_k])
                              K.op("dve", lambda e, c=c: e.tensor_tensor(out=glub[:, c, 0:bw], in0=pa[:, 0:bw], in1=sgt[:, 0:bw], op=ALU.mult),
                                   reads=[pa_k, sgt_k], writes=[glub_k])
                          if sg["prompt"]:
                              if bi == 0:
                                  K.op("dve", lambda e: e.tensor_scalar(out=glub[:, :, 0:128], in0=glub[:, :, 0:128], scalar1=mk[:, 0:1], scalar2=None, op0=ALU.mult),
                                       reads=[glub_k, mk_k], writes=[glub_k])
                              if bi == len(blks) - 1:
                                  K.op("dve", lambda e: e.tensor_scalar(out=glub[:, :, bw - 128:bw], in0=glub[:, :, bw - 128:bw], scalar1=mk[:, 1:2], scalar2=None, op0=ALU.mult),
                                       reads=[glub_k, mk_k], writes=[glub_k])
                          K.dma("sp", sg["glu"][:, :, 15 + b0:15 + b0 + bw], glub[:, :, 0:bw], glub_k, reads=[glub_k], acc=[sg["glu_k"]])
                          ug, ug_k = ug_r.next()
                          for g in range(4):
                              pu, pu_k = psr()
                              fm(1024 + g * 128, pu, pu_k)
                              K.op("act", lambda e, g=g: e.activation(out=ug[:, g, 0:bw], in_=pu[:, 0:bw], func=AF.Gelu_apprx_tanh), reads=[pu_k], writes=[ug_k])
                          pss = [PSB[4 + g] for g in range(4)]
                          vnbs = []
                          for j in range(nj):
                              pv, pv_k = psr()
                              for kt in range(KT):
                                  K.op("pe", lambda e, kt=kt: e.matmul(pv[:], lhsT=hT[:, kt, j * 128:(j + 1) * 128], rhs=wbig[:, kt, 1536:2048],
                                                                       start=(kt == 0), stop=(kt == KT - 1)),
                                       reads=[hT_k, wbig_k], writes=[pv_k], inc=(kt == KT - 1))
                              vg, vg_k = vg_r.next()
                              K.op("act", lambda e: e.activation(out=vg[:], in_=pv[:], func=AF.Gelu_apprx_tanh), reads=[pv_k], writes=[vg_k])
                              s8, s8_k = s8_r.next()
                              jk, jk_k = jk_r.next()
                              K.op("dve", lambda e: e.memset(s8[:], 0.0), writes=[s8_k])
                              K.op("act", lambda e: e.activation(out=jk[:], in_=vg[:], func=AF.Square, accum_out=s8[:, 1:2]), reads=[vg_k, s8_k], writes=[jk_k, s8_k])
                              K.op("dve", lambda e: e.reduce_sum(out=s8[:, 0:1], in_=vg[:], axis=AX.X), reads=[vg_k, s8_k], writes=[s8_k])
                              K.op("dve", lambda e: e.tensor_scalar(out=s8[:, 2:3], in0=s8[:, 0:1], scalar1=1.0 / 512, scalar2=None, op0=ALU.mult), reads=[s8_k], writes=[s8_k])
                              K.op("dve", lambda e: e.tensor_tensor(out=s8[:, 3:4], in0=s8[:, 2:3], in1=s8[:, 2:3], op=ALU.mult), reads=[s8_k], writes=[s8_k])
                              K.op("dve", lambda e: e.scalar_tensor_tensor(out=s8[:, 4:5], in0=s8[:, 1:2], scalar=1.0 / 512, in1=s8[:, 3:4], op0=ALU.mult, op1=ALU.subtract),
                                   reads=[s8_k], writes=[s8_k])
                              K.op("act", lambda e: e.activation(out=s8[:, 5:6], in_=s8[:, 4:5], func=AF.Sqrt, bias=EPS, scale=1.0), reads=[s8_k], writes=[s8_k])
                              K.op("dve", lambda e: e.reciprocal(out=s8[:, 6:7], in_=s8[:, 5:6]), reads=[s8_k], writes=[s8_k])
                              K.op("dve", lambda e: e.tensor_scalar(out=vg[:], in0=vg[:], scalar1=s8[:, 2:3], scalar2=s8[:, 6:7], op0=ALU.subtract, op1=ALU.mult),
                                   reads=[vg_k, s8_k], writes=[vg_k])
                              K.op("dve", lambda e: e.tensor_tensor(out=vg[:], in0=vg[:], in1=lng[:], op=ALU.mult), reads=[vg_k, lng_k], writes=[vg_k])
                              vnb, vnb_k = vnb_r.next()
                              K.op("pool", lambda e: e.tensor_tensor(out=vnb[:], in0=vg[:], in1=lnb[:], op=ALU.add), reads=[vg_k, lnb_k], writes=[vnb_k])
                              vnbs.append((vnb, vnb_k))
                          gst, gst_k = gst_r.next()
                          for m in range(24):
                              pg, pg_k = psr()
                              fm(2048 + m * 128, pg, pg_k)
                              K.op("act", lambda e, m=m: e.activation(out=gst[:, m, 0:bw], in_=pg[:, 0:bw], func=AF.Sigmoid), reads=[pg_k], writes=[gst_k])
                          K.dma("sp", sg["gat"][:, :, b0:b0 + bw], gst[:, :, 0:bw], gst_k, reads=[gst_k], acc=[sg["gat_k"]])
                          for j in range(nj):
                              vnb, vnb_k = vnbs[j]
                              for g in range(4):
                                  K.op("pe", lambda e, g=g: e.matmul(pss[g][0][:, j * 128:(j + 1) * 128], lhsT=vnb[:, g * 128:(g + 1) * 128], rhs=wsT[:, g, :],
                                                                     start=True, stop=True),
                                       reads=[vnb_k, wsT_k], writes=[pss[g][1]], inc=(g == 3))
                          usb, usb_k = usb_r.next()
                          for g in range(4):
                              tq, tq_k = tq_r.next()
                              K.op("dve", lambda e, g=g: e.tensor_tensor(out=tq[:, 0:bw], in0=pss[g][0][:, 0:bw],
                                                                         in1=bsb[:, g, :, :].rearrange("p j q -> p (j q)")[:, 0:bw], op=ALU.add),
                                   reads=[pss[g][1], bsb_k], writes=[tq_k])
                              K.op("dve", lambda e, g=g: e.tensor_tensor(out=usb[:, g, 0:bw], in0=tq[:, 0:bw], in1=ug[:, g, 0:bw], op=ALU.mult),
                                   reads=[tq_k, ug_k], writes=[usb_k])
                          K.dma("sp", sg["us"][:, :, b0:b0 + bw], usb[:, :, 0:bw], usb_k, reads=[usb_k], acc=[sg["us_k"]])
                  K.end_phase()
                  chk(l)

              with ExitStack() as ph:
                  PSR_HI[0] = 4
                  KSB = 4096
                  qtr = K.ring(ph, "qtb", 2, [128, NH, 512], BF16, dma=True)
                  ktr = K.ring(ph, "ktb", 2, [128, KSB], BF16, dma=True)
                  vtr = K.ring(ph, "vtb", 2, [128, KSB // 128, 65], BF16, dma=True)
                  ptr_ = K.ring(ph, "ptb", 4, [128, 512], BF16)
                  aor = K.ring(ph, "aob", 2, [64, NH, 512], BF16, dma=True)
                  rsr = K.ring(ph, "rsb", 2, [128, 512], F32)
                  rir = K.ring(ph, "rib", 2, [64, 512], F32)
                  scale = float(DQK) ** -0.5
                  for sg in segs:
                      n, S = sg["n"], sg["s"]
                      for (b0, bw) in blocks_of(n):
                          qt, qt_k = qtr.next()
                          K.dma("sp", qt[0:DQK, :, 0:bw], sg["qT"][:, :, b0:b0 + bw], qt_k, writes=[qt_k], dram_reads=[sg["qT_k"]])
                          ao, ao_k = aor.next()
                          for h in range(NH):
                              po, po_k = PSB[4 + (h % 2)]
                              first = True
                              for (s0, sw) in blocks_of(S, KSB):
                                  kt_, kt_k = ktr.next()
                                  vt_, vt_k = vtr.next()
                                  K.dma("sp", kt_[0:DQK, 0:sw], sg["kT"][h, :, s0:s0 + sw], kt_k, writes=[kt_k], dram_reads=[sg["kT_k"]])
                                  K.dma("sp", vt_[:, 0:sw // 128, :], sg["v"][h, :, s0 // 128:(s0 + sw) // 128, :], vt_k, writes=[vt_k], dram_reads=[sg["v_k"]])
                                  nk = sw // 128
                                  pend = None
                                  for ki in range(nk + 1):
                                      if ki < nk:
                                          ps, ps_k = psr(0, 4)
                                          K.op("pe", lambda e, ki=ki: e.matmul(ps[:, 0:bw], lhsT=kt_[0:DQK, ki * 128:(ki + 1) * 128], rhs=qt[0:DQK, h, 0:bw],
                                                                               start=True, stop=True),
                                               reads=[kt_k, qt_k], writes=[ps_k])
                                          pt, pt_k = ptr_.next()
                                          K.op("act", lambda e: e.activation(out=pt[:, 0:bw], in_=ps[:, 0:bw], func=AF.Exp, scale=scale), reads=[ps_k], writes=[pt_k])
                                          cur = (pt, pt_k, ki)
                                      else:
                                          cur = None
                                      if pend is not None:
                                          ppt, ppt_k, pki = pend
                                          last = (s0 + sw >= S) and (pki == nk - 1)
                                          K.op("pe", lambda e, pki=pki, ppt=ppt, f=first, last=last: e.matmul(po[0:65, 0:bw], lhsT=vt_[:, pki, 0:65], rhs=ppt[:, 0:bw],
                                                                                                            start=f, stop=last),
                                               reads=[vt_k, ppt_k], writes=[po_k])
                                          first = False
                                      pend = cur
                              rs, rs_k = rsr.next()
                              K.op("dve", lambda e: e.tensor_copy(out=rs[64:65, 0:bw], in_=po[64:65, 0:bw]), reads=[po_k], writes=[rs_k])
                              pb, pb_k = PSB[6 + (h % 2)]
                              K.op("pe", lambda e: e.matmul(pb[0:64, 0:bw], lhsT=ones_f[64:65, 0:64], rhs=rs[64:65, 0:bw], start=True, stop=True),
                                   reads=[onesf_k, rs_k], writes=[pb_k])
                              ri, ri_k = rir.next()
                              K.op("dve", lambda e: e.reciprocal(out=ri[:, 0:bw], in_=pb[0:64, 0:bw]), reads=[pb_k], writes=[ri_k])
                              K.op("dve", lambda e, h=h: e.tensor_tensor(out=ao[:, h, 0:bw], in0=po[0:64, 0:bw], in1=ri[:, 0:bw], op=ALU.mult),
                                   reads=[po_k, ri_k], writes=[ao_k])
                          K.dma("sp", sg["ao"][:, :, b0:b0 + bw], ao[:, :, 0:bw], ao_k, reads=[ao_k], acc=[sg["ao_k"]])
                  K.end_phase()
                  chk(l)

              with ExitStack() as ph:
                  PSR_HI[0] = 8
                  wao, wao_k = K.sb(ph, "wao", [64, NH, D], BF16, dma=True)
                  wco, wco_k = K.sb(ph, "wco", [128, 4, D], BF16, dma=True)
                  wso, wso_k = K.sb(ph, "wso", [128, 4, D], BF16, dma=True)
                  wout, wout_k = K.sb(ph, "wout", [128, KT, D], BF16, dma=True)
                  K.dma("pool", wao[:], Wl["w_ao"].rearrange("p (h n) -> p h n", h=NH), wao_k, writes=[wao_k])
                  K.dma("pool", wco[:], wview(Wl["w_co"]), wco_k, writes=[wco_k])
                  K.dma("pool", wso[:], wview(Wl["w_so"]), wso_k, writes=[wso_k])
                  K.dma("pool", wout[:], wview(Wl["w_out"]), wout_k, writes=[wout_k])
                  cdw, cdw_k = K.sb(ph, "cdw", [128, 4, 31], F32, dma=True)
                  cdwb, cdwb_k = K.sb(ph, "cdwb", [128, 4], F32, dma=True)
                  clng, clng_k = K.sb(ph, "clng", [128, 4], F32, dma=True)
                  clnb, clnb_k = K.sb(ph, "clnb", [128, 4], F32, dma=True)
                  K.dma("sp", cdw[:], Wl["cdw"].rearrange("p (c k) -> p c k", c=4), cdw_k, writes=[cdw_k])
                  K.dma("sp", cdwb[:], Wl["cdwb"], cdwb_k, writes=[cdwb_k])
                  K.dma("sp", clng[:], Wl["clng"], clng_k, writes=[clng_k])
                  K.dma("sp", clnb[:], Wl["clnb"], clnb_k, writes=[clnb_k])
                  dgt, dgt_k = K.sb(ph, "dgt", [128, 4, 31, 128], BF16)
                  for c in range(4):
                      for k in range(31):
                          K.op("dve", lambda e, c=c, k=k: e.tensor_scalar(out=dgt[:, c, k, :], in0=ident[:], scalar1=cdw[:, c, k:k + 1], scalar2=None, op0=ALU.mult),
                               reads=[ident_k, cdw_k], writes=[dgt_k])
                  g1t, g1_k = K.sb(ph, "g1t", [128, D], F32, dma=True)
                  glr = K.ring(ph, "glt", 2, [128, 4, 512 + 30], BF16, dma=True)
                  usr = K.ring(ph, "ust", 2, [128, 4, 512], BF16, dma=True)
                  gtr = K.ring(ph, "gtt", 1, [128, 24, 512], BF16, dma=True)
                  aor = K.ring(ph, "aot", 2, [64, NH, 512], BF16, dma=True)
                  hc, hc_k = K.sb(ph, "hc", [128, 4, 512], F32)
                  hcb, hcb_k = K.sb(ph, "hcb", [128, 4, 512], BF16)
                  sqb, sqb_k = K.sb(ph, "sqb", [128, 4, 512], BF16)
                  mean, mean_k = K.sb(ph, "mean", [128, 512], F32)
                  rstd, rstd_k = K.sb(ph, "rstd", [128, 512], F32)
                  cvn, cvn_k = K.sb(ph, "cvn", [128, 4, 512], BF16)
                  mrg, mrg_k = K.sb(ph, "mrg", [128, KT, 512], BF16)
                  tmr = K.ring(ph, "tm", 4, [128, 512], F32)
                  xr = K.ring(ph, "xr2", 2, [128, D], F32, dma=True)
                  for sg in segs:
                      n = sg["n"]
                      K.dma("sp", g1t[:], sg["mod"][:, 2 * D:3 * D], g1_k, writes=[g1_k], dram_reads=[sg["mod_k"]])
                      for (b0, bw) in blocks_of(n):
                          nj = bw // 128
                          glt, glt_k = glr.next()
                          K.dma("sp", glt[:, :, 0:bw + 30], sg["glu"][:, :, b0:b0 + bw + 30], glt_k, writes=[glt_k], dram_reads=[sg["glu_k"]])
                          ust, ust_k = usr.next()
                          K.dma("sp", ust[:, :, 0:bw], sg["us"][:, :, b0:b0 + bw], ust_k, writes=[ust_k], dram_reads=[sg["us_k"]])
                          gtt, gtt_k = gtr.next()
                          K.dma("sp", gtt[:, :, 0:bw], sg["gat"][:, :, b0:b0 + bw], gtt_k, writes=[gtt_k], dram_reads=[sg["gat_k"]])
                          aot, aot_k = aor.next()
                          K.dma("sp", aot[:, :, 0:bw], sg["ao"][:, :, b0:b0 + bw], aot_k, writes=[aot_k], dram_reads=[sg["ao_k"]])
                          for c in range(4):
                              ps, ps_k = psr()
                              for k in range(31):
                                  K.op("pe", lambda e, c=c, k=k: e.matmul(ps[:, 0:bw], lhsT=dgt[:, c, k, :], rhs=glt[:, c, k:k + bw], start=(k == 0), stop=(k == 30)),
                                       reads=[dgt_k, glt_k], writes=[ps_k], inc=(k == 30))
                              K.op("act", lambda e, c=c: e.activation(out=hc[:, c, 0:bw], in_=ps[:, 0:bw], func=AF.Identity, bias=cdwb[:, c:c + 1], scale=1.0),
                                   reads=[ps_k, cdwb_k], writes=[hc_k])
                          K.op("pool", lambda e: e.tensor_copy(out=hcb[:, :, 0:bw], in_=hc[:, :, 0:bw]), reads=[hc_k], writes=[hcb_k])
                          K.op("dve", lambda e: e.tensor_tensor(out=sqb[:, :, 0:bw], in0=hc[:, :, 0:bw], in1=hc[:, :, 0:bw], op=ALU.mult), reads=[hc_k], writes=[sqb_k])
                          p1, p1_k = psr()
                          p2, p2_k = psr()
                          for c in range(4):
                              K.op("pe", lambda e, c=c: e.matmul(p1[:, 0:bw], lhsT=ones_bf[:], rhs=hcb[:, c, 0:bw], start=(c == 0), stop=(c == 3)),
                                   reads=[ones_k, hcb_k], writes=[p1_k], inc=(c == 3))
                          for c in range(4):
                              K.op("pe", lambda e, c=c: e.matmul(p2[:, 0:bw], lhsT=ones_bf[:], rhs=sqb[:, c, 0:bw], start=(c == 0), stop=(c == 3)),
                                   reads=[ones_k, sqb_k], writes=[p2_k], inc=(c == 3))
                          K.op("dve", lambda e: e.tensor_scalar(out=mean[:, 0:bw], in0=p1[:, 0:bw], scalar1=1.0 / 512, scalar2=None, op0=ALU.mult), reads=[p1_k], writes=[mean_k])
                          tm, tm_k = tmr.next()
                          K.op("dve", lambda e: e.tensor_tensor(out=tm[:, 0:bw], in0=mean[:, 0:bw], in1=mean[:, 0:bw], op=ALU.mult), reads=[mean_k], writes=[tm_k])
                          K.op("dve", lambda e: e.scalar_tensor_tensor(out=tm[:, 0:bw], in0=p2[:, 0:bw], scalar=1.0 / 512, in1=tm[:, 0:bw], op0=ALU.mult, op1=ALU.subtract),
                               reads=[p2_k, tm_k], writes=[tm_k])
                          K.op("act", lambda e: e.activation(out=tm[:, 0:bw], in_=tm[:, 0:bw], func=AF.Sqrt, bias=EPS, scale=1.0), reads=[tm_k], writes=[tm_k])
                          K.op("dve", lambda e: e.reciprocal(out=rstd[:, 0:bw], in_=tm[:, 0:bw]), reads=[tm_k], writes=[rstd_k])
                          for c in range(4):
                              t2, t2_k = tmr.next()
                              K.op("dve", lambda e, c=c: e.tensor_tensor(out=t2[:, 0:bw], in0=hc[:, c, 0:bw], in1=mean[:, 0:bw], op=ALU.subtract), reads=[hc_k, mean_k], writes=[t2_k])
                              K.op("dve", lambda e: e.tensor_tensor(out=t2[:, 0:bw], in0=t2[:, 0:bw], in1=rstd[:, 0:bw], op=ALU.mult), reads=[t2_k, rstd_k], writes=[t2_k])
                              K.op("act", lambda e, c=c: e.activation(out=cvn[:, c, 0:bw], in_=t2[:, 0:bw], func=AF.Silu, bias=clnb[:, c:c + 1], scale=clng[:, c:c + 1]),
                                   reads=[t2_k, clnb_k, clng_k], writes=[cvn_k])
                          for j in range(8):
                              pa, pa_k = psr()
                              pc, pc_k = psr()
                              pss_, pss_k = psr()
                              for h in range(NH):
                                  K.op("pe", lambda e, h=h: e.matmul(pa[:, 0:bw], lhsT=wao[:, h, j * 128:(j + 1) * 128], rhs=aot[:, h, 0:bw], start=(h == 0), stop=(h == NH - 1)),
                                       reads=[wao_k, aot_k], writes=[pa_k], inc=(h == NH - 1))
                              for c in range(4):
                                  K.op("pe", lambda e, c=c: e.matmul(pc[:, 0:bw], lhsT=wco[:, c, j * 128:(j + 1) * 128], rhs=cvn[:, c, 0:bw], start=(c == 0), stop=(c == 3)),
                                       reads=[wco_k, cvn_k], writes=[pc_k], inc=(c == 3))
                              for c in range(4):
                                  K.op("pe", lambda e, c=c: e.matmul(pss_[:, 0:bw], lhsT=wso[:, c, j * 128:(j + 1) * 128], rhs=ust[:, c, 0:bw], start=(c == 0), stop=(c == 3)),
                                       reads=[wso_k, ust_k], writes=[pss_k], inc=(c == 3))
                              m1, m1_k = tmr.next()
                              m2, m2_k = tmr.next()
                              m3, m3_k = tmr.next()
                              K.op("dve", lambda e: e.tensor_tensor(out=m1[:, 0:bw], in0=pa[:, 0:bw], in1=gtt[:, j, 0:bw], op=ALU.mult), reads=[pa_k, gtt_k], writes=[m1_k])
                              K.op("dve", lambda e: e.tensor_tensor(out=m2[:, 0:bw], in0=pc[:, 0:bw], in1=gtt[:, 8 + j, 0:bw], op=ALU.mult), reads=[pc_k, gtt_k], writes=[m2_k])
                              K.op("dve", lambda e: e.tensor_tensor(out=m3[:, 0:bw], in0=pss_[:, 0:bw], in1=gtt[:, 16 + j, 0:bw], op=ALU.mult), reads=[pss_k, gtt_k], writes=[m3_k])
                              K.op("pool", lambda e: e.tensor_tensor(out=m1[:, 0:bw], in0=m1[:, 0:bw], in1=m2[:, 0:bw], op=ALU.add), reads=[m1_k, m2_k], writes=[m1_k])
                              K.op("pool", lambda e, j=j: e.tensor_tensor(out=mrg[:, j, 0:bw], in0=m1[:, 0:bw], in1=m3[:, 0:bw], op=ALU.add), reads=[m1_k, m3_k], writes=[mrg_k])
                          for t in range(nj):
                              t0 = b0 + t * 128
                              xt, xk = xr.next()
                              K.dma("sp", xt[:], sg["x"][t0:t0 + 128, :], xk, writes=[xk], dram_reads=[sg["x_k"]])
                              for half in range(2):
                                  po, po_k = psr()
                                  for j in range(8):
                                      K.op("pe", lambda e, j=j: e.matmul(po[:], lhsT=mrg[:, j, t * 128:(t + 1) * 128], rhs=wout[:, j, half * 512:(half + 1) * 512],
                                                                         start=(j == 0), stop=(j == 7)),
                                           reads=[mrg_k, wout_k], writes=[po_k], inc=(j == 7))
                                  tm2, tm2_k = tmr.next()
                                  K.op("dve", lambda e: e.tensor_tensor(out=tm2[:], in0=po[:], in1=g1t[:, half * 512:(half + 1) * 512], op=ALU.mult), reads=[po_k, g1_k], writes=[tm2_k])
                                  K.op("pool", lambda e: e.tensor_tensor(out=xt[:, half * 512:(half + 1) * 512], in0=xt[:, half * 512:(half + 1) * 512], in1=tm2[:], op=ALU.add),
                                       reads=[xk, tm2_k], writes=[xk])
                              K.dma("sp", sg["xm"][t0:t0 + 128, :], xt[:], xk, reads=[xk], acc=[sg["xm_k"]])
                  K.end_phase()
                  chk(l)

              with ExitStack() as ph:
                  P = dict(junk=K.ring(ph, "junkc", 2, [128, D], F32), ss=K.ring(ph, "ssc", 3, [128, 4], F32),
                           hb=K.ring(ph, "hbc", 2, [128, D], BF16))
                  xr = K.ring(ph, "xr3", 3, [128, D], F32, dma=True)
                  hst = K.ring(ph, "hst2", 2, [128, KT, 512], BF16, dma=True)
                  gamt, gam_k = K.sb(ph, "gam2", [128, D], F32, dma=True)
                  sht, sh_k = K.sb(ph, "sh2", [128, D], F32, dma=True)
                  for sg in segs:
                      n = sg["n"]
                      K.dma("sp", sht[:], sg["mod"][:, 3 * D:4 * D], sh_k, writes=[sh_k], dram_reads=[sg["mod_k"]])
                      K.dma("sp", gamt[:], sg["mod"][:, 4 * D:5 * D], gam_k, writes=[gam_k], dram_reads=[sg["mod_k"]])
                      for (b0, bw) in blocks_of(n):
                          nj = bw // 128
                          hs, hs_k = hst.next()
                          for j in range(nj):
                              t0 = b0 + j * 128
                              xt, xk = xr.next()
                              K.dma("sp", xt[:], sg["xm"][t0:t0 + 128, :], xk, writes=[xk], dram_reads=[sg["xm_k"]])
                              mc = None
                              if sg["prompt"] and t0 == 0:
                                  mc = 0
                              if sg["prompt"] and t0 == n - 128:
                                  mc = 1
                              hb, hb_k = norm_tile(P, xt, xk, (gamt, gam_k), (sht, sh_k), mask_col=mc)
                              transpose_to(P, hb, hb_k, hs[:, :, j * 128:(j + 1) * 128], hs_k)
                          K.dma("sp", sg["h2T"][:, :, 1 + b0:1 + b0 + bw], hs[:, :, 0:bw], hs_k, reads=[hs_k], acc=[sg["h2T_k"]])
                  K.end_phase()
                  chk(l)

              with ExitStack() as ph:
                  wup, wup_k = K.sb(ph, "wup", [128, KT, 2 * D_FF], BF16, dma=True)
                  wdn, wdn_k = K.sb(ph, "wdn", [128, NFT, D], BF16, dma=True)
                  wuv = wview(Wl["w_up"])
                  for c in range(0, 2 * D_FF, 704):
                      K.dma("pool", wup[:, :, c:c + 704], wuv[:, :, c:c + 704], wup_k, acc=[wup_k])
                  K.dma("pool", wdn[:], wview(Wl["w_down"]), wdn_k, writes=[wdn_k])
                  fdw, fdw_k = K.sb(ph, "fdw", [128, 44, 3], F32, dma=True)
                  fdwb, fdwb_k = K.sb(ph, "fdwb", [128, 44], F32, dma=True)
                  K.dma("sp", fdw[:], Wl["fdw"].rearrange("p (c k) -> p c k", c=44), fdw_k, writes=[fdw_k])
                  K.dma("sp", fdwb[:], Wl["fdwb"], fdwb_k, writes=[fdwb_k])
                  g2t, g2_k = K.sb(ph, "g2t", [128, D], F32, dma=True)
                  h2r = K.ring(ph, "h2b", 2, [128, KT, 514], BF16, dma=True)
                  zr = K.ring(ph, "zt_", 3, [128, 514], F32)
                  accr = K.ring(ph, "acc", 4, [128, 512], F32)
                  sgr = K.ring(ph, "sgf", 2, [128, 512], F32)
                  uT, uT_k = K.sb(ph, "uT", [128, NFT, 512], BF16)
                  xr = K.ring(ph, "xr4", 2, [128, D], F32, dma=True)
                  tmr = K.ring(ph, "tm4", 2, [128, 512], F32)
                  for sg in segs:
                      n = sg["n"]
                      K.dma("sp", g2t[:], sg["mod"][:, 5 * D:6 * D], g2_k, writes=[g2_k], dram_reads=[sg["mod_k"]])
                      for (b0, bw) in blocks_of(n):
                          nj = bw // 128
                          h2, h2_k = h2r.next()
                          K.dma("sp", h2[:, :, 0:bw + 2], sg["h2T"][:, :, b0:b0 + bw + 2], h2_k, writes=[h2_k], dram_reads=[sg["h2T_k"]])
                          half = (bw + 2) // 2
                          for i in range(NFT):
                              accs = []
                              for which in range(2):
                                  ci = which * NFT + i
                                  col0 = ci * 128
                                  z, z_k = zr.next()
                                  for (c0, c1) in ((0, half), (half, bw + 2)):
                                      ps, ps_k = psr(0, 8)
                                      for kt in range(KT):
                                          K.op("pe", lambda e, kt=kt: e.matmul(ps[:, 0:c1 - c0], lhsT=wup[:, kt, col0:col0 + 128], rhs=h2[:, kt, c0:c1],
                                                                               start=(kt == 0), stop=(kt == KT - 1)),
                                               reads=[wup_k, h2_k], writes=[ps_k], inc=(kt == KT - 1))
                                      K.op("act", lambda e: e.activation(out=z[:, c0:c1], in_=ps[:, 0:c1 - c0], func=AF.Copy), reads=[ps_k], writes=[z_k])
                                  acc, acc_k = accr.next()
                                  K.op("dve", lambda e: e.tensor_scalar(out=acc[:, 0:bw], in0=z[:, 0:bw], scalar1=fdw[:, ci, 0:1], scalar2=fdwb[:, ci:ci + 1],
                                                                        op0=ALU.mult, op1=ALU.add),
                                       reads=[z_k, fdw_k, fdwb_k], writes=[acc_k])
                                  K.op("dve", lambda e: e.scalar_tensor_tensor(out=acc[:, 0:bw], in0=z[:, 1:bw + 1], scalar=fdw[:, ci, 1:2], in1=acc[:, 0:bw],
                                                                               op0=ALU.mult, op1=ALU.add),
                                       reads=[z_k, fdw_k, acc_k], writes=[acc_k])
                                  K.op("dve", lambda e: e.scalar_tensor_tensor(out=acc[:, 0:bw], in0=z[:, 2:bw + 2], scalar=fdw[:, ci, 2:3], in1=acc[:, 0:bw],
                                                                               op0=ALU.mult, op1=ALU.add),
                                       reads=[z_k, fdw_k, acc_k], writes=[acc_k])
                                  accs.append((acc, acc_k))
                              sgf, sgf_k = sgr.next()
                              K.op("act", lambda e: e.activation(out=sgf[:, 0:bw], in_=accs[0][0][:, 0:bw], func=AF.Silu), reads=[accs[0][1]], writes=[sgf_k])
                              K.op("pool", lambda e, i=i: e.tensor_tensor(out=uT[:, i, 0:bw], in0=sgf[:, 0:bw], in1=accs[1][0][:, 0:bw], op=ALU.mult),
                                   reads=[sgf_k, accs[1][1]], writes=[uT_k])
                          for t in range(nj):
                              t0 = b0 + t * 128
                              if sg["prompt"] and (t0 < HALO or t0 >= n - HALO):
                                  continue
                              xt, xk = xr.next()
                              K.dma("sp", xt[:], sg["xm"][t0:t0 + 128, :], xk, writes=[xk], dram_reads=[sg["xm_k"]])
                              for hf in range(2):
                                  po, po_k = psr(0, 8)
                                  for i in range(NFT):
                                      K.op("pe", lambda e, i=i: e.matmul(po[:], lhsT=uT[:, i, t * 128:(t + 1) * 128], rhs=wdn[:, i, hf * 512:(hf + 1) * 512],
                                                                         start=(i == 0), stop=(i == NFT - 1)),
                                           reads=[uT_k, wdn_k], writes=[po_k], inc=(i == NFT - 1))
                                  tm2, tm2_k = tmr.next()
                                  K.op("dve", lambda e: e.tensor_tensor(out=tm2[:], in0=po[:], in1=g2t[:, hf * 512:(hf + 1) * 512], op=ALU.mult), reads=[po_k, g2_k], writes=[tm2_k])
                                  K.op("pool", lambda e: e.tensor_tensor(out=xt[:, hf * 512:(hf + 1) * 512], in0=xt[:, hf * 512:(hf + 1) * 512], in1=tm2[:], op=ALU.add),
                                       reads=[xk, tm2_k], writes=[xk])
                              yoff = t0 - HALO if sg["prompt"] else t0
                              K.dma("sp", sg["y"][yoff:yoff + 128, :], xt[:], xk, reads=[xk], acc=[sg["y_k"]])
                  K.end_phase()
                  chk(l)
        except _Stop:
            print('STOPPED at', _stop)
        K.barrier()
        print("instructions emitted:", K.nops)
    return nc


def rope_table(pos):
    inv = np.power(np.float32(10000.0), -np.arange(0, 32, 2, dtype=np.float32) / np.float32(32)).astype(np.float32)
    ang = pos.astype(np.float32)[:, None] * inv[None, :]
    return np.concatenate([np.cos(ang), np.sin(ang)], axis=1).astype(np.float32)


def layer_weights(inp, l):
    f = lambda a: np.ascontiguousarray(a, dtype=np.float32)
    ukv = inp["w_ukv"][l].reshape(128, NH, 128)
    w = dict(
        w_ada=f(inp["w_ada"][l]), b_ada=f(inp["b_ada"][l][None, :]), norm1=f(inp["norm1"][l][None, :]), norm2=f(inp["norm2"][l][None, :]),
        w_in=f(inp["w_in"][l]), w_uq=f(inp["w_uq"][l]),
        w_ukv=f(np.concatenate([ukv[:, :, 0:64].reshape(128, 512), ukv[:, :, 64:128].reshape(128, 512)], axis=1)),
        qan=f(inp["q_a_norm"][l].reshape(2, 128).T), kvan=f(inp["kv_a_norm"][l].reshape(128, 1)),
        qhn=f(inp["q_head_norm"][l][None, :]), khn=f(inp["k_head_norm"][l][None, :]),
        w_ao=f(inp["w_attn_o"][l].reshape(NH, 64, D).transpose(1, 0, 2).reshape(64, NH * D)),
        cdw=f(inp["conv_dw"][l].T.reshape(4, 128, 31).transpose(1, 0, 2).reshape(128, 4 * 31)),
        cdwb=f(inp["conv_dw_b"][l].reshape(4, 128).T), clng=f(inp["conv_ln_g"][l].reshape(4, 128).T), clnb=f(inp["conv_ln_b"][l].reshape(4, 128).T),
        w_co=f(inp["w_conv_o"][l]), sglng=f(inp["sg_ln_g"][l][None, :]), sglnb=f(inp["sg_ln_b"][l][None, :]),
        sgwT=f(inp["sg_w"][l].transpose(2, 0, 1).reshape(128, 4 * 128)),
        sgb=f(inp["sg_b"][l].reshape(1, 512)), w_so=f(inp["w_sg_o"][l]), w_out=f(inp["w_out"][l]), w_up=f(inp["w_up"][l]),
        fdw=f(inp["ffn_dw"][l].T.reshape(44, 128, 3).transpose(1, 0, 2).reshape(128, 44 * 3)),
        fdwb=f(inp["ffn_dw_b"][l].reshape(44, 128).T), w_down=f(inp["w_down"][l]))
    return w


_PROG = {}


def run_model(inp, cfg, n_cores=8):
    NS, SS, PCH, PS = cfg["NS"], cfg["SS"], cfg["PCH"], cfg["PS"]
    PSEG = PCH + 2 * HALO
    key = (NS, SS, PCH, PS)
    L = inp["w_ada"].shape[0]
    assert L == 2
    if key not in _PROG:
        _PROG[key] = build_program(cfg, L)
    nc = _PROG[key]
    xp = np.asarray(inp["x_prompt"], dtype=np.float32)
    xs = np.asarray(inp["x_sample"], dtype=np.float32)
    cp = np.asarray(inp["c_prompt"], dtype=np.float32)
    cs = np.asarray(inp["c_sample"], dtype=np.float32)
    nchunk = PS // PCH
    rope_s = rope_table(np.arange(SS))
    rope_pc = rope_table(np.arange(PS))
    wls = [layer_weights(inp, l) for l in range(L)]
    in_maps = []
    for c in range(n_cores):
        b = c // nchunk
        r = c % nchunk
        lo = r * PCH - HALO
        pm = np.zeros((128, 2), np.float32)
        pm[:, 0] = 1.0 if r > 0 else 0.0
        pm[:, 1] = 1.0 if r < nchunk - 1 else 0.0
        sel = np.zeros((128, nchunk), np.float32)
        sel[:, r] = 1.0
        m = dict(xs=np.ascontiguousarray(xs[c * NS:(c + 1) * NS].reshape(NS * SS, D)),
                 xpctx=np.ascontiguousarray(xp[b]),
                 cvT=np.ascontiguousarray(np.concatenate([cs[c * NS:(c + 1) * NS], cp[b:b + 1]], axis=0).reshape((NS + 1) * KT, 128).T),
                 pmask=pm, psel=sel, rope_s=rope_s, rope_pc=rope_pc, rope_pq=rope_table(np.arange(lo, lo + PSEG)))
        for l in range(L):
            for k, v in wls[l].items():
                m["%s_%d" % (k, l)] = v
        in_maps.append(m)
    res = run_bass_kernel_spmd(nc, in_maps, core_ids=list(range(n_cores)))
    ys = np.stack([np.asarray(r_["ys"]).reshape(NS, SS, D) for r_ in res.results], axis=0).reshape(n_cores * NS, SS, D)
    ypo = np.stack([np.asarray(r_["yp"]) for r_ in res.results], axis=0).reshape(n_cores // nchunk, PS, D)
    return ypo.astype(np.float32), ys.astype(np.float32)


def kernel(**inputs):
    yp, ys = run_model(inputs, CFG, 8)
    return (yp, ys)
```

```python
import numpy as np
from contextlib import ExitStack
import concourse.bass as bass
import concourse.mybir as mybir
from concourse.bass_utils import run_bass_kernel_spmd

F32 = mybir.dt.float32
BF16 = mybir.dt.bfloat16
AF = mybir.ActivationFunctionType
ALU = mybir.AluOpType
AX = mybir.AxisListType

D = 1024
KT = 8
NH = 8
DQK = 96
EPS = 1e-6
D_IN = 5536
D_FF = 2816
NFT = 22
HALO = 128

CFG = dict(NS=4, SS=2048, PCH=2048, PS=8192)


class Trk:
    __slots__ = ("w", "r", "dsem", "dcnt")

    def __init__(self):
        self.w = {}
        self.r = {}
        self.dsem = None
        self.dcnt = 0


class KB:
    def __init__(self, nc, es):
        self.nc = nc
        self.es = es
        self.eng = {"pe": nc.tensor, "act": nc.scalar, "dve": nc.vector, "pool": nc.gpsimd, "sp": nc.sync}
        self.sem = {k: es.enter_context(nc.semaphore("s_" + k)) for k in ["pe", "act", "dve", "pool"]}
        self.cnt = {k: 0 for k in self.sem}
        self.seen = {k: {} for k in self.eng}
        self.pend = {k: [] for k in self.eng}
        self.dma_trks = []
        self.nops = 0
        self.sem_free = []
        self.phase_trks = []
        self.nsem = 0

    def get_dsem(self):
        if self.sem_free:
            return self.sem_free.pop()
        self.nsem += 1
        return (self.es.enter_context(self.nc.semaphore("dq%d" % self.nsem)), 0)

    def release(self, trks):
        for k in trks:
            if k.dsem is not None:
                self.sem_free.append((k.dsem, k.dcnt))
                if k in self.dma_trks:
                    self.dma_trks.remove(k)

    def _wait(self, e, sem, val):
        d = self.seen[e]
        if d.get(sem, 0) >= val:
            return
        self.eng[e].wait_ge(sem, val)
        d[sem] = val

    def _deps(self, e, reads, writes):
        for t in reads:
            for s, (v, ek) in t.w.items():
                self._wait(e, s, v)
        for t in writes:
            for s, (v, ek) in t.w.items():
                if ek != e:
                    self._wait(e, s, v)
            for ek, (s, v) in t.r.items():
                if ek != e:
                    self._wait(e, s, v)

    def op(self, e, fn, reads=(), writes=(), inc=True):
        if getattr(self, 'skip', False):
            return
        self._deps(e, reads, writes)
        ins = fn(self.eng[e])
        self.nops += 1
        if not inc:
            self.pend[e].append((reads, writes))
            return
        self.cnt[e] += 1
        c = self.cnt[e]
        s = self.sem[e]
        ins.then_inc(s, 1)
        self.pend[e].append((reads, writes))
        for rd, wr in self.pend[e]:
            for t in wr:
                t.w = {s: (c, e)}
                t.r = {}
        for rd, wr in self.pend[e]:
            for t in rd:
                t.r[e] = (s, c)
        self.pend[e] = []

    def dma(self, q, out, in_, own, reads=(), writes=(), acc=(), dram_reads=(), slow=False):
        if getattr(self, 'skip', False):
            return
        self._deps(q, list(reads) + list(dram_reads), writes)
        for t in acc:
            for ek, (s, v) in t.r.items():
                self._wait(q, s, v)
        own.dcnt += 16
        if slow:
            self.eng[q].dma_start(out=out, in_=in_, allow_slow_non_contiguous=True).then_inc(own.dsem, 16)
        else:
            self.eng[q].dma_start(out=out, in_=in_).then_inc(own.dsem, 16)
        self.nops += 1
        key = ("dma", own.dsem)
        for t in writes:
            t.w = {own.dsem: (own.dcnt, key)}
            t.r = {}
        for t in acc:
            t.w[own.dsem] = (own.dcnt, key)
        for t in reads:
            t.r[key] = (own.dsem, own.dcnt)

    def end_phase(self):
        self.barrier()
        self.release(list(self.phase_trks))
        self.phase_trks = []

    def barrier(self):
        for e in self.eng:
            for k in self.sem:
                if k != e and self.cnt[k] > 0:
                    self._wait(e, self.sem[k], self.cnt[k])
            for t in self.dma_trks:
                if t.dcnt > 0:
                    self._wait(e, t.dsem, t.dcnt)

    def sb(self, es, name, shape, dt, dma=False):
        self.nalloc = getattr(self, "nalloc", 0) + 1
        t = es.enter_context(self.nc.sbuf_tensor("%s_u%d" % (name, self.nalloc), list(shape), dt))
        k = Trk()
        if dma:
            k.dsem, k.dcnt = self.get_dsem()
            self.dma_trks.append(k)
            self.phase_trks.append(k)
        return t, k

    def ring(self, es, name, n, shape, dt, dma=False):
        return Ring([self.sb(es, "%s%d" % (name, i), shape, dt, dma) for i in range(n)])


class Ring:
    def __init__(self, items):
        self.items = items
        self.i = 0

    def next(self):
        it = self.items[self.i % len(self.items)]
        self.i += 1
        return it


def blocks_of(n, bs=512):
    out = []
    o = 0
    while o < n:
        b = min(bs, n - o)
        out.append((o, b))
        o += b
    return out


def build_program(cfg, n_layers=1):
    NS, SS, PCH, PS = cfg["NS"], cfg["SS"], cfg["PCH"], cfg["PS"]
    PSEG = PCH + 2 * HALO
    nc = bass.Bass("TRN2", target_bir_lowering=False)

    def din(name, shape):
        return nc.dram_tensor(name, list(shape), F32, kind="ExternalInput").ap()

    xs = din("xs", [NS * SS, D])
    psel = din("psel", [128, PS // PCH])
    xpctx = din("xpctx", [PS, D])
    cvT = din("cvT", [128, (NS + 1) * KT])
    pmask = din("pmask", [128, 2])
    rope_s = din("rope_s", [SS, 32])
    rope_pc = din("rope_pc", [PS, 32])
    rope_pq = din("rope_pq", [PSEG, 32])
    W = {}
    wshapes = dict(
        w_ada=[D, 6 * D], b_ada=[1, 6 * D], norm1=[1, D], norm2=[1, D], w_in=[D, D_IN],
        w_uq=[256, 768], w_ukv=[128, 1024], qan=[128, 2], kvan=[128, 1], qhn=[1, 96], khn=[1, 96],
        w_ao=[64, 8 * D], cdw=[128, 4 * 31], cdwb=[128, 4], clng=[128, 4], clnb=[128, 4],
        w_co=[512, D], sglng=[1, 512], sglnb=[1, 512], sgwT=[128, 4 * 128], sgb=[1, 512], w_so=[512, D],
        w_out=[D, D], w_up=[D, 2 * D_FF], fdw=[128, 44 * 3], fdwb=[128, 44], w_down=[D_FF, D])
    for l in range(n_layers):
        for k, shp in wshapes.items():
            W[(l, k)] = din("%s_%d" % (k, l), shp)
    ys = nc.dram_tensor("ys", [NS * SS, D], F32, kind="ExternalOutput").ap()
    yp = nc.dram_tensor("yp", [PCH, D], F32, kind="ExternalOutput").ap()

    x1s = nc.dram_tensor("x1s", [NS * SS, D], F32).ap()
    x1full = nc.dram_tensor("x1full", [PS, D], F32).ap()
    x1seg = nc.dram_tensor("x1seg", [PSEG, D], F32).ap()
    x1s_k = [Trk() for _ in range(NS)]
    x1full_k = Trk()
    x1seg_k = Trk()
    assert n_layers == 2

    def make_segs(l):
        segs = []
        for i in range(NS):
            if l == 0:
                segs.append(dict(n=SS, s=SS, x=xs[i * SS:(i + 1) * SS, :], x_k=Trk(), ctx=None, ctx_k=None, rq=rope_s, rk=rope_s,
                                 y=x1s[i * SS:(i + 1) * SS, :], y_k=x1s_k[i], prompt=False))
            else:
                segs.append(dict(n=SS, s=SS, x=x1s[i * SS:(i + 1) * SS, :], x_k=x1s_k[i], ctx=None, ctx_k=None, rq=rope_s, rk=rope_s,
                                 y=ys[i * SS:(i + 1) * SS, :], y_k=Trk(), prompt=False))
        if l == 0:
            segs.append(dict(n=PS, s=PS, x=xpctx, x_k=Trk(), ctx=None, ctx_k=None, rq=rope_pc, rk=rope_pc,
                             y=x1full, y_k=x1full_k, prompt=False))
        else:
            segs.append(dict(n=PSEG, s=PS, x=x1seg, x_k=x1seg_k, ctx=x1full, ctx_k=x1full_k, rq=rope_pq, rk=rope_pc,
                             y=yp, y_k=Trk(), prompt=True))
        for si, sg in enumerate(segs):
            n, s_ = sg["n"], sg["s"]

            def dsc(nm, shape, dt=BF16):
                return nc.dram_tensor("%s_%d_%d" % (nm, l, si), list(shape), dt).ap()
            sg["hT"] = dsc("hT", [128, KT, n]); sg["hT_k"] = Trk()
            sg["qT"] = dsc("qT", [DQK, NH, n]); sg["qT_k"] = Trk()
            sg["kT"] = dsc("kT", [NH, DQK, s_]); sg["kT_k"] = Trk()
            sg["v"] = dsc("v", [NH, 128, s_ // 128, 65]); sg["v_k"] = Trk()
            sg["gat"] = dsc("gat", [128, 24, n]); sg["gat_k"] = Trk()
            sg["glu"] = dsc("glu", [128, 4, n + 30]); sg["glu_k"] = Trk()
            sg["us"] = dsc("us", [128, 4, n]); sg["us_k"] = Trk()
            sg["ao"] = dsc("ao", [64, NH, n]); sg["ao_k"] = Trk()
            sg["xm"] = dsc("xm", [n, D], F32); sg["xm_k"] = Trk()
            sg["h2T"] = dsc("h2T", [128, KT, n + 2]); sg["h2T_k"] = Trk()
            sg["mod"] = dsc("mod", [128, 6 * D], F32); sg["mod_k"] = Trk()
        return segs

    all_segs = [make_segs(l) for l in range(n_layers)]

    with ExitStack() as es:
        K = KB(nc, es)
        ident, ident_k = K.sb(es, "ident", [128, 128], BF16)
        ones_bf, ones_k = K.sb(es, "ones_bf", [128, 128], BF16)
        ones_f, onesf_k = K.sb(es, "ones_f", [128, 64], F32)
        zt, zt_k = K.sb(es, "zt", [128, 64], BF16, dma=True)
        mk, mk_k = K.sb(es, "mk", [128, 2], F32, dma=True)
        K.op("dve", lambda e: e.memset(ident[:], 1.0), writes=[ident_k])
        K.op("pool", lambda e: e.affine_select(out=ident[:], in_=ident[:], pattern=[[-1, 128]],
                                               compare_op=ALU.is_equal, fill=0.0, base=0, channel_multiplier=1),
             reads=[ident_k], writes=[ident_k])
        K.op("dve", lambda e: e.memset(ones_bf[:], 1.0), writes=[ones_k])
        K.op("dve", lambda e: e.memset(ones_f[:], 1.0), writes=[onesf_k])
        K.op("dve", lambda e: e.memset(zt[:], 0.0), writes=[zt_k])
        K.dma("sp", mk[:], pmask, mk_k, writes=[mk_k])
        K.phase_trks = []
        PSB = []
        for i in range(8):
            t = es.enter_context(nc.psum_tensor("psb%d" % i, [128, 512], F32))
            PSB.append((t, Trk()))
        psr_state = [0]

        PSR_HI = [8]

        def psr(lo=0, hi=None):
            if hi is None:
                hi = PSR_HI[0]
            i = lo + psr_state[0] % (hi - lo)
            psr_state[0] += 1
            return PSB[i]

        for sg in [g for sl in all_segs for g in sl]:
            n = sg["n"]
            K.dma("sp", sg["glu"][:, :, 0:15], zt[:, 0:60].rearrange("p (a b) -> p a b", a=4), zt_k, reads=[zt_k], acc=[sg["glu_k"]])
            K.dma("sp", sg["glu"][:, :, n + 15:n + 30], zt[:, 0:60].rearrange("p (a b) -> p a b", a=4), zt_k, reads=[zt_k], acc=[sg["glu_k"]])
            K.dma("sp", sg["h2T"][:, :, 0:1], zt[:, 0:8].rearrange("p (a b) -> p a b", a=8), zt_k, reads=[zt_k], acc=[sg["h2T_k"]], slow=True)
            K.dma("sp", sg["h2T"][:, :, n + 1:n + 2], zt[:, 0:8].rearrange("p (a b) -> p a b", a=8), zt_k, reads=[zt_k], acc=[sg["h2T_k"]], slow=True)

        def wview(ap_, p=128):
            return ap_.rearrange("(kt p) n -> p kt n", p=p)

        def norm_tile(P, xt, xk, gam, sh, mask_col=None):
            junk, junk_k = P["junk"].next()
            ss, ss_k = P["ss"].next()
            K.op("dve", lambda e: e.memset(ss[:], 0.0), writes=[ss_k])
            K.op("act", lambda e: e.activation(out=junk[:], in_=xt[:], func=AF.Square, accum_out=ss[:, 0:1]),
                 reads=[xk, ss_k], writes=[junk_k, ss_k])
            K.op("act", lambda e: e.activation(out=ss[:, 1:2], in_=ss[:, 0:1], func=AF.Sqrt, bias=EPS, scale=1.0 / D),
                 reads=[ss_k], writes=[ss_k])
            K.op("dve", lambda e: e.reciprocal(out=ss[:, 2:3], in_=ss[:, 1:2]), reads=[ss_k], writes=[ss_k])
            K.op("dve", lambda e: e.scalar_tensor_tensor(out=junk[:], in0=xt[:], scalar=ss[:, 2:3], in1=gam[0][:],
                                                         op0=ALU.mult, op1=ALU.mult),
                 reads=[xk, ss_k, gam[1], junk_k], writes=[junk_k])
            hb, hb_k = P["hb"].next()
            K.op("pool", lambda e: e.tensor_tensor(out=hb[:], in0=junk[:], in1=sh[0][:], op=ALU.add),
                 reads=[junk_k, sh[1]], writes=[hb_k])
            if mask_col is not None:
                K.op("dve", lambda e: e.tensor_scalar(out=hb[:], in0=hb[:], scalar1=mk[:, mask_col:mask_col + 1],
                                                      scalar2=None, op0=ALU.mult),
                     reads=[hb_k, mk_k], writes=[hb_k])
            return hb, hb_k

        def transpose_to(P, hb, hb_k, dst, dst_k, ncols_src=D, rows=128, chunk=128):
            nchunk = ncols_src // chunk
            ps, ps_k = psr()
            psb = ps[:].bitcast(BF16)
            for i in range(nchunk):
                K.op("pe", lambda e, i=i: e.transpose(psb[0:chunk, i * 128:(i + 1) * 128], hb[:, i * chunk:(i + 1) * chunk], ident[:]),
                     reads=[hb_k, ident_k], writes=[ps_k], inc=(i == nchunk - 1))
            K.op("act", lambda e: e.activation(out=dst, in_=psb[0:chunk, 0:nchunk * 128].rearrange("p (a b) -> p a b", a=nchunk),
                                               func=AF.Copy),
                 reads=[ps_k], writes=[dst_k])

        def norm_block(P, xb, xb_k, G, gam, sh, masks=()):
            junk, junk_k = P["junkb"].next()
            ss, ss_k = P["ssb"].next()
            K.op("dve", lambda e: e.memset(ss[:], 0.0), writes=[ss_k])
            for j in range(G):
                K.op("act", lambda e, j=j: e.activation(out=junk[:, j, :], in_=xb[:, j, :], func=AF.Square, accum_out=ss[:, j:j + 1]),
                     reads=[xb_k], writes=[junk_k, ss_k])
            K.op("act", lambda e: e.activation(out=ss[:, 4:4 + G], in_=ss[:, 0:G], func=AF.Sqrt, bias=EPS, scale=1.0 / D),
                 reads=[ss_k], writes=[ss_k])
            K.op("dve", lambda e: e.reciprocal(out=ss[:, 8:8 + G], in_=ss[:, 4:4 + G]), reads=[ss_k], writes=[ss_k])
            for j in range(G):
                K.op("dve", lambda e, j=j: e.scalar_tensor_tensor(out=junk[:, j, :], in0=xb[:, j, :], scalar=ss[:, 8 + j:9 + j], in1=gam[0][:],
                                                                  op0=ALU.mult, op1=ALU.mult),
                     reads=[xb_k, ss_k, gam[1], junk_k] if j == 0 else [xb_k, gam[1]], writes=[junk_k])
            hbb, hbb_k = P["hbb"].next()
            K.op("pool", lambda e: e.tensor_tensor(out=hbb[:, 0:G, :], in0=junk[:, 0:G, :], in1=sh[0][:].unsqueeze(1).to_broadcast([128, G, D]), op=ALU.add),
                 reads=[junk_k, sh[1]], writes=[hbb_k])
            for (j, mc) in masks:
                K.op("dve", lambda e, j=j, mc=mc: e.tensor_scalar(out=hbb[:, j, :], in0=hbb[:, j, :], scalar1=mk[:, mc:mc + 1], scalar2=None, op0=ALU.mult),
                     reads=[hbb_k, mk_k], writes=[hbb_k])
            return hbb, hbb_k

        def rms_block(P, src3, src_k, G, n, dst3, dst_k):
            junk, junk_k = P["junkb"].next()
            s4, s4_k = P["ssb"].next()
            K.op("dve", lambda e: e.memset(s4[:], 0.0), writes=[s4_k])
            for j in range(G):
                K.op("act", lambda e, j=j: e.activation(out=junk[:, j, 0:n], in_=src3[:, j, :], func=AF.Square, accum_out=s4[:, j:j + 1]),
                     reads=[src_k], writes=[junk_k, s4_k])
            K.op("act", lambda e: e.activation(out=s4[:, 4:4 + G], in_=s4[:, 0:G], func=AF.Sqrt, bias=EPS, scale=1.0 / n), reads=[s4_k], writes=[s4_k])
            K.op("dve", lambda e: e.reciprocal(out=s4[:, 8:8 + G], in_=s4[:, 4:4 + G]), reads=[s4_k], writes=[s4_k])
            K.op("dve", lambda e: e.tensor_tensor(out=dst3, in0=src3, in1=s4[:, 8:8 + G].unsqueeze(2).to_broadcast([128, G, n]), op=ALU.mult),
                 reads=[src_k, s4_k], writes=[dst_k])

        def headnorm_rope_block(P, f, f_k, G, gain, rp, rp_k, out, out_k):
            f3 = f[:, 0:G, :].rearrange("p j (h d) -> p (j h) d", h=NH)
            f4 = f[:, 0:G, :].rearrange("p j (h d) -> p j h d", h=NH)
            o4 = out[:, 0:G, :].rearrange("p j (h d) -> p j h d", h=NH)
            sq, sq_k = P["junkb"].next()
            st, st_k = P["stb"].next()
            sq3 = sq[:].rearrange("p j c -> p (j c)")[:, 0:G * 768].rearrange("p (a d) -> p a d", d=DQK)
            K.op("dve", lambda e: e.tensor_tensor(out=sq3, in0=f3, in1=f3, op=ALU.mult), reads=[f_k], writes=[sq_k])
            K.op("dve", lambda e: e.reduce_sum(out=st[:, 0:G * NH], in_=sq3, axis=AX.X), reads=[sq_k], writes=[st_k])
            K.op("act", lambda e: e.activation(out=st[:, 32:32 + G * NH], in_=st[:, 0:G * NH], func=AF.Sqrt, bias=EPS, scale=1.0 / DQK),
                 reads=[st_k], writes=[st_k])
            K.op("dve", lambda e: e.reciprocal(out=st[:, 64:64 + G * NH], in_=st[:, 32:32 + G * NH]), reads=[st_k], writes=[st_k])
            K.op("dve", lambda e: e.tensor_tensor(out=f3, in0=f3, in1=st[:, 64:64 + G * NH].unsqueeze(2).to_broadcast([128, G * NH, DQK]), op=ALU.mult),
                 reads=[f_k, st_k], writes=[f_k])
            K.op("dve", lambda e: e.tensor_tensor(out=f3, in0=f3, in1=gain[0][:].unsqueeze(1).to_broadcast([128, G * NH, DQK]), op=ALU.mult),
                 reads=[f_k, gain[1]], writes=[f_k])
            tt, tt_k = P["ttb"].next()
            t5 = tt[:].rearrange("p (a j h d) -> p a j h d", a=4, j=4, h=NH)
            x1 = f4[:, :, :, 64:80]
            x2 = f4[:, :, :, 80:96]
            cs = rp[:, 0:G, 0:16].unsqueeze(2).to_broadcast([128, G, NH, 16])
            sn = rp[:, 0:G, 16:32].unsqueeze(2).to_broadcast([128, G, NH, 16])
            K.op("dve", lambda e: e.tensor_tensor(out=t5[:, 0, 0:G], in0=x1, in1=cs, op=ALU.mult), reads=[f_k, rp_k], writes=[tt_k])
            K.op("dve", lambda e: e.tensor_tensor(out=t5[:, 1, 0:G], in0=x2, in1=sn, op=ALU.mult), reads=[f_k, rp_k], writes=[tt_k])
            K.op("dve", lambda e: e.tensor_tensor(out=t5[:, 2, 0:G], in0=x1, in1=sn, op=ALU.mult), reads=[f_k, rp_k], writes=[tt_k])
            K.op("dve", lambda e: e.tensor_tensor(out=t5[:, 3, 0:G], in0=x2, in1=cs, op=ALU.mult), reads=[f_k, rp_k], writes=[tt_k])
            K.op("dve", lambda e: e.tensor_tensor(out=o4[:, :, :, 64:80], in0=t5[:, 0, 0:G], in1=t5[:, 1, 0:G], op=ALU.subtract), reads=[tt_k], writes=[out_k])
            K.op("dve", lambda e: e.tensor_tensor(out=o4[:, :, :, 80:96], in0=t5[:, 2, 0:G], in1=t5[:, 3, 0:G], op=ALU.add), reads=[tt_k], writes=[out_k])
            K.op("pool", lambda e: e.tensor_copy(out=o4[:, :, :, 0:64], in_=f4[:, :, :, 0:64]), reads=[f_k], writes=[out_k])

        def headnorm_rope(P, f, f_k, gain, rp, rp_k, out, out_k):
            f3 = f[:].rearrange("p (h d) -> p h d", h=NH)
            o3 = out[:].rearrange("p (h d) -> p h d", h=NH)
            sq, sq_k = P["sq"].next()
            st, st_k = P["st"].next()
            K.op("dve", lambda e: e.tensor_tensor(out=sq[:], in0=f[:], in1=f[:], op=ALU.mult), reads=[f_k], writes=[sq_k])
            K.op("dve", lambda e: e.reduce_sum(out=st[:, 0:8], in_=sq[:].rearrange("p (h d) -> p h d", h=NH), axis=AX.X),
                 reads=[sq_k], writes=[st_k])
            K.op("act", lambda e: e.activation(out=st[:, 8:16], in_=st[:, 0:8], func=AF.Sqrt, bias=EPS, scale=1.0 / DQK),
                 reads=[st_k], writes=[st_k])
            K.op("dve", lambda e: e.reciprocal(out=st[:, 16:24], in_=st[:, 8:16]), reads=[st_k], writes=[st_k])
            K.op("dve", lambda e: e.tensor_tensor(out=f3, in0=f3, in1=st[:, 16:24].unsqueeze(2).to_broadcast([128, NH, DQK]), op=ALU.mult),
                 reads=[f_k, st_k], writes=[f_k])
            K.op("dve", lambda e: e.tensor_tensor(out=f3, in0=f3, in1=gain[0][:].unsqueeze(1).to_broadcast([128, NH, DQK]), op=ALU.mult),
                 reads=[f_k, gain[1]], writes=[f_k])
            tt, tt_k = P["tt"].next()
            t4 = tt[:].rearrange("p (a h d) -> p a h d", a=4, h=NH)
            x1 = f3[:, :, 64:80]
            x2 = f3[:, :, 80:96]
            cs = rp[:, 0:16].unsqueeze(1).to_broadcast([128, NH, 16])
            sn = rp[:, 16:32].unsqueeze(1).to_broadcast([128, NH, 16])
            K.op("dve", lambda e: e.tensor_tensor(out=t4[:, 0], in0=x1, in1=cs, op=ALU.mult), reads=[f_k, rp_k], writes=[tt_k])
            K.op("dve", lambda e: e.tensor_tensor(out=t4[:, 1], in0=x2, in1=sn, op=ALU.mult), reads=[f_k, rp_k], writes=[tt_k])
            K.op("dve", lambda e: e.tensor_tensor(out=t4[:, 2], in0=x1, in1=sn, op=ALU.mult), reads=[f_k, rp_k], writes=[tt_k])
            K.op("dve", lambda e: e.tensor_tensor(out=t4[:, 3], in0=x2, in1=cs, op=ALU.mult), reads=[f_k, rp_k], writes=[tt_k])
            K.op("dve", lambda e: e.tensor_tensor(out=o3[:, :, 64:80], in0=t4[:, 0], in1=t4[:, 1], op=ALU.subtract), reads=[tt_k], writes=[out_k])
            K.op("dve", lambda e: e.tensor_tensor(out=o3[:, :, 80:96], in0=t4[:, 2], in1=t4[:, 3], op=ALU.add), reads=[tt_k], writes=[out_k])
            K.op("pool", lambda e: e.tensor_copy(out=o3[:, :, 0:64], in_=f3[:, :, 0:64]), reads=[f_k], writes=[out_k])

        import os as _os
        _stop = _os.environ.get("KSTOP", "")
        _phc = [0]

        class _Stop(Exception):
            pass

        def chk(l):
            _phc[0] += 1
            if _stop and _stop == "%d,%d" % (l, _phc[0]):
                K.skip = True
                print('STOPPED at', _stop)

        try:
          for l in range(n_layers):
              _phc[0] = 0
              Wl = {k: W[(l, k)] for k in wshapes}
              segs = all_segs[l]
              if l == 1:
                  with ExitStack() as ph:
                      selt, selt_k = K.sb(ph, "selt", [128, PS // PCH], F32, dma=True)
                      K.dma("sp", selt[:], psel, selt_k, writes=[selt_k])
                      accr_ = K.ring(ph, "xacc", 2, [128, D], F32, dma=True)
                      ldr_ = K.ring(ph, "xld", 4, [128, D], F32, dma=True)
                      for j in range(PSEG // 128):
                          ac, ac_k = accr_.next()
                          K.op("dve", lambda e: e.memset(ac[:], 0.0), writes=[ac_k])
                          for r_ in range(PS // PCH):
                              row = r_ * PCH - HALO + j * 128
                              if row < 0 or row + 128 > PS:
                                  continue
                              ld, ld_k = ldr_.next()
                              K.dma("sp", ld[:], x1full[row:row + 128, :], ld_k, writes=[ld_k], dram_reads=[x1full_k])
                              K.op("dve", lambda e, r_=r_: e.scalar_tensor_tensor(out=ac[:], in0=ld[:], scalar=selt[:, r_:r_ + 1], in1=ac[:],
                                                                                  op0=ALU.mult, op1=ALU.add),
                                   reads=[ld_k, selt_k, ac_k], writes=[ac_k])
                          K.dma("sp", x1seg[j * 128:(j + 1) * 128, :], ac[:], ac_k, reads=[ac_k], acc=[x1seg_k])
                      K.end_phase()
                      chk(l)
              with ExitStack() as ph:
                  PSR_HI[0] = 8
                  nseg = len(segs)
                  cT, cT_k = K.sb(ph, "cT", [128, nseg * KT], F32, dma=True)
                  crep, crep_k = K.sb(ph, "crep", [128, nseg * KT, 128], BF16)
                  bb, bb_k = K.sb(ph, "bb", [1, 6 * D], BF16, dma=True)
                  n1b, n1b_k = K.sb(ph, "n1b", [128, D], F32, dma=True)
                  n2b, n2b_k = K.sb(ph, "n2b", [128, D], F32, dma=True)
                  wr = K.ring(ph, "wada", 2, [128, KT, 512], BF16, dma=True)
                  modt = [K.sb(ph, "modt%d" % s, [128, 6 * D], F32, dma=True) for s in range(nseg)]
                  K.dma("sp", cT[:], cvT, cT_k, writes=[cT_k])
                  K.op("act", lambda e: e.activation(out=cT[:], in_=cT[:], func=AF.Silu), reads=[cT_k], writes=[cT_k])
                  K.op("dve", lambda e: e.tensor_copy(out=crep[:], in_=cT[:].unsqueeze(2).to_broadcast([128, nseg * KT, 128])),
                       reads=[cT_k], writes=[crep_k])
                  K.dma("pool", bb[:], Wl["b_ada"], bb_k, writes=[bb_k])
                  K.dma("sp", n1b[:], Wl["norm1"][0, :].partition_broadcast(128), n1b_k, writes=[n1b_k])
                  K.dma("sp", n2b[:], Wl["norm2"][0, :].partition_broadcast(128), n2b_k, writes=[n2b_k])
                  wav = wview(Wl["w_ada"])
                  for c in range(12):
                      wt, wt_k = wr.next()
                      K.dma("pool", wt[:], wav[:, :, c * 512:(c + 1) * 512], wt_k, writes=[wt_k])
                      for s in range(nseg):
                          ps, ps_k = psr()
                          for kt in range(KT):
                              K.op("pe", lambda e, kt=kt: e.matmul(ps[:], lhsT=crep[:, s * KT + kt, :], rhs=wt[:, kt, :],
                                                                   start=(kt == 0), stop=False),
                                   reads=[crep_k, wt_k], writes=[ps_k], inc=False)
                          K.op("pe", lambda e: e.matmul(ps[:], lhsT=ones_bf[0:1, :], rhs=bb[0:1, c * 512:(c + 1) * 512],
                                                        start=False, stop=True),
                               reads=[ones_k, bb_k], writes=[ps_k])
                          mt, mt_k = modt[s]
                          K.op("act", lambda e: e.activation(out=mt[:, c * 512:(c + 1) * 512], in_=ps[:], func=AF.Copy),
                               reads=[ps_k], writes=[mt_k])
                  for s in range(nseg):
                      mt, mt_k = modt[s]
                      K.op("dve", lambda e: e.scalar_tensor_tensor(out=mt[:, D:2 * D], in0=mt[:, D:2 * D], scalar=1.0, in1=n1b[:],
                                                                   op0=ALU.add, op1=ALU.mult),
                           reads=[mt_k, n1b_k], writes=[mt_k])
                      K.op("dve", lambda e: e.scalar_tensor_tensor(out=mt[:, 4 * D:5 * D], in0=mt[:, 4 * D:5 * D], scalar=1.0, in1=n2b[:],
                                                                   op0=ALU.add, op1=ALU.mult),
                           reads=[mt_k, n2b_k], writes=[mt_k])
                      K.dma("sp", segs[s]["mod"], mt[:], mt_k, reads=[mt_k], acc=[segs[s]["mod_k"]])
                  K.end_phase()
                  chk(l)

              with ExitStack() as ph:
                  wlat, wlat_k = K.sb(ph, "wlat", [128, KT, 416], BF16, dma=True)
                  wuq, wuq_k = K.sb(ph, "wuq", [128, 2, 768], BF16, dma=True)
                  wukv, wukv_k = K.sb(ph, "wukv", [128, 1024], BF16, dma=True)
                  qan, qan_k = K.sb(ph, "qan", [128, 2], F32, dma=True)
                  kvan, kvan_k = K.sb(ph, "kvan", [128, 1], F32, dma=True)
                  gq, gq_k = K.sb(ph, "gq", [128, DQK], F32, dma=True)
                  gk, gk_k = K.sb(ph, "gk", [128, DQK], F32, dma=True)
                  K.dma("pool", wlat[:], wview(Wl["w_in"])[:, :, 0:416], wlat_k, writes=[wlat_k])
                  K.dma("pool", wuq[:], wview(Wl["w_uq"]), wuq_k, writes=[wuq_k])
                  K.dma("pool", wukv[:], Wl["w_ukv"], wukv_k, writes=[wukv_k])
                  K.dma("sp", qan[:], Wl["qan"], qan_k, writes=[qan_k])
                  K.dma("sp", kvan[:], Wl["kvan"], kvan_k, writes=[kvan_k])
                  K.dma("sp", gq[:], Wl["qhn"][0, :].partition_broadcast(128), gq_k, writes=[gq_k])
                  K.dma("sp", gk[:], Wl["khn"][0, :].partition_broadcast(128), gk_k, writes=[gk_k])
                  P = dict(junkb=K.ring(ph, "junkb", 1, [128, 4, D], F32), ssb=K.ring(ph, "ssb", 3, [128, 16], F32),
                           hbb=K.ring(ph, "hbb", 2, [128, 4, D], BF16), stb=K.ring(ph, "stb", 2, [128, 96], F32),
                           ttb=K.ring(ph, "ttb", 1, [128, 4 * 4 * NH * 16], F32))
                  xbr = K.ring(ph, "xb", 2, [128, 4, D], F32, dma=True)
                  rpbr = K.ring(ph, "rpb", 2, [128, 4, 32], F32, dma=True)
                  hst = K.ring(ph, "hst", 2, [128, KT, 512], BF16, dma=True)
                  qst = K.ring(ph, "qst", 1, [128, NH, 512], BF16, dma=True)
                  kst = K.ring(ph, "kst", 1, [128, NH, 512], BF16, dma=True)
                  vst = K.ring(ph, "vst", 2, [128, NH, 4, 65], BF16, dma=True)
                  for vt, vk in vst.items:
                      K.op("dve", lambda e, vt=vt: e.memset(vt[:], 1.0), writes=[vk])
                  gamt, gam_k = K.sb(ph, "gam1", [128, D], F32, dma=True)
                  sht, sh_k = K.sb(ph, "sh1", [128, D], F32, dma=True)
                  lat_r = K.ring(ph, "latb", 2, [128, 4, 416], F32)
                  cqn_r = K.ring(ph, "cqnb", 2, [128, 4, 256], BF16)
                  cqT_r = K.ring(ph, "cqTb", 2, [128, 2, 512], BF16)
                  ckn_r = K.ring(ph, "cknb", 2, [128, 4, 128], BF16)
                  ckT_r = K.ring(ph, "ckTb", 2, [128, 512], BF16)
                  qf_r = K.ring(ph, "qfb", 2, [128, 4, 768], F32)
                  qb_r = K.ring(ph, "qbb", 2, [128, 4, 768], BF16)

                  for sg in segs:
                      K.dma("sp", sht[:], sg["mod"][:, 0:D], sh_k, writes=[sh_k], dram_reads=[sg["mod_k"]])
                      K.dma("sp", gamt[:], sg["mod"][:, D:2 * D], gam_k, writes=[gam_k], dram_reads=[sg["mod_k"]])
                      passes = []
                      if sg["prompt"]:
                          passes.append((sg["ctx"], sg["s"], False, True, sg["rk"], sg["ctx_k"]))
                          passes.append((sg["x"], sg["n"], True, False, sg["rq"], sg["x_k"]))
                      else:
                          passes.append((sg["x"], sg["n"], True, True, sg["rq"], sg["x_k"]))
                      for (xsrc, ntok, want_q, want_k, rtab, xsrc_k) in passes:
                          for (b0, bw) in blocks_of(ntok):
                              G = bw // 128
                              hs, hs_k = hst.next()
                              xb, xb_k = xbr.next()
                              K.dma("sp", xb[:, 0:G, :], xsrc[b0:b0 + bw, :].rearrange("(j p) d -> p j d", p=128), xb_k, writes=[xb_k], dram_reads=[xsrc_k])
                              rp, rp_k = rpbr.next()
                              K.dma("sp", rp[:, 0:G, :], rtab[b0:b0 + bw, :].rearrange("(j p) d -> p j d", p=128), rp_k, writes=[rp_k])
                              hbb, hbb_k = norm_block(P, xb, xb_k, G, (gamt, gam_k), (sht, sh_k))
                              lat, lat_k = lat_r.next()
                              for j in range(G):
                                  transpose_to(P, hbb[:, j, :], hbb_k, hs[:, :, j * 128:(j + 1) * 128], hs_k)
                              for j in range(G):
                                  pl, pl_k = psr()
                                  for kt in range(KT):
                                      K.op("pe", lambda e, kt=kt, j=j: e.matmul(pl[:, 0:416], lhsT=hs[:, kt, j * 128:(j + 1) * 128], rhs=wlat[:, kt, :],
                                                                               start=(kt == 0), stop=(kt == KT - 1)),
                                           reads=[hs_k, wlat_k], writes=[pl_k], inc=(kt == KT - 1))
                                  K.op("act", lambda e, j=j: e.activation(out=lat[:, j, :], in_=pl[:, 0:416], func=AF.Copy), reads=[pl_k], writes=[lat_k])
                              if want_q:
                                  qs, qs_k = qst.next()
                                  cqn, cqn_k = cqn_r.next()
                                  rms_block(P, lat[:, 0:G, 0:256], lat_k, G, 256, cqn[:, 0:G, :], cqn_k)
                                  cqT, cqT_k = cqT_r.next()
                                  p2, p2_k = psr()
                                  p2b = p2[:].bitcast(BF16)
                                  for j in range(G):
                                      for i in range(2):
                                          K.op("pe", lambda e, i=i, j=j: e.transpose(p2b[:, (j * 2 + i) * 128:(j * 2 + i + 1) * 128], cqn[:, j, i * 128:(i + 1) * 128], ident[:]),
                                               reads=[cqn_k, ident_k], writes=[p2_k], inc=(j == G - 1 and i == 1))
                                  for i in range(2):
                                      K.op("act", lambda e, i=i: e.activation(out=cqT[:, i, 0:G * 128].rearrange("p (j c) -> p j c", j=G),
                                                                              in_=p2b[:, 0:G * 256].rearrange("p (j i c) -> p j i c", j=G, i=2)[:, :, i, :],
                                                                              func=AF.Copy, scale=qan[:, i:i + 1]),
                                           reads=[p2_k, qan_k], writes=[cqT_k])
                                  qf, qf_k = qf_r.next()
                                  for j in range(G):
                                      pq0, pq0_k = psr()
                                      pq1, pq1_k = psr()
                                      for i in range(2):
                                          K.op("pe", lambda e, i=i, j=j: e.matmul(pq0[:, 0:480], lhsT=cqT[:, i, j * 128:(j + 1) * 128], rhs=wuq[:, i, 0:480], start=(i == 0), stop=(i == 1)),
                                               reads=[cqT_k, wuq_k], writes=[pq0_k], inc=(i == 1))
                                      for i in range(2):
                                          K.op("pe", lambda e, i=i, j=j: e.matmul(pq1[:, 0:288], lhsT=cqT[:, i, j * 128:(j + 1) * 128], rhs=wuq[:, i, 480:768], start=(i == 0), stop=(i == 1)),
                                               reads=[cqT_k, wuq_k], writes=[pq1_k], inc=(i == 1))
                                      K.op("act", lambda e, j=j: e.activation(out=qf[:, j, 0:480], in_=pq0[:, 0:480], func=AF.Copy), reads=[pq0_k], writes=[qf_k])
                                      K.op("dve", lambda e, j=j: e.tensor_copy(out=qf[:, j, 480:768], in_=pq1[:, 0:288]), reads=[pq1_k], writes=[qf_k])
                                  qb, qb_k = qb_r.next()
                                  headnorm_rope_block(P, qf, qf_k, G, (gq, gq_k), rp, rp_k, qb, qb_k)
                                  for j in range(G):
                                      transpose_to(P, qb[:, j, :], qb_k, qs[0:DQK, :, j * 128:(j + 1) * 128], qs_k, ncols_src=768, chunk=DQK)
                              if want_k:
                                  ks, ks_k = kst.next()
                                  vs, vs_k = vst.next()
                                  ckn, ckn_k = ckn_r.next()
                                  rms_block(P, lat[:, 0:G, 256:384], lat_k, G, 128, ckn[:, 0:G, :], ckn_k)
                                  ckT, ckT_k = ckT_r.next()
                                  p3, p3_k = psr()
                                  p3b = p3[:].bitcast(BF16)
                                  for j in range(G):
                                      K.op("pe", lambda e, j=j: e.transpose(p3b[:, j * 128:(j + 1) * 128], ckn[:, j, :], ident[:]),
                                           reads=[ckn_k, ident_k], writes=[p3_k], inc=(j == G - 1))
                                  K.op("act", lambda e: e.activation(out=ckT[:, 0:G * 128], in_=p3b[:, 0:G * 128], func=AF.Copy, scale=kvan[:, 0:1]),
                                       reads=[p3_k, kvan_k], writes=[ckT_k])
                                  kf, kf_k = qf_r.next()
                                  kf4 = kf[:].rearrange("p j (h d) -> p j h d", h=NH)
                                  for j in range(G):
                                      pk, pk_k = psr()
                                      pv, pv_k = psr()
                                      K.op("pe", lambda e, j=j: e.matmul(pk[:], lhsT=ckT[:, j * 128:(j + 1) * 128], rhs=wukv[:, 0:512], start=True, stop=True),
                                           reads=[ckT_k, wukv_k], writes=[pk_k])
                                      K.op("pe", lambda e, j=j: e.matmul(pv[:], lhsT=ckT[:, j * 128:(j + 1) * 128], rhs=wukv[:, 512:1024], start=True, stop=True),
                                           reads=[ckT_k, wukv_k], writes=[pv_k])
                                      K.op("act", lambda e, j=j: e.activation(out=kf4[:, j, :, 0:64], in_=pk[:].rearrange("p (h d) -> p h d", h=NH), func=AF.Copy),
                                           reads=[pk_k], writes=[kf_k])
                                      K.op("act", lambda e, j=j: e.activation(out=vs[:, :, j, 0:64], in_=pv[:].rearrange("p (h d) -> p h d", h=NH), func=AF.Copy),
                                           reads=[pv_k], writes=[vs_k])
                                  K.op("dve", lambda e: e.tensor_copy(out=kf4[:, 0:G, :, 64:96], in_=lat[:, 0:G, 384:416].unsqueeze(2).to_broadcast([128, G, NH, 32])),
                                       reads=[lat_k], writes=[kf_k])
                                  kb, kb_k = qb_r.next()
                                  headnorm_rope_block(P, kf, kf_k, G, (gk, gk_k), rp, rp_k, kb, kb_k)
                                  for j in range(G):
                                      transpose_to(P, kb[:, j, :], kb_k, ks[0:DQK, :, j * 128:(j + 1) * 128], ks_k, ncols_src=768, chunk=DQK)
                              nj = G
                              if want_q:
                                  K.dma("sp", sg["hT"][:, :, b0:b0 + bw], hs[:, :, 0:bw], hs_k, reads=[hs_k], acc=[sg["hT_k"]])
                                  K.dma("sp", sg["qT"][:, :, b0:b0 + bw], qs[0:DQK, :, 0:bw], qs_k, reads=[qs_k], acc=[sg["qT_k"]])
                              if want_k:
                                  K.dma("sp", sg["kT"].rearrange("h d s -> d h s")[:, :, b0:b0 + bw], ks[0:DQK, :, 0:bw], ks_k, reads=[ks_k], acc=[sg["kT_k"]])
                                  K.dma("sp", sg["v"].rearrange("h p k c -> p h k c")[:, :, b0 // 128:b0 // 128 + nj, :], vs[:, :, 0:nj, :], vs_k,
                                        reads=[vs_k], acc=[sg["v_k"]])
                  K.end_phase()
                  chk(l)

              with ExitStack() as ph:
                  PSR_HI[0] = 4
                  NW = D_IN - 416
                  wbig, wbig_k = K.sb(ph, "wbig", [128, KT, NW], BF16, dma=True)
                  wv = wview(Wl["w_in"])
                  for c in range(0, NW, 640):
                      K.dma("pool", wbig[:, :, c:c + 640], wv[:, :, 416 + c:416 + c + 640], wbig_k, acc=[wbig_k])
                  wsT, wsT_k = K.sb(ph, "wsT", [128, 4, 128], BF16, dma=True)
                  K.dma("pool", wsT[:], Wl["sgwT"].rearrange("p (g q) -> p g q", g=4), wsT_k, writes=[wsT_k])
                  lng, lng_k = K.sb(ph, "lng", [128, 512], F32, dma=True)
                  lnb, lnb_k = K.sb(ph, "lnb", [128, 512], F32, dma=True)
                  bsb, bsb_k = K.sb(ph, "bsb", [128, 4, 4, 128], F32, dma=True)
                  K.dma("sp", lng[:], Wl["sglng"][0, :].partition_broadcast(128), lng_k, writes=[lng_k])
                  K.dma("sp", lnb[:], Wl["sglnb"][0, :].partition_broadcast(128), lnb_k, writes=[lnb_k])
                  for j in range(4):
                      K.dma("sp", bsb[:, :, j, :], Wl["sgb"][0, :].partition_broadcast(128).rearrange("p (g q) -> p g q", g=4), bsb_k, acc=[bsb_k])
                  hbr = K.ring(ph, "hTb", 2, [128, KT, 512], BF16, dma=True)
                  sgt_r = K.ring(ph, "sgt", 2, [128, 512], F32)
                  glub_r = K.ring(ph, "glub", 2, [128, 4, 512], BF16, dma=True)
                  ug_r = K.ring(ph, "ug", 2, [128, 4, 512], BF16)
                  vg_r = K.ring(ph, "vg", 2, [128, 512], F32)
                  jk_r = K.ring(ph, "jk2", 1, [128, 512], F32)
                  s8_r = K.ring(ph, "s8", 3, [128, 8], F32)
                  vnb_r = K.ring(ph, "vnb", 4, [128, 512], BF16)
                  tq_r = K.ring(ph, "tq", 2, [128, 512], F32)
                  usb_r = K.ring(ph, "usb", 2, [128, 4, 512], BF16, dma=True)
                  gst_r = K.ring(ph, "gst", 2, [128, 24, 512], BF16, dma=True)
                  for sg in segs:
                      n = sg["n"]
                      blks = blocks_of(n)
                      for bi, (b0, bw) in enumerate(blks):
                          nj = bw // 128
                          hT, hT_k = hbr.next()
                          K.dma("sp", hT[:, :, 0:bw], sg["hT"][:, :, b0:b0 + bw], hT_k, writes=[hT_k], dram_reads=[sg["hT_k"]])

                          def fm(col0, ps, ps_k):
                              for kt in range(KT):
                                  K.op("pe", lambda e, kt=kt: e.matmul(ps[:, 0:bw], lhsT=wbig[:, kt, col0:col0 + 128], rhs=hT[:, kt, 0:bw],
                                                                       start=(kt == 0), stop=(kt == KT - 1)),
                                       reads=[wbig_k, hT_k], writes=[ps_k], inc=(kt == KT - 1))
                          glub, glub_k = glub_r.next()
                          for c in range(4):
                              pa, pa_k = psr()
                              pg, pg_k = psr()
                              fm(c * 128, pa, pa_k)
                              fm(512 + c * 128, pg, pg_k)
                              sgt, sgt_k = sgt_r.next()
                              K.op("act", lambda e: e.activation(out=sgt[:, 0:bw], in_=pg[:, 0:bw], func=AF.Sigmoid), reads=[pg_k], writes=[sgt_k])
                              K.op("dve", lambda e, c=c: e.tensor_tensor(out=glub[:, c, 0:bw], in0=pa[:, 0:bw], in1=sgt[:, 0:bw], op=ALU.mult),
                                   reads=[pa_k, sgt_k], writes=[glub_k])
                          if sg["prompt"]:
                              if bi == 0:
                                  K.op("dve", lambda e: e.tensor_scalar(out=glub[:, :, 0:128], in0=glub[:, :, 0:128], scalar1=mk[:, 0:1], scalar2=None, op0=ALU.mult),
                                       reads=[glub_k, mk_k], writes=[glub_k])
                              if bi == len(blks) - 1:
                                  K.op("dve", lambda e: e.tensor_scalar(out=glub[:, :, bw - 128:bw], in0=glub[:, :, bw - 128:bw], scalar1=mk[:, 1:2], scalar2=None, op0=ALU.mult),
                                       reads=[glub_k, mk_k], writes=[glub_k])
                          K.dma("sp", sg["glu"][:, :, 15 + b0:15 + b0 + bw], glub[:, :, 0:bw], glub_k, reads=[glub_k], acc=[sg["glu_k"]])
                          ug, ug_k = ug_r.next()
                          for g in range(4):
                              pu, pu_k = psr()
                              fm(1024 + g * 128, pu, pu_k)
                              K.op("act", lambda e, g=g: e.activation(out=ug[:, g, 0:bw], in_=pu[:, 0:bw], func=AF.Gelu_apprx_tanh), reads=[pu_k], writes=[ug_k])
                          pss = [PSB[4 + g] for g in range(4)]
                          vnbs = []
                          for j in range(nj):
                              pv, pv_k = psr()
                              for kt in range(KT):
                                  K.op("pe", lambda e, kt=kt: e.matmul(pv[:], lhsT=hT[:, kt, j * 128:(j + 1) * 128], rhs=wbig[:, kt, 1536:2048],
                                                                       start=(kt == 0), stop=(kt == KT - 1)),
                                       reads=[hT_k, wbig_k], writes=[pv_k], inc=(kt == KT - 1))
                              vg, vg_k = vg_r.next()
                              K.op("act", lambda e: e.activation(out=vg[:], in_=pv[:], func=AF.Gelu_apprx_tanh), reads=[pv_k], writes=[vg_k])
                              s8, s8_k = s8_r.next()
                              jk, jk_k = jk_r.next()
                              K.op("dve", lambda e: e.memset(s8[:], 0.0), writes=[s8_k])
                              K.op("act", lambda e: e.activation(out=jk[:], in_=vg[:], func=AF.Square, accum_out=s8[:, 1:2]), reads=[vg_k, s8_k], writes=[jk_k, s8_k])
                              K.op("dve", lambda e: e.reduce_sum(out=s8[:, 0:1], in_=vg[:], axis=AX.X), reads=[vg_k, s8_k], writes=[s8_k])
                              K.op("dve", lambda e: e.tensor_scalar(out=s8[:, 2:3], in0=s8[:, 0:1], scalar1=1.0 / 512, scalar2=None, op0=ALU.mult), reads=[s8_k], writes=[s8_k])
                              K.op("dve", lambda e: e.tensor_tensor(out=s8[:, 3:4], in0=s8[:, 2:3], in1=s8[:, 2:3], op=ALU.mult), reads=[s8_k], writes=[s8_k])
                              K.op("dve", lambda e: e.scalar_tensor_tensor(out=s8[:, 4:5], in0=s8[:, 1:2], scalar=1.0 / 512, in1=s8[:, 3:4], op0=ALU.mult, op1=ALU.subtract),
                                   reads=[s8_k], writes=[s8_k])
                              K.op("act", lambda e: e.activation(out=s8[:, 5:6], in_=s8[:, 4:5], func=AF.Sqrt, bias=EPS, scale=1.0), reads=[s8_k], writes=[s8_k])
                              K.op("dve", lambda e: e.reciprocal(out=s8[:, 6:7], in_=s8[:, 5:6]), reads=[s8_k], writes=[s8_k])
                              K.op("dve", lambda e: e.tensor_scalar(out=vg[:], in0=vg[:], scalar1=s8[:, 2:3], scalar2=s8[:, 6:7], op0=ALU.subtract, op1=ALU.mult),
                                   reads=[vg_k, s8_k], writes=[vg_k])
                              K.op("dve", lambda e: e.tensor_tensor(out=vg[:], in0=vg[:], in1=lng[:], op=ALU.mult), reads=[vg_k, lng_k], writes=[vg_k])
                              vnb, vnb_k = vnb_r.next()
                              K.op("pool", lambda e: e.tensor_tensor(out=vnb[:], in0=vg[:], in1=lnb[:], op=ALU.add), reads=[vg_k, lnb_k], writes=[vnb_k])
                              vnbs.append((vnb, vnb_k))
                          gst, gst_k = gst_r.next()
                          for m in range(24):
                              pg, pg_k = psr()
                              fm(2048 + m * 128, pg, pg_k)
                              K.op("act", lambda e, m=m: e.activation(out=gst[:, m, 0:bw], in_=pg[:, 0:bw], func=AF.Sigmoid), reads=[pg_k], writes=[gst_k])
                          K.dma("sp", sg["gat"][:, :, b0:b0 + bw], gst[:, :, 0:bw], gst_k, reads=[gst_k], acc=[sg["gat_k"]])
                          for j in range(nj):
                              vnb, vnb_k = vnbs[j]
                              for g in range(4):
                                  K.op("pe", lambda e, g=g: e.matmul(pss[g][0][:, j * 128:(j + 1) * 128], lhsT=vnb[:, g * 128:(g + 1) * 128], rhs=wsT[:, g, :],
                                                                     start=True, stop=True),
                                       reads=[vnb_k, wsT_k], writes=[pss[g][1]], inc=(g == 3))
                          usb, usb_k = usb_r.next()
                          for g in range(4):
                              tq, tq_k = tq_r.next()
                              K.op("dve", lambda e, g=g: e.tensor_tensor(out=tq[:, 0:bw], in0=pss[g][0][:, 0:bw],
                                                                         in1=bsb[:, g, :, :].rearrange("p j q -> p (j q)")[:, 0:bw], op=ALU.add),
                                   reads=[pss[g][1], bsb_k], writes=[tq_k])
                              K.op("dve", lambda e, g=g: e.tensor_tensor(out=usb[:, g, 0:bw], in0=tq[:, 0:bw], in1=ug[:, g, 0:bw], op=ALU.mult),
                                   reads=[tq_k, ug_k], writes=[usb_k])
                          K.dma("sp", sg["us"][:, :, b0:b0 + bw], usb[:, :, 0:bw], usb_k, reads=[usb_k], acc=[sg["us_k"]])
                  K.end_phase()
                  chk(l)

              with ExitStack() as ph:
                  PSR_HI[0] = 4
                  KSB = 4096
                  qtr = K.ring(ph, "qtb", 2, [128, NH, 512], BF16, dma=True)
                  ktr = K.ring(ph, "ktb", 2, [128, KSB], BF16, dma=True)
                  vtr = K.ring(ph, "vtb", 2, [128, KSB // 128, 65], BF16, dma=True)
                  ptr_ = K.ring(ph, "ptb", 4, [128, 512], BF16)
                  aor = K.ring(ph, "aob", 2, [64, NH, 512], BF16, dma=True)
                  rsr = K.ring(ph, "rsb", 2, [128, 512], F32)
                  rir = K.ring(ph, "rib", 2, [64, 512], F32)
                  scale = float(DQK) ** -0.5
                  for sg in segs:
                      n, S = sg["n"], sg["s"]
                      for (b0, bw) in blocks_of(n):
                          qt, qt_k = qtr.next()
                          K.dma("sp", qt[0:DQK, :, 0:bw], sg["qT"][:, :, b0:b0 + bw], qt_k, writes=[qt_k], dram_reads=[sg["qT_k"]])
                          ao, ao_k = aor.next()
                          for h in range(NH):
                              po, po_k = PSB[4 + (h % 2)]
                              first = True
                              for (s0, sw) in blocks_of(S, KSB):
                                  kt_, kt_k = ktr.next()
                                  vt_, vt_k = vtr.next()
                                  K.dma("sp", kt_[0:DQK, 0:sw], sg["kT"][h, :, s0:s0 + sw], kt_k, writes=[kt_k], dram_reads=[sg["kT_k"]])
                                  K.dma("sp", vt_[:, 0:sw // 128, :], sg["v"][h, :, s0 // 128:(s0 + sw) // 128, :], vt_k, writes=[vt_k], dram_reads=[sg["v_k"]])
                                  nk = sw // 128
                                  pend = None
                                  for ki in range(nk + 1):
                                      if ki < nk:
                                          ps, ps_k = psr(0, 4)
                                          K.op("pe", lambda e, ki=ki: e.matmul(ps[:, 0:bw], lhsT=kt_[0:DQK, ki * 128:(ki + 1) * 128], rhs=qt[0:DQK, h, 0:bw],
                                                                               start=True, stop=True),
                                               reads=[kt_k, qt_k], writes=[ps_k])
                                          pt, pt_k = ptr_.next()
                                          K.op("act", lambda e: e.activation(out=pt[:, 0:bw], in_=ps[:, 0:bw], func=AF.Exp, scale=scale), reads=[ps_k], writes=[pt_k])
                                          cur = (pt, pt_k, ki)
                                      else:
                                          cur = None
                                      if pend is not None:
                                          ppt, ppt_k, pki = pend
                                          last = (s0 + sw >= S) and (pki == nk - 1)
                                          K.op("pe", lambda e, pki=pki, ppt=ppt, f=first, last=last: e.matmul(po[0:65, 0:bw], lhsT=vt_[:, pki, 0:65], rhs=ppt[:, 0:bw],
                                                                                                            start=f, stop=last),
                                               reads=[vt_k, ppt_k], writes=[po_k])
                                          first = False
                                      pend = cur
                              rs, rs_k = rsr.next()
                              K.op("dve", lambda e: e.tensor_copy(out=rs[64:65, 0:bw], in_=po[64:65, 0:bw]), reads=[po_k], writes=[rs_k])
                              pb, pb_k = PSB[6 + (h % 2)]
                              K.op("pe", lambda e: e.matmul(pb[0:64, 0:bw], lhsT=ones_f[64:65, 0:64], rhs=rs[64:65, 0:bw], start=True, stop=True),
                                   reads=[onesf_k, rs_k], writes=[pb_k])
                              ri, ri_k = rir.next()
                              K.op("dve", lambda e: e.reciprocal(out=ri[:, 0:bw], in_=pb[0:64, 0:bw]), reads=[pb_k], writes=[ri_k])
                              K.op("dve", lambda e, h=h: e.tensor_tensor(out=ao[:, h, 0:bw], in0=po[0:64, 0:bw], in1=ri[:, 0:bw], op=ALU.mult),
                                   reads=[po_k, ri_k], writes=[ao_k])
                          K.dma("sp", sg["ao"][:, :, b0:b0 + bw], ao[:, :, 0:bw], ao_k, reads=[ao_k], acc=[sg["ao_k"]])
                  K.end_phase()
                  chk(l)

              with ExitStack() as ph:
                  PSR_HI[0] = 8
                  wao, wao_k = K.sb(ph, "wao", [64, NH, D], BF16, dma=True)
                  wco, wco_k = K.sb(ph, "wco", [128, 4, D], BF16, dma=True)
                  wso, wso_k = K.sb(ph, "wso", [128, 4, D], BF16, dma=True)
                  wout, wout_k = K.sb(ph, "wout", [128, KT, D], BF16, dma=True)
                  K.dma("pool", wao[:], Wl["w_ao"].rearrange("p (h n) -> p h n", h=NH), wao_k, writes=[wao_k])
                  K.dma("pool", wco[:], wview(Wl["w_co"]), wco_k, writes=[wco_k])
                  K.dma("pool", wso[:], wview(Wl["w_so"]), wso_k, writes=[wso_k])
                  K.dma("pool", wout[:], wview(Wl["w_out"]), wout_k, writes=[wout_k])
                  cdw, cdw_k = K.sb(ph, "cdw", [128, 4, 31], F32, dma=True)
                  cdwb, cdwb_k = K.sb(ph, "cdwb", [128, 4], F32, dma=True)
                  clng, clng_k = K.sb(ph, "clng", [128, 4], F32, dma=True)
                  clnb, clnb_k = K.sb(ph, "clnb", [128, 4], F32, dma=True)
                  K.dma("sp", cdw[:], Wl["cdw"].rearrange("p (c k) -> p c k", c=4), cdw_k, writes=[cdw_k])
                  K.dma("sp", cdwb[:], Wl["cdwb"], cdwb_k, writes=[cdwb_k])
                  K.dma("sp", clng[:], Wl["clng"], clng_k, writes=[clng_k])
                  K.dma("sp", clnb[:], Wl["clnb"], clnb_k, writes=[clnb_k])
                  dgt, dgt_k = K.sb(ph, "dgt", [128, 4, 31, 128], BF16)
                  for c in range(4):
                      for k in range(31):
                          K.op("dve", lambda e, c=c, k=k: e.tensor_scalar(out=dgt[:, c, k, :], in0=ident[:], scalar1=cdw[:, c, k:k + 1], scalar2=None, op0=ALU.mult),
                               reads=[ident_k, cdw_k], writes=[dgt_k])
                  g1t, g1_k = K.sb(ph, "g1t", [128, D], F32, dma=True)
                  glr = K.ring(ph, "glt", 2, [128, 4, 512 + 30], BF16, dma=True)
                  usr = K.ring(ph, "ust", 2, [128, 4, 512], BF16, dma=True)
                  gtr = K.ring(ph, "gtt", 1, [128, 24, 512], BF16, dma=True)
                  aor = K.ring(ph, "aot", 2, [64, NH, 512], BF16, dma=True)
                  hc, hc_k = K.sb(ph, "hc", [128, 4, 512], F32)
                  hcb, hcb_k = K.sb(ph, "hcb", [128, 4, 512], BF16)
                  sqb, sqb_k = K.sb(ph, "sqb", [128, 4, 512], BF16)
                  mean, mean_k = K.sb(ph, "mean", [128, 512], F32)
                  rstd, rstd_k = K.sb(ph, "rstd", [128, 512], F32)
                  cvn, cvn_k = K.sb(ph, "cvn", [128, 4, 512], BF16)
                  mrg, mrg_k = K.sb(ph, "mrg", [128, KT, 512], BF16)
                  tmr = K.ring(ph, "tm", 4, [128, 512], F32)
                  xr = K.ring(ph, "xr2", 2, [128, D], F32, dma=True)
                  for sg in segs:
                      n = sg["n"]
                      K.dma("sp", g1t[:], sg["mod"][:, 2 * D:3 * D], g1_k, writes=[g1_k], dram_reads=[sg["mod_k"]])
                      for (b0, bw) in blocks_of(n):
                          nj = bw // 128
                          glt, glt_k = glr.next()
                          K.dma("sp", glt[:, :, 0:bw + 30], sg["glu"][:, :, b0:b0 + bw + 30], glt_k, writes=[glt_k], dram_reads=[sg["glu_k"]])
                          ust, ust_k = usr.next()
                          K.dma("sp", ust[:, :, 0:bw], sg["us"][:, :, b0:b0 + bw], ust_k, writes=[ust_k], dram_reads=[sg["us_k"]])
                          gtt, gtt_k = gtr.next()
                          K.dma("sp", gtt[:, :, 0:bw], sg["gat"][:, :, b0:b0 + bw], gtt_k, writes=[gtt_k], dram_reads=[sg["gat_k"]])
                          aot, aot_k = aor.next()
                          K.dma("sp", aot[:, :, 0:bw], sg["ao"][:, :, b0:b0 + bw], aot_k, writes=[aot_k], dram_reads=[sg["ao_k"]])
                          for c in range(4):
                              ps, ps_k = psr()
                              for k in range(31):
                                  K.op("pe", lambda e, c=c, k=k: e.matmul(ps[:, 0:bw], lhsT=dgt[:, c, k, :], rhs=glt[:, c, k:k + bw], start=(k == 0), stop=(k == 30)),
                                       reads=[dgt_k, glt_k], writes=[ps_k], inc=(k == 30))
                              K.op("act", lambda e, c=c: e.activation(out=hc[:, c, 0:bw], in_=ps[:, 0:bw], func=AF.Identity, bias=cdwb[:, c:c + 1], scale=1.0),
                                   reads=[ps_k, cdwb_k], writes=[hc_k])
                          K.op("pool", lambda e: e.tensor_copy(out=hcb[:, :, 0:bw], in_=hc[:, :, 0:bw]), reads=[hc_k], writes=[hcb_k])
                          K.op("dve", lambda e: e.tensor_tensor(out=sqb[:, :, 0:bw], in0=hc[:, :, 0:bw], in1=hc[:, :, 0:bw], op=ALU.mult), reads=[hc_k], writes=[sqb_k])
                          p1, p1_k = psr()
                          p2, p2_k = psr()
                          for c in range(4):
                              K.op("pe", lambda e, c=c: e.matmul(p1[:, 0:bw], lhsT=ones_bf[:], rhs=hcb[:, c, 0:bw], start=(c == 0), stop=(c == 3)),
                                   reads=[ones_k, hcb_k], writes=[p1_k], inc=(c == 3))
                          for c in range(4):
                              K.op("pe", lambda e, c=c: e.matmul(p2[:, 0:bw], lhsT=ones_bf[:], rhs=sqb[:, c, 0:bw], start=(c == 0), stop=(c == 3)),
                                   reads=[ones_k, sqb_k], writes=[p2_k], inc=(c == 3))
                          K.op("dve", lambda e: e.tensor_scalar(out=mean[:, 0:bw], in0=p1[:, 0:bw], scalar1=1.0 / 512, scalar2=None, op0=ALU.mult), reads=[p1_k], writes=[mean_k])
                          tm, tm_k = tmr.next()
                          K.op("dve", lambda e: e.tensor_tensor(out=tm[:, 0:bw], in0=mean[:, 0:bw], in1=mean[:, 0:bw], op=ALU.mult), reads=[mean_k], writes=[tm_k])
                          K.op("dve", lambda e: e.scalar_tensor_tensor(out=tm[:, 0:bw], in0=p2[:, 0:bw], scalar=1.0 / 512, in1=tm[:, 0:bw], op0=ALU.mult, op1=ALU.subtract),
                               reads=[p2_k, tm_k], writes=[tm_k])
                          K.op("act", lambda e: e.activation(out=tm[:, 0:bw], in_=tm[:, 0:bw], func=AF.Sqrt, bias=EPS, scale=1.0), reads=[tm_k], writes=[tm_k])
                          K.op("dve", lambda e: e.reciprocal(out=rstd[:, 0:bw], in_=tm[:, 0:bw]), reads=[tm_k], writes=[rstd_k])
                          for c in range(4):
                              t2, t2_k = tmr.next()
                              K.op("dve", lambda e, c=c: e.tensor_tensor(out=t2[:, 0:bw], in0=hc[:, c, 0:bw], in1=mean[:, 0:bw], op=ALU.subtract), reads=[hc_k, mean_k], writes=[t2_k])
                              K.op("dve", lambda e: e.tensor_tensor(out=t2[:, 0:bw], in0=t2[:, 0:bw], in1=rstd[:, 0:bw], op=ALU.mult), reads=[t2_k, rstd_k], writes=[t2_k])
                              K.op("act", lambda e, c=c: e.activation(out=cvn[:, c, 0:bw], in_=t2[:, 0:bw], func=AF.Silu, bias=clnb[:, c:c + 1], scale=clng[:, c:c + 1]),
                                   reads=[t2_k, clnb_k, clng_k], writes=[cvn_k])
                          for j in range(8):
                              pa, pa_k = psr()
                              pc, pc_k = psr()
                              pss_, pss_k = psr()
                              for h in range(NH):
                                  K.op("pe", lambda e, h=h: e.matmul(pa[:, 0:bw], lhsT=wao[:, h, j * 128:(j + 1) * 128], rhs=aot[:, h, 0:bw], start=(h == 0), stop=(h == NH - 1)),
                                       reads=[wao_k, aot_k], writes=[pa_k], inc=(h == NH - 1))
                              for c in range(4):
                                  K.op("pe", lambda e, c=c: e.matmul(pc[:, 0:bw], lhsT=wco[:, c, j * 128:(j + 1) * 128], rhs=cvn[:, c, 0:bw], start=(c == 0), stop=(c == 3)),
                                       reads=[wco_k, cvn_k], writes=[pc_k], inc=(c == 3))
                              for c in range(4):
                                  K.op("pe", lambda e, c=c: e.matmul(pss_[:, 0:bw], lhsT=wso[:, c, j * 128:(j + 1) * 128], rhs=ust[:, c, 0:bw], start=(c == 0), stop=(c == 3)),
                                       reads=[wso_k, ust_k], writes=[pss_k], inc=(c == 3))
                              m1, m1_k = tmr.next()
                              m2, m2_k = tmr.next()
                              m3, m3_k = tmr.next()
                              K.op("dve", lambda e: e.tensor_tensor(out=m1[:, 0:bw], in0=pa[:, 0:bw], in1=gtt[:, j, 0:bw], op=ALU.mult), reads=[pa_k, gtt_k], writes=[m1_k])
                              K.op("dve", lambda e: e.tensor_tensor(out=m2[:, 0:bw], in0=pc[:, 0:bw], in1=gtt[:, 8 + j, 0:bw], op=ALU.mult), reads=[pc_k, gtt_k], writes=[m2_k])
                              K.op("dve", lambda e: e.tensor_tensor(out=m3[:, 0:bw], in0=pss_[:, 0:bw], in1=gtt[:, 16 + j, 0:bw], op=ALU.mult), reads=[pss_k, gtt_k], writes=[m3_k])
                              K.op("pool", lambda e: e.tensor_tensor(out=m1[:, 0:bw], in0=m1[:, 0:bw], in1=m2[:, 0:bw], op=ALU.add), reads=[m1_k, m2_k], writes=[m1_k])
                              K.op("pool", lambda e, j=j: e.tensor_tensor(out=mrg[:, j, 0:bw], in0=m1[:, 0:bw], in1=m3[:, 0:bw], op=ALU.add), reads=[m1_k, m3_k], writes=[mrg_k])
                          for t in range(nj):
                              t0 = b0 + t * 128
                              xt, xk = xr.next()
                              K.dma("sp", xt[:], sg["x"][t0:t0 + 128, :], xk, writes=[xk], dram_reads=[sg["x_k"]])
                              for half in range(2):
                                  po, po_k = psr()
                                  for j in range(8):
                                      K.op("pe", lambda e, j=j: e.matmul(po[:], lhsT=mrg[:, j, t * 128:(t + 1) * 128], rhs=wout[:, j, half * 512:(half + 1) * 512],
                                                                         start=(j == 0), stop=(j == 7)),
                                           reads=[mrg_k, wout_k], writes=[po_k], inc=(j == 7))
                                  tm2, tm2_k = tmr.next()
                                  K.op("dve", lambda e: e.tensor_tensor(out=tm2[:], in0=po[:], in1=g1t[:, half * 512:(half + 1) * 512], op=ALU.mult), reads=[po_k, g1_k], writes=[tm2_k])
                                  K.op("pool", lambda e: e.tensor_tensor(out=xt[:, half * 512:(half + 1) * 512], in0=xt[:, half * 512:(half + 1) * 512], in1=tm2[:], op=ALU.add),
                                       reads=[xk, tm2_k], writes=[xk])
                              K.dma("sp", sg["xm"][t0:t0 + 128, :], xt[:], xk, reads=[xk], acc=[sg["xm_k"]])
                  K.end_phase()
                  chk(l)

              with ExitStack() as ph:
                  P = dict(junkb=K.ring(ph, "junkc", 2, [128, 4, D], F32), ssb=K.ring(ph, "ssc", 3, [128, 16], F32),
                           hbb=K.ring(ph, "hbc", 2, [128, 4, D], BF16))
                  xbr = K.ring(ph, "xb3", 3, [128, 4, D], F32, dma=True)
                  hst = K.ring(ph, "hst2", 2, [128, KT, 512], BF16, dma=True)
                  gamt, gam_k = K.sb(ph, "gam2", [128, D], F32, dma=True)
                  sht, sh_k = K.sb(ph, "sh2", [128, D], F32, dma=True)
                  for sg in segs:
                      n = sg["n"]
                      K.dma("sp", sht[:], sg["mod"][:, 3 * D:4 * D], sh_k, writes=[sh_k], dram_reads=[sg["mod_k"]])
                      K.dma("sp", gamt[:], sg["mod"][:, 4 * D:5 * D], gam_k, writes=[gam_k], dram_reads=[sg["mod_k"]])
                      for (b0, bw) in blocks_of(n):
                          G = bw // 128
                          hs, hs_k = hst.next()
                          xb, xb_k = xbr.next()
                          K.dma("sp", xb[:, 0:G, :], sg["xm"][b0:b0 + bw, :].rearrange("(j p) d -> p j d", p=128), xb_k, writes=[xb_k], dram_reads=[sg["xm_k"]])
                          masks = []
                          if sg["prompt"]:
                              for j in range(G):
                                  if b0 + j * 128 == 0:
                                      masks.append((j, 0))
                                  if b0 + j * 128 == n - 128:
                                      masks.append((j, 1))
                          hbb, hbb_k = norm_block(P, xb, xb_k, G, (gamt, gam_k), (sht, sh_k), masks=masks)
                          for j in range(G):
                              transpose_to(P, hbb[:, j, :], hbb_k, hs[:, :, j * 128:(j + 1) * 128], hs_k)
                          K.dma("sp", sg["h2T"][:, :, 1 + b0:1 + b0 + bw], hs[:, :, 0:bw], hs_k, reads=[hs_k], acc=[sg["h2T_k"]])
                  K.end_phase()
                  chk(l)
              with ExitStack() as ph:
                  wup, wup_k = K.sb(ph, "wup", [128, KT, 2 * D_FF], BF16, dma=True)
                  wdn, wdn_k = K.sb(ph, "wdn", [128, NFT, D], BF16, dma=True)
                  wuv = wview(Wl["w_up"])
                  for c in range(0, 2 * D_FF, 704):
                      K.dma("pool", wup[:, :, c:c + 704], wuv[:, :, c:c + 704], wup_k, acc=[wup_k])
                  K.dma("pool", wdn[:], wview(Wl["w_down"]), wdn_k, writes=[wdn_k])
                  fdw, fdw_k = K.sb(ph, "fdw", [128, 44, 3], F32, dma=True)
                  fdwb, fdwb_k = K.sb(ph, "fdwb", [128, 44], F32, dma=True)
                  K.dma("sp", fdw[:], Wl["fdw"].rearrange("p (c k) -> p c k", c=44), fdw_k, writes=[fdw_k])
                  K.dma("sp", fdwb[:], Wl["fdwb"], fdwb_k, writes=[fdwb_k])
                  g2t, g2_k = K.sb(ph, "g2t", [128, D], F32, dma=True)
                  h2r = K.ring(ph, "h2b", 2, [128, KT, 514], BF16, dma=True)
                  zr = K.ring(ph, "zt_", 3, [128, 514], F32)
                  accr = K.ring(ph, "acc", 4, [128, 512], F32)
                  sgr = K.ring(ph, "sgf", 2, [128, 512], F32)
                  uT, uT_k = K.sb(ph, "uT", [128, NFT, 512], BF16)
                  xr = K.ring(ph, "xr4", 2, [128, D], F32, dma=True)
                  tmr = K.ring(ph, "tm4", 2, [128, 512], F32)
                  for sg in segs:
                      n = sg["n"]
                      K.dma("sp", g2t[:], sg["mod"][:, 5 * D:6 * D], g2_k, writes=[g2_k], dram_reads=[sg["mod_k"]])
                      for (b0, bw) in blocks_of(n):
                          nj = bw // 128
                          h2, h2_k = h2r.next()
                          K.dma("sp", h2[:, :, 0:bw + 2], sg["h2T"][:, :, b0:b0 + bw + 2], h2_k, writes=[h2_k], dram_reads=[sg["h2T_k"]])
                          half = (bw + 2) // 2
                          for i in range(NFT):
                              accs = []
                              for which in range(2):
                                  ci = which * NFT + i
                                  col0 = ci * 128
                                  z, z_k = zr.next()
                                  for (c0, c1) in ((0, half), (half, bw + 2)):
                                      ps, ps_k = psr(0, 8)
                                      for kt in range(KT):
                                          K.op("pe", lambda e, kt=kt: e.matmul(ps[:, 0:c1 - c0], lhsT=wup[:, kt, col0:col0 + 128], rhs=h2[:, kt, c0:c1],
                                                                               start=(kt == 0), stop=(kt == KT - 1)),
                                               reads=[wup_k, h2_k], writes=[ps_k], inc=(kt == KT - 1))
                                      K.op("act", lambda e: e.activation(out=z[:, c0:c1], in_=ps[:, 0:c1 - c0], func=AF.Copy), reads=[ps_k], writes=[z_k])
                                  acc, acc_k = accr.next()
                                  K.op("dve", lambda e: e.tensor_scalar(out=acc[:, 0:bw], in0=z[:, 0:bw], scalar1=fdw[:, ci, 0:1], scalar2=fdwb[:, ci:ci + 1],
                                                                        op0=ALU.mult, op1=ALU.add),
                                       reads=[z_k, fdw_k, fdwb_k], writes=[acc_k])
                                  K.op("dve", lambda e: e.scalar_tensor_tensor(out=acc[:, 0:bw], in0=z[:, 1:bw + 1], scalar=fdw[:, ci, 1:2], in1=acc[:, 0:bw],
                                                                               op0=ALU.mult, op1=ALU.add),
                                       reads=[z_k, fdw_k, acc_k], writes=[acc_k])
                                  K.op("dve", lambda e: e.scalar_tensor_tensor(out=acc[:, 0:bw], in0=z[:, 2:bw + 2], scalar=fdw[:, ci, 2:3], in1=acc[:, 0:bw],
                                                                               op0=ALU.mult, op1=ALU.add),
                                       reads=[z_k, fdw_k, acc_k], writes=[acc_k])
                                  accs.append((acc, acc_k))
                              sgf, sgf_k = sgr.next()
                              K.op("act", lambda e: e.activation(out=sgf[:, 0:bw], in_=accs[0][0][:, 0:bw], func=AF.Silu), reads=[accs[0][1]], writes=[sgf_k])
                              K.op("pool", lambda e, i=i: e.tensor_tensor(out=uT[:, i, 0:bw], in0=sgf[:, 0:bw], in1=accs[1][0][:, 0:bw], op=ALU.mult),
                                   reads=[sgf_k, accs[1][1]], writes=[uT_k])
                          for t in range(nj):
                              t0 = b0 + t * 128
                              if sg["prompt"] and (t0 < HALO or t0 >= n - HALO):
                                  continue
                              xt, xk = xr.next()
                              K.dma("sp", xt[:], sg["xm"][t0:t0 + 128, :], xk, writes=[xk], dram_reads=[sg["xm_k"]])
                              for hf in range(2):
                                  po, po_k = psr(0, 8)
                                  for i in range(NFT):
                                      K.op("pe", lambda e, i=i: e.matmul(po[:], lhsT=uT[:, i, t * 128:(t + 1) * 128], rhs=wdn[:, i, hf * 512:(hf + 1) * 512],
                                                                         start=(i == 0), stop=(i == NFT - 1)),
                                           reads=[uT_k, wdn_k], writes=[po_k], inc=(i == NFT - 1))
                                  tm2, tm2_k = tmr.next()
                                  K.op("dve", lambda e: e.tensor_tensor(out=tm2[:], in0=po[:], in1=g2t[:, hf * 512:(hf + 1) * 512], op=ALU.mult), reads=[po_k, g2_k], writes=[tm2_k])
                                  K.op("pool", lambda e: e.tensor_tensor(out=xt[:, hf * 512:(hf + 1) * 512], in0=xt[:, hf * 512:(hf + 1) * 512], in1=tm2[:], op=ALU.add),
                                       reads=[xk, tm2_k], writes=[xk])
                              yoff = t0 - HALO if sg["prompt"] else t0
                              K.dma("sp", sg["y"][yoff:yoff + 128, :], xt[:], xk, reads=[xk], acc=[sg["y_k"]])
                  K.end_phase()
                  chk(l)
        except _Stop:
            print('STOPPED at', _stop)
        K.barrier()
        print("instructions emitted:", K.nops)
    return nc


def rope_table(pos):
    inv = np.power(np.float32(10000.0), -np.arange(0, 32, 2, dtype=np.float32) / np.float32(32)).astype(np.float32)
    ang = pos.astype(np.float32)[:, None] * inv[None, :]
    return np.concatenate([np.cos(ang), np.sin(ang)], axis=1).astype(np.float32)


def layer_weights(inp, l):
    f = lambda a: np.ascontiguousarray(a, dtype=np.float32)
    ukv = inp["w_ukv"][l].reshape(128, NH, 128)
    w = dict(
        w_ada=f(inp["w_ada"][l]), b_ada=f(inp["b_ada"][l][None, :]), norm1=f(inp["norm1"][l][None, :]), norm2=f(inp["norm2"][l][None, :]),
        w_in=f(inp["w_in"][l]), w_uq=f(inp["w_uq"][l]),
        w_ukv=f(np.concatenate([ukv[:, :, 0:64].reshape(128, 512), ukv[:, :, 64:128].reshape(128, 512)], axis=1)),
        qan=f(inp["q_a_norm"][l].reshape(2, 128).T), kvan=f(inp["kv_a_norm"][l].reshape(128, 1)),
        qhn=f(inp["q_head_norm"][l][None, :]), khn=f(inp["k_head_norm"][l][None, :]),
        w_ao=f(inp["w_attn_o"][l].reshape(NH, 64, D).transpose(1, 0, 2).reshape(64, NH * D)),
        cdw=f(inp["conv_dw"][l].T.reshape(4, 128, 31).transpose(1, 0, 2).reshape(128, 4 * 31)),
        cdwb=f(inp["conv_dw_b"][l].reshape(4, 128).T), clng=f(inp["conv_ln_g"][l].reshape(4, 128).T), clnb=f(inp["conv_ln_b"][l].reshape(4, 128).T),
        w_co=f(inp["w_conv_o"][l]), sglng=f(inp["sg_ln_g"][l][None, :]), sglnb=f(inp["sg_ln_b"][l][None, :]),
        sgwT=f(inp["sg_w"][l].transpose(2, 0, 1).reshape(128, 4 * 128)),
        sgb=f(inp["sg_b"][l].reshape(1, 512)), w_so=f(inp["w_sg_o"][l]), w_out=f(inp["w_out"][l]), w_up=f(inp["w_up"][l]),
        fdw=f(inp["ffn_dw"][l].T.reshape(44, 128, 3).transpose(1, 0, 2).reshape(128, 44 * 3)),
        fdwb=f(inp["ffn_dw_b"][l].reshape(44, 128).T), w_down=f(inp["w_down"][l]))
    return w


_PROG = {}


def run_model(inp, cfg, n_cores=8):
    NS, SS, PCH, PS = cfg["NS"], cfg["SS"], cfg["PCH"], cfg["PS"]
    PSEG = PCH + 2 * HALO
    key = (NS, SS, PCH, PS)
    L = inp["w_ada"].shape[0]
    assert L == 2
    if key not in _PROG:
        _PROG[key] = build_program(cfg, L)
    nc = _PROG[key]
    xp = np.asarray(inp["x_prompt"], dtype=np.float32)
    xs = np.asarray(inp["x_sample"], dtype=np.float32)
    cp = np.asarray(inp["c_prompt"], dtype=np.float32)
    cs = np.asarray(inp["c_sample"], dtype=np.float32)
    nchunk = PS // PCH
    rope_s = rope_table(np.arange(SS))
    rope_pc = rope_table(np.arange(PS))
    wls = [layer_weights(inp, l) for l in range(L)]
    in_maps = []
    for c in range(n_cores):
        b = c // nchunk
        r = c % nchunk
        lo = r * PCH - HALO
        pm = np.zeros((128, 2), np.float32)
        pm[:, 0] = 1.0 if r > 0 else 0.0
        pm[:, 1] = 1.0 if r < nchunk - 1 else 0.0
        sel = np.zeros((128, nchunk), np.float32)
        sel[:, r] = 1.0
        m = dict(xs=np.ascontiguousarray(xs[c * NS:(c + 1) * NS].reshape(NS * SS, D)),
                 xpctx=np.ascontiguousarray(xp[b]),
                 cvT=np.ascontiguousarray(np.concatenate([cs[c * NS:(c + 1) * NS], cp[b:b + 1]], axis=0).reshape((NS + 1) * KT, 128).T),
                 pmask=pm, psel=sel, rope_s=rope_s, rope_pc=rope_pc, rope_pq=rope_table(np.arange(lo, lo + PSEG)))
        for l in range(L):
            for k, v in wls[l].items():
                m["%s_%d" % (k, l)] = v
        in_maps.append(m)
    res = run_bass_kernel_spmd(nc, in_maps, core_ids=list(range(n_cores)))
    ys = np.stack([np.asarray(r_["ys"]).reshape(NS, SS, D) for r_ in res.results], axis=0).reshape(n_cores * NS, SS, D)
    ypo = np.stack([np.asarray(r_["yp"]) for r_ in res.results], axis=0).reshape(n_cores // nchunk, PS, D)
    return ypo.astype(np.float32), ys.astype(np.float32)


def kernel(**inputs):
    yp, ys = run_model(inputs, CFG, 8)
    return (yp, ys)
```

```python
import numpy as np
from contextlib import ExitStack
import concourse.bass as bass
import concourse.mybir as mybir
from concourse.bass_utils import run_bass_kernel_spmd

F32 = mybir.dt.float32
BF16 = mybir.dt.bfloat16
AF = mybir.ActivationFunctionType
ALU = mybir.AluOpType
AX = mybir.AxisListType

D = 1024
KT = 8
NH = 8
DQK = 96
EPS = 1e-6
D_IN = 5536
D_FF = 2816
NFT = 22
HALO = 128

CFG = dict(NS=4, SS=2048, PCH=2048, PS=8192)


class Trk:
    __slots__ = ("w", "r", "dsem", "dcnt")

    def __init__(self):
        self.w = {}
        self.r = {}
        self.dsem = None
        self.dcnt = 0


class KB:
    def __init__(self, nc, es):
        self.nc = nc
        self.es = es
        self.eng = {"pe": nc.tensor, "act": nc.scalar, "dve": nc.vector, "pool": nc.gpsimd, "sp": nc.sync}
        self.sem = {k: es.enter_context(nc.semaphore("s_" + k)) for k in ["pe", "act", "dve", "pool"]}
        self.cnt = {k: 0 for k in self.sem}
        self.seen = {k: {} for k in self.eng}
        self.pend = {k: [] for k in self.eng}
        self.dma_trks = []
        self.nops = 0
        self.sem_free = []
        self.phase_trks = []
        self.nsem = 0

    def get_dsem(self):
        if self.sem_free:
            return self.sem_free.pop()
        self.nsem += 1
        return (self.es.enter_context(self.nc.semaphore("dq%d" % self.nsem)), 0)

    def release(self, trks):
        for k in trks:
            if k.dsem is not None:
                self.sem_free.append((k.dsem, k.dcnt))
                if k in self.dma_trks:
                    self.dma_trks.remove(k)

    def _wait(self, e, sem, val):
        d = self.seen[e]
        if d.get(sem, 0) >= val:
            return
        self.eng[e].wait_ge(sem, val)
        d[sem] = val

    def _deps(self, e, reads, writes):
        for t in reads:
            for s, (v, ek) in t.w.items():
                self._wait(e, s, v)
        for t in writes:
            for s, (v, ek) in t.w.items():
                if ek != e:
                    self._wait(e, s, v)
            for ek, (s, v) in t.r.items():
                if ek != e:
                    self._wait(e, s, v)

    def op(self, e, fn, reads=(), writes=(), inc=True):
        if getattr(self, 'skip', False):
            return
        self._deps(e, reads, writes)
        ins = fn(self.eng[e])
        self.nops += 1
        if not inc:
            self.pend[e].append((reads, writes))
            return
        self.cnt[e] += 1
        c = self.cnt[e]
        s = self.sem[e]
        ins.then_inc(s, 1)
        self.pend[e].append((reads, writes))
        for rd, wr in self.pend[e]:
            for t in wr:
                t.w = {s: (c, e)}
                t.r = {}
        for rd, wr in self.pend[e]:
            for t in rd:
                t.r[e] = (s, c)
        self.pend[e] = []

    def dma(self, q, out, in_, own, reads=(), writes=(), acc=(), dram_reads=(), slow=False):
        if getattr(self, 'skip', False):
            return
        self._deps(q, list(reads) + list(dram_reads), writes)
        for t in acc:
            for ek, (s, v) in t.r.items():
                self._wait(q, s, v)
        own.dcnt += 16
        if slow:
            self.eng[q].dma_start(out=out, in_=in_, allow_slow_non_contiguous=True).then_inc(own.dsem, 16)
        else:
            self.eng[q].dma_start(out=out, in_=in_).then_inc(own.dsem, 16)
        self.nops += 1
        key = ("dma", own.dsem)
        for t in writes:
            t.w = {own.dsem: (own.dcnt, key)}
            t.r = {}
        for t in acc:
            t.w[own.dsem] = (own.dcnt, key)
        for t in reads:
            t.r[key] = (own.dsem, own.dcnt)

    def end_phase(self):
        self.barrier()
        self.release(list(self.phase_trks))
        self.phase_trks = []

    def barrier(self):
        for e in self.eng:
            for k in self.sem:
                if k != e and self.cnt[k] > 0:
                    self._wait(e, self.sem[k], self.cnt[k])
            for t in self.dma_trks:
                if t.dcnt > 0:
                    self._wait(e, t.dsem, t.dcnt)

    def sb(self, es, name, shape, dt, dma=False):
        self.nalloc = getattr(self, "nalloc", 0) + 1
        t = es.enter_context(self.nc.sbuf_tensor("%s_u%d" % (name, self.nalloc), list(shape), dt))
        k = Trk()
        if dma:
            k.dsem, k.dcnt = self.get_dsem()
            self.dma_trks.append(k)
            self.phase_trks.append(k)
        return t, k

    def ring(self, es, name, n, shape, dt, dma=False):
        return Ring([self.sb(es, "%s%d" % (name, i), shape, dt, dma) for i in range(n)])


class Ring:
    def __init__(self, items):
        self.items = items
        self.i = 0

    def next(self):
        it = self.items[self.i % len(self.items)]
        self.i += 1
        return it


def blocks_of(n, bs=512):
    out = []
    o = 0
    while o < n:
        b = min(bs, n - o)
        out.append((o, b))
        o += b
    return out


def build_program(cfg, n_layers=1):
    NS, SS, PCH, PS = cfg["NS"], cfg["SS"], cfg["PCH"], cfg["PS"]
    PSEG = PCH + 2 * HALO
    nc = bass.Bass("TRN2", target_bir_lowering=False)

    def din(name, shape):
        return nc.dram_tensor(name, list(shape), F32, kind="ExternalInput").ap()

    xs = din("xs", [NS * SS, D])
    psel = din("psel", [128, PS // PCH])
    xpctx = din("xpctx", [PS, D])
    cvT = din("cvT", [128, (NS + 1) * KT])
    pmask = din("pmask", [128, 2])
    rope_s = din("rope_s", [SS, 32])
    rope_pc = din("rope_pc", [PS, 32])
    rope_pq = din("rope_pq", [PSEG, 32])
    W = {}
    wshapes = dict(
        w_ada=[D, 6 * D], b_ada=[1, 6 * D], norm1=[1, D], norm2=[1, D], w_in=[D, D_IN],
        w_uq=[256, 768], w_ukv=[128, 1024], qan=[128, 2], kvan=[128, 1], qhn=[1, 96], khn=[1, 96],
        w_ao=[64, 8 * D], cdw=[128, 4 * 31], cdwb=[128, 4], clng=[128, 4], clnb=[128, 4],
        w_co=[512, D], sglng=[1, 512], sglnb=[1, 512], sgwT=[128, 4 * 128], sgb=[1, 512], w_so=[512, D],
        w_out=[D, D], w_up=[D, 2 * D_FF], fdw=[128, 44 * 3], fdwb=[128, 44], w_down=[D_FF, D])
    for l in range(n_layers):
        for k, shp in wshapes.items():
            W[(l, k)] = din("%s_%d" % (k, l), shp)
    ys = nc.dram_tensor("ys", [NS * SS, D], F32, kind="ExternalOutput").ap()
    yp = nc.dram_tensor("yp", [PCH, D], F32, kind="ExternalOutput").ap()

    x1s = nc.dram_tensor("x1s", [NS * SS, D], F32).ap()
    x1full = nc.dram_tensor("x1full", [PS, D], F32).ap()
    x1seg = nc.dram_tensor("x1seg", [PSEG, D], F32).ap()
    x1s_k = [Trk() for _ in range(NS)]
    x1full_k = Trk()
    x1seg_k = Trk()
    assert n_layers == 2

    def make_segs(l):
        segs = []
        for i in range(NS):
            if l == 0:
                segs.append(dict(n=SS, s=SS, x=xs[i * SS:(i + 1) * SS, :], x_k=Trk(), ctx=None, ctx_k=None, rq=rope_s, rk=rope_s,
                                 y=x1s[i * SS:(i + 1) * SS, :], y_k=x1s_k[i], prompt=False))
            else:
                segs.append(dict(n=SS, s=SS, x=x1s[i * SS:(i + 1) * SS, :], x_k=x1s_k[i], ctx=None, ctx_k=None, rq=rope_s, rk=rope_s,
                                 y=ys[i * SS:(i + 1) * SS, :], y_k=Trk(), prompt=False))
        if l == 0:
            segs.append(dict(n=PS, s=PS, x=xpctx, x_k=Trk(), ctx=None, ctx_k=None, rq=rope_pc, rk=rope_pc,
                             y=x1full, y_k=x1full_k, prompt=False))
        else:
            segs.append(dict(n=PSEG, s=PS, x=x1seg, x_k=x1seg_k, ctx=x1full, ctx_k=x1full_k, rq=rope_pq, rk=rope_pc,
                             y=yp, y_k=Trk(), prompt=True))
        for si, sg in enumerate(segs):
            n, s_ = sg["n"], sg["s"]

            def dsc(nm, shape, dt=BF16):
                return nc.dram_tensor("%s_%d_%d" % (nm, l, si), list(shape), dt).ap()
            sg["hT"] = dsc("hT", [128, KT, n]); sg["hT_k"] = Trk()
            sg["qT"] = dsc("qT", [DQK, NH, n]); sg["qT_k"] = Trk()
            sg["kT"] = dsc("kT", [NH, DQK, s_]); sg["kT_k"] = Trk()
            sg["v"] = dsc("v", [NH, 128, s_ // 128, 65]); sg["v_k"] = Trk()
            sg["gat"] = dsc("gat", [128, 24, n]); sg["gat_k"] = Trk()
            sg["glu"] = dsc("glu", [128, 4, n + 30]); sg["glu_k"] = Trk()
            sg["us"] = dsc("us", [128, 4, n]); sg["us_k"] = Trk()
            sg["ao"] = dsc("ao", [64, NH, n]); sg["ao_k"] = Trk()
            sg["xm"] = dsc("xm", [n, D], F32); sg["xm_k"] = Trk()
            sg["h2T"] = dsc("h2T", [128, KT, n + 2]); sg["h2T_k"] = Trk()
            sg["mod"] = dsc("mod", [128, 6 * D], F32); sg["mod_k"] = Trk()
        return segs

    all_segs = [make_segs(l) for l in range(n_layers)]

    with ExitStack() as es:
        K = KB(nc, es)
        ident, ident_k = K.sb(es, "ident", [128, 128], BF16)
        ones_bf, ones_k = K.sb(es, "ones_bf", [128, 128], BF16)
        ones_f, onesf_k = K.sb(es, "ones_f", [128, 64], F32)
        zt, zt_k = K.sb(es, "zt", [128, 64], BF16, dma=True)
        mk, mk_k = K.sb(es, "mk", [128, 2], F32, dma=True)
        K.op("dve", lambda e: e.memset(ident[:], 1.0), writes=[ident_k])
        K.op("pool", lambda e: e.affine_select(out=ident[:], in_=ident[:], pattern=[[-1, 128]],
                                               compare_op=ALU.is_equal, fill=0.0, base=0, channel_multiplier=1),
             reads=[ident_k], writes=[ident_k])
        K.op("dve", lambda e: e.memset(ones_bf[:], 1.0), writes=[ones_k])
        K.op("dve", lambda e: e.memset(ones_f[:], 1.0), writes=[onesf_k])
        K.op("dve", lambda e: e.memset(zt[:], 0.0), writes=[zt_k])
        K.dma("sp", mk[:], pmask, mk_k, writes=[mk_k])
        K.phase_trks = []
        PSB = []
        for i in range(8):
            t = es.enter_context(nc.psum_tensor("psb%d" % i, [128, 512], F32))
            PSB.append((t, Trk()))
        psr_state = [0]

        PSR_HI = [8]

        def psr(lo=0, hi=None):
            if hi is None:
                hi = PSR_HI[0]
            i = lo + psr_state[0] % (hi - lo)
            psr_state[0] += 1
            return PSB[i]

        for sg in [g for sl in all_segs for g in sl]:
            n = sg["n"]
            K.dma("sp", sg["glu"][:, :, 0:15], zt[:, 0:60].rearrange("p (a b) -> p a b", a=4), zt_k, reads=[zt_k], acc=[sg["glu_k"]])
            K.dma("sp", sg["glu"][:, :, n + 15:n + 30], zt[:, 0:60].rearrange("p (a b) -> p a b", a=4), zt_k, reads=[zt_k], acc=[sg["glu_k"]])
            K.dma("sp", sg["h2T"][:, :, 0:1], zt[:, 0:8].rearrange("p (a b) -> p a b", a=8), zt_k, reads=[zt_k], acc=[sg["h2T_k"]], slow=True)
            K.dma("sp", sg["h2T"][:, :, n + 1:n + 2], zt[:, 0:8].rearrange("p (a b) -> p a b", a=8), zt_k, reads=[zt_k], acc=[sg["h2T_k"]], slow=True)

        def wview(ap_, p=128):
            return ap_.rearrange("(kt p) n -> p kt n", p=p)

        def norm_tile(P, xt, xk, gam, sh, mask_col=None):
            junk, junk_k = P["junk"].next()
            ss, ss_k = P["ss"].next()
            K.op("dve", lambda e: e.memset(ss[:], 0.0), writes=[ss_k])
            K.op("act", lambda e: e.activation(out=junk[:], in_=xt[:], func=AF.Square, accum_out=ss[:, 0:1]),
                 reads=[xk, ss_k], writes=[junk_k, ss_k])
            K.op("act", lambda e: e.activation(out=ss[:, 1:2], in_=ss[:, 0:1], func=AF.Sqrt, bias=EPS, scale=1.0 / D),
                 reads=[ss_k], writes=[ss_k])
            K.op("dve", lambda e: e.reciprocal(out=ss[:, 2:3], in_=ss[:, 1:2]), reads=[ss_k], writes=[ss_k])
            K.op("dve", lambda e: e.scalar_tensor_tensor(out=junk[:], in0=xt[:], scalar=ss[:, 2:3], in1=gam[0][:],
                                                         op0=ALU.mult, op1=ALU.mult),
                 reads=[xk, ss_k, gam[1], junk_k], writes=[junk_k])
            hb, hb_k = P["hb"].next()
            K.op("pool", lambda e: e.tensor_tensor(out=hb[:], in0=junk[:], in1=sh[0][:], op=ALU.add),
                 reads=[junk_k, sh[1]], writes=[hb_k])
            if mask_col is not None:
                K.op("dve", lambda e: e.tensor_scalar(out=hb[:], in0=hb[:], scalar1=mk[:, mask_col:mask_col + 1],
                                                      scalar2=None, op0=ALU.mult),
                     reads=[hb_k, mk_k], writes=[hb_k])
            return hb, hb_k

        def transpose_to(P, hb, hb_k, dst, dst_k, ncols_src=D, rows=128, chunk=128):
            nchunk = ncols_src // chunk
            ps, ps_k = psr()
            psb = ps[:].bitcast(BF16)
            for i in range(nchunk):
                K.op("pe", lambda e, i=i: e.transpose(psb[0:chunk, i * 128:(i + 1) * 128], hb[:, i * chunk:(i + 1) * chunk], ident[:]),
                     reads=[hb_k, ident_k], writes=[ps_k], inc=(i == nchunk - 1))
            K.op("act", lambda e: e.activation(out=dst, in_=psb[0:chunk, 0:nchunk * 128].rearrange("p (a b) -> p a b", a=nchunk),
                                               func=AF.Copy),
                 reads=[ps_k], writes=[dst_k])

        def norm_block(P, xb, xb_k, G, gam, sh, masks=()):
            junk, junk_k = P["junkb"].next()
            ss, ss_k = P["ssb"].next()
            K.op("dve", lambda e: e.memset(ss[:], 0.0), writes=[ss_k])
            for j in range(G):
                K.op("act", lambda e, j=j: e.activation(out=junk[:, j, :], in_=xb[:, j, :], func=AF.Square, accum_out=ss[:, j:j + 1]),
                     reads=[xb_k], writes=[junk_k, ss_k])
            K.op("act", lambda e: e.activation(out=ss[:, 4:4 + G], in_=ss[:, 0:G], func=AF.Sqrt, bias=EPS, scale=1.0 / D),
                 reads=[ss_k], writes=[ss_k])
            K.op("dve", lambda e: e.reciprocal(out=ss[:, 8:8 + G], in_=ss[:, 4:4 + G]), reads=[ss_k], writes=[ss_k])
            for j in range(G):
                K.op("dve", lambda e, j=j: e.scalar_tensor_tensor(out=junk[:, j, :], in0=xb[:, j, :], scalar=ss[:, 8 + j:9 + j], in1=gam[0][:],
                                                                  op0=ALU.mult, op1=ALU.mult),
                     reads=[xb_k, ss_k, gam[1], junk_k] if j == 0 else [xb_k, gam[1]], writes=[junk_k])
            hbb, hbb_k = P["hbb"].next()
            K.op("pool", lambda e: e.tensor_tensor(out=hbb[:, 0:G, :], in0=junk[:, 0:G, :], in1=sh[0][:].unsqueeze(1).to_broadcast([128, G, D]), op=ALU.add),
                 reads=[junk_k, sh[1]], writes=[hbb_k])
            for (j, mc) in masks:
                K.op("dve", lambda e, j=j, mc=mc: e.tensor_scalar(out=hbb[:, j, :], in0=hbb[:, j, :], scalar1=mk[:, mc:mc + 1], scalar2=None, op0=ALU.mult),
                     reads=[hbb_k, mk_k], writes=[hbb_k])
            return hbb, hbb_k

        def rms_block(P, src3, src_k, G, n, dst3, dst_k):
            junk, junk_k = P["junkb"].next()
            s4, s4_k = P["ssb"].next()
            K.op("dve", lambda e: e.memset(s4[:], 0.0), writes=[s4_k])
            for j in range(G):
                K.op("act", lambda e, j=j: e.activation(out=junk[:, j, 0:n], in_=src3[:, j, :], func=AF.Square, accum_out=s4[:, j:j + 1]),
                     reads=[src_k], writes=[junk_k, s4_k])
            K.op("act", lambda e: e.activation(out=s4[:, 4:4 + G], in_=s4[:, 0:G], func=AF.Sqrt, bias=EPS, scale=1.0 / n), reads=[s4_k], writes=[s4_k])
            K.op("dve", lambda e: e.reciprocal(out=s4[:, 8:8 + G], in_=s4[:, 4:4 + G]), reads=[s4_k], writes=[s4_k])
            K.op("dve", lambda e: e.tensor_tensor(out=dst3, in0=src3, in1=s4[:, 8:8 + G].unsqueeze(2).to_broadcast([128, G, n]), op=ALU.mult),
                 reads=[src_k, s4_k], writes=[dst_k])

        def headnorm_rope_block(P, f, f_k, G, gain, rp, rp_k, out, out_k):
            f3 = f[:, 0:G, :].rearrange("p j (h d) -> p (j h) d", h=NH)
            f4 = f[:, 0:G, :].rearrange("p j (h d) -> p j h d", h=NH)
            o4 = out[:, 0:G, :].rearrange("p j (h d) -> p j h d", h=NH)
            sq, sq_k = P["junkb"].next()
            st, st_k = P["stb"].next()
            sq3 = sq[:].rearrange("p j c -> p (j c)")[:, 0:G * 768].rearrange("p (a d) -> p a d", d=DQK)
            K.op("dve", lambda e: e.tensor_tensor(out=sq3, in0=f3, in1=f3, op=ALU.mult), reads=[f_k], writes=[sq_k])
            K.op("dve", lambda e: e.reduce_sum(out=st[:, 0:G * NH], in_=sq3, axis=AX.X), reads=[sq_k], writes=[st_k])
            K.op("act", lambda e: e.activation(out=st[:, 32:32 + G * NH], in_=st[:, 0:G * NH], func=AF.Sqrt, bias=EPS, scale=1.0 / DQK),
                 reads=[st_k], writes=[st_k])
            K.op("dve", lambda e: e.reciprocal(out=st[:, 64:64 + G * NH], in_=st[:, 32:32 + G * NH]), reads=[st_k], writes=[st_k])
            K.op("dve", lambda e: e.tensor_tensor(out=f3, in0=f3, in1=st[:, 64:64 + G * NH].unsqueeze(2).to_broadcast([128, G * NH, DQK]), op=ALU.mult),
                 reads=[f_k, st_k], writes=[f_k])
            K.op("dve", lambda e: e.tensor_tensor(out=f3, in0=f3, in1=gain[0][:].unsqueeze(1).to_broadcast([128, G * NH, DQK]), op=ALU.mult),
                 reads=[f_k, gain[1]], writes=[f_k])
            tt, tt_k = P["ttb"].next()
            t5 = tt[:].rearrange("p (a j h d) -> p a j h d", a=4, j=4, h=NH)
            x1 = f4[:, :, :, 64:80]
            x2 = f4[:, :, :, 80:96]
            cs = rp[:, 0:G, 0:16].unsqueeze(2).to_broadcast([128, G, NH, 16])
            sn = rp[:, 0:G, 16:32].unsqueeze(2).to_broadcast([128, G, NH, 16])
            K.op("dve", lambda e: e.tensor_tensor(out=t5[:, 0, 0:G], in0=x1, in1=cs, op=ALU.mult), reads=[f_k, rp_k], writes=[tt_k])
            K.op("dve", lambda e: e.tensor_tensor(out=t5[:, 1, 0:G], in0=x2, in1=sn, op=ALU.mult), reads=[f_k, rp_k], writes=[tt_k])
            K.op("dve", lambda e: e.tensor_tensor(out=t5[:, 2, 0:G], in0=x1, in1=sn, op=ALU.mult), reads=[f_k, rp_k], writes=[tt_k])
            K.op("dve", lambda e: e.tensor_tensor(out=t5[:, 3, 0:G], in0=x2, in1=cs, op=ALU.mult), reads=[f_k, rp_k], writes=[tt_k])
            K.op("dve", lambda e: e.tensor_tensor(out=o4[:, :, :, 64:80], in0=t5[:, 0, 0:G], in1=t5[:, 1, 0:G], op=ALU.subtract), reads=[tt_k], writes=[out_k])
            K.op("dve", lambda e: e.tensor_tensor(out=o4[:, :, :, 80:96], in0=t5[:, 2, 0:G], in1=t5[:, 3, 0:G], op=ALU.add), reads=[tt_k], writes=[out_k])
            K.op("pool", lambda e: e.tensor_copy(out=o4[:, :, :, 0:64], in_=f4[:, :, :, 0:64]), reads=[f_k], writes=[out_k])

        def headnorm_rope(P, f, f_k, gain, rp, rp_k, out, out_k):
            f3 = f[:].rearrange("p (h d) -> p h d", h=NH)
            o3 = out[:].rearrange("p (h d) -> p h d", h=NH)
            sq, sq_k = P["sq"].next()
            st, st_k = P["st"].next()
            K.op("dve", lambda e: e.tensor_tensor(out=sq[:], in0=f[:], in1=f[:], op=ALU.mult), reads=[f_k], writes=[sq_k])
            K.op("dve", lambda e: e.reduce_sum(out=st[:, 0:8], in_=sq[:].rearrange("p (h d) -> p h d", h=NH), axis=AX.X),
                 reads=[sq_k], writes=[st_k])
            K.op("act", lambda e: e.activation(out=st[:, 8:16], in_=st[:, 0:8], func=AF.Sqrt, bias=EPS, scale=1.0 / DQK),
                 reads=[st_k], writes=[st_k])
            K.op("dve", lambda e: e.reciprocal(out=st[:, 16:24], in_=st[:, 8:16]), reads=[st_k], writes=[st_k])
            K.op("dve", lambda e: e.tensor_tensor(out=f3, in0=f3, in1=st[:, 16:24].unsqueeze(2).to_broadcast([128, NH, DQK]), op=ALU.mult),
                 reads=[f_k, st_k], writes=[f_k])
            K.op("dve", lambda e: e.tensor_tensor(out=f3, in0=f3, in1=gain[0][:].unsqueeze(1).to_broadcast([128, NH, DQK]), op=ALU.mult),
                 reads=[f_k, gain[1]], writes=[f_k])
            tt, tt_k = P["tt"].next()
            t4 = tt[:].rearrange("p (a h d) -> p a h d", a=4, h=NH)
            x1 = f3[:, :, 64:80]
            x2 = f3[:, :, 80:96]
            cs = rp[:, 0:16].unsqueeze(1).to_broadcast([128, NH, 16])
            sn = rp[:, 16:32].unsqueeze(1).to_broadcast([128, NH, 16])
            K.op("dve", lambda e: e.tensor_tensor(out=t4[:, 0], in0=x1, in1=cs, op=ALU.mult), reads=[f_k, rp_k], writes=[tt_k])
            K.op("dve", lambda e: e.tensor_tensor(out=t4[:, 1], in0=x2, in1=sn, op=ALU.mult), reads=[f_k, rp_k], writes=[tt_k])
            K.op("dve", lambda e: e.tensor_tensor(out=t4[:, 2], in0=x1, in1=sn, op=ALU.mult), reads=[f_k, rp_k], writes=[tt_k])
            K.op("dve", lambda e: e.tensor_tensor(out=t4[:, 3], in0=x2, in1=cs, op=ALU.mult), reads=[f_k, rp_k], writes=[tt_k])
            K.op("dve", lambda e: e.tensor_tensor(out=o3[:, :, 64:80], in0=t4[:, 0], in1=t4[:, 1], op=ALU.subtract), reads=[tt_k], writes=[out_k])
            K.op("dve", lambda e: e.tensor_tensor(out=o3[:, :, 80:96], in0=t4[:, 2], in1=t4[:, 3], op=ALU.add), reads=[tt_k], writes=[out_k])
            K.op("pool", lambda e: e.tensor_copy(out=o3[:, :, 0:64], in_=f3[:, :, 0:64]), reads=[f_k], writes=[out_k])

        import os as _os
        _stop = _os.environ.get("KSTOP", "")
        _phc = [0]

        class _Stop(Exception):
            pass

        def chk(l):
            _phc[0] += 1
            if _stop and _stop == "%d,%d" % (l, _phc[0]):
                K.skip = True
                print('STOPPED at', _stop)

        try:
          for l in range(n_layers):
              _phc[0] = 0
              Wl = {k: W[(l, k)] for k in wshapes}
              segs = all_segs[l]
              if l == 1:
                  with ExitStack() as ph:
                      selt, selt_k = K.sb(ph, "selt", [128, PS // PCH], F32, dma=True)
                      K.dma("sp", selt[:], psel, selt_k, writes=[selt_k])
                      accr_ = K.ring(ph, "xacc", 2, [128, D], F32, dma=True)
                      ldr_ = K.ring(ph, "xld", 4, [128, D], F32, dma=True)
                      for j in range(PSEG // 128):
                          ac, ac_k = accr_.next()
                          K.op("dve", lambda e: e.memset(ac[:], 0.0), writes=[ac_k])
                          for r_ in range(PS // PCH):
                              row = r_ * PCH - HALO + j * 128
                              if row < 0 or row + 128 > PS:
                                  continue
                              ld, ld_k = ldr_.next()
                              K.dma("sp", ld[:], x1full[row:row + 128, :], ld_k, writes=[ld_k], dram_reads=[x1full_k])
                              K.op("dve", lambda e, r_=r_: e.scalar_tensor_tensor(out=ac[:], in0=ld[:], scalar=selt[:, r_:r_ + 1], in1=ac[:],
                                                                                  op0=ALU.mult, op1=ALU.add),
                                   reads=[ld_k, selt_k, ac_k], writes=[ac_k])
                          K.dma("sp", x1seg[j * 128:(j + 1) * 128, :], ac[:], ac_k, reads=[ac_k], acc=[x1seg_k])
                      K.end_phase()
                      chk(l)
              with ExitStack() as ph:
                  PSR_HI[0] = 8
                  nseg = len(segs)
                  cT, cT_k = K.sb(ph, "cT", [128, nseg * KT], F32, dma=True)
                  crep, crep_k = K.sb(ph, "crep", [128, nseg * KT, 128], BF16)
                  bb, bb_k = K.sb(ph, "bb", [1, 6 * D], BF16, dma=True)
                  n1b, n1b_k = K.sb(ph, "n1b", [128, D], F32, dma=True)
                  n2b, n2b_k = K.sb(ph, "n2b", [128, D], F32, dma=True)
                  wr = K.ring(ph, "wada", 2, [128, KT, 512], BF16, dma=True)
                  modt = [K.sb(ph, "modt%d" % s, [128, 6 * D], F32, dma=True) for s in range(nseg)]
                  K.dma("sp", cT[:], cvT, cT_k, writes=[cT_k])
                  K.op("act", lambda e: e.activation(out=cT[:], in_=cT[:], func=AF.Silu), reads=[cT_k], writes=[cT_k])
                  K.op("dve", lambda e: e.tensor_copy(out=crep[:], in_=cT[:].unsqueeze(2).to_broadcast([128, nseg * KT, 128])),
                       reads=[cT_k], writes=[crep_k])
                  K.dma("pool", bb[:], Wl["b_ada"], bb_k, writes=[bb_k])
                  K.dma("sp", n1b[:], Wl["norm1"][0, :].partition_broadcast(128), n1b_k, writes=[n1b_k])
                  K.dma("sp", n2b[:], Wl["norm2"][0, :].partition_broadcast(128), n2b_k, writes=[n2b_k])
                  wav = wview(Wl["w_ada"])
                  for c in range(12):
                      wt, wt_k = wr.next()
                      K.dma("pool", wt[:], wav[:, :, c * 512:(c + 1) * 512], wt_k, writes=[wt_k])
                      for s in range(nseg):
                          ps, ps_k = psr()
                          for kt in range(KT):
                              K.op("pe", lambda e, kt=kt: e.matmul(ps[:], lhsT=crep[:, s * KT + kt, :], rhs=wt[:, kt, :],
                                                                   start=(kt == 0), stop=False),
                                   reads=[crep_k, wt_k], writes=[ps_k], inc=False)
                          K.op("pe", lambda e: e.matmul(ps[:], lhsT=ones_bf[0:1, :], rhs=bb[0:1, c * 512:(c + 1) * 512],
                                                        start=False, stop=True),
                               reads=[ones_k, bb_k], writes=[ps_k])
                          mt, mt_k = modt[s]
                          K.op("act", lambda e: e.activation(out=mt[:, c * 512:(c + 1) * 512], in_=ps[:], func=AF.Copy),
                               reads=[ps_k], writes=[mt_k])
                  for s in range(nseg):
                      mt, mt_k = modt[s]
                      K.op("dve", lambda e: e.scalar_tensor_tensor(out=mt[:, D:2 * D], in0=mt[:, D:2 * D], scalar=1.0, in1=n1b[:],
                                                                   op0=ALU.add, op1=ALU.mult),
                           reads=[mt_k, n1b_k], writes=[mt_k])
                      K.op("dve", lambda e: e.scalar_tensor_tensor(out=mt[:, 4 * D:5 * D], in0=mt[:, 4 * D:5 * D], scalar=1.0, in1=n2b[:],
                                                                   op0=ALU.add, op1=ALU.mult),
                           reads=[mt_k, n2b_k], writes=[mt_k])
                      K.dma("sp", segs[s]["mod"], mt[:], mt_k, reads=[mt_k], acc=[segs[s]["mod_k"]])
                  K.end_phase()
                  chk(l)

              with ExitStack() as ph:
                  wlat, wlat_k = K.sb(ph, "wlat", [128, KT, 416], BF16, dma=True)
                  wuq, wuq_k = K.sb(ph, "wuq", [128, 2, 768], BF16, dma=True)
                  wukv, wukv_k = K.sb(ph, "wukv", [128, 1024], BF16, dma=True)
                  qan, qan_k = K.sb(ph, "qan", [128, 2], F32, dma=True)
                  kvan, kvan_k = K.sb(ph, "kvan", [128, 1], F32, dma=True)
                  gq, gq_k = K.sb(ph, "gq", [128, DQK], F32, dma=True)
                  gk, gk_k = K.sb(ph, "gk", [128, DQK], F32, dma=True)
                  K.dma("pool", wlat[:], wview(Wl["w_in"])[:, :, 0:416], wlat_k, writes=[wlat_k])
                  K.dma("pool", wuq[:], wview(Wl["w_uq"]), wuq_k, writes=[wuq_k])
                  K.dma("pool", wukv[:], Wl["w_ukv"], wukv_k, writes=[wukv_k])
                  K.dma("sp", qan[:], Wl["qan"], qan_k, writes=[qan_k])
                  K.dma("sp", kvan[:], Wl["kvan"], kvan_k, writes=[kvan_k])
                  K.dma("sp", gq[:], Wl["qhn"][0, :].partition_broadcast(128), gq_k, writes=[gq_k])
                  K.dma("sp", gk[:], Wl["khn"][0, :].partition_broadcast(128), gk_k, writes=[gk_k])
                  P = dict(junkb=K.ring(ph, "junkb", 1, [128, 4, D], F32), ssb=K.ring(ph, "ssb", 3, [128, 16], F32),
                           hbb=K.ring(ph, "hbb", 2, [128, 4, D], BF16), stb=K.ring(ph, "stb", 2, [128, 96], F32),
                           ttb=K.ring(ph, "ttb", 1, [128, 4 * 4 * NH * 16], F32))
                  xbr = K.ring(ph, "xb", 2, [128, 4, D], F32, dma=True)
                  rpbr = K.ring(ph, "rpb", 2, [128, 4, 32], F32, dma=True)
                  hst = K.ring(ph, "hst", 2, [128, KT, 512], BF16, dma=True)
                  qst = K.ring(ph, "qst", 1, [128, NH, 512], BF16, dma=True)
                  kst = K.ring(ph, "kst", 1, [128, NH, 512], BF16, dma=True)
                  vst = K.ring(ph, "vst", 2, [128, NH, 4, 65], BF16, dma=True)
                  for vt, vk in vst.items:
                      K.op("dve", lambda e, vt=vt: e.memset(vt[:], 1.0), writes=[vk])
                  gamt, gam_k = K.sb(ph, "gam1", [128, D], F32, dma=True)
                  sht, sh_k = K.sb(ph, "sh1", [128, D], F32, dma=True)
                  lat_r = K.ring(ph, "latb", 2, [128, 4, 416], F32)
                  cqn_r = K.ring(ph, "cqnb", 2, [128, 4, 256], BF16)
                  cqT_r = K.ring(ph, "cqTb", 2, [128, 2, 512], BF16)
                  ckn_r = K.ring(ph, "cknb", 2, [128, 4, 128], BF16)
                  ckT_r = K.ring(ph, "ckTb", 2, [128, 512], BF16)
                  qf_r = K.ring(ph, "qfb", 2, [128, 4, 768], F32)
                  qb_r = K.ring(ph, "qbb", 2, [128, 4, 768], BF16)

                  def passes_of(sg):
                      if sg["prompt"]:
                          return [(sg["ctx"], sg["s"], False, True, sg["rk"], sg["ctx_k"]),
                                  (sg["x"], sg["n"], True, False, sg["rq"], sg["x_k"])]
                      return [(sg["x"], sg["n"], True, True, sg["rq"], sg["x_k"])]

                  items = [(p_[0], p_[5], p_[4], b0, bw) for sg in segs for p_ in passes_of(sg) for (b0, bw) in blocks_of(p_[1])]
                  loaded = {}
                  ctr = [0]

                  def issue(i):
                      if i >= len(items) or i in loaded:
                          return
                      xsrc_, xsrc_k_, rtab_, b0_, bw_ = items[i]
                      G_ = bw_ // 128
                      xb_, xb_k_ = xbr.next()
                      K.dma("sp", xb_[:, 0:G_, :], xsrc_[b0_:b0_ + bw_, :].rearrange("(j p) d -> p j d", p=128), xb_k_, writes=[xb_k_], dram_reads=[xsrc_k_])
                      rp_, rp_k_ = rpbr.next()
                      K.dma("sp", rp_[:, 0:G_, :], rtab_[b0_:b0_ + bw_, :].rearrange("(j p) d -> p j d", p=128), rp_k_, writes=[rp_k_])
                      loaded[i] = (xb_, xb_k_, rp_, rp_k_)

                  for sg in segs:
                      K.dma("sp", sht[:], sg["mod"][:, 0:D], sh_k, writes=[sh_k], dram_reads=[sg["mod_k"]])
                      K.dma("sp", gamt[:], sg["mod"][:, D:2 * D], gam_k, writes=[gam_k], dram_reads=[sg["mod_k"]])
                      for (xsrc, ntok, want_q, want_k, rtab, xsrc_k) in passes_of(sg):
                          for (b0, bw) in blocks_of(ntok):
                              G = bw // 128
                              hs, hs_k = hst.next()
                              i_ = ctr[0]
                              ctr[0] += 1
                              issue(i_)
                              issue(i_ + 1)
                              xb, xb_k, rp, rp_k = loaded.pop(i_)
                              hbb, hbb_k = norm_block(P, xb, xb_k, G, (gamt, gam_k), (sht, sh_k))
                              lat, lat_k = lat_r.next()
                              for j in range(G):
                                  transpose_to(P, hbb[:, j, :], hbb_k, hs[:, :, j * 128:(j + 1) * 128], hs_k)
                              for j in range(G):
                                  pl, pl_k = psr()
                                  for kt in range(KT):
                                      K.op("pe", lambda e, kt=kt, j=j: e.matmul(pl[:, 0:416], lhsT=hs[:, kt, j * 128:(j + 1) * 128], rhs=wlat[:, kt, :],
                                                                               start=(kt == 0), stop=(kt == KT - 1)),
                                           reads=[hs_k, wlat_k], writes=[pl_k], inc=(kt == KT - 1))
                                  K.op("act", lambda e, j=j: e.activation(out=lat[:, j, :], in_=pl[:, 0:416], func=AF.Copy), reads=[pl_k], writes=[lat_k])
                              if want_q:
                                  qs, qs_k = qst.next()
                                  cqn, cqn_k = cqn_r.next()
                                  rms_block(P, lat[:, 0:G, 0:256], lat_k, G, 256, cqn[:, 0:G, :], cqn_k)
                                  cqT, cqT_k = cqT_r.next()
                                  p2, p2_k = psr()
                                  p2b = p2[:].bitcast(BF16)
                                  for j in range(G):
                                      for i in range(2):
                                          K.op("pe", lambda e, i=i, j=j: e.transpose(p2b[:, (j * 2 + i) * 128:(j * 2 + i + 1) * 128], cqn[:, j, i * 128:(i + 1) * 128], ident[:]),
                                               reads=[cqn_k, ident_k], writes=[p2_k], inc=(j == G - 1 and i == 1))
                                  for i in range(2):
                                      K.op("act", lambda e, i=i: e.activation(out=cqT[:, i, 0:G * 128].rearrange("p (j c) -> p j c", j=G),
                                                                              in_=p2b[:, 0:G * 256].rearrange("p (j i c) -> p j i c", j=G, i=2)[:, :, i, :],
                                                                              func=AF.Copy, scale=qan[:, i:i + 1]),
                                           reads=[p2_k, qan_k], writes=[cqT_k])
                                  qf, qf_k = qf_r.next()
                                  for j in range(G):
                                      pq0, pq0_k = psr()
                                      pq1, pq1_k = psr()
                                      for i in range(2):
                                          K.op("pe", lambda e, i=i, j=j: e.matmul(pq0[:, 0:480], lhsT=cqT[:, i, j * 128:(j + 1) * 128], rhs=wuq[:, i, 0:480], start=(i == 0), stop=(i == 1)),
                                               reads=[cqT_k, wuq_k], writes=[pq0_k], inc=(i == 1))
                                      for i in range(2):
                                          K.op("pe", lambda e, i=i, j=j: e.matmul(pq1[:, 0:288], lhsT=cqT[:, i, j * 128:(j + 1) * 128], rhs=wuq[:, i, 480:768], start=(i == 0), stop=(i == 1)),
                                               reads=[cqT_k, wuq_k], writes=[pq1_k], inc=(i == 1))
                                      K.op("act", lambda e, j=j: e.activation(out=qf[:, j, 0:480], in_=pq0[:, 0:480], func=AF.Copy), reads=[pq0_k], writes=[qf_k])
                                      K.op("dve", lambda e, j=j: e.tensor_copy(out=qf[:, j, 480:768], in_=pq1[:, 0:288]), reads=[pq1_k], writes=[qf_k])
                                  qb, qb_k = qb_r.next()
                                  headnorm_rope_block(P, qf, qf_k, G, (gq, gq_k), rp, rp_k, qb, qb_k)
                                  for j in range(G):
                                      transpose_to(P, qb[:, j, :], qb_k, qs[0:DQK, :, j * 128:(j + 1) * 128], qs_k, ncols_src=768, chunk=DQK)
                              if want_k:
                                  ks, ks_k = kst.next()
                                  vs, vs_k = vst.next()
                                  ckn, ckn_k = ckn_r.next()
                                  rms_block(P, lat[:, 0:G, 256:384], lat_k, G, 128, ckn[:, 0:G, :], ckn_k)
                                  ckT, ckT_k = ckT_r.next()
                                  p3, p3_k = psr()
                                  p3b = p3[:].bitcast(BF16)
                                  for j in range(G):
                                      K.op("pe", lambda e, j=j: e.transpose(p3b[:, j * 128:(j + 1) * 128], ckn[:, j, :], ident[:]),
                                           reads=[ckn_k, ident_k], writes=[p3_k], inc=(j == G - 1))
                                  K.op("act", lambda e: e.activation(out=ckT[:, 0:G * 128], in_=p3b[:, 0:G * 128], func=AF.Copy, scale=kvan[:, 0:1]),
                                       reads=[p3_k, kvan_k], writes=[ckT_k])
                                  kf, kf_k = qf_r.next()
                                  kf4 = kf[:].rearrange("p j (h d) -> p j h d", h=NH)
                                  for j in range(G):
                                      pk, pk_k = psr()
                                      pv, pv_k = psr()
                                      K.op("pe", lambda e, j=j: e.matmul(pk[:], lhsT=ckT[:, j * 128:(j + 1) * 128], rhs=wukv[:, 0:512], start=True, stop=True),
                                           reads=[ckT_k, wukv_k], writes=[pk_k])
                                      K.op("pe", lambda e, j=j: e.matmul(pv[:], lhsT=ckT[:, j * 128:(j + 1) * 128], rhs=wukv[:, 512:1024], start=True, stop=True),
                                           reads=[ckT_k, wukv_k], writes=[pv_k])
                                      K.op("act", lambda e, j=j: e.activation(out=kf4[:, j, :, 0:64], in_=pk[:].rearrange("p (h d) -> p h d", h=NH), func=AF.Copy),
                                           reads=[pk_k], writes=[kf_k])
                                      K.op("act", lambda e, j=j: e.activation(out=vs[:, :, j, 0:64], in_=pv[:].rearrange("p (h d) -> p h d", h=NH), func=AF.Copy),
                                           reads=[pv_k], writes=[vs_k])
                                  K.op("dve", lambda e: e.tensor_copy(out=kf4[:, 0:G, :, 64:96], in_=lat[:, 0:G, 384:416].unsqueeze(2).to_broadcast([128, G, NH, 32])),
                                       reads=[lat_k], writes=[kf_k])
                                  kb, kb_k = qb_r.next()
                                  headnorm_rope_block(P, kf, kf_k, G, (gk, gk_k), rp, rp_k, kb, kb_k)
                                  for j in range(G):
                                      transpose_to(P, kb[:, j, :], kb_k, ks[0:DQK, :, j * 128:(j + 1) * 128], ks_k, ncols_src=768, chunk=DQK)
                              nj = G
                              if want_q:
                                  K.dma("sp", sg["hT"][:, :, b0:b0 + bw], hs[:, :, 0:bw], hs_k, reads=[hs_k], acc=[sg["hT_k"]])
                                  K.dma("sp", sg["qT"][:, :, b0:b0 + bw], qs[0:DQK, :, 0:bw], qs_k, reads=[qs_k], acc=[sg["qT_k"]])
                              if want_k:
                                  K.dma("sp", sg["kT"].rearrange("h d s -> d h s")[:, :, b0:b0 + bw], ks[0:DQK, :, 0:bw], ks_k, reads=[ks_k], acc=[sg["kT_k"]])
                                  K.dma("sp", sg["v"].rearrange("h p k c -> p h k c")[:, :, b0 // 128:b0 // 128 + nj, :], vs[:, :, 0:nj, :], vs_k,
                                        reads=[vs_k], acc=[sg["v_k"]])
                  K.end_phase()
                  chk(l)

              with ExitStack() as ph:
                  PSR_HI[0] = 4
                  NW = D_IN - 416
                  wbig, wbig_k = K.sb(ph, "wbig", [128, KT, NW], BF16, dma=True)
                  wv = wview(Wl["w_in"])
                  for c in range(0, NW, 640):
                      K.dma("pool", wbig[:, :, c:c + 640], wv[:, :, 416 + c:416 + c + 640], wbig_k, acc=[wbig_k])
                  wsT, wsT_k = K.sb(ph, "wsT", [128, 4, 128], BF16, dma=True)
                  K.dma("pool", wsT[:], Wl["sgwT"].rearrange("p (g q) -> p g q", g=4), wsT_k, writes=[wsT_k])
                  lng, lng_k = K.sb(ph, "lng", [128, 512], F32, dma=True)
                  lnb, lnb_k = K.sb(ph, "lnb", [128, 512], F32, dma=True)
                  bsb, bsb_k = K.sb(ph, "bsb", [128, 4, 4, 128], F32, dma=True)
                  K.dma("sp", lng[:], Wl["sglng"][0, :].partition_broadcast(128), lng_k, writes=[lng_k])
                  K.dma("sp", lnb[:], Wl["sglnb"][0, :].partition_broadcast(128), lnb_k, writes=[lnb_k])
                  for j in range(4):
                      K.dma("sp", bsb[:, :, j, :], Wl["sgb"][0, :].partition_broadcast(128).rearrange("p (g q) -> p g q", g=4), bsb_k, acc=[bsb_k])
                  hbr = K.ring(ph, "hTb", 2, [128, KT, 512], BF16, dma=True)
                  sgt_r = K.ring(ph, "sgt", 2, [128, 512], F32)
                  glub_r = K.ring(ph, "glub", 2, [128, 4, 512], BF16, dma=True)
                  ug_r = K.ring(ph, "ug", 2, [128, 4, 512], BF16)
                  vg_r = K.ring(ph, "vg", 2, [128, 512], F32)
                  jk_r = K.ring(ph, "jk2", 1, [128, 512], F32)
                  s8_r = K.ring(ph, "s8", 3, [128, 8], F32)
                  vnb_r = K.ring(ph, "vnb", 4, [128, 512], BF16)
                  tq_r = K.ring(ph, "tq", 2, [128, 512], F32)
                  usb_r = K.ring(ph, "usb", 2, [128, 4, 512], BF16, dma=True)
                  gst_r = K.ring(ph, "gst", 2, [128, 24, 512], BF16, dma=True)
                  items = [(sg, b0, bw) for sg in segs for (b0, bw) in blocks_of(sg["n"])]
                  loaded = {}
                  ctr = [0]

                  def issue(i):
                      if i >= len(items) or i in loaded:
                          return
                      sg_, b0_, bw_ = items[i]
                      hT_, hT_k_ = hbr.next()
                      K.dma("sp", hT_[:, :, 0:bw_], sg_["hT"][:, :, b0_:b0_ + bw_], hT_k_, writes=[hT_k_], dram_reads=[sg_["hT_k"]])
                      loaded[i] = (hT_, hT_k_)

                  for sg in segs:
                      n = sg["n"]
                      blks = blocks_of(n)
                      for bi, (b0, bw) in enumerate(blks):
                          nj = bw // 128
                          i_ = ctr[0]
                          ctr[0] += 1
                          issue(i_)
                          issue(i_ + 1)
                          hT, hT_k = loaded.pop(i_)

                          def fm(col0, ps, ps_k):
                              for kt in range(KT):
                                  K.op("pe", lambda e, kt=kt: e.matmul(ps[:, 0:bw], lhsT=wbig[:, kt, col0:col0 + 128], rhs=hT[:, kt, 0:bw],
                                                                       start=(kt == 0), stop=(kt == KT - 1)),
                                       reads=[wbig_k, hT_k], writes=[ps_k], inc=(kt == KT - 1))
                          glub, glub_k = glub_r.next()
                          for c in range(4):
                              pa, pa_k = psr()
                              pg, pg_k = psr()
                              fm(c * 128, pa, pa_k)
                              fm(512 + c * 128, pg, pg_k)
                              sgt, sgt_k = sgt_r.next()
                              K.op("act", lambda e: e.activation(out=sgt[:, 0:bw], in_=pg[:, 0:bw], func=AF.Sigmoid), reads=[pg_k], writes=[sgt_k])
                              K.op("dve", lambda e, c=c: e.tensor_tensor(out=glub[:, c, 0:bw], in0=pa[:, 0:bw], in1=sgt[:, 0:bw], op=ALU.mult),
                                   reads=[pa_k, sgt_k], writes=[glub_k])
                          if sg["prompt"]:
                              if bi == 0:
                                  K.op("dve", lambda e: e.tensor_scalar(out=glub[:, :, 0:128], in0=glub[:, :, 0:128], scalar1=mk[:, 0:1], scalar2=None, op0=ALU.mult),
                                       reads=[glub_k, mk_k], writes=[glub_k])
                              if bi == len(blks) - 1:
                                  K.op("dve", lambda e: e.tensor_scalar(out=glub[:, :, bw - 128:bw], in0=glub[:, :, bw - 128:bw], scalar1=mk[:, 1:2], scalar2=None, op0=ALU.mult),
                                       reads=[glub_k, mk_k], writes=[glub_k])
                          K.dma("sp", sg["glu"][:, :, 15 + b0:15 + b0 + bw], glub[:, :, 0:bw], glub_k, reads=[glub_k], acc=[sg["glu_k"]])
                          ug, ug_k = ug_r.next()
                          for g in range(4):
                              pu, pu_k = psr()
                              fm(1024 + g * 128, pu, pu_k)
                              K.op("act", lambda e, g=g: e.activation(out=ug[:, g, 0:bw], in_=pu[:, 0:bw], func=AF.Gelu_apprx_tanh), reads=[pu_k], writes=[ug_k])
                          pss = [PSB[4 + g] for g in range(4)]
                          vnbs = []
                          for j in range(nj):
                              pv, pv_k = psr()
                              for kt in range(KT):
                                  K.op("pe", lambda e, kt=kt: e.matmul(pv[:], lhsT=hT[:, kt, j * 128:(j + 1) * 128], rhs=wbig[:, kt, 1536:2048],
                                                                       start=(kt == 0), stop=(kt == KT - 1)),
                                       reads=[hT_k, wbig_k], writes=[pv_k], inc=(kt == KT - 1))
                              vg, vg_k = vg_r.next()
                              K.op("act", lambda e: e.activation(out=vg[:], in_=pv[:], func=AF.Gelu_apprx_tanh), reads=[pv_k], writes=[vg_k])
                              s8, s8_k = s8_r.next()
                              jk, jk_k = jk_r.next()
                              K.op("dve", lambda e: e.memset(s8[:], 0.0), writes=[s8_k])
                              K.op("act", lambda e: e.activation(out=jk[:], in_=vg[:], func=AF.Square, accum_out=s8[:, 1:2]), reads=[vg_k, s8_k], writes=[jk_k, s8_k])
                              K.op("dve", lambda e: e.reduce_sum(out=s8[:, 0:1], in_=vg[:], axis=AX.X), reads=[vg_k, s8_k], writes=[s8_k])
                              K.op("dve", lambda e: e.tensor_scalar(out=s8[:, 2:3], in0=s8[:, 0:1], scalar1=1.0 / 512, scalar2=None, op0=ALU.mult), reads=[s8_k], writes=[s8_k])
                              K.op("dve", lambda e: e.tensor_tensor(out=s8[:, 3:4], in0=s8[:, 2:3], in1=s8[:, 2:3], op=ALU.mult), reads=[s8_k], writes=[s8_k])
                              K.op("dve", lambda e: e.scalar_tensor_tensor(out=s8[:, 4:5], in0=s8[:, 1:2], scalar=1.0 / 512, in1=s8[:, 3:4], op0=ALU.mult, op1=ALU.subtract),
                                   reads=[s8_k], writes=[s8_k])
                              K.op("act", lambda e: e.activation(out=s8[:, 5:6], in_=s8[:, 4:5], func=AF.Sqrt, bias=EPS, scale=1.0), reads=[s8_k], writes=[s8_k])
                              K.op("dve", lambda e: e.reciprocal(out=s8[:, 6:7], in_=s8[:, 5:6]), reads=[s8_k], writes=[s8_k])
                              K.op("dve", lambda e: e.tensor_scalar(out=vg[:], in0=vg[:], scalar1=s8[:, 2:3], scalar2=s8[:, 6:7], op0=ALU.subtract, op1=ALU.mult),
                                   reads=[vg_k, s8_k], writes=[vg_k])
                              K.op("dve", lambda e: e.tensor_tensor(out=vg[:], in0=vg[:], in1=lng[:], op=ALU.mult), reads=[vg_k, lng_k], writes=[vg_k])
                              vnb, vnb_k = vnb_r.next()
                              K.op("pool", lambda e: e.tensor_tensor(out=vnb[:], in0=vg[:], in1=lnb[:], op=ALU.add), reads=[vg_k, lnb_k], writes=[vnb_k])
                              vnbs.append((vnb, vnb_k))
                          gst, gst_k = gst_r.next()
                          for m in range(24):
                              pg, pg_k = psr()
                              fm(2048 + m * 128, pg, pg_k)
                              K.op("act", lambda e, m=m: e.activation(out=gst[:, m, 0:bw], in_=pg[:, 0:bw], func=AF.Sigmoid), reads=[pg_k], writes=[gst_k])
                          K.dma("sp", sg["gat"][:, :, b0:b0 + bw], gst[:, :, 0:bw], gst_k, reads=[gst_k], acc=[sg["gat_k"]])
                          for j in range(nj):
                              vnb, vnb_k = vnbs[j]
                              for g in range(4):
                                  K.op("pe", lambda e, g=g: e.matmul(pss[g][0][:, j * 128:(j + 1) * 128], lhsT=vnb[:, g * 128:(g + 1) * 128], rhs=wsT[:, g, :],
                                                                     start=True, stop=True),
                                       reads=[vnb_k, wsT_k], writes=[pss[g][1]], inc=(g == 3))
                          usb, usb_k = usb_r.next()
                          for g in range(4):
                              tq, tq_k = tq_r.next()
                              K.op("dve", lambda e, g=g: e.tensor_tensor(out=tq[:, 0:bw], in0=pss[g][0][:, 0:bw],
                                                                         in1=bsb[:, g, :, :].rearrange("p j q -> p (j q)")[:, 0:bw], op=ALU.add),
                                   reads=[pss[g][1], bsb_k], writes=[tq_k])
                              K.op("dve", lambda e, g=g: e.tensor_tensor(out=usb[:, g, 0:bw], in0=tq[:, 0:bw], in1=ug[:, g, 0:bw], op=ALU.mult),
                                   reads=[tq_k, ug_k], writes=[usb_k])
                          K.dma("sp", sg["us"][:, :, b0:b0 + bw], usb[:, :, 0:bw], usb_k, reads=[usb_k], acc=[sg["us_k"]])
                  K.end_phase()
                  chk(l)

              with ExitStack() as ph:
                  PSR_HI[0] = 4
                  KSB = 4096
                  qtr = K.ring(ph, "qtb", 2, [128, NH, 512], BF16, dma=True)
                  ktr = K.ring(ph, "ktb", 2, [128, KSB], BF16, dma=True)
                  vtr = K.ring(ph, "vtb", 2, [128, KSB // 128, 65], BF16, dma=True)
                  ptr_ = K.ring(ph, "ptb", 4, [128, 512], BF16)
                  aor = K.ring(ph, "aob", 2, [64, NH, 512], BF16, dma=True)
                  rsr = K.ring(ph, "rsb", 2, [128, 512], F32)
                  rir = K.ring(ph, "rib", 2, [64, 512], F32)
                  scale = float(DQK) ** -0.5
                  items = [(sg, b0, bw) for sg in segs for (b0, bw) in blocks_of(sg["n"])]
                  loaded = {}
                  ctr = [0]

                  def issue(i):
                      if i >= len(items) or i in loaded:
                          return
                      sg_, b0_, bw_ = items[i]
                      qt_, qt_k_ = qtr.next()
                      K.dma("sp", qt_[0:DQK, :, 0:bw_], sg_["qT"][:, :, b0_:b0_ + bw_], qt_k_, writes=[qt_k_], dram_reads=[sg_["qT_k"]])
                      loaded[i] = (qt_, qt_k_)

                  for sg in segs:
                      n, S = sg["n"], sg["s"]
                      for (b0, bw) in blocks_of(n):
                          i_ = ctr[0]
                          ctr[0] += 1
                          issue(i_)
                          issue(i_ + 1)
                          qt, qt_k = loaded.pop(i_)
                          ao, ao_k = aor.next()
                          for h in range(NH):
                              po, po_k = PSB[4 + (h % 2)]
                              first = True
                              for (s0, sw) in blocks_of(S, KSB):
                                  kt_, kt_k = ktr.next()
                                  vt_, vt_k = vtr.next()
                                  K.dma("sp", kt_[0:DQK, 0:sw], sg["kT"][h, :, s0:s0 + sw], kt_k, writes=[kt_k], dram_reads=[sg["kT_k"]])
                                  K.dma("sp", vt_[:, 0:sw // 128, :], sg["v"][h, :, s0 // 128:(s0 + sw) // 128, :], vt_k, writes=[vt_k], dram_reads=[sg["v_k"]])
                                  nk = sw // 128
                                  pend = None
                                  for ki in range(nk + 1):
                                      if ki < nk:
                                          ps, ps_k = psr(0, 4)
                                          K.op("pe", lambda e, ki=ki: e.matmul(ps[:, 0:bw], lhsT=kt_[0:DQK, ki * 128:(ki + 1) * 128], rhs=qt[0:DQK, h, 0:bw],
                                                                               start=True, stop=True),
                                               reads=[kt_k, qt_k], writes=[ps_k])
                                          pt, pt_k = ptr_.next()
                                          K.op("act", lambda e: e.activation(out=pt[:, 0:bw], in_=ps[:, 0:bw], func=AF.Exp, scale=scale), reads=[ps_k], writes=[pt_k])
                                          cur = (pt, pt_k, ki)
                                      else:
                                          cur = None
                                      if pend is not None:
                                          ppt, ppt_k, pki = pend
                                          last = (s0 + sw >= S) and (pki == nk - 1)
                                          K.op("pe", lambda e, pki=pki, ppt=ppt, f=first, last=last: e.matmul(po[0:65, 0:bw], lhsT=vt_[:, pki, 0:65], rhs=ppt[:, 0:bw],
                                                                                                            start=f, stop=last),
                                               reads=[vt_k, ppt_k], writes=[po_k])
                                          first = False
                                      pend = cur
                              rs, rs_k = rsr.next()
                              K.op("dve", lambda e: e.tensor_copy(out=rs[64:65, 0:bw], in_=po[64:65, 0:bw]), reads=[po_k], writes=[rs_k])
                              pb, pb_k = PSB[6 + (h % 2)]
                              K.op("pe", lambda e: e.matmul(pb[0:64, 0:bw], lhsT=ones_f[64:65, 0:64], rhs=rs[64:65, 0:bw], start=True, stop=True),
                                   reads=[onesf_k, rs_k], writes=[pb_k])
                              ri, ri_k = rir.next()
                              K.op("dve", lambda e: e.reciprocal(out=ri[:, 0:bw], in_=pb[0:64, 0:bw]), reads=[pb_k], writes=[ri_k])
                              K.op("dve", lambda e, h=h: e.tensor_tensor(out=ao[:, h, 0:bw], in0=po[0:64, 0:bw], in1=ri[:, 0:bw], op=ALU.mult),
                                   reads=[po_k, ri_k], writes=[ao_k])
                          K.dma("sp", sg["ao"][:, :, b0:b0 + bw], ao[:, :, 0:bw], ao_k, reads=[ao_k], acc=[sg["ao_k"]])
                  K.end_phase()
                  chk(l)

              with ExitStack() as ph:
                  PSR_HI[0] = 8
                  wao, wao_k = K.sb(ph, "wao", [64, NH, D], BF16, dma=True)
                  wco, wco_k = K.sb(ph, "wco", [128, 4, D], BF16, dma=True)
                  wso, wso_k = K.sb(ph, "wso", [128, 4, D], BF16, dma=True)
                  wout, wout_k = K.sb(ph, "wout", [128, KT, D], BF16, dma=True)
                  K.dma("pool", wao[:], Wl["w_ao"].rearrange("p (h n) -> p h n", h=NH), wao_k, writes=[wao_k])
                  K.dma("pool", wco[:], wview(Wl["w_co"]), wco_k, writes=[wco_k])
                  K.dma("pool", wso[:], wview(Wl["w_so"]), wso_k, writes=[wso_k])
                  K.dma("pool", wout[:], wview(Wl["w_out"]), wout_k, writes=[wout_k])
                  cdw, cdw_k = K.sb(ph, "cdw", [128, 4, 31], F32, dma=True)
                  cdwb, cdwb_k = K.sb(ph, "cdwb", [128, 4], F32, dma=True)
                  clng, clng_k = K.sb(ph, "clng", [128, 4], F32, dma=True)
                  clnb, clnb_k = K.sb(ph, "clnb", [128, 4], F32, dma=True)
                  K.dma("sp", cdw[:], Wl["cdw"].rearrange("p (c k) -> p c k", c=4), cdw_k, writes=[cdw_k])
                  K.dma("sp", cdwb[:], Wl["cdwb"], cdwb_k, writes=[cdwb_k])
                  K.dma("sp", clng[:], Wl["clng"], clng_k, writes=[clng_k])
                  K.dma("sp", clnb[:], Wl["clnb"], clnb_k, writes=[clnb_k])
                  dgt, dgt_k = K.sb(ph, "dgt", [128, 4, 31, 128], BF16)
                  for c in range(4):
                      for k in range(31):
                          K.op("dve", lambda e, c=c, k=k: e.tensor_scalar(out=dgt[:, c, k, :], in0=ident[:], scalar1=cdw[:, c, k:k + 1], scalar2=None, op0=ALU.mult),
                               reads=[ident_k, cdw_k], writes=[dgt_k])
                  g1t, g1_k = K.sb(ph, "g1t", [128, D], F32, dma=True)
                  glr = K.ring(ph, "glt", 2, [128, 4, 512 + 30], BF16, dma=True)
                  usr = K.ring(ph, "ust", 2, [128, 4, 512], BF16, dma=True)
                  gtr = K.ring(ph, "gtt", 1, [128, 24, 512], BF16, dma=True)
                  aor = K.ring(ph, "aot", 2, [64, NH, 512], BF16, dma=True)
                  hc, hc_k = K.sb(ph, "hc", [128, 4, 512], F32)
                  hcb, hcb_k = K.sb(ph, "hcb", [128, 4, 512], BF16)
                  sqb, sqb_k = K.sb(ph, "sqb", [128, 4, 512], BF16)
                  mean, mean_k = K.sb(ph, "mean", [128, 512], F32)
                  rstd, rstd_k = K.sb(ph, "rstd", [128, 512], F32)
                  cvn, cvn_k = K.sb(ph, "cvn", [128, 4, 512], BF16)
                  mrg, mrg_k = K.sb(ph, "mrg", [128, KT, 512], BF16)
                  tmr = K.ring(ph, "tm", 4, [128, 512], F32)
                  xr = K.ring(ph, "xr2", 2, [128, D], F32, dma=True)
                  items = [(sg, b0, bw) for sg in segs for (b0, bw) in blocks_of(sg["n"])]
                  loaded = {}
                  ctr = [0]

                  def issue(i):
                      if i >= len(items) or i in loaded:
                          return
                      sg_, b0_, bw_ = items[i]
                      glt_, glt_k_ = glr.next()
                      K.dma("sp", glt_[:, :, 0:bw_ + 30], sg_["glu"][:, :, b0_:b0_ + bw_ + 30], glt_k_, writes=[glt_k_], dram_reads=[sg_["glu_k"]])
                      ust_, ust_k_ = usr.next()
                      K.dma("sp", ust_[:, :, 0:bw_], sg_["us"][:, :, b0_:b0_ + bw_], ust_k_, writes=[ust_k_], dram_reads=[sg_["us_k"]])
                      aot_, aot_k_ = aor.next()
                      K.dma("sp", aot_[:, :, 0:bw_], sg_["ao"][:, :, b0_:b0_ + bw_], aot_k_, writes=[aot_k_], dram_reads=[sg_["ao_k"]])
                      loaded[i] = (glt_, glt_k_, ust_, ust_k_, aot_, aot_k_)

                  for sg in segs:
                      n = sg["n"]
                      K.dma("sp", g1t[:], sg["mod"][:, 2 * D:3 * D], g1_k, writes=[g1_k], dram_reads=[sg["mod_k"]])
                      for (b0, bw) in blocks_of(n):
                          nj = bw // 128
                          i_ = ctr[0]
                          ctr[0] += 1
                          issue(i_)
                          issue(i_ + 1)
                          glt, glt_k, ust, ust_k, aot, aot_k = loaded.pop(i_)
                          gtt, gtt_k = gtr.next()
                          K.dma("sp", gtt[:, :, 0:bw], sg["gat"][:, :, b0:b0 + bw], gtt_k, writes=[gtt_k], dram_reads=[sg["gat_k"]])
                          for c in range(4):
                              ps, ps_k = psr()
                              for k in range(31):
                                  K.op("pe", lambda e, c=c, k=k: e.matmul(ps[:, 0:bw], lhsT=dgt[:, c, k, :], rhs=glt[:, c, k:k + bw], start=(k == 0), stop=(k == 30)),
                                       reads=[dgt_k, glt_k], writes=[ps_k], inc=(k == 30))
                              K.op("act", lambda e, c=c: e.activation(out=hc[:, c, 0:bw], in_=ps[:, 0:bw], func=AF.Identity, bias=cdwb[:, c:c + 1], scale=1.0),
                                   reads=[ps_k, cdwb_k], writes=[hc_k])
                          K.op("pool", lambda e: e.tensor_copy(out=hcb[:, :, 0:bw], in_=hc[:, :, 0:bw]), reads=[hc_k], writes=[hcb_k])
                          K.op("dve", lambda e: e.tensor_tensor(out=sqb[:, :, 0:bw], in0=hc[:, :, 0:bw], in1=hc[:, :, 0:bw], op=ALU.mult), reads=[hc_k], writes=[sqb_k])
                          p1, p1_k = psr()
                          p2, p2_k = psr()
                          for c in range(4):
                              K.op("pe", lambda e, c=c: e.matmul(p1[:, 0:bw], lhsT=ones_bf[:], rhs=hcb[:, c, 0:bw], start=(c == 0), stop=(c == 3)),
                                   reads=[ones_k, hcb_k], writes=[p1_k], inc=(c == 3))
                          for c in range(4):
                              K.op("pe", lambda e, c=c: e.matmul(p2[:, 0:bw], lhsT=ones_bf[:], rhs=sqb[:, c, 0:bw], start=(c == 0), stop=(c == 3)),
                                   reads=[ones_k, sqb_k], writes=[p2_k], inc=(c == 3))
                          K.op("dve", lambda e: e.tensor_scalar(out=mean[:, 0:bw], in0=p1[:, 0:bw], scalar1=1.0 / 512, scalar2=None, op0=ALU.mult), reads=[p1_k], writes=[mean_k])
                          tm, tm_k = tmr.next()
                          K.op("dve", lambda e: e.tensor_tensor(out=tm[:, 0:bw], in0=mean[:, 0:bw], in1=mean[:, 0:bw], op=ALU.mult), reads=[mean_k], writes=[tm_k])
                          K.op("dve", lambda e: e.scalar_tensor_tensor(out=tm[:, 0:bw], in0=p2[:, 0:bw], scalar=1.0 / 512, in1=tm[:, 0:bw], op0=ALU.mult, op1=ALU.subtract),
                               reads=[p2_k, tm_k], writes=[tm_k])
                          K.op("act", lambda e: e.activation(out=tm[:, 0:bw], in_=tm[:, 0:bw], func=AF.Sqrt, bias=EPS, scale=1.0), reads=[tm_k], writes=[tm_k])
                          K.op("dve", lambda e: e.reciprocal(out=rstd[:, 0:bw], in_=tm[:, 0:bw]), reads=[tm_k], writes=[rstd_k])
                          for c in range(4):
                              t2, t2_k = tmr.next()
                              K.op("dve", lambda e, c=c: e.tensor_tensor(out=t2[:, 0:bw], in0=hc[:, c, 0:bw], in1=mean[:, 0:bw], op=ALU.subtract), reads=[hc_k, mean_k], writes=[t2_k])
                              K.op("dve", lambda e: e.tensor_tensor(out=t2[:, 0:bw], in0=t2[:, 0:bw], in1=rstd[:, 0:bw], op=ALU.mult), reads=[t2_k, rstd_k], writes=[t2_k])
                              K.op("act", lambda e, c=c: e.activation(out=cvn[:, c, 0:bw], in_=t2[:, 0:bw], func=AF.Silu, bias=clnb[:, c:c + 1], scale=clng[:, c:c + 1]),
                                   reads=[t2_k, clnb_k, clng_k], writes=[cvn_k])
                          for j in range(8):
                              pa, pa_k = psr()
                              pc, pc_k = psr()
                              pss_, pss_k = psr()
                              for h in range(NH):
                                  K.op("pe", lambda e, h=h: e.matmul(pa[:, 0:bw], lhsT=wao[:, h, j * 128:(j + 1) * 128], rhs=aot[:, h, 0:bw], start=(h == 0), stop=(h == NH - 1)),
                                       reads=[wao_k, aot_k], writes=[pa_k], inc=(h == NH - 1))
                              for c in range(4):
                                  K.op("pe", lambda e, c=c: e.matmul(pc[:, 0:bw], lhsT=wco[:, c, j * 128:(j + 1) * 128], rhs=cvn[:, c, 0:bw], start=(c == 0), stop=(c == 3)),
                                       reads=[wco_k, cvn_k], writes=[pc_k], inc=(c == 3))
                              for c in range(4):
                                  K.op("pe", lambda e, c=c: e.matmul(pss_[:, 0:bw], lhsT=wso[:, c, j * 128:(j + 1) * 128], rhs=ust[:, c, 0:bw], start=(c == 0), stop=(c == 3)),
                                       reads=[wso_k, ust_k], writes=[pss_k], inc=(c == 3))
                              m1, m1_k = tmr.next()
                              m2, m2_k = tmr.next()
                              m3, m3_k = tmr.next()
                              K.op("dve", lambda e: e.tensor_tensor(out=m1[:, 0:bw], in0=pa[:, 0:bw], in1=gtt[:, j, 0:bw], op=ALU.mult), reads=[pa_k, gtt_k], writes=[m1_k])
                              K.op("dve", lambda e: e.tensor_tensor(out=m2[:, 0:bw], in0=pc[:, 0:bw], in1=gtt[:, 8 + j, 0:bw], op=ALU.mult), reads=[pc_k, gtt_k], writes=[m2_k])
                              K.op("dve", lambda e: e.tensor_tensor(out=m3[:, 0:bw], in0=pss_[:, 0:bw], in1=gtt[:, 16 + j, 0:bw], op=ALU.mult), reads=[pss_k, gtt_k], writes=[m3_k])
                              K.op("pool", lambda e: e.tensor_tensor(out=m1[:, 0:bw], in0=m1[:, 0:bw], in1=m2[:, 0:bw], op=ALU.add), reads=[m1_k, m2_k], writes=[m1_k])
                              K.op("pool", lambda e, j=j: e.tensor_tensor(out=mrg[:, j, 0:bw], in0=m1[:, 0:bw], in1=m3[:, 0:bw], op=ALU.add), reads=[m1_k, m3_k], writes=[mrg_k])
                          for t in range(nj):
                              t0 = b0 + t * 128
                              xt, xk = xr.next()
                              K.dma("sp", xt[:], sg["x"][t0:t0 + 128, :], xk, writes=[xk], dram_reads=[sg["x_k"]])
                              for half in range(2):
                                  po, po_k = psr()
                                  for j in range(8):
                                      K.op("pe", lambda e, j=j: e.matmul(po[:], lhsT=mrg[:, j, t * 128:(t + 1) * 128], rhs=wout[:, j, half * 512:(half + 1) * 512],
                                                                         start=(j == 0), stop=(j == 7)),
                                           reads=[mrg_k, wout_k], writes=[po_k], inc=(j == 7))
                                  tm2, tm2_k = tmr.next()
                                  K.op("dve", lambda e: e.tensor_tensor(out=tm2[:], in0=po[:], in1=g1t[:, half * 512:(half + 1) * 512], op=ALU.mult), reads=[po_k, g1_k], writes=[tm2_k])
                                  K.op("pool", lambda e: e.tensor_tensor(out=xt[:, half * 512:(half + 1) * 512], in0=xt[:, half * 512:(half + 1) * 512], in1=tm2[:], op=ALU.add),
                                       reads=[xk, tm2_k], writes=[xk])
                              K.dma("sp", sg["xm"][t0:t0 + 128, :], xt[:], xk, reads=[xk], acc=[sg["xm_k"]])
                  K.end_phase()
                  chk(l)

              with ExitStack() as ph:
                  P = dict(junkb=K.ring(ph, "junkc", 2, [128, 4, D], F32), ssb=K.ring(ph, "ssc", 3, [128, 16], F32),
                           hbb=K.ring(ph, "hbc", 2, [128, 4, D], BF16))
                  xbr = K.ring(ph, "xb3", 3, [128, 4, D], F32, dma=True)
                  hst = K.ring(ph, "hst2", 2, [128, KT, 512], BF16, dma=True)
                  gamt, gam_k = K.sb(ph, "gam2", [128, D], F32, dma=True)
                  sht, sh_k = K.sb(ph, "sh2", [128, D], F32, dma=True)
                  items = [(sg, b0, bw) for sg in segs for (b0, bw) in blocks_of(sg["n"])]
                  loaded = {}
                  ctr = [0]

                  def issue(i):
                      if i >= len(items) or i in loaded:
                          return
                      sg_, b0_, bw_ = items[i]
                      xb_, xb_k_ = xbr.next()
                      K.dma("sp", xb_[:, 0:bw_ // 128, :], sg_["xm"][b0_:b0_ + bw_, :].rearrange("(j p) d -> p j d", p=128), xb_k_, writes=[xb_k_], dram_reads=[sg_["xm_k"]])
                      loaded[i] = (xb_, xb_k_)

                  for sg in segs:
                      n = sg["n"]
                      K.dma("sp", sht[:], sg["mod"][:, 3 * D:4 * D], sh_k, writes=[sh_k], dram_reads=[sg["mod_k"]])
                      K.dma("sp", gamt[:], sg["mod"][:, 4 * D:5 * D], gam_k, writes=[gam_k], dram_reads=[sg["mod_k"]])
                      for (b0, bw) in blocks_of(n):
                          G = bw // 128
                          hs, hs_k = hst.next()
                          i_ = ctr[0]
                          ctr[0] += 1
                          issue(i_)
                          issue(i_ + 1)
                          xb, xb_k = loaded.pop(i_)
                          masks = []
                          if sg["prompt"]:
                              for j in range(G):
                                  if b0 + j * 128 == 0:
                                      masks.append((j, 0))
                                  if b0 + j * 128 == n - 128:
                                      masks.append((j, 1))
                          hbb, hbb_k = norm_block(P, xb, xb_k, G, (gamt, gam_k), (sht, sh_k), masks=masks)
                          for j in range(G):
                              transpose_to(P, hbb[:, j, :], hbb_k, hs[:, :, j * 128:(j + 1) * 128], hs_k)
                          K.dma("sp", sg["h2T"][:, :, 1 + b0:1 + b0 + bw], hs[:, :, 0:bw], hs_k, reads=[hs_k], acc=[sg["h2T_k"]])
                  K.end_phase()
                  chk(l)
              with ExitStack() as ph:
                  wup, wup_k = K.sb(ph, "wup", [128, KT, 2 * D_FF], BF16, dma=True)
                  wdn, wdn_k = K.sb(ph, "wdn", [128, NFT, D], BF16, dma=True)
                  wuv = wview(Wl["w_up"])
                  for c in range(0, 2 * D_FF, 704):
                      K.dma("pool", wup[:, :, c:c + 704], wuv[:, :, c:c + 704], wup_k, acc=[wup_k])
                  K.dma("pool", wdn[:], wview(Wl["w_down"]), wdn_k, writes=[wdn_k])
                  fdw, fdw_k = K.sb(ph, "fdw", [128, 44, 3], F32, dma=True)
                  fdwb, fdwb_k = K.sb(ph, "fdwb", [128, 44], F32, dma=True)
                  K.dma("sp", fdw[:], Wl["fdw"].rearrange("p (c k) -> p c k", c=44), fdw_k, writes=[fdw_k])
                  K.dma("sp", fdwb[:], Wl["fdwb"], fdwb_k, writes=[fdwb_k])
                  g2t, g2_k = K.sb(ph, "g2t", [128, D], F32, dma=True)
                  h2r = K.ring(ph, "h2b", 2, [128, KT, 514], BF16, dma=True)
                  zr = K.ring(ph, "zt_", 3, [128, 514], F32)
                  accr = K.ring(ph, "acc", 4, [128, 512], F32)
                  sgr = K.ring(ph, "sgf", 2, [128, 512], F32)
                  uT, uT_k = K.sb(ph, "uT", [128, NFT, 512], BF16)
                  xr = K.ring(ph, "xr4", 2, [128, D], F32, dma=True)
                  tmr = K.ring(ph, "tm4", 2, [128, 512], F32)
                  items = [(sg, b0, bw) for sg in segs for (b0, bw) in blocks_of(sg["n"])]
                  loaded = {}
                  ctr = [0]

                  def issue(i):
                      if i >= len(items) or i in loaded:
                          return
                      sg_, b0_, bw_ = items[i]
                      h2_, h2_k_ = h2r.next()
                      K.dma("sp", h2_[:, :, 0:bw_ + 2], sg_["h2T"][:, :, b0_:b0_ + bw_ + 2], h2_k_, writes=[h2_k_], dram_reads=[sg_["h2T_k"]])
                      loaded[i] = (h2_, h2_k_)

                  for sg in segs:
                      n = sg["n"]
                      K.dma("sp", g2t[:], sg["mod"][:, 5 * D:6 * D], g2_k, writes=[g2_k], dram_reads=[sg["mod_k"]])
                      for (b0, bw) in blocks_of(n):
                          nj = bw // 128
                          i_ = ctr[0]
                          ctr[0] += 1
                          issue(i_)
                          issue(i_ + 1)
                          h2, h2_k = loaded.pop(i_)
                          half = (bw + 2) // 2
                          for i in range(NFT):
                              accs = []
                              for which in range(2):
                                  ci = which * NFT + i
                                  col0 = ci * 128
                                  z, z_k = zr.next()
                                  for (c0, c1) in ((0, half), (half, bw + 2)):
                                      ps, ps_k = psr(0, 8)
                                      for kt in range(KT):
                                          K.op("pe", lambda e, kt=kt: e.matmul(ps[:, 0:c1 - c0], lhsT=wup[:, kt, col0:col0 + 128], rhs=h2[:, kt, c0:c1],
                                                                               start=(kt == 0), stop=(kt == KT - 1)),
                                               reads=[wup_k, h2_k], writes=[ps_k], inc=(kt == KT - 1))
                                      K.op("act", lambda e: e.activation(out=z[:, c0:c1], in_=ps[:, 0:c1 - c0], func=AF.Copy), reads=[ps_k], writes=[z_k])
                                  acc, acc_k = accr.next()
                                  K.op("dve", lambda e: e.tensor_scalar(out=acc[:, 0:bw], in0=z[:, 0:bw], scalar1=fdw[:, ci, 0:1], scalar2=fdwb[:, ci:ci + 1],
                                                                        op0=ALU.mult, op1=ALU.add),
                                       reads=[z_k, fdw_k, fdwb_k], writes=[acc_k])
                                  K.op("dve", lambda e: e.scalar_tensor_tensor(out=acc[:, 0:bw], in0=z[:, 1:bw + 1], scalar=fdw[:, ci, 1:2], in1=acc[:, 0:bw],
                                                                               op0=ALU.mult, op1=ALU.add),
                                       reads=[z_k, fdw_k, acc_k], writes=[acc_k])
                                  K.op("dve", lambda e: e.scalar_tensor_tensor(out=acc[:, 0:bw], in0=z[:, 2:bw + 2], scalar=fdw[:, ci, 2:3], in1=acc[:, 0:bw],
                                                                               op0=ALU.mult, op1=ALU.add),
                                       reads=[z_k, fdw_k, acc_k], writes=[acc_k])
                                  accs.append((acc, acc_k))
                              sgf, sgf_k = sgr.next()
                              K.op("act", lambda e: e.activation(out=sgf[:, 0:bw], in_=accs[0][0][:, 0:bw], func=AF.Silu), reads=[accs[0][1]], writes=[sgf_k])
                              K.op("pool", lambda e, i=i: e.tensor_tensor(out=uT[:, i, 0:bw], in0=sgf[:, 0:bw], in1=accs[1][0][:, 0:bw], op=ALU.mult),
                                   reads=[sgf_k, accs[1][1]], writes=[uT_k])
                          for t in range(nj):
                              t0 = b0 + t * 128
                              if sg["prompt"] and (t0 < HALO or t0 >= n - HALO):
                                  continue
                              xt, xk = xr.next()
                              K.dma("sp", xt[:], sg["xm"][t0:t0 + 128, :], xk, writes=[xk], dram_reads=[sg["xm_k"]])
                              for hf in range(2):
                                  po, po_k = psr(0, 8)
                                  for i in range(NFT):
                                      K.op("pe", lambda e, i=i: e.matmul(po[:], lhsT=uT[:, i, t * 128:(t + 1) * 128], rhs=wdn[:, i, hf * 512:(hf + 1) * 512],
                                                                         start=(i == 0), stop=(i == NFT - 1)),
                                           reads=[uT_k, wdn_k], writes=[po_k], inc=(i == NFT - 1))
                                  tm2, tm2_k = tmr.next()
                                  K.op("dve", lambda e: e.tensor_tensor(out=tm2[:], in0=po[:], in1=g2t[:, hf * 512:(hf + 1) * 512], op=ALU.mult), reads=[po_k, g2_k], writes=[tm2_k])
                                  K.op("pool", lambda e: e.tensor_tensor(out=xt[:, hf * 512:(hf + 1) * 512], in0=xt[:, hf * 512:(hf + 1) * 512], in1=tm2[:], op=ALU.add),
                                       reads=[xk, tm2_k], writes=[xk])
                              yoff = t0 - HALO if sg["prompt"] else t0
                              K.dma("sp", sg["y"][yoff:yoff + 128, :], xt[:], xk, reads=[xk], acc=[sg["y_k"]])
                  K.end_phase()
                  chk(l)
        except _Stop:
            print('STOPPED at', _stop)
        K.barrier()
        print("instructions emitted:", K.nops)
    return nc


def rope_table(pos):
    inv = np.power(np.float32(10000.0), -np.arange(0, 32, 2, dtype=np.float32) / np.float32(32)).astype(np.float32)
    ang = pos.astype(np.float32)[:, None] * inv[None, :]
    return np.concatenate([np.cos(ang), np.sin(ang)], axis=1).astype(np.float32)


def layer_weights(inp, l):
    f = lambda a: np.ascontiguousarray(a, dtype=np.float32)
    ukv = inp["w_ukv"][l].reshape(128, NH, 128)
    w = dict(
        w_ada=f(inp["w_ada"][l]), b_ada=f(inp["b_ada"][l][None, :]), norm1=f(inp["norm1"][l][None, :]), norm2=f(inp["norm2"][l][None, :]),
        w_in=f(inp["w_in"][l]), w_uq=f(inp["w_uq"][l]),
        w_ukv=f(np.concatenate([ukv[:, :, 0:64].reshape(128, 512), ukv[:, :, 64:128].reshape(128, 512)], axis=1)),
        qan=f(inp["q_a_norm"][l].reshape(2, 128).T), kvan=f(inp["kv_a_norm"][l].reshape(128, 1)),
        qhn=f(inp["q_head_norm"][l][None, :]), khn=f(inp["k_head_norm"][l][None, :]),
        w_ao=f(inp["w_attn_o"][l].reshape(NH, 64, D).transpose(1, 0, 2).reshape(64, NH * D)),
        cdw=f(inp["conv_dw"][l].T.reshape(4, 128, 31).transpose(1, 0, 2).reshape(128, 4 * 31)),
        cdwb=f(inp["conv_dw_b"][l].reshape(4, 128).T), clng=f(inp["conv_ln_g"][l].reshape(4, 128).T), clnb=f(inp["conv_ln_b"][l].reshape(4, 128).T),
        w_co=f(inp["w_conv_o"][l]), sglng=f(inp["sg_ln_g"][l][None, :]), sglnb=f(inp["sg_ln_b"][l][None, :]),
        sgwT=f(inp["sg_w"][l].transpose(2, 0, 1).reshape(128, 4 * 128)),
        sgb=f(inp["sg_b"][l].reshape(1, 512)), w_so=f(inp["w_sg_o"][l]), w_out=f(inp["w_out"][l]), w_up=f(inp["w_up"][l]),
        fdw=f(inp["ffn_dw"][l].T.reshape(44, 128, 3).transpose(1, 0, 2).reshape(128, 44 * 3)),
        fdwb=f(inp["ffn_dw_b"][l].reshape(44, 128).T), w_down=f(inp["w_down"][l]))
    return w


_PROG = {}


def run_model(inp, cfg, n_cores=8):
    NS, SS, PCH, PS = cfg["NS"], cfg["SS"], cfg["PCH"], cfg["PS"]
    PSEG = PCH + 2 * HALO
    key = (NS, SS, PCH, PS)
    L = inp["w_ada"].shape[0]
    assert L == 2
    if key not in _PROG:
        _PROG[key] = build_program(cfg, L)
    nc = _PROG[key]
    xp = np.asarray(inp["x_prompt"], dtype=np.float32)
    xs = np.asarray(inp["x_sample"], dtype=np.float32)
    cp = np.asarray(inp["c_prompt"], dtype=np.float32)
    cs = np.asarray(inp["c_sample"], dtype=np.float32)
    nchunk = PS // PCH
    rope_s = rope_table(np.arange(SS))
    rope_pc = rope_table(np.arange(PS))
    wls = [layer_weights(inp, l) for l in range(L)]
    in_maps = []
    for c in range(n_cores):
        b = c // nchunk
        r = c % nchunk
        lo = r * PCH - HALO
        pm = np.zeros((128, 2), np.float32)
        pm[:, 0] = 1.0 if r > 0 else 0.0
        pm[:, 1] = 1.0 if r < nchunk - 1 else 0.0
        sel = np.zeros((128, nchunk), np.float32)
        sel[:, r] = 1.0
        m = dict(xs=np.ascontiguousarray(xs[c * NS:(c + 1) * NS].reshape(NS * SS, D)),
                 xpctx=np.ascontiguousarray(xp[b]),
                 cvT=np.ascontiguousarray(np.concatenate([cs[c * NS:(c + 1) * NS], cp[b:b + 1]], axis=0).reshape((NS + 1) * KT, 128).T),
                 pmask=pm, psel=sel, rope_s=rope_s, rope_pc=rope_pc, rope_pq=rope_table(np.arange(lo, lo + PSEG)))
        for l in range(L):
            for k, v in wls[l].items():
                m["%s_%d" % (k, l)] = v
        in_maps.append(m)
    res = run_bass_kernel_spmd(nc, in_maps, core_ids=list(range(n_cores)))
    ys = np.stack([np.asarray(r_["ys"]).reshape(NS, SS, D) for r_ in res.results], axis=0).reshape(n_cores * NS, SS, D)
    ypo = np.stack([np.asarray(r_["yp"]) for r_ in res.results], axis=0).reshape(n_cores // nchunk, PS, D)
    return ypo.astype(np.float32), ys.astype(np.float32)


def kernel(**inputs):
    yp, ys = run_model(inputs, CFG, 8)
    return (yp, ys)
```

```python
import numpy as np
from contextlib import ExitStack
import concourse.bass as bass
import concourse.mybir as mybir
from concourse.bass_utils import run_bass_kernel_spmd

F32 = mybir.dt.float32
BF16 = mybir.dt.bfloat16
AF = mybir.ActivationFunctionType
ALU = mybir.AluOpType
AX = mybir.AxisListType

D = 1024
KT = 8
NH = 8
DQK = 96
EPS = 1e-6
D_IN = 5536
D_FF = 2816
NFT = 22
HALO = 128

CFG = dict(NS=4, SS=2048, PCH=2048, PS=8192)


class Trk:
    __slots__ = ("w", "r", "dsem", "dcnt")

    def __init__(self):
        self.w = {}
        self.r = {}
        self.dsem = None
        self.dcnt = 0


class KB:
    def __init__(self, nc, es):
        self.nc = nc
        self.es = es
        self.eng = {"pe": nc.tensor, "act": nc.scalar, "dve": nc.vector, "pool": nc.gpsimd, "sp": nc.sync}
        self.sem = {k: es.enter_context(nc.semaphore("s_" + k)) for k in ["pe", "act", "dve", "pool"]}
        self.cnt = {k: 0 for k in self.sem}
        self.seen = {k: {} for k in self.eng}
        self.pend = {k: [] for k in self.eng}
        self.dma_trks = []
        self.nops = 0
        self.sem_free = []
        self.phase_trks = []
        self.nsem = 0

    def get_dsem(self):
        if self.sem_free:
            return self.sem_free.pop()
        self.nsem += 1
        return (self.es.enter_context(self.nc.semaphore("dq%d" % self.nsem)), 0)

    def release(self, trks):
        for k in trks:
            if k.dsem is not None:
                self.sem_free.append((k.dsem, k.dcnt))
                if k in self.dma_trks:
                    self.dma_trks.remove(k)

    def _wait(self, e, sem, val):
        d = self.seen[e]
        if d.get(sem, 0) >= val:
            return
        self.eng[e].wait_ge(sem, val)
        d[sem] = val

    def _deps(self, e, reads, writes):
        for t in reads:
            for s, (v, ek) in t.w.items():
                self._wait(e, s, v)
        for t in writes:
            for s, (v, ek) in t.w.items():
                if ek != e:
                    self._wait(e, s, v)
            for ek, (s, v) in t.r.items():
                if ek != e:
                    self._wait(e, s, v)

    def op(self, e, fn, reads=(), writes=(), inc=True):
        if getattr(self, 'skip', False):
            return
        self._deps(e, reads, writes)
        ins = fn(self.eng[e])
        self.nops += 1
        if not inc:
            self.pend[e].append((reads, writes))
            return
        self.cnt[e] += 1
        c = self.cnt[e]
        s = self.sem[e]
        ins.then_inc(s, 1)
        self.pend[e].append((reads, writes))
        for rd, wr in self.pend[e]:
            for t in wr:
                t.w = {s: (c, e)}
                t.r = {}
        for rd, wr in self.pend[e]:
            for t in rd:
                t.r[e] = (s, c)
        self.pend[e] = []

    def dma(self, q, out, in_, own, reads=(), writes=(), acc=(), dram_reads=(), slow=False):
        if getattr(self, 'skip', False):
            return
        self._deps(q, list(reads) + list(dram_reads), writes)
        for t in acc:
            for ek, (s, v) in t.r.items():
                self._wait(q, s, v)
        own.dcnt += 16
        if slow:
            self.eng[q].dma_start(out=out, in_=in_, allow_slow_non_contiguous=True).then_inc(own.dsem, 16)
        else:
            self.eng[q].dma_start(out=out, in_=in_).then_inc(own.dsem, 16)
        self.nops += 1
        key = ("dma", own.dsem)
        for t in writes:
            t.w = {own.dsem: (own.dcnt, key)}
            t.r = {}
        for t in acc:
            t.w[own.dsem] = (own.dcnt, key)
        for t in reads:
            t.r[key] = (own.dsem, own.dcnt)

    def end_phase(self):
        self.barrier()
        self.release(list(self.phase_trks))
        self.phase_trks = []

    def barrier(self):
        for e in self.eng:
            for k in self.sem:
                if k != e and self.cnt[k] > 0:
                    self._wait(e, self.sem[k], self.cnt[k])
            for t in self.dma_trks:
                if t.dcnt > 0:
                    self._wait(e, t.dsem, t.dcnt)

    def sb(self, es, name, shape, dt, dma=False):
        self.nalloc = getattr(self, "nalloc", 0) + 1
        t = es.enter_context(self.nc.sbuf_tensor("%s_u%d" % (name, self.nalloc), list(shape), dt))
        k = Trk()
        if dma:
            k.dsem, k.dcnt = self.get_dsem()
            self.dma_trks.append(k)
            self.phase_trks.append(k)
        return t, k

    def ring(self, es, name, n, shape, dt, dma=False):
        return Ring([self.sb(es, "%s%d" % (name, i), shape, dt, dma) for i in range(n)])


class Ring:
    def __init__(self, items):
        self.items = items
        self.i = 0

    def next(self):
        it = self.items[self.i % len(self.items)]
        self.i += 1
        return it


def blocks_of(n, bs=512):
    out = []
    o = 0
    while o < n:
        b = min(bs, n - o)
        out.append((o, b))
        o += b
    return out


def build_program(cfg, n_layers=1):
    NS, SS, PCH, PS = cfg["NS"], cfg["SS"], cfg["PCH"], cfg["PS"]
    PSEG = PCH + 2 * HALO
    nc = bass.Bass("TRN2", target_bir_lowering=False)

    def din(name, shape):
        return nc.dram_tensor(name, list(shape), F32, kind="ExternalInput").ap()

    xs = din("xs", [NS * SS, D])
    psel = din("psel", [128, PS // PCH])
    xpctx = din("xpctx", [PS, D])
    cvT = din("cvT", [128, (NS + 1) * KT])
    pmask = din("pmask", [128, 2])
    rope_s = din("rope_s", [SS, 32])
    rope_pc = din("rope_pc", [PS, 32])
    rope_pq = din("rope_pq", [PSEG, 32])
    W = {}
    wshapes = dict(
        w_ada=[D, 6 * D], b_ada=[1, 6 * D], norm1=[1, D], norm2=[1, D], w_in=[D, D_IN],
        w_uq=[256, 768], w_ukv=[128, 1024], qan=[128, 2], kvan=[128, 1], qhn=[1, 96], khn=[1, 96],
        w_ao=[64, 8 * D], cdw=[128, 4 * 31], cdwb=[128, 4], clng=[128, 4], clnb=[128, 4],
        w_co=[512, D], sglng=[1, 512], sglnb=[1, 512], sgwT=[128, 4 * 128], sgb=[1, 512], w_so=[512, D],
        w_out=[D, D], w_up=[D, 2 * D_FF], fdw=[128, 44 * 3], fdwb=[128, 44], w_down=[D_FF, D])
    for l in range(n_layers):
        for k, shp in wshapes.items():
            W[(l, k)] = din("%s_%d" % (k, l), shp)
    ys = nc.dram_tensor("ys", [NS * SS, D], F32, kind="ExternalOutput").ap()
    yp = nc.dram_tensor("yp", [PCH, D], F32, kind="ExternalOutput").ap()

    x1s = nc.dram_tensor("x1s", [NS * SS, D], F32).ap()
    x1full = nc.dram_tensor("x1full", [PS, D], F32).ap()
    x1seg = nc.dram_tensor("x1seg", [PSEG, D], F32).ap()
    x1s_k = [Trk() for _ in range(NS)]
    x1full_k = Trk()
    x1seg_k = Trk()
    assert n_layers == 2

    def make_segs(l):
        segs = []
        for i in range(NS):
            if l == 0:
                segs.append(dict(n=SS, s=SS, x=xs[i * SS:(i + 1) * SS, :], x_k=Trk(), ctx=None, ctx_k=None, rq=rope_s, rk=rope_s,
                                 y=x1s[i * SS:(i + 1) * SS, :], y_k=x1s_k[i], prompt=False))
            else:
                segs.append(dict(n=SS, s=SS, x=x1s[i * SS:(i + 1) * SS, :], x_k=x1s_k[i], ctx=None, ctx_k=None, rq=rope_s, rk=rope_s,
                                 y=ys[i * SS:(i + 1) * SS, :], y_k=Trk(), prompt=False))
        if l == 0:
            segs.append(dict(n=PS, s=PS, x=xpctx, x_k=Trk(), ctx=None, ctx_k=None, rq=rope_pc, rk=rope_pc,
                             y=x1full, y_k=x1full_k, prompt=False))
        else:
            segs.append(dict(n=PSEG, s=PS, x=x1seg, x_k=x1seg_k, ctx=x1full, ctx_k=x1full_k, rq=rope_pq, rk=rope_pc,
                             y=yp, y_k=Trk(), prompt=True))
        for si, sg in enumerate(segs):
            n, s_ = sg["n"], sg["s"]

            def dsc(nm, shape, dt=BF16):
                return nc.dram_tensor("%s_%d_%d" % (nm, l, si), list(shape), dt).ap()
            sg["hT"] = dsc("hT", [128, KT, n]); sg["hT_k"] = Trk()
            sg["qT"] = dsc("qT", [DQK, NH, n]); sg["qT_k"] = Trk()
            sg["kT"] = dsc("kT", [NH, DQK, s_]); sg["kT_k"] = Trk()
            sg["v"] = dsc("v", [NH, 128, s_ // 128, 65]); sg["v_k"] = Trk()
            sg["gat"] = dsc("gat", [128, 24, n]); sg["gat_k"] = Trk()
            sg["glu"] = dsc("glu", [128, 4, n + 30]); sg["glu_k"] = Trk()
            sg["us"] = dsc("us", [128, 4, n]); sg["us_k"] = Trk()
            sg["ao"] = dsc("ao", [64, NH, n]); sg["ao_k"] = Trk()
            sg["xm"] = dsc("xm", [n, D], F32); sg["xm_k"] = Trk()
            sg["h2T"] = dsc("h2T", [128, KT, n + 2]); sg["h2T_k"] = Trk()
            sg["mod"] = dsc("mod", [128, 6 * D], F32); sg["mod_k"] = Trk()
        return segs

    all_segs = [make_segs(l) for l in range(n_layers)]

    with ExitStack() as es:
        K = KB(nc, es)
        ident, ident_k = K.sb(es, "ident", [128, 128], BF16)
        ones_bf, ones_k = K.sb(es, "ones_bf", [128, 128], BF16)
        ones_f, onesf_k = K.sb(es, "ones_f", [128, 64], F32)
        zt, zt_k = K.sb(es, "zt", [128, 64], BF16, dma=True)
        mk, mk_k = K.sb(es, "mk", [128, 2], F32, dma=True)
        K.op("dve", lambda e: e.memset(ident[:], 1.0), writes=[ident_k])
        K.op("pool", lambda e: e.affine_select(out=ident[:], in_=ident[:], pattern=[[-1, 128]],
                                               compare_op=ALU.is_equal, fill=0.0, base=0, channel_multiplier=1),
             reads=[ident_k], writes=[ident_k])
        K.op("dve", lambda e: e.memset(ones_bf[:], 1.0), writes=[ones_k])
        K.op("dve", lambda e: e.memset(ones_f[:], 1.0), writes=[onesf_k])
        K.op("dve", lambda e: e.memset(zt[:], 0.0), writes=[zt_k])
        K.dma("sp", mk[:], pmask, mk_k, writes=[mk_k])
        K.phase_trks = []
        PSB = []
        for i in range(8):
            t = es.enter_context(nc.psum_tensor("psb%d" % i, [128, 512], F32))
            PSB.append((t, Trk()))
        psr_state = [0]

        PSR_HI = [8]

        def psr(lo=0, hi=None):
            if hi is None:
                hi = PSR_HI[0]
            i = lo + psr_state[0] % (hi - lo)
            psr_state[0] += 1
            return PSB[i]

        for sg in [g for sl in all_segs for g in sl]:
            n = sg["n"]
            K.dma("sp", sg["glu"][:, :, 0:15], zt[:, 0:60].rearrange("p (a b) -> p a b", a=4), zt_k, reads=[zt_k], acc=[sg["glu_k"]])
            K.dma("sp", sg["glu"][:, :, n + 15:n + 30], zt[:, 0:60].rearrange("p (a b) -> p a b", a=4), zt_k, reads=[zt_k], acc=[sg["glu_k"]])
            K.dma("sp", sg["h2T"][:, :, 0:1], zt[:, 0:8].rearrange("p (a b) -> p a b", a=8), zt_k, reads=[zt_k], acc=[sg["h2T_k"]], slow=True)
            K.dma("sp", sg["h2T"][:, :, n + 1:n + 2], zt[:, 0:8].rearrange("p (a b) -> p a b", a=8), zt_k, reads=[zt_k], acc=[sg["h2T_k"]], slow=True)

        def wview(ap_, p=128):
            return ap_.rearrange("(kt p) n -> p kt n", p=p)

        def norm_tile(P, xt, xk, gam, sh, mask_col=None):
            junk, junk_k = P["junk"].next()
            ss, ss_k = P["ss"].next()
            K.op("dve", lambda e: e.memset(ss[:], 0.0), writes=[ss_k])
            K.op("act", lambda e: e.activation(out=junk[:], in_=xt[:], func=AF.Square, accum_out=ss[:, 0:1]),
                 reads=[xk, ss_k], writes=[junk_k, ss_k])
            K.op("act", lambda e: e.activation(out=ss[:, 1:2], in_=ss[:, 0:1], func=AF.Sqrt, bias=EPS, scale=1.0 / D),
                 reads=[ss_k], writes=[ss_k])
            K.op("dve", lambda e: e.reciprocal(out=ss[:, 2:3], in_=ss[:, 1:2]), reads=[ss_k], writes=[ss_k])
            K.op("dve", lambda e: e.scalar_tensor_tensor(out=junk[:], in0=xt[:], scalar=ss[:, 2:3], in1=gam[0][:],
                                                         op0=ALU.mult, op1=ALU.mult),
                 reads=[xk, ss_k, gam[1], junk_k], writes=[junk_k])
            hb, hb_k = P["hb"].next()
            K.op("pool", lambda e: e.tensor_tensor(out=hb[:], in0=junk[:], in1=sh[0][:], op=ALU.add),
                 reads=[junk_k, sh[1]], writes=[hb_k])
            if mask_col is not None:
                K.op("dve", lambda e: e.tensor_scalar(out=hb[:], in0=hb[:], scalar1=mk[:, mask_col:mask_col + 1],
                                                      scalar2=None, op0=ALU.mult),
                     reads=[hb_k, mk_k], writes=[hb_k])
            return hb, hb_k

        def transpose_to(P, hb, hb_k, dst, dst_k, ncols_src=D, rows=128, chunk=128):
            nchunk = ncols_src // chunk
            ps, ps_k = psr()
            psb = ps[:].bitcast(BF16)
            for i in range(nchunk):
                K.op("pe", lambda e, i=i: e.transpose(psb[0:chunk, i * 128:(i + 1) * 128], hb[:, i * chunk:(i + 1) * chunk], ident[:]),
                     reads=[hb_k, ident_k], writes=[ps_k], inc=(i == nchunk - 1))
            K.op("act", lambda e: e.activation(out=dst, in_=psb[0:chunk, 0:nchunk * 128].rearrange("p (a b) -> p a b", a=nchunk),
                                               func=AF.Copy),
                 reads=[ps_k], writes=[dst_k])

        def norm_block(P, xb, xb_k, G, gam, sh, masks=()):
            junk, junk_k = P["junkb"].next()
            ss, ss_k = P["ssb"].next()
            K.op("dve", lambda e: e.memset(ss[:], 0.0), writes=[ss_k])
            for j in range(G):
                K.op("act", lambda e, j=j: e.activation(out=junk[:, j, :], in_=xb[:, j, :], func=AF.Square, accum_out=ss[:, j:j + 1]),
                     reads=[xb_k], writes=[junk_k, ss_k])
            K.op("act", lambda e: e.activation(out=ss[:, 4:4 + G], in_=ss[:, 0:G], func=AF.Sqrt, bias=EPS, scale=1.0 / D),
                 reads=[ss_k], writes=[ss_k])
            K.op("dve", lambda e: e.reciprocal(out=ss[:, 8:8 + G], in_=ss[:, 4:4 + G]), reads=[ss_k], writes=[ss_k])
            for j in range(G):
                K.op("dve", lambda e, j=j: e.scalar_tensor_tensor(out=junk[:, j, :], in0=xb[:, j, :], scalar=ss[:, 8 + j:9 + j], in1=gam[0][:],
                                                                  op0=ALU.mult, op1=ALU.mult),
                     reads=[xb_k, ss_k, gam[1], junk_k] if j == 0 else [xb_k, gam[1]], writes=[junk_k])
            hbb, hbb_k = P["hbb"].next()
            K.op("pool", lambda e: e.tensor_tensor(out=hbb[:, 0:G, :], in0=junk[:, 0:G, :], in1=sh[0][:].unsqueeze(1).to_broadcast([128, G, D]), op=ALU.add),
                 reads=[junk_k, sh[1]], writes=[hbb_k])
            for (j, mc) in masks:
                K.op("dve", lambda e, j=j, mc=mc: e.tensor_scalar(out=hbb[:, j, :], in0=hbb[:, j, :], scalar1=mk[:, mc:mc + 1], scalar2=None, op0=ALU.mult),
                     reads=[hbb_k, mk_k], writes=[hbb_k])
            return hbb, hbb_k

        def rms_block(P, src3, src_k, G, n, dst3, dst_k):
            junk, junk_k = P["junkb"].next()
            s4, s4_k = P["ssb"].next()
            K.op("dve", lambda e: e.memset(s4[:], 0.0), writes=[s4_k])
            for j in range(G):
                K.op("act", lambda e, j=j: e.activation(out=junk[:, j, 0:n], in_=src3[:, j, :], func=AF.Square, accum_out=s4[:, j:j + 1]),
                     reads=[src_k], writes=[junk_k, s4_k])
            K.op("act", lambda e: e.activation(out=s4[:, 4:4 + G], in_=s4[:, 0:G], func=AF.Sqrt, bias=EPS, scale=1.0 / n), reads=[s4_k], writes=[s4_k])
            K.op("dve", lambda e: e.reciprocal(out=s4[:, 8:8 + G], in_=s4[:, 4:4 + G]), reads=[s4_k], writes=[s4_k])
            K.op("dve", lambda e: e.tensor_tensor(out=dst3, in0=src3, in1=s4[:, 8:8 + G].unsqueeze(2).to_broadcast([128, G, n]), op=ALU.mult),
                 reads=[src_k, s4_k], writes=[dst_k])

        def headnorm_rope_block(P, f, f_k, G, gain, rp, rp_k, out, out_k):
            f3 = f[:, 0:G, :].rearrange("p j (h d) -> p (j h) d", h=NH)
            f4 = f[:, 0:G, :].rearrange("p j (h d) -> p j h d", h=NH)
            o4 = out[:, 0:G, :].rearrange("p j (h d) -> p j h d", h=NH)
            sq, sq_k = P["junkb"].next()
            st, st_k = P["stb"].next()
            sq3 = sq[:].rearrange("p j c -> p (j c)")[:, 0:G * 768].rearrange("p (a d) -> p a d", d=DQK)
            K.op("dve", lambda e: e.tensor_tensor(out=sq3, in0=f3, in1=f3, op=ALU.mult), reads=[f_k], writes=[sq_k])
            K.op("dve", lambda e: e.reduce_sum(out=st[:, 0:G * NH], in_=sq3, axis=AX.X), reads=[sq_k], writes=[st_k])
            K.op("act", lambda e: e.activation(out=st[:, 32:32 + G * NH], in_=st[:, 0:G * NH], func=AF.Sqrt, bias=EPS, scale=1.0 / DQK),
                 reads=[st_k], writes=[st_k])
            K.op("dve", lambda e: e.reciprocal(out=st[:, 64:64 + G * NH], in_=st[:, 32:32 + G * NH]), reads=[st_k], writes=[st_k])
            K.op("dve", lambda e: e.tensor_tensor(out=f3, in0=f3, in1=st[:, 64:64 + G * NH].unsqueeze(2).to_broadcast([128, G * NH, DQK]), op=ALU.mult),
                 reads=[f_k, st_k], writes=[f_k])
            K.op("dve", lambda e: e.tensor_tensor(out=f3, in0=f3, in1=gain[0][:].unsqueeze(1).to_broadcast([128, G * NH, DQK]), op=ALU.mult),
                 reads=[f_k, gain[1]], writes=[f_k])
            tt, tt_k = P["ttb"].next()
            t5 = tt[:].rearrange("p (a j h d) -> p a j h d", a=4, j=4, h=NH)
            x1 = f4[:, :, :, 64:80]
            x2 = f4[:, :, :, 80:96]
            cs = rp[:, 0:G, 0:16].unsqueeze(2).to_broadcast([128, G, NH, 16])
            sn = rp[:, 0:G, 16:32].unsqueeze(2).to_broadcast([128, G, NH, 16])
            K.op("dve", lambda e: e.tensor_tensor(out=t5[:, 0, 0:G], in0=x1, in1=cs, op=ALU.mult), reads=[f_k, rp_k], writes=[tt_k])
            K.op("dve", lambda e: e.tensor_tensor(out=t5[:, 1, 0:G], in0=x2, in1=sn, op=ALU.mult), reads=[f_k, rp_k], writes=[tt_k])
            K.op("dve", lambda e: e.tensor_tensor(out=t5[:, 2, 0:G], in0=x1, in1=sn, op=ALU.mult), reads=[f_k, rp_k], writes=[tt_k])
            K.op("dve", lambda e: e.tensor_tensor(out=t5[:, 3, 0:G], in0=x2, in1=cs, op=ALU.mult), reads=[f_k, rp_k], writes=[tt_k])
            K.op("dve", lambda e: e.tensor_tensor(out=o4[:, :, :, 64:80], in0=t5[:, 0, 0:G], in1=t5[:, 1, 0:G], op=ALU.subtract), reads=[tt_k], writes=[out_k])
            K.op("dve", lambda e: e.tensor_tensor(out=o4[:, :, :, 80:96], in0=t5[:, 2, 0:G], in1=t5[:, 3, 0:G], op=ALU.add), reads=[tt_k], writes=[out_k])
            K.op("pool", lambda e: e.tensor_copy(out=o4[:, :, :, 0:64], in_=f4[:, :, :, 0:64]), reads=[f_k], writes=[out_k])

        def headnorm_rope(P, f, f_k, gain, rp, rp_k, out, out_k):
            f3 = f[:].rearrange("p (h d) -> p h d", h=NH)
            o3 = out[:].rearrange("p (h d) -> p h d", h=NH)
            sq, sq_k = P["sq"].next()
            st, st_k = P["st"].next()
            K.op("dve", lambda e: e.tensor_tensor(out=sq[:], in0=f[:], in1=f[:], op=ALU.mult), reads=[f_k], writes=[sq_k])
            K.op("dve", lambda e: e.reduce_sum(out=st[:, 0:8], in_=sq[:].rearrange("p (h d) -> p h d", h=NH), axis=AX.X),
                 reads=[sq_k], writes=[st_k])
            K.op("act", lambda e: e.activation(out=st[:, 8:16], in_=st[:, 0:8], func=AF.Sqrt, bias=EPS, scale=1.0 / DQK),
                 reads=[st_k], writes=[st_k])
            K.op("dve", lambda e: e.reciprocal(out=st[:, 16:24], in_=st[:, 8:16]), reads=[st_k], writes=[st_k])
            K.op("dve", lambda e: e.tensor_tensor(out=f3, in0=f3, in1=st[:, 16:24].unsqueeze(2).to_broadcast([128, NH, DQK]), op=ALU.mult),
                 reads=[f_k, st_k], writes=[f_k])
            K.op("dve", lambda e: e.tensor_tensor(out=f3, in0=f3, in1=gain[0][:].unsqueeze(1).to_broadcast([128, NH, DQK]), op=ALU.mult),
                 reads=[f_k, gain[1]], writes=[f_k])
            tt, tt_k = P["tt"].next()
            t4 = tt[:].rearrange("p (a h d) -> p a h d", a=4, h=NH)
            x1 = f3[:, :, 64:80]
            x2 = f3[:, :, 80:96]
            cs = rp[:, 0:16].unsqueeze(1).to_broadcast([128, NH, 16])
            sn = rp[:, 16:32].unsqueeze(1).to_broadcast([128, NH, 16])
            K.op("dve", lambda e: e.tensor_tensor(out=t4[:, 0], in0=x1, in1=cs, op=ALU.mult), reads=[f_k, rp_k], writes=[tt_k])
            K.op("dve", lambda e: e.tensor_tensor(out=t4[:, 1], in0=x2, in1=sn, op=ALU.mult), reads=[f_k, rp_k], writes=[tt_k])
            K.op("dve", lambda e: e.tensor_tensor(out=t4[:, 2], in0=x1, in1=sn, op=ALU.mult), reads=[f_k, rp_k], writes=[tt_k])
            K.op("dve", lambda e: e.tensor_tensor(out=t4[:, 3], in0=x2, in1=cs, op=ALU.mult), reads=[f_k, rp_k], writes=[tt_k])
            K.op("dve", lambda e: e.tensor_tensor(out=o3[:, :, 64:80], in0=t4[:, 0], in1=t4[:, 1], op=ALU.subtract), reads=[tt_k], writes=[out_k])
            K.op("dve", lambda e: e.tensor_tensor(out=o3[:, :, 80:96], in0=t4[:, 2], in1=t4[:, 3], op=ALU.add), reads=[tt_k], writes=[out_k])
            K.op("pool", lambda e: e.tensor_copy(out=o3[:, :, 0:64], in_=f3[:, :, 0:64]), reads=[f_k], writes=[out_k])

        import os as _os
        _stop = _os.environ.get("KSTOP", "")
        _phc = [0]

        class _Stop(Exception):
            pass

        def chk(l):
            _phc[0] += 1
            if _stop and _stop == "%d,%d" % (l, _phc[0]):
                K.skip = True
                print('STOPPED at', _stop)

        try:
          for l in range(n_layers):
              _phc[0] = 0
              Wl = {k: W[(l, k)] for k in wshapes}
              segs = all_segs[l]
              if l == 1:
                  with ExitStack() as ph:
                      selt, selt_k = K.sb(ph, "selt", [128, PS // PCH], F32, dma=True)
                      K.dma("sp", selt[:], psel, selt_k, writes=[selt_k])
                      accr_ = K.ring(ph, "xacc", 2, [128, D], F32, dma=True)
                      ldr_ = K.ring(ph, "xld", 4, [128, D], F32, dma=True)
                      for j in range(PSEG // 128):
                          ac, ac_k = accr_.next()
                          K.op("dve", lambda e: e.memset(ac[:], 0.0), writes=[ac_k])
                          for r_ in range(PS // PCH):
                              row = r_ * PCH - HALO + j * 128
                              if row < 0 or row + 128 > PS:
                                  continue
                              ld, ld_k = ldr_.next()
                              K.dma("sp", ld[:], x1full[row:row + 128, :], ld_k, writes=[ld_k], dram_reads=[x1full_k])
                              K.op("dve", lambda e, r_=r_: e.scalar_tensor_tensor(out=ac[:], in0=ld[:], scalar=selt[:, r_:r_ + 1], in1=ac[:],
                                                                                  op0=ALU.mult, op1=ALU.add),
                                   reads=[ld_k, selt_k, ac_k], writes=[ac_k])
                          K.dma("sp", x1seg[j * 128:(j + 1) * 128, :], ac[:], ac_k, reads=[ac_k], acc=[x1seg_k])
                      K.end_phase()
                      chk(l)
              with ExitStack() as ph:
                  PSR_HI[0] = 8
                  nseg = len(segs)
                  cT, cT_k = K.sb(ph, "cT", [128, nseg * KT], F32, dma=True)
                  crep, crep_k = K.sb(ph, "crep", [128, nseg * KT, 128], BF16)
                  bb, bb_k = K.sb(ph, "bb", [1, 6 * D], BF16, dma=True)
                  n1b, n1b_k = K.sb(ph, "n1b", [128, D], F32, dma=True)
                  n2b, n2b_k = K.sb(ph, "n2b", [128, D], F32, dma=True)
                  wr = K.ring(ph, "wada", 2, [128, KT, 512], BF16, dma=True)
                  modt = [K.sb(ph, "modt%d" % s, [128, 6 * D], F32, dma=True) for s in range(nseg)]
                  K.dma("sp", cT[:], cvT, cT_k, writes=[cT_k])
                  K.op("act", lambda e: e.activation(out=cT[:], in_=cT[:], func=AF.Silu), reads=[cT_k], writes=[cT_k])
                  K.op("dve", lambda e: e.tensor_copy(out=crep[:], in_=cT[:].unsqueeze(2).to_broadcast([128, nseg * KT, 128])),
                       reads=[cT_k], writes=[crep_k])
                  K.dma("pool", bb[:], Wl["b_ada"], bb_k, writes=[bb_k])
                  K.dma("sp", n1b[:], Wl["norm1"][0, :].partition_broadcast(128), n1b_k, writes=[n1b_k])
                  K.dma("sp", n2b[:], Wl["norm2"][0, :].partition_broadcast(128), n2b_k, writes=[n2b_k])
                  wav = wview(Wl["w_ada"])
                  for c in range(12):
                      wt, wt_k = wr.next()
                      K.dma("pool", wt[:], wav[:, :, c * 512:(c + 1) * 512], wt_k, writes=[wt_k])
                      for s in range(nseg):
                          ps, ps_k = psr()
                          for kt in range(KT):
                              K.op("pe", lambda e, kt=kt: e.matmul(ps[:], lhsT=crep[:, s * KT + kt, :], rhs=wt[:, kt, :],
                                                                   start=(kt == 0), stop=False),
                                   reads=[crep_k, wt_k], writes=[ps_k], inc=False)
                          K.op("pe", lambda e: e.matmul(ps[:], lhsT=ones_bf[0:1, :], rhs=bb[0:1, c * 512:(c + 1) * 512],
                                                        start=False, stop=True),
                               reads=[ones_k, bb_k], writes=[ps_k])
                          mt, mt_k = modt[s]
                          K.op("act", lambda e: e.activation(out=mt[:, c * 512:(c + 1) * 512], in_=ps[:], func=AF.Copy),
                               reads=[ps_k], writes=[mt_k])
                  for s in range(nseg):
                      mt, mt_k = modt[s]
                      K.op("dve", lambda e: e.scalar_tensor_tensor(out=mt[:, D:2 * D], in0=mt[:, D:2 * D], scalar=1.0, in1=n1b[:],
                                                                   op0=ALU.add, op1=ALU.mult),
                           reads=[mt_k, n1b_k], writes=[mt_k])
                      K.op("dve", lambda e: e.scalar_tensor_tensor(out=mt[:, 4 * D:5 * D], in0=mt[:, 4 * D:5 * D], scalar=1.0, in1=n2b[:],
                                                                   op0=ALU.add, op1=ALU.mult),
                           reads=[mt_k, n2b_k], writes=[mt_k])
                      K.dma("sp", segs[s]["mod"], mt[:], mt_k, reads=[mt_k], acc=[segs[s]["mod_k"]])
                  K.end_phase()
                  chk(l)

              with ExitStack() as ph:
                  wlat, wlat_k = K.sb(ph, "wlat", [128, KT, 416], BF16, dma=True)
                  wuq, wuq_k = K.sb(ph, "wuq", [128, 2, 768], BF16, dma=True)
                  wukv, wukv_k = K.sb(ph, "wukv", [128, 1024], BF16, dma=True)
                  qan, qan_k = K.sb(ph, "qan", [128, 2], F32, dma=True)
                  kvan, kvan_k = K.sb(ph, "kvan", [128, 1], F32, dma=True)
                  gq, gq_k = K.sb(ph, "gq", [128, DQK], F32, dma=True)
                  gk, gk_k = K.sb(ph, "gk", [128, DQK], F32, dma=True)
                  K.dma("pool", wlat[:], wview(Wl["w_in"])[:, :, 0:416], wlat_k, writes=[wlat_k])
                  K.dma("pool", wuq[:], wview(Wl["w_uq"]), wuq_k, writes=[wuq_k])
                  K.dma("pool", wukv[:], Wl["w_ukv"], wukv_k, writes=[wukv_k])
                  K.dma("sp", qan[:], Wl["qan"], qan_k, writes=[qan_k])
                  K.dma("sp", kvan[:], Wl["kvan"], kvan_k, writes=[kvan_k])
                  K.dma("sp", gq[:], Wl["qhn"][0, :].partition_broadcast(128), gq_k, writes=[gq_k])
                  K.dma("sp", gk[:], Wl["khn"][0, :].partition_broadcast(128), gk_k, writes=[gk_k])
                  P = dict(junkb=K.ring(ph, "junkb", 1, [128, 4, D], F32), ssb=K.ring(ph, "ssb", 3, [128, 16], F32),
                           hbb=K.ring(ph, "hbb", 2, [128, 4, D], BF16), stb=K.ring(ph, "stb", 2, [128, 96], F32),
                           ttb=K.ring(ph, "ttb", 1, [128, 4 * 4 * NH * 16], F32))
                  xbr = K.ring(ph, "xb", 2, [128, 4, D], F32, dma=True)
                  rpbr = K.ring(ph, "rpb", 2, [128, 4, 32], F32, dma=True)
                  hst = K.ring(ph, "hst", 2, [128, KT, 512], BF16, dma=True)
                  qst = K.ring(ph, "qst", 1, [128, NH, 512], BF16, dma=True)
                  kst = K.ring(ph, "kst", 1, [128, NH, 512], BF16, dma=True)
                  vst = K.ring(ph, "vst", 2, [128, NH, 4, 65], BF16, dma=True)
                  for vt, vk in vst.items:
                      K.op("dve", lambda e, vt=vt: e.memset(vt[:], 1.0), writes=[vk])
                  gamt, gam_k = K.sb(ph, "gam1", [128, D], F32, dma=True)
                  sht, sh_k = K.sb(ph, "sh1", [128, D], F32, dma=True)
                  lat_r = K.ring(ph, "latb", 2, [128, 4, 416], F32)
                  cqn_r = K.ring(ph, "cqnb", 2, [128, 4, 256], BF16)
                  cqT_r = K.ring(ph, "cqTb", 2, [128, 2, 512], BF16)
                  ckn_r = K.ring(ph, "cknb", 2, [128, 4, 128], BF16)
                  ckT_r = K.ring(ph, "ckTb", 2, [128, 512], BF16)
                  qf_r = K.ring(ph, "qfb", 2, [128, 4, 768], F32)
                  qb_r = K.ring(ph, "qbb", 2, [128, 4, 768], BF16)

                  def passes_of(sg):
                      if sg["prompt"]:
                          return [(sg["ctx"], sg["s"], False, True, sg["rk"], sg["ctx_k"]),
                                  (sg["x"], sg["n"], True, False, sg["rq"], sg["x_k"])]
                      return [(sg["x"], sg["n"], True, True, sg["rq"], sg["x_k"])]

                  items = [(p_[0], p_[5], p_[4], b0, bw) for sg in segs for p_ in passes_of(sg) for (b0, bw) in blocks_of(p_[1])]
                  loaded = {}
                  ctr = [0]

                  def issue(i):
                      if i >= len(items) or i in loaded:
                          return
                      xsrc_, xsrc_k_, rtab_, b0_, bw_ = items[i]
                      G_ = bw_ // 128
                      xb_, xb_k_ = xbr.next()
                      K.dma("sp", xb_[:, 0:G_, :], xsrc_[b0_:b0_ + bw_, :].rearrange("(j p) d -> p j d", p=128), xb_k_, writes=[xb_k_], dram_reads=[xsrc_k_])
                      rp_, rp_k_ = rpbr.next()
                      K.dma("sp", rp_[:, 0:G_, :], rtab_[b0_:b0_ + bw_, :].rearrange("(j p) d -> p j d", p=128), rp_k_, writes=[rp_k_])
                      loaded[i] = (xb_, xb_k_, rp_, rp_k_)

                  for sg in segs:
                      K.dma("sp", sht[:], sg["mod"][:, 0:D], sh_k, writes=[sh_k], dram_reads=[sg["mod_k"]])
                      K.dma("sp", gamt[:], sg["mod"][:, D:2 * D], gam_k, writes=[gam_k], dram_reads=[sg["mod_k"]])
                      for (xsrc, ntok, want_q, want_k, rtab, xsrc_k) in passes_of(sg):
                          for (b0, bw) in blocks_of(ntok):
                              G = bw // 128
                              hs, hs_k = hst.next()
                              i_ = ctr[0]
                              ctr[0] += 1
                              issue(i_)
                              issue(i_ + 1)
                              xb, xb_k, rp, rp_k = loaded.pop(i_)
                              hbb, hbb_k = norm_block(P, xb, xb_k, G, (gamt, gam_k), (sht, sh_k))
                              lat, lat_k = lat_r.next()
                              for j in range(G):
                                  transpose_to(P, hbb[:, j, :], hbb_k, hs[:, :, j * 128:(j + 1) * 128], hs_k)
                              for j in range(G):
                                  pl, pl_k = psr()
                                  for kt in range(KT):
                                      K.op("pe", lambda e, kt=kt, j=j: e.matmul(pl[:, 0:416], lhsT=hs[:, kt, j * 128:(j + 1) * 128], rhs=wlat[:, kt, :],
                                                                               start=(kt == 0), stop=(kt == KT - 1)),
                                           reads=[hs_k, wlat_k], writes=[pl_k], inc=(kt == KT - 1))
                                  K.op("act", lambda e, j=j: e.activation(out=lat[:, j, :], in_=pl[:, 0:416], func=AF.Copy), reads=[pl_k], writes=[lat_k])
                              if want_q:
                                  qs, qs_k = qst.next()
                                  cqn, cqn_k = cqn_r.next()
                                  rms_block(P, lat[:, 0:G, 0:256], lat_k, G, 256, cqn[:, 0:G, :], cqn_k)
                                  cqT, cqT_k = cqT_r.next()
                                  p2, p2_k = psr()
                                  p2b = p2[:].bitcast(BF16)
                                  for j in range(G):
                                      for i in range(2):
                                          K.op("pe", lambda e, i=i, j=j: e.transpose(p2b[:, (j * 2 + i) * 128:(j * 2 + i + 1) * 128], cqn[:, j, i * 128:(i + 1) * 128], ident[:]),
                                               reads=[cqn_k, ident_k], writes=[p2_k], inc=(j == G - 1 and i == 1))
                                  for i in range(2):
                                      K.op("act", lambda e, i=i: e.activation(out=cqT[:, i, 0:G * 128].rearrange("p (j c) -> p j c", j=G),
                                                                              in_=p2b[:, 0:G * 256].rearrange("p (j i c) -> p j i c", j=G, i=2)[:, :, i, :],
                                                                              func=AF.Copy, scale=qan[:, i:i + 1]),
                                           reads=[p2_k, qan_k], writes=[cqT_k])
                                  qf, qf_k = qf_r.next()
                                  for j in range(G):
                                      pq0, pq0_k = psr()
                                      pq1, pq1_k = psr()
                                      for i in range(2):
                                          K.op("pe", lambda e, i=i, j=j: e.matmul(pq0[:, 0:480], lhsT=cqT[:, i, j * 128:(j + 1) * 128], rhs=wuq[:, i, 0:480], start=(i == 0), stop=(i == 1)),
                                               reads=[cqT_k, wuq_k], writes=[pq0_k], inc=(i == 1))
                                      for i in range(2):
                                          K.op("pe", lambda e, i=i, j=j: e.matmul(pq1[:, 0:288], lhsT=cqT[:, i, j * 128:(j + 1) * 128], rhs=wuq[:, i, 480:768], start=(i == 0), stop=(i == 1)),
                                               reads=[cqT_k, wuq_k], writes=[pq1_k], inc=(i == 1))
                                      K.op("act", lambda e, j=j: e.activation(out=qf[:, j, 0:480], in_=pq0[:, 0:480], func=AF.Copy), reads=[pq0_k], writes=[qf_k])
                                      K.op("dve", lambda e, j=j: e.tensor_copy(out=qf[:, j, 480:768], in_=pq1[:, 0:288]), reads=[pq1_k], writes=[qf_k])
                                  qb, qb_k = qb_r.next()
                                  headnorm_rope_block(P, qf, qf_k, G, (gq, gq_k), rp, rp_k, qb, qb_k)
                                  for j in range(G):
                                      transpose_to(P, qb[:, j, :], qb_k, qs[0:DQK, :, j * 128:(j + 1) * 128], qs_k, ncols_src=768, chunk=DQK)
                              if want_k:
                                  ks, ks_k = kst.next()
                                  vs, vs_k = vst.next()
                                  ckn, ckn_k = ckn_r.next()
                                  rms_block(P, lat[:, 0:G, 256:384], lat_k, G, 128, ckn[:, 0:G, :], ckn_k)
                                  ckT, ckT_k = ckT_r.next()
                                  p3, p3_k = psr()
                                  p3b = p3[:].bitcast(BF16)
                                  for j in range(G):
                                      K.op("pe", lambda e, j=j: e.transpose(p3b[:, j * 128:(j + 1) * 128], ckn[:, j, :], ident[:]),
                                           reads=[ckn_k, ident_k], writes=[p3_k], inc=(j == G - 1))
                                  K.op("act", lambda e: e.activation(out=ckT[:, 0:G * 128], in_=p3b[:, 0:G * 128], func=AF.Copy, scale=kvan[:, 0:1]),
                                       reads=[p3_k, kvan_k], writes=[ckT_k])
                                  kf, kf_k = qf_r.next()
                                  kf4 = kf[:].rearrange("p j (h d) -> p j h d", h=NH)
                                  for j in range(G):
                                      pk, pk_k = psr()
                                      pv, pv_k = psr()
                                      K.op("pe", lambda e, j=j: e.matmul(pk[:], lhsT=ckT[:, j * 128:(j + 1) * 128], rhs=wukv[:, 0:512], start=True, stop=True),
                                           reads=[ckT_k, wukv_k], writes=[pk_k])
                                      K.op("pe", lambda e, j=j: e.matmul(pv[:], lhsT=ckT[:, j * 128:(j + 1) * 128], rhs=wukv[:, 512:1024], start=True, stop=True),
                                           reads=[ckT_k, wukv_k], writes=[pv_k])
                                      K.op("act", lambda e, j=j: e.activation(out=kf4[:, j, :, 0:64], in_=pk[:].rearrange("p (h d) -> p h d", h=NH), func=AF.Copy),
                                           reads=[pk_k], writes=[kf_k])
                                      K.op("act", lambda e, j=j: e.activation(out=vs[:, :, j, 0:64], in_=pv[:].rearrange("p (h d) -> p h d", h=NH), func=AF.Copy),
                                           reads=[pv_k], writes=[vs_k])
                                  K.op("dve", lambda e: e.tensor_copy(out=kf4[:, 0:G, :, 64:96], in_=lat[:, 0:G, 384:416].unsqueeze(2).to_broadcast([128, G, NH, 32])),
                                       reads=[lat_k], writes=[kf_k])
                                  kb, kb_k = qb_r.next()
                                  headnorm_rope_block(P, kf, kf_k, G, (gk, gk_k), rp, rp_k, kb, kb_k)
                                  for j in range(G):
                                      transpose_to(P, kb[:, j, :], kb_k, ks[0:DQK, :, j * 128:(j + 1) * 128], ks_k, ncols_src=768, chunk=DQK)
                              nj = G
                              if want_q:
                                  K.dma("sp", sg["hT"][:, :, b0:b0 + bw], hs[:, :, 0:bw], hs_k, reads=[hs_k], acc=[sg["hT_k"]])
                                  K.dma("sp", sg["qT"][:, :, b0:b0 + bw], qs[0:DQK, :, 0:bw], qs_k, reads=[qs_k], acc=[sg["qT_k"]])
                              if want_k:
                                  K.dma("sp", sg["kT"].rearrange("h d s -> d h s")[:, :, b0:b0 + bw], ks[0:DQK, :, 0:bw], ks_k, reads=[ks_k], acc=[sg["kT_k"]])
                                  K.dma("sp", sg["v"].rearrange("h p k c -> p h k c")[:, :, b0 // 128:b0 // 128 + nj, :], vs[:, :, 0:nj, :], vs_k,
                                        reads=[vs_k], acc=[sg["v_k"]])
                  K.end_phase()
                  chk(l)

              with ExitStack() as ph:
                  PSR_HI[0] = 4
                  NW = D_IN - 416
                  wbig, wbig_k = K.sb(ph, "wbig", [128, KT, NW], BF16, dma=True)
                  wv = wview(Wl["w_in"])
                  for c in range(0, NW, 640):
                      K.dma("pool", wbig[:, :, c:c + 640], wv[:, :, 416 + c:416 + c + 640], wbig_k, acc=[wbig_k])
                  wsT, wsT_k = K.sb(ph, "wsT", [128, 4, 128], BF16, dma=True)
                  K.dma("pool", wsT[:], Wl["sgwT"].rearrange("p (g q) -> p g q", g=4), wsT_k, writes=[wsT_k])
                  lng, lng_k = K.sb(ph, "lng", [128, 512], F32, dma=True)
                  lnb, lnb_k = K.sb(ph, "lnb", [128, 512], F32, dma=True)
                  bsb, bsb_k = K.sb(ph, "bsb", [128, 4, 4, 128], F32, dma=True)
                  K.dma("sp", lng[:], Wl["sglng"][0, :].partition_broadcast(128), lng_k, writes=[lng_k])
                  K.dma("sp", lnb[:], Wl["sglnb"][0, :].partition_broadcast(128), lnb_k, writes=[lnb_k])
                  for j in range(4):
                      K.dma("sp", bsb[:, :, j, :], Wl["sgb"][0, :].partition_broadcast(128).rearrange("p (g q) -> p g q", g=4), bsb_k, acc=[bsb_k])
                  hbr = K.ring(ph, "hTb", 2, [128, KT, 512], BF16, dma=True)
                  sgt_r = K.ring(ph, "sgt", 2, [128, 512], F32)
                  glub_r = K.ring(ph, "glub", 2, [128, 4, 512], BF16, dma=True)
                  ug_r = K.ring(ph, "ug", 2, [128, 4, 512], BF16)
                  vg_r = K.ring(ph, "vg", 2, [128, 512], F32)
                  jk_r = K.ring(ph, "jk2", 1, [128, 512], F32)
                  s8_r = K.ring(ph, "s8", 3, [128, 8], F32)
                  vnb_r = K.ring(ph, "vnb", 4, [128, 512], BF16)
                  tq_r = K.ring(ph, "tq", 2, [128, 512], F32)
                  usb_r = K.ring(ph, "usb", 2, [128, 4, 512], BF16, dma=True)
                  gst_r = K.ring(ph, "gst", 2, [128, 24, 512], BF16, dma=True)
                  items = [(sg, b0, bw) for sg in segs for (b0, bw) in blocks_of(sg["n"])]
                  loaded = {}
                  ctr = [0]

                  def issue(i):
                      if i >= len(items) or i in loaded:
                          return
                      sg_, b0_, bw_ = items[i]
                      hT_, hT_k_ = hbr.next()
                      K.dma("sp", hT_[:, :, 0:bw_], sg_["hT"][:, :, b0_:b0_ + bw_], hT_k_, writes=[hT_k_], dram_reads=[sg_["hT_k"]])
                      loaded[i] = (hT_, hT_k_)

                  for sg in segs:
                      n = sg["n"]
                      blks = blocks_of(n)
                      for bi, (b0, bw) in enumerate(blks):
                          nj = bw // 128
                          i_ = ctr[0]
                          ctr[0] += 1
                          issue(i_)
                          issue(i_ + 1)
                          hT, hT_k = loaded.pop(i_)

                          def fm(col0, ps, ps_k):
                              for kt in range(KT):
                                  K.op("pe", lambda e, kt=kt: e.matmul(ps[:, 0:bw], lhsT=wbig[:, kt, col0:col0 + 128], rhs=hT[:, kt, 0:bw],
                                                                       start=(kt == 0), stop=(kt == KT - 1)),
                                       reads=[wbig_k, hT_k], writes=[ps_k], inc=(kt == KT - 1))
                          glub, glub_k = glub_r.next()
                          for c in range(4):
                              pa, pa_k = psr()
                              pg, pg_k = psr()
                              fm(c * 128, pa, pa_k)
                              fm(512 + c * 128, pg, pg_k)
                              sgt, sgt_k = sgt_r.next()
                              K.op("act", lambda e: e.activation(out=sgt[:, 0:bw], in_=pg[:, 0:bw], func=AF.Sigmoid), reads=[pg_k], writes=[sgt_k])
                              K.op("dve", lambda e, c=c: e.tensor_tensor(out=glub[:, c, 0:bw], in0=pa[:, 0:bw], in1=sgt[:, 0:bw], op=ALU.mult),
                                   reads=[pa_k, sgt_k], writes=[glub_k])
                          if sg["prompt"]:
                              if bi == 0:
                                  K.op("dve", lambda e: e.tensor_scalar(out=glub[:, :, 0:128], in0=glub[:, :, 0:128], scalar1=mk[:, 0:1], scalar2=None, op0=ALU.mult),
                                       reads=[glub_k, mk_k], writes=[glub_k])
                              if bi == len(blks) - 1:
                                  K.op("dve", lambda e: e.tensor_scalar(out=glub[:, :, bw - 128:bw], in0=glub[:, :, bw - 128:bw], scalar1=mk[:, 1:2], scalar2=None, op0=ALU.mult),
                                       reads=[glub_k, mk_k], writes=[glub_k])
                          K.dma("sp", sg["glu"][:, :, 15 + b0:15 + b0 + bw], glub[:, :, 0:bw], glub_k, reads=[glub_k], acc=[sg["glu_k"]])
                          ug, ug_k = ug_r.next()
                          for g in range(4):
                              pu, pu_k = psr()
                              fm(1024 + g * 128, pu, pu_k)
                              K.op("act", lambda e, g=g: e.activation(out=ug[:, g, 0:bw], in_=pu[:, 0:bw], func=AF.Gelu_apprx_tanh), reads=[pu_k], writes=[ug_k])
                          pss = [PSB[4 + g] for g in range(4)]
                          vnbs = []
                          for j in range(nj):
                              pv, pv_k = psr()
                              for kt in range(KT):
                                  K.op("pe", lambda e, kt=kt: e.matmul(pv[:], lhsT=hT[:, kt, j * 128:(j + 1) * 128], rhs=wbig[:, kt, 1536:2048],
                                                                       start=(kt == 0), stop=(kt == KT - 1)),
                                       reads=[hT_k, wbig_k], writes=[pv_k], inc=(kt == KT - 1))
                              vg, vg_k = vg_r.next()
                              K.op("act", lambda e: e.activation(out=vg[:], in_=pv[:], func=AF.Gelu_apprx_tanh), reads=[pv_k], writes=[vg_k])
                              s8, s8_k = s8_r.next()
                              jk, jk_k = jk_r.next()
                              K.op("dve", lambda e: e.memset(s8[:], 0.0), writes=[s8_k])
                              K.op("act", lambda e: e.activation(out=jk[:], in_=vg[:], func=AF.Square, accum_out=s8[:, 1:2]), reads=[vg_k, s8_k], writes=[jk_k, s8_k])
                              K.op("dve", lambda e: e.reduce_sum(out=s8[:, 0:1], in_=vg[:], axis=AX.X), reads=[vg_k, s8_k], writes=[s8_k])
                              K.op("dve", lambda e: e.tensor_scalar(out=s8[:, 2:3], in0=s8[:, 0:1], scalar1=1.0 / 512, scalar2=None, op0=ALU.mult), reads=[s8_k], writes=[s8_k])
                              K.op("dve", lambda e: e.tensor_tensor(out=s8[:, 3:4], in0=s8[:, 2:3], in1=s8[:, 2:3], op=ALU.mult), reads=[s8_k], writes=[s8_k])
                              K.op("dve", lambda e: e.scalar_tensor_tensor(out=s8[:, 4:5], in0=s8[:, 1:2], scalar=1.0 / 512, in1=s8[:, 3:4], op0=ALU.mult, op1=ALU.subtract),
                                   reads=[s8_k], writes=[s8_k])
                              K.op("act", lambda e: e.activation(out=s8[:, 5:6], in_=s8[:, 4:5], func=AF.Sqrt, bias=EPS, scale=1.0), reads=[s8_k], writes=[s8_k])
                              K.op("dve", lambda e: e.reciprocal(out=s8[:, 6:7], in_=s8[:, 5:6]), reads=[s8_k], writes=[s8_k])
                              K.op("dve", lambda e: e.tensor_scalar(out=vg[:], in0=vg[:], scalar1=s8[:, 2:3], scalar2=s8[:, 6:7], op0=ALU.subtract, op1=ALU.mult),
                                   reads=[vg_k, s8_k], writes=[vg_k])
                              K.op("dve", lambda e: e.tensor_tensor(out=vg[:], in0=vg[:], in1=lng[:], op=ALU.mult), reads=[vg_k, lng_k], writes=[vg_k])
                              vnb, vnb_k = vnb_r.next()
                              K.op("pool", lambda e: e.tensor_tensor(out=vnb[:], in0=vg[:], in1=lnb[:], op=ALU.add), reads=[vg_k, lnb_k], writes=[vnb_k])
                              vnbs.append((vnb, vnb_k))
                          gst, gst_k = gst_r.next()
                          for m in range(24):
                              pg, pg_k = psr()
                              fm(2048 + m * 128, pg, pg_k)
                              K.op("act", lambda e, m=m: e.activation(out=gst[:, m, 0:bw], in_=pg[:, 0:bw], func=AF.Sigmoid), reads=[pg_k], writes=[gst_k])
                          K.dma("sp", sg["gat"][:, :, b0:b0 + bw], gst[:, :, 0:bw], gst_k, reads=[gst_k], acc=[sg["gat_k"]])
                          for j in range(nj):
                              vnb, vnb_k = vnbs[j]
                              for g in range(4):
                                  K.op("pe", lambda e, g=g: e.matmul(pss[g][0][:, j * 128:(j + 1) * 128], lhsT=vnb[:, g * 128:(g + 1) * 128], rhs=wsT[:, g, :],
                                                                     start=True, stop=True),
                                       reads=[vnb_k, wsT_k], writes=[pss[g][1]], inc=(g == 3))
                          usb, usb_k = usb_r.next()
                          for g in range(4):
                              tq, tq_k = tq_r.next()
                              K.op("dve", lambda e, g=g: e.tensor_tensor(out=tq[:, 0:bw], in0=pss[g][0][:, 0:bw],
                                                                         in1=bsb[:, g, :, :].rearrange("p j q -> p (j q)")[:, 0:bw], op=ALU.add),
                                   reads=[pss[g][1], bsb_k], writes=[tq_k])
                              K.op("dve", lambda e, g=g: e.tensor_tensor(out=usb[:, g, 0:bw], in0=tq[:, 0:bw], in1=ug[:, g, 0:bw], op=ALU.mult),
                                   reads=[tq_k, ug_k], writes=[usb_k])
                          K.dma("sp", sg["us"][:, :, b0:b0 + bw], usb[:, :, 0:bw], usb_k, reads=[usb_k], acc=[sg["us_k"]])
                  K.end_phase()
                  chk(l)

              with ExitStack() as ph:
                  PSR_HI[0] = 4
                  KSB = 4096
                  qtr = K.ring(ph, "qtb", 2, [128, NH, 512], BF16, dma=True)
                  ktr = K.ring(ph, "ktb", 2, [128, KSB], BF16, dma=True)
                  vtr = K.ring(ph, "vtb", 2, [128, KSB // 128, 65], BF16, dma=True)
                  ptr_ = K.ring(ph, "ptb", 4, [128, 512], BF16)
                  aor = K.ring(ph, "aob", 2, [64, NH, 512], BF16, dma=True)
                  rsr = K.ring(ph, "rsb", 2, [128, 512], F32)
                  rir = K.ring(ph, "rib", 2, [64, 512], F32)
                  scale = float(DQK) ** -0.5
                  items = [(sg, b0, bw) for sg in segs for (b0, bw) in blocks_of(sg["n"])]
                  loaded = {}
                  ctr = [0]

                  def issue(i):
                      if i >= len(items) or i in loaded:
                          return
                      sg_, b0_, bw_ = items[i]
                      qt_, qt_k_ = qtr.next()
                      K.dma("sp", qt_[0:DQK, :, 0:bw_], sg_["qT"][:, :, b0_:b0_ + bw_], qt_k_, writes=[qt_k_], dram_reads=[sg_["qT_k"]])
                      loaded[i] = (qt_, qt_k_)

                  for sg in segs:
                      n, S = sg["n"], sg["s"]
                      for (b0, bw) in blocks_of(n):
                          i_ = ctr[0]
                          ctr[0] += 1
                          issue(i_)
                          issue(i_ + 1)
                          qt, qt_k = loaded.pop(i_)
                          ao, ao_k = aor.next()
                          for h in range(NH):
                              po, po_k = PSB[4 + (h % 2)]
                              first = True
                              for (s0, sw) in blocks_of(S, KSB):
                                  kt_, kt_k = ktr.next()
                                  vt_, vt_k = vtr.next()
                                  K.dma("sp", kt_[0:DQK, 0:sw], sg["kT"][h, :, s0:s0 + sw], kt_k, writes=[kt_k], dram_reads=[sg["kT_k"]])
                                  K.dma("sp", vt_[:, 0:sw // 128, :], sg["v"][h, :, s0 // 128:(s0 + sw) // 128, :], vt_k, writes=[vt_k], dram_reads=[sg["v_k"]])
                                  nk = sw // 128
                                  pend = None
                                  for ki in range(nk + 1):
                                      if ki < nk:
                                          ps, ps_k = psr(0, 4)
                                          K.op("pe", lambda e, ki=ki: e.matmul(ps[:, 0:bw], lhsT=kt_[0:DQK, ki * 128:(ki + 1) * 128], rhs=qt[0:DQK, h, 0:bw],
                                                                               start=True, stop=True),
                                               reads=[kt_k, qt_k], writes=[ps_k])
                                          pt, pt_k = ptr_.next()
                                          K.op("act", lambda e: e.activation(out=pt[:, 0:bw], in_=ps[:, 0:bw], func=AF.Exp, scale=scale), reads=[ps_k], writes=[pt_k])
                                          cur = (pt, pt_k, ki)
                                      else:
                                          cur = None
                                      if pend is not None:
                                          ppt, ppt_k, pki = pend
                                          last = (s0 + sw >= S) and (pki == nk - 1)
                                          K.op("pe", lambda e, pki=pki, ppt=ppt, f=first, last=last: e.matmul(po[0:65, 0:bw], lhsT=vt_[:, pki, 0:65], rhs=ppt[:, 0:bw],
                                                                                                            start=f, stop=last),
                                               reads=[vt_k, ppt_k], writes=[po_k])
                                          first = False
                                      pend = cur
                              rs, rs_k = rsr.next()
                              K.op("dve", lambda e: e.tensor_copy(out=rs[64:65, 0:bw], in_=po[64:65, 0:bw]), reads=[po_k], writes=[rs_k])
                              pb, pb_k = PSB[6 + (h % 2)]
                              K.op("pe", lambda e: e.matmul(pb[0:64, 0:bw], lhsT=ones_f[64:65, 0:64], rhs=rs[64:65, 0:bw], start=True, stop=True),
                                   reads=[onesf_k, rs_k], writes=[pb_k])
                              ri, ri_k = rir.next()
                              K.op("dve", lambda e: e.reciprocal(out=ri[:, 0:bw], in_=pb[0:64, 0:bw]), reads=[pb_k], writes=[ri_k])
                              K.op("dve", lambda e, h=h: e.tensor_tensor(out=ao[:, h, 0:bw], in0=po[0:64, 0:bw], in1=ri[:, 0:bw], op=ALU.mult),
                                   reads=[po_k, ri_k], writes=[ao_k])
                          K.dma("pool", sg["ao"][:, :, b0:b0 + bw], ao[:, :, 0:bw], ao_k, reads=[ao_k], acc=[sg["ao_k"]])
                  K.end_phase()
                  chk(l)

              with ExitStack() as ph:
                  PSR_HI[0] = 8
                  wao, wao_k = K.sb(ph, "wao", [64, NH, D], BF16, dma=True)
                  wco, wco_k = K.sb(ph, "wco", [128, 4, D], BF16, dma=True)
                  wso, wso_k = K.sb(ph, "wso", [128, 4, D], BF16, dma=True)
                  wout, wout_k = K.sb(ph, "wout", [128, KT, D], BF16, dma=True)
                  K.dma("pool", wao[:], Wl["w_ao"].rearrange("p (h n) -> p h n", h=NH), wao_k, writes=[wao_k])
                  K.dma("pool", wco[:], wview(Wl["w_co"]), wco_k, writes=[wco_k])
                  K.dma("pool", wso[:], wview(Wl["w_so"]), wso_k, writes=[wso_k])
                  K.dma("pool", wout[:], wview(Wl["w_out"]), wout_k, writes=[wout_k])
                  cdw, cdw_k = K.sb(ph, "cdw", [128, 4, 31], F32, dma=True)
                  cdwb, cdwb_k = K.sb(ph, "cdwb", [128, 4], F32, dma=True)
                  clng, clng_k = K.sb(ph, "clng", [128, 4], F32, dma=True)
                  clnb, clnb_k = K.sb(ph, "clnb", [128, 4], F32, dma=True)
                  K.dma("sp", cdw[:], Wl["cdw"].rearrange("p (c k) -> p c k", c=4), cdw_k, writes=[cdw_k])
                  K.dma("sp", cdwb[:], Wl["cdwb"], cdwb_k, writes=[cdwb_k])
                  K.dma("sp", clng[:], Wl["clng"], clng_k, writes=[clng_k])
                  K.dma("sp", clnb[:], Wl["clnb"], clnb_k, writes=[clnb_k])
                  dgt, dgt_k = K.sb(ph, "dgt", [128, 4, 31, 128], BF16)
                  for c in range(4):
                      for k in range(31):
                          K.op("dve", lambda e, c=c, k=k: e.tensor_scalar(out=dgt[:, c, k, :], in0=ident[:], scalar1=cdw[:, c, k:k + 1], scalar2=None, op0=ALU.mult),
                               reads=[ident_k, cdw_k], writes=[dgt_k])
                  g1t, g1_k = K.sb(ph, "g1t", [128, D], F32, dma=True)
                  glr = K.ring(ph, "glt", 2, [128, 4, 512 + 30], BF16, dma=True)
                  usr = K.ring(ph, "ust", 2, [128, 4, 512], BF16, dma=True)
                  gtr = K.ring(ph, "gtt", 1, [128, 24, 512], BF16, dma=True)
                  aor = K.ring(ph, "aot", 2, [64, NH, 512], BF16, dma=True)
                  hc, hc_k = K.sb(ph, "hc", [128, 4, 512], F32)
                  hcb, hcb_k = K.sb(ph, "hcb", [128, 4, 512], BF16)
                  sqb, sqb_k = K.sb(ph, "sqb", [128, 4, 512], BF16)
                  mean, mean_k = K.sb(ph, "mean", [128, 512], F32)
                  rstd, rstd_k = K.sb(ph, "rstd", [128, 512], F32)
                  cvn, cvn_k = K.sb(ph, "cvn", [128, 4, 512], BF16)
                  mrg, mrg_k = K.sb(ph, "mrg", [128, KT, 512], BF16)
                  tmr = K.ring(ph, "tm", 4, [128, 512], F32)
                  xr = K.ring(ph, "xr2", 2, [128, D], F32, dma=True)
                  items = [(sg, b0, bw) for sg in segs for (b0, bw) in blocks_of(sg["n"])]
                  loaded = {}
                  ctr = [0]

                  def issue(i):
                      if i >= len(items) or i in loaded:
                          return
                      sg_, b0_, bw_ = items[i]
                      glt_, glt_k_ = glr.next()
                      K.dma("sp", glt_[:, :, 0:bw_ + 30], sg_["glu"][:, :, b0_:b0_ + bw_ + 30], glt_k_, writes=[glt_k_], dram_reads=[sg_["glu_k"]])
                      ust_, ust_k_ = usr.next()
                      K.dma("sp", ust_[:, :, 0:bw_], sg_["us"][:, :, b0_:b0_ + bw_], ust_k_, writes=[ust_k_], dram_reads=[sg_["us_k"]])
                      aot_, aot_k_ = aor.next()
                      K.dma("sp", aot_[:, :, 0:bw_], sg_["ao"][:, :, b0_:b0_ + bw_], aot_k_, writes=[aot_k_], dram_reads=[sg_["ao_k"]])
                      loaded[i] = (glt_, glt_k_, ust_, ust_k_, aot_, aot_k_)

                  for sg in segs:
                      n = sg["n"]
                      K.dma("sp", g1t[:], sg["mod"][:, 2 * D:3 * D], g1_k, writes=[g1_k], dram_reads=[sg["mod_k"]])
                      for (b0, bw) in blocks_of(n):
                          nj = bw // 128
                          i_ = ctr[0]
                          ctr[0] += 1
                          issue(i_)
                          issue(i_ + 1)
                          glt, glt_k, ust, ust_k, aot, aot_k = loaded.pop(i_)
                          gtt, gtt_k = gtr.next()
                          K.dma("sp", gtt[:, :, 0:bw], sg["gat"][:, :, b0:b0 + bw], gtt_k, writes=[gtt_k], dram_reads=[sg["gat_k"]])
                          for c in range(4):
                              ps, ps_k = psr()
                              for k in range(31):
                                  K.op("pe", lambda e, c=c, k=k: e.matmul(ps[:, 0:bw], lhsT=dgt[:, c, k, :], rhs=glt[:, c, k:k + bw], start=(k == 0), stop=(k == 30)),
                                       reads=[dgt_k, glt_k], writes=[ps_k], inc=(k == 30))
                              K.op("act", lambda e, c=c: e.activation(out=hc[:, c, 0:bw], in_=ps[:, 0:bw], func=AF.Identity, bias=cdwb[:, c:c + 1], scale=1.0),
                                   reads=[ps_k, cdwb_k], writes=[hc_k])
                          K.op("pool", lambda e: e.tensor_copy(out=hcb[:, :, 0:bw], in_=hc[:, :, 0:bw]), reads=[hc_k], writes=[hcb_k])
                          K.op("dve", lambda e: e.tensor_tensor(out=sqb[:, :, 0:bw], in0=hc[:, :, 0:bw], in1=hc[:, :, 0:bw], op=ALU.mult), reads=[hc_k], writes=[sqb_k])
                          p1, p1_k = psr()
                          p2, p2_k = psr()
                          for c in range(4):
                              K.op("pe", lambda e, c=c: e.matmul(p1[:, 0:bw], lhsT=ones_bf[:], rhs=hcb[:, c, 0:bw], start=(c == 0), stop=(c == 3)),
                                   reads=[ones_k, hcb_k], writes=[p1_k], inc=(c == 3))
                          for c in range(4):
                              K.op("pe", lambda e, c=c: e.matmul(p2[:, 0:bw], lhsT=ones_bf[:], rhs=sqb[:, c, 0:bw], start=(c == 0), stop=(c == 3)),
                                   reads=[ones_k, sqb_k], writes=[p2_k], inc=(c == 3))
                          K.op("dve", lambda e: e.tensor_scalar(out=mean[:, 0:bw], in0=p1[:, 0:bw], scalar1=1.0 / 512, scalar2=None, op0=ALU.mult), reads=[p1_k], writes=[mean_k])
                          tm, tm_k = tmr.next()
                          K.op("dve", lambda e: e.tensor_tensor(out=tm[:, 0:bw], in0=mean[:, 0:bw], in1=mean[:, 0:bw], op=ALU.mult), reads=[mean_k], writes=[tm_k])
                          K.op("dve", lambda e: e.scalar_tensor_tensor(out=tm[:, 0:bw], in0=p2[:, 0:bw], scalar=1.0 / 512, in1=tm[:, 0:bw], op0=ALU.mult, op1=ALU.subtract),
                               reads=[p2_k, tm_k], writes=[tm_k])
                          K.op("act", lambda e: e.activation(out=tm[:, 0:bw], in_=tm[:, 0:bw], func=AF.Sqrt, bias=EPS, scale=1.0), reads=[tm_k], writes=[tm_k])
                          K.op("dve", lambda e: e.reciprocal(out=rstd[:, 0:bw], in_=tm[:, 0:bw]), reads=[tm_k], writes=[rstd_k])
                          for c in range(4):
                              t2, t2_k = tmr.next()
                              K.op("dve", lambda e, c=c: e.tensor_tensor(out=t2[:, 0:bw], in0=hc[:, c, 0:bw], in1=mean[:, 0:bw], op=ALU.subtract), reads=[hc_k, mean_k], writes=[t2_k])
                              K.op("dve", lambda e: e.tensor_tensor(out=t2[:, 0:bw], in0=t2[:, 0:bw], in1=rstd[:, 0:bw], op=ALU.mult), reads=[t2_k, rstd_k], writes=[t2_k])
                              K.op("act", lambda e, c=c: e.activation(out=cvn[:, c, 0:bw], in_=t2[:, 0:bw], func=AF.Silu, bias=clnb[:, c:c + 1], scale=clng[:, c:c + 1]),
                                   reads=[t2_k, clnb_k, clng_k], writes=[cvn_k])
                          for j in range(8):
                              pa, pa_k = psr()
                              pc, pc_k = psr()
                              pss_, pss_k = psr()
                              for h in range(NH):
                                  K.op("pe", lambda e, h=h: e.matmul(pa[:, 0:bw], lhsT=wao[:, h, j * 128:(j + 1) * 128], rhs=aot[:, h, 0:bw], start=(h == 0), stop=(h == NH - 1)),
                                       reads=[wao_k, aot_k], writes=[pa_k], inc=(h == NH - 1))
                              for c in range(4):
                                  K.op("pe", lambda e, c=c: e.matmul(pc[:, 0:bw], lhsT=wco[:, c, j * 128:(j + 1) * 128], rhs=cvn[:, c, 0:bw], start=(c == 0), stop=(c == 3)),
                                       reads=[wco_k, cvn_k], writes=[pc_k], inc=(c == 3))
                              for c in range(4):
                                  K.op("pe", lambda e, c=c: e.matmul(pss_[:, 0:bw], lhsT=wso[:, c, j * 128:(j + 1) * 128], rhs=ust[:, c, 0:bw], start=(c == 0), stop=(c == 3)),
                                       reads=[wso_k, ust_k], writes=[pss_k], inc=(c == 3))
                              m1, m1_k = tmr.next()
                              m2, m2_k = tmr.next()
                              m3, m3_k = tmr.next()
                              K.op("dve", lambda e: e.tensor_tensor(out=m1[:, 0:bw], in0=pa[:, 0:bw], in1=gtt[:, j, 0:bw], op=ALU.mult), reads=[pa_k, gtt_k], writes=[m1_k])
                              K.op("dve", lambda e: e.tensor_tensor(out=m2[:, 0:bw], in0=pc[:, 0:bw], in1=gtt[:, 8 + j, 0:bw], op=ALU.mult), reads=[pc_k, gtt_k], writes=[m2_k])
                              K.op("dve", lambda e: e.tensor_tensor(out=m3[:, 0:bw], in0=pss_[:, 0:bw], in1=gtt[:, 16 + j, 0:bw], op=ALU.mult), reads=[pss_k, gtt_k], writes=[m3_k])
                              K.op("pool", lambda e: e.tensor_tensor(out=m1[:, 0:bw], in0=m1[:, 0:bw], in1=m2[:, 0:bw], op=ALU.add), reads=[m1_k, m2_k], writes=[m1_k])
                              K.op("pool", lambda e, j=j: e.tensor_tensor(out=mrg[:, j, 0:bw], in0=m1[:, 0:bw], in1=m3[:, 0:bw], op=ALU.add), reads=[m1_k, m3_k], writes=[mrg_k])
                          for t in range(nj):
                              t0 = b0 + t * 128
                              xt, xk = xr.next()
                              K.dma("sp", xt[:], sg["x"][t0:t0 + 128, :], xk, writes=[xk], dram_reads=[sg["x_k"]])
                              for half in range(2):
                                  po, po_k = psr()
                                  for j in range(8):
                                      K.op("pe", lambda e, j=j: e.matmul(po[:], lhsT=mrg[:, j, t * 128:(t + 1) * 128], rhs=wout[:, j, half * 512:(half + 1) * 512],
                                                                         start=(j == 0), stop=(j == 7)),
                                           reads=[mrg_k, wout_k], writes=[po_k], inc=(j == 7))
                                  tm2, tm2_k = tmr.next()
                                  K.op("dve", lambda e: e.tensor_tensor(out=tm2[:], in0=po[:], in1=g1t[:, half * 512:(half + 1) * 512], op=ALU.mult), reads=[po_k, g1_k], writes=[tm2_k])
                                  K.op("pool", lambda e: e.tensor_tensor(out=xt[:, half * 512:(half + 1) * 512], in0=xt[:, half * 512:(half + 1) * 512], in1=tm2[:], op=ALU.add),
                                       reads=[xk, tm2_k], writes=[xk])
                              K.dma("pool", sg["xm"][t0:t0 + 128, :], xt[:], xk, reads=[xk], acc=[sg["xm_k"]])
                  K.end_phase()
                  chk(l)

              with ExitStack() as ph:
                  P = dict(junkb=K.ring(ph, "junkc", 2, [128, 4, D], F32), ssb=K.ring(ph, "ssc", 3, [128, 16], F32),
                           hbb=K.ring(ph, "hbc", 2, [128, 4, D], BF16))
                  xbr = K.ring(ph, "xb3", 3, [128, 4, D], F32, dma=True)
                  hst = K.ring(ph, "hst2", 2, [128, KT, 512], BF16, dma=True)
                  gamt, gam_k = K.sb(ph, "gam2", [128, D], F32, dma=True)
                  sht, sh_k = K.sb(ph, "sh2", [128, D], F32, dma=True)
                  items = [(sg, b0, bw) for sg in segs for (b0, bw) in blocks_of(sg["n"])]
                  loaded = {}
                  ctr = [0]

                  def issue(i):
                      if i >= len(items) or i in loaded:
                          return
                      sg_, b0_, bw_ = items[i]
                      xb_, xb_k_ = xbr.next()
                      K.dma("sp", xb_[:, 0:bw_ // 128, :], sg_["xm"][b0_:b0_ + bw_, :].rearrange("(j p) d -> p j d", p=128), xb_k_, writes=[xb_k_], dram_reads=[sg_["xm_k"]])
                      loaded[i] = (xb_, xb_k_)

                  for sg in segs:
                      n = sg["n"]
                      K.dma("sp", sht[:], sg["mod"][:, 3 * D:4 * D], sh_k, writes=[sh_k], dram_reads=[sg["mod_k"]])
                      K.dma("sp", gamt[:], sg["mod"][:, 4 * D:5 * D], gam_k, writes=[gam_k], dram_reads=[sg["mod_k"]])
                      for (b0, bw) in blocks_of(n):
                          G = bw // 128
                          hs, hs_k = hst.next()
                          i_ = ctr[0]
                          ctr[0] += 1
                          issue(i_)
                          issue(i_ + 1)
                          xb, xb_k = loaded.pop(i_)
                          masks = []
                          if sg["prompt"]:
                              for j in range(G):
                                  if b0 + j * 128 == 0:
                                      masks.append((j, 0))
                                  if b0 + j * 128 == n - 128:
                                      masks.append((j, 1))
                          hbb, hbb_k = norm_block(P, xb, xb_k, G, (gamt, gam_k), (sht, sh_k), masks=masks)
                          for j in range(G):
                              transpose_to(P, hbb[:, j, :], hbb_k, hs[:, :, j * 128:(j + 1) * 128], hs_k)
                          K.dma("sp", sg["h2T"][:, :, 1 + b0:1 + b0 + bw], hs[:, :, 0:bw], hs_k, reads=[hs_k], acc=[sg["h2T_k"]])
                  K.end_phase()
                  chk(l)
              with ExitStack() as ph:
                  wup, wup_k = K.sb(ph, "wup", [128, KT, 2 * D_FF], BF16, dma=True)
                  wdn, wdn_k = K.sb(ph, "wdn", [128, NFT, D], BF16, dma=True)
                  wuv = wview(Wl["w_up"])
                  for c in range(0, 2 * D_FF, 704):
                      K.dma("pool", wup[:, :, c:c + 704], wuv[:, :, c:c + 704], wup_k, acc=[wup_k])
                  K.dma("pool", wdn[:], wview(Wl["w_down"]), wdn_k, writes=[wdn_k])
                  fdw, fdw_k = K.sb(ph, "fdw", [128, 44, 3], F32, dma=True)
                  fdwb, fdwb_k = K.sb(ph, "fdwb", [128, 44], F32, dma=True)
                  K.dma("sp", fdw[:], Wl["fdw"].rearrange("p (c k) -> p c k", c=44), fdw_k, writes=[fdw_k])
                  K.dma("sp", fdwb[:], Wl["fdwb"], fdwb_k, writes=[fdwb_k])
                  g2t, g2_k = K.sb(ph, "g2t", [128, D], F32, dma=True)
                  h2r = K.ring(ph, "h2b", 2, [128, KT, 514], BF16, dma=True)
                  zr = K.ring(ph, "zt_", 3, [128, 514], F32)
                  accr = K.ring(ph, "acc", 4, [128, 512], F32)
                  sgr = K.ring(ph, "sgf", 2, [128, 512], F32)
                  uT, uT_k = K.sb(ph, "uT", [128, NFT, 512], BF16)
                  xr = K.ring(ph, "xr4", 2, [128, D], F32, dma=True)
                  tmr = K.ring(ph, "tm4", 2, [128, 512], F32)
                  items = [(sg, b0, bw) for sg in segs for (b0, bw) in blocks_of(sg["n"])]
                  loaded = {}
                  ctr = [0]

                  def issue(i):
                      if i >= len(items) or i in loaded:
                          return
                      sg_, b0_, bw_ = items[i]
                      h2_, h2_k_ = h2r.next()
                      K.dma("sp", h2_[:, :, 0:bw_ + 2], sg_["h2T"][:, :, b0_:b0_ + bw_ + 2], h2_k_, writes=[h2_k_], dram_reads=[sg_["h2T_k"]])
                      loaded[i] = (h2_, h2_k_)

                  for sg in segs:
                      n = sg["n"]
                      K.dma("sp", g2t[:], sg["mod"][:, 5 * D:6 * D], g2_k, writes=[g2_k], dram_reads=[sg["mod_k"]])
                      for (b0, bw) in blocks_of(n):
                          nj = bw // 128
                          i_ = ctr[0]
                          ctr[0] += 1
                          issue(i_)
                          issue(i_ + 1)
                          h2, h2_k = loaded.pop(i_)
                          half = (bw + 2) // 2
                          for i in range(NFT):
                              accs = []
                              for which in range(2):
                                  ci = which * NFT + i
                                  col0 = ci * 128
                                  z, z_k = zr.next()
                                  for (c0, c1) in ((0, half), (half, bw + 2)):
                                      ps, ps_k = psr(0, 8)
                                      for kt in range(KT):
                                          K.op("pe", lambda e, kt=kt: e.matmul(ps[:, 0:c1 - c0], lhsT=wup[:, kt, col0:col0 + 128], rhs=h2[:, kt, c0:c1],
                                                                               start=(kt == 0), stop=(kt == KT - 1)),
                                               reads=[wup_k, h2_k], writes=[ps_k], inc=(kt == KT - 1))
                                      K.op("act", lambda e: e.activation(out=z[:, c0:c1], in_=ps[:, 0:c1 - c0], func=AF.Copy), reads=[ps_k], writes=[z_k])
                                  acc, acc_k = accr.next()
                                  K.op("dve", lambda e: e.tensor_scalar(out=acc[:, 0:bw], in0=z[:, 0:bw], scalar1=fdw[:, ci, 0:1], scalar2=fdwb[:, ci:ci + 1],
                                                                        op0=ALU.mult, op1=ALU.add),
                                       reads=[z_k, fdw_k, fdwb_k], writes=[acc_k])
                                  K.op("dve", lambda e: e.scalar_tensor_tensor(out=acc[:, 0:bw], in0=z[:, 1:bw + 1], scalar=fdw[:, ci, 1:2], in1=acc[:, 0:bw],
                                                                               op0=ALU.mult, op1=ALU.add),
                                       reads=[z_k, fdw_k, acc_k], writes=[acc_k])
                                  K.op("dve", lambda e: e.scalar_tensor_tensor(out=acc[:, 0:bw], in0=z[:, 2:bw + 2], scalar=fdw[:, ci, 2:3], in1=acc[:, 0:bw],
                                                                               op0=ALU.mult, op1=ALU.add),
                                       reads=[z_k, fdw_k, acc_k], writes=[acc_k])
                                  accs.append((acc, acc_k))
                              sgf, sgf_k = sgr.next()
                              K.op("act", lambda e: e.activation(out=sgf[:, 0:bw], in_=accs[0][0][:, 0:bw], func=AF.Silu), reads=[accs[0][1]], writes=[sgf_k])
                              K.op("pool", lambda e, i=i: e.tensor_tensor(out=uT[:, i, 0:bw], in0=sgf[:, 0:bw], in1=accs[1][0][:, 0:bw], op=ALU.mult),
                                   reads=[sgf_k, accs[1][1]], writes=[uT_k])
                          for t in range(nj):
                              t0 = b0 + t * 128
                              if sg["prompt"] and (t0 < HALO or t0 >= n - HALO):
                                  continue
                              xt, xk = xr.next()
                              K.dma("sp", xt[:], sg["xm"][t0:t0 + 128, :], xk, writes=[xk], dram_reads=[sg["xm_k"]])
                              for hf in range(2):
                                  po, po_k = psr(0, 8)
                                  for i in range(NFT):
                                      K.op("pe", lambda e, i=i: e.matmul(po[:], lhsT=uT[:, i, t * 128:(t + 1) * 128], rhs=wdn[:, i, hf * 512:(hf + 1) * 512],
                                                                         start=(i == 0), stop=(i == NFT - 1)),
                                           reads=[uT_k, wdn_k], writes=[po_k], inc=(i == NFT - 1))
                                  tm2, tm2_k = tmr.next()
                                  K.op("dve", lambda e: e.tensor_tensor(out=tm2[:], in0=po[:], in1=g2t[:, hf * 512:(hf + 1) * 512], op=ALU.mult), reads=[po_k, g2_k], writes=[tm2_k])
                                  K.op("pool", lambda e: e.tensor_tensor(out=xt[:, hf * 512:(hf + 1) * 512], in0=xt[:, hf * 512:(hf + 1) * 512], in1=tm2[:], op=ALU.add),
                                       reads=[xk, tm2_k], writes=[xk])
                              yoff = t0 - HALO if sg["prompt"] else t0
                              K.dma("pool", sg["y"][yoff:yoff + 128, :], xt[:], xk, reads=[xk], acc=[sg["y_k"]])
                  K.end_phase()
                  chk(l)
        except _Stop:
            print('STOPPED at', _stop)
        K.barrier()
        print("instructions emitted:", K.nops)
    return nc


def rope_table(pos):
    inv = np.power(np.float32(10000.0), -np.arange(0, 32, 2, dtype=np.float32) / np.float32(32)).astype(np.float32)
    ang = pos.astype(np.float32)[:, None] * inv[None, :]
    return np.concatenate([np.cos(ang), np.sin(ang)], axis=1).astype(np.float32)


def layer_weights(inp, l):
    f = lambda a: np.ascontiguousarray(a, dtype=np.float32)
    ukv = inp["w_ukv"][l].reshape(128, NH, 128)
    w = dict(
        w_ada=f(inp["w_ada"][l]), b_ada=f(inp["b_ada"][l][None, :]), norm1=f(inp["norm1"][l][None, :]), norm2=f(inp["norm2"][l][None, :]),
        w_in=f(inp["w_in"][l]), w_uq=f(inp["w_uq"][l]),
        w_ukv=f(np.concatenate([ukv[:, :, 0:64].reshape(128, 512), ukv[:, :, 64:128].reshape(128, 512)], axis=1)),
        qan=f(inp["q_a_norm"][l].reshape(2, 128).T), kvan=f(inp["kv_a_norm"][l].reshape(128, 1)),
        qhn=f(inp["q_head_norm"][l][None, :]), khn=f(inp["k_head_norm"][l][None, :]),
        w_ao=f(inp["w_attn_o"][l].reshape(NH, 64, D).transpose(1, 0, 2).reshape(64, NH * D)),
        cdw=f(inp["conv_dw"][l].T.reshape(4, 128, 31).transpose(1, 0, 2).reshape(128, 4 * 31)),
        cdwb=f(inp["conv_dw_b"][l].reshape(4, 128).T), clng=f(inp["conv_ln_g"][l].reshape(4, 128).T), clnb=f(inp["conv_ln_b"][l].reshape(4, 128).T),
        w_co=f(inp["w_conv_o"][l]), sglng=f(inp["sg_ln_g"][l][None, :]), sglnb=f(inp["sg_ln_b"][l][None, :]),
        sgwT=f(inp["sg_w"][l].transpose(2, 0, 1).reshape(128, 4 * 128)),
        sgb=f(inp["sg_b"][l].reshape(1, 512)), w_so=f(inp["w_sg_o"][l]), w_out=f(inp["w_out"][l]), w_up=f(inp["w_up"][l]),
        fdw=f(inp["ffn_dw"][l].T.reshape(44, 128, 3).transpose(1, 0, 2).reshape(128, 44 * 3)),
        fdwb=f(inp["ffn_dw_b"][l].reshape(44, 128).T), w_down=f(inp["w_down"][l]))
    return w


_PROG = {}


def run_model(inp, cfg, n_cores=8):
    NS, SS, PCH, PS = cfg["NS"], cfg["SS"], cfg["PCH"], cfg["PS"]
    PSEG = PCH + 2 * HALO
    key = (NS, SS, PCH, PS)
    L = inp["w_ada"].shape[0]
    assert L == 2
    if key not in _PROG:
        _PROG[key] = build_program(cfg, L)
    nc = _PROG[key]
    xp = np.asarray(inp["x_prompt"], dtype=np.float32)
    xs = np.asarray(inp["x_sample"], dtype=np.float32)
    cp = np.asarray(inp["c_prompt"], dtype=np.float32)
    cs = np.asarray(inp["c_sample"], dtype=np.float32)
    nchunk = PS // PCH
    rope_s = rope_table(np.arange(SS))
    rope_pc = rope_table(np.arange(PS))
    wls = [layer_weights(inp, l) for l in range(L)]
    in_maps = []
    for c in range(n_cores):
        b = c // nchunk
        r = c % nchunk
        lo = r * PCH - HALO
        pm = np.zeros((128, 2), np.float32)
        pm[:, 0] = 1.0 if r > 0 else 0.0
        pm[:, 1] = 1.0 if r < nchunk - 1 else 0.0
        sel = np.zeros((128, nchunk), np.float32)
        sel[:, r] = 1.0
        m = dict(xs=np.ascontiguousarray(xs[c * NS:(c + 1) * NS].reshape(NS * SS, D)),
                 xpctx=np.ascontiguousarray(xp[b]),
                 cvT=np.ascontiguousarray(np.concatenate([cs[c * NS:(c + 1) * NS], cp[b:b + 1]], axis=0).reshape((NS + 1) * KT, 128).T),
                 pmask=pm, psel=sel, rope_s=rope_s, rope_pc=rope_pc, rope_pq=rope_table(np.arange(lo, lo + PSEG)))
        for l in range(L):
            for k, v in wls[l].items():
                m["%s_%d" % (k, l)] = v
        in_maps.append(m)
    res = run_bass_kernel_spmd(nc, in_maps, core_ids=list(range(n_cores)))
    ys = np.stack([np.asarray(r_["ys"]).reshape(NS, SS, D) for r_ in res.results], axis=0).reshape(n_cores * NS, SS, D)
    ypo = np.stack([np.asarray(r_["yp"]) for r_ in res.results], axis=0).reshape(n_cores // nchunk, PS, D)
    return ypo.astype(np.float32), ys.astype(np.float32)


def kernel(**inputs):
    yp, ys = run_model(inputs, CFG, 8)
    return (yp, ys)
```

```python
import numpy as np
from contextlib import ExitStack
import concourse.bass as bass
import concourse.mybir as mybir
from concourse.bass_utils import run_bass_kernel_spmd

F32 = mybir.dt.float32
BF16 = mybir.dt.bfloat16
AF = mybir.ActivationFunctionType
ALU = mybir.AluOpType
AX = mybir.AxisListType

D = 1024
KT = 8
NH = 8
DQK = 96
EPS = 1e-6
D_IN = 5536
D_FF = 2816
NFT = 22
HALO = 128

CFG = dict(NS=4, SS=2048, PCH=2048, PS=8192)


class Trk:
    __slots__ = ("w", "r", "dsem", "dcnt")

    def __init__(self):
        self.w = {}
        self.r = {}
        self.dsem = None
        self.dcnt = 0


class KB:
    def __init__(self, nc, es):
        self.nc = nc
        self.es = es
        self.eng = {"pe": nc.tensor, "act": nc.scalar, "dve": nc.vector, "pool": nc.gpsimd, "sp": nc.sync}
        self.sem = {k: es.enter_context(nc.semaphore("s_" + k)) for k in ["pe", "act", "dve", "pool"]}
        self.cnt = {k: 0 for k in self.sem}
        self.seen = {k: {} for k in self.eng}
        self.pend = {k: [] for k in self.eng}
        self.dma_trks = []
        self.nops = 0
        self.sem_free = []
        self.phase_trks = []
        self.nsem = 0

    def get_dsem(self):
        if self.sem_free:
            return self.sem_free.pop()
        self.nsem += 1
        return (self.es.enter_context(self.nc.semaphore("dq%d" % self.nsem)), 0)

    def release(self, trks):
        for k in trks:
            if k.dsem is not None:
                self.sem_free.append((k.dsem, k.dcnt))
                if k in self.dma_trks:
                    self.dma_trks.remove(k)

    def _wait(self, e, sem, val):
        d = self.seen[e]
        if d.get(sem, 0) >= val:
            return
        self.eng[e].wait_ge(sem, val)
        d[sem] = val

    def _deps(self, e, reads, writes):
        for t in reads:
            for s, (v, ek) in t.w.items():
                self._wait(e, s, v)
        for t in writes:
            for s, (v, ek) in t.w.items():
                if ek != e:
                    self._wait(e, s, v)
            for ek, (s, v) in t.r.items():
                if ek != e:
                    self._wait(e, s, v)

    def op(self, e, fn, reads=(), writes=(), inc=True):
        if getattr(self, 'skip', False):
            return
        self._deps(e, reads, writes)
        ins = fn(self.eng[e])
        self.nops += 1
        if not inc:
            self.pend[e].append((reads, writes))
            return
        self.cnt[e] += 1
        c = self.cnt[e]
        s = self.sem[e]
        ins.then_inc(s, 1)
        self.pend[e].append((reads, writes))
        for rd, wr in self.pend[e]:
            for t in wr:
                t.w = {s: (c, e)}
                t.r = {}
        for rd, wr in self.pend[e]:
            for t in rd:
                t.r[e] = (s, c)
        self.pend[e] = []

    def dma(self, q, out, in_, own, reads=(), writes=(), acc=(), dram_reads=(), slow=False):
        if getattr(self, 'skip', False):
            return
        self._deps(q, list(reads) + list(dram_reads), writes)
        for t in acc:
            for ek, (s, v) in t.r.items():
                self._wait(q, s, v)
        own.dcnt += 16
        if slow:
            self.eng[q].dma_start(out=out, in_=in_, allow_slow_non_contiguous=True).then_inc(own.dsem, 16)
        else:
            self.eng[q].dma_start(out=out, in_=in_).then_inc(own.dsem, 16)
        self.nops += 1
        key = ("dma", own.dsem)
        for t in writes:
            t.w = {own.dsem: (own.dcnt, key)}
            t.r = {}
        for t in acc:
            t.w[own.dsem] = (own.dcnt, key)
        for t in reads:
            t.r[key] = (own.dsem, own.dcnt)

    def end_phase(self):
        self.barrier()
        self.release(list(self.phase_trks))
        self.phase_trks = []

    def barrier(self):
        for e in self.eng:
            for k in self.sem:
                if k != e and self.cnt[k] > 0:
                    self._wait(e, self.sem[k], self.cnt[k])
            for t in self.dma_trks:
                if t.dcnt > 0:
                    self._wait(e, t.dsem, t.dcnt)

    def sb(self, es, name, shape, dt, dma=False):
        self.nalloc = getattr(self, "nalloc", 0) + 1
        t = es.enter_context(self.nc.sbuf_tensor("%s_u%d" % (name, self.nalloc), list(shape), dt))
        k = Trk()
        if dma:
            k.dsem, k.dcnt = self.get_dsem()
            self.dma_trks.append(k)
            self.phase_trks.append(k)
        return t, k

    def ring(self, es, name, n, shape, dt, dma=False):
        return Ring([self.sb(es, "%s%d" % (name, i), shape, dt, dma) for i in range(n)])


class Ring:
    def __init__(self, items):
        self.items = items
        self.i = 0

    def next(self):
        it = self.items[self.i % len(self.items)]
        self.i += 1
        return it


def blocks_of(n, bs=512):
    out = []
    o = 0
    while o < n:
        b = min(bs, n - o)
        out.append((o, b))
        o += b
    return out


def build_program(cfg, n_layers=1):
    NS, SS, PCH, PS = cfg["NS"], cfg["SS"], cfg["PCH"], cfg["PS"]
    PSEG = PCH + 2 * HALO
    nc = bass.Bass("TRN2", target_bir_lowering=False)

    def din(name, shape):
        return nc.dram_tensor(name, list(shape), F32, kind="ExternalInput").ap()

    xs = din("xs", [NS * SS, D])
    psel = din("psel", [128, PS // PCH])
    xpctx = din("xpctx", [PS, D])
    cvT = din("cvT", [128, (NS + 1) * KT])
    pmask = din("pmask", [128, 2])
    rope_s = din("rope_s", [SS, 32])
    rope_pc = din("rope_pc", [PS, 32])
    rope_pq = din("rope_pq", [PSEG, 32])
    W = {}
    wshapes = dict(
        w_ada=[D, 6 * D], b_ada=[1, 6 * D], norm1=[1, D], norm2=[1, D], w_in=[D, D_IN],
        w_uq=[256, 768], w_ukv=[128, 1024], qan=[128, 2], kvan=[128, 1], qhn=[1, 96], khn=[1, 96],
        w_ao=[64, 8 * D], cdw=[128, 4 * 31], cdwb=[128, 4], clng=[128, 4], clnb=[128, 4],
        w_co=[512, D], sglng=[1, 512], sglnb=[1, 512], sgwT=[128, 4 * 128], sgb=[1, 512], w_so=[512, D],
        w_out=[D, D], w_up=[D, 2 * D_FF], fdw=[128, 44 * 3], fdwb=[128, 44], w_down=[D_FF, D])
    for l in range(n_layers):
        for k, shp in wshapes.items():
            W[(l, k)] = din("%s_%d" % (k, l), shp)
    ys = nc.dram_tensor("ys", [NS * SS, D], F32, kind="ExternalOutput").ap()
    yp = nc.dram_tensor("yp", [PCH, D], F32, kind="ExternalOutput").ap()

    x1s = nc.dram_tensor("x1s", [NS * SS, D], F32).ap()
    x1full = nc.dram_tensor("x1full", [PS, D], F32).ap()
    x1seg = nc.dram_tensor("x1seg", [PSEG, D], F32).ap()
    x1s_k = [Trk() for _ in range(NS)]
    x1full_k = Trk()
    x1seg_k = Trk()
    assert n_layers == 2

    def make_segs(l):
        segs = []
        for i in range(NS):
            if l == 0:
                segs.append(dict(n=SS, s=SS, x=xs[i * SS:(i + 1) * SS, :], x_k=Trk(), ctx=None, ctx_k=None, rq=rope_s, rk=rope_s,
                                 y=x1s[i * SS:(i + 1) * SS, :], y_k=x1s_k[i], prompt=False))
            else:
                segs.append(dict(n=SS, s=SS, x=x1s[i * SS:(i + 1) * SS, :], x_k=x1s_k[i], ctx=None, ctx_k=None, rq=rope_s, rk=rope_s,
                                 y=ys[i * SS:(i + 1) * SS, :], y_k=Trk(), prompt=False))
        if l == 0:
            segs.append(dict(n=PS, s=PS, x=xpctx, x_k=Trk(), ctx=None, ctx_k=None, rq=rope_pc, rk=rope_pc,
                             y=x1full, y_k=x1full_k, prompt=False))
        else:
            segs.append(dict(n=PSEG, s=PS, x=x1seg, x_k=x1seg_k, ctx=x1full, ctx_k=x1full_k, rq=rope_pq, rk=rope_pc,
                             y=yp, y_k=Trk(), prompt=True))
        for si, sg in enumerate(segs):
            n, s_ = sg["n"], sg["s"]

            def dsc(nm, shape, dt=BF16):
                return nc.dram_tensor("%s_%d_%d" % (nm, l, si), list(shape), dt).ap()
            sg["hT"] = dsc("hT", [128, KT, n]); sg["hT_k"] = Trk()
            sg["qT"] = dsc("qT", [DQK, NH, n]); sg["qT_k"] = Trk()
            sg["kT"] = dsc("kT", [NH, DQK, s_]); sg["kT_k"] = Trk()
            sg["v"] = dsc("v", [NH, 128, s_ // 128, 65]); sg["v_k"] = Trk()
            sg["gat"] = dsc("gat", [128, 24, n]); sg["gat_k"] = Trk()
            sg["glu"] = dsc("glu", [128, 4, n + 30]); sg["glu_k"] = Trk()
            sg["us"] = dsc("us", [128, 4, n]); sg["us_k"] = Trk()
            sg["ao"] = dsc("ao", [64, NH, n]); sg["ao_k"] = Trk()
            sg["xm"] = dsc("xm", [n, D], F32); sg["xm_k"] = Trk()
            sg["h2T"] = dsc("h2T", [128, KT, n + 2]); sg["h2T_k"] = Trk()
            sg["mod"] = dsc("mod", [128, 6 * D], F32); sg["mod_k"] = Trk()
        return segs

    all_segs = [make_segs(l) for l in range(n_layers)]

    with ExitStack() as es:
        K = KB(nc, es)
        ident, ident_k = K.sb(es, "ident", [128, 128], BF16)
        ones_bf, ones_k = K.sb(es, "ones_bf", [128, 128], BF16)
        ones_f, onesf_k = K.sb(es, "ones_f", [128, 64], F32)
        zt, zt_k = K.sb(es, "zt", [128, 64], BF16, dma=True)
        mk, mk_k = K.sb(es, "mk", [128, 2], F32, dma=True)
        K.op("dve", lambda e: e.memset(ident[:], 1.0), writes=[ident_k])
        K.op("pool", lambda e: e.affine_select(out=ident[:], in_=ident[:], pattern=[[-1, 128]],
                                               compare_op=ALU.is_equal, fill=0.0, base=0, channel_multiplier=1),
             reads=[ident_k], writes=[ident_k])
        K.op("dve", lambda e: e.memset(ones_bf[:], 1.0), writes=[ones_k])
        K.op("dve", lambda e: e.memset(ones_f[:], 1.0), writes=[onesf_k])
        K.op("dve", lambda e: e.memset(zt[:], 0.0), writes=[zt_k])
        K.dma("sp", mk[:], pmask, mk_k, writes=[mk_k])
        K.phase_trks = []
        PSB = []
        for i in range(8):
            t = es.enter_context(nc.psum_tensor("psb%d" % i, [128, 512], F32))
            PSB.append((t, Trk()))
        psr_state = [0]

        PSR_HI = [8]

        def psr(lo=0, hi=None):
            if hi is None:
                hi = PSR_HI[0]
            i = lo + psr_state[0] % (hi - lo)
            psr_state[0] += 1
            return PSB[i]

        for sg in [g for sl in all_segs for g in sl]:
            n = sg["n"]
            K.dma("sp", sg["glu"][:, :, 0:15], zt[:, 0:60].rearrange("p (a b) -> p a b", a=4), zt_k, reads=[zt_k], acc=[sg["glu_k"]])
            K.dma("sp", sg["glu"][:, :, n + 15:n + 30], zt[:, 0:60].rearrange("p (a b) -> p a b", a=4), zt_k, reads=[zt_k], acc=[sg["glu_k"]])
            K.dma("sp", sg["h2T"][:, :, 0:1], zt[:, 0:8].rearrange("p (a b) -> p a b", a=8), zt_k, reads=[zt_k], acc=[sg["h2T_k"]], slow=True)
            K.dma("sp", sg["h2T"][:, :, n + 1:n + 2], zt[:, 0:8].rearrange("p (a b) -> p a b", a=8), zt_k, reads=[zt_k], acc=[sg["h2T_k"]], slow=True)

        def wview(ap_, p=128):
            return ap_.rearrange("(kt p) n -> p kt n", p=p)

        def norm_tile(P, xt, xk, gam, sh, mask_col=None):
            junk, junk_k = P["junk"].next()
            ss, ss_k = P["ss"].next()
            K.op("dve", lambda e: e.memset(ss[:], 0.0), writes=[ss_k])
            K.op("act", lambda e: e.activation(out=junk[:], in_=xt[:], func=AF.Square, accum_out=ss[:, 0:1]),
                 reads=[xk, ss_k], writes=[junk_k, ss_k])
            K.op("act", lambda e: e.activation(out=ss[:, 1:2], in_=ss[:, 0:1], func=AF.Sqrt, bias=EPS, scale=1.0 / D),
                 reads=[ss_k], writes=[ss_k])
            K.op("dve", lambda e: e.reciprocal(out=ss[:, 2:3], in_=ss[:, 1:2]), reads=[ss_k], writes=[ss_k])
            K.op("dve", lambda e: e.scalar_tensor_tensor(out=junk[:], in0=xt[:], scalar=ss[:, 2:3], in1=gam[0][:],
                                                         op0=ALU.mult, op1=ALU.mult),
                 reads=[xk, ss_k, gam[1], junk_k], writes=[junk_k])
            hb, hb_k = P["hb"].next()
            K.op("pool", lambda e: e.tensor_tensor(out=hb[:], in0=junk[:], in1=sh[0][:], op=ALU.add),
                 reads=[junk_k, sh[1]], writes=[hb_k])
            if mask_col is not None:
                K.op("dve", lambda e: e.tensor_scalar(out=hb[:], in0=hb[:], scalar1=mk[:, mask_col:mask_col + 1],
                                                      scalar2=None, op0=ALU.mult),
                     reads=[hb_k, mk_k], writes=[hb_k])
            return hb, hb_k

        def transpose_to(P, hb, hb_k, dst, dst_k, ncols_src=D, rows=128, chunk=128):
            nchunk = ncols_src // chunk
            ps, ps_k = psr()
            psb = ps[:].bitcast(BF16)
            for i in range(nchunk):
                K.op("pe", lambda e, i=i: e.transpose(psb[0:chunk, i * 128:(i + 1) * 128], hb[:, i * chunk:(i + 1) * chunk], ident[:]),
                     reads=[hb_k, ident_k], writes=[ps_k], inc=(i == nchunk - 1))
            K.op("act", lambda e: e.activation(out=dst, in_=psb[0:chunk, 0:nchunk * 128].rearrange("p (a b) -> p a b", a=nchunk),
                                               func=AF.Copy),
                 reads=[ps_k], writes=[dst_k])

        def norm_block(P, xb, xb_k, G, gam, sh, masks=()):
            junk, junk_k = P["junkb"].next()
            ss, ss_k = P["ssb"].next()
            K.op("dve", lambda e: e.memset(ss[:], 0.0), writes=[ss_k])
            for j in range(G):
                K.op("act", lambda e, j=j: e.activation(out=junk[:, j, :], in_=xb[:, j, :], func=AF.Square, accum_out=ss[:, j:j + 1]),
                     reads=[xb_k], writes=[junk_k, ss_k])
            K.op("act", lambda e: e.activation(out=ss[:, 4:4 + G], in_=ss[:, 0:G], func=AF.Sqrt, bias=EPS, scale=1.0 / D),
                 reads=[ss_k], writes=[ss_k])
            K.op("dve", lambda e: e.reciprocal(out=ss[:, 8:8 + G], in_=ss[:, 4:4 + G]), reads=[ss_k], writes=[ss_k])
            for j in range(G):
                K.op("dve", lambda e, j=j: e.scalar_tensor_tensor(out=junk[:, j, :], in0=xb[:, j, :], scalar=ss[:, 8 + j:9 + j], in1=gam[0][:],
                                                                  op0=ALU.mult, op1=ALU.mult),
                     reads=[xb_k, ss_k, gam[1], junk_k] if j == 0 else [xb_k, gam[1]], writes=[junk_k])
            hbb, hbb_k = P["hbb"].next()
            K.op("pool", lambda e: e.tensor_tensor(out=hbb[:, 0:G, :], in0=junk[:, 0:G, :], in1=sh[0][:].unsqueeze(1).to_broadcast([128, G, D]), op=ALU.add),
                 reads=[junk_k, sh[1]], writes=[hbb_k])
            for (j, mc) in masks:
                K.op("dve", lambda e, j=j, mc=mc: e.tensor_scalar(out=hbb[:, j, :], in0=hbb[:, j, :], scalar1=mk[:, mc:mc + 1], scalar2=None, op0=ALU.mult),
                     reads=[hbb_k, mk_k], writes=[hbb_k])
            return hbb, hbb_k

        def rms_block(P, src3, src_k, G, n, dst3, dst_k):
            junk, junk_k = P["junkb"].next()
            s4, s4_k = P["ssb"].next()
            K.op("dve", lambda e: e.memset(s4[:], 0.0), writes=[s4_k])
            for j in range(G):
                K.op("act", lambda e, j=j: e.activation(out=junk[:, j, 0:n], in_=src3[:, j, :], func=AF.Square, accum_out=s4[:, j:j + 1]),
                     reads=[src_k], writes=[junk_k, s4_k])
            K.op("act", lambda e: e.activation(out=s4[:, 4:4 + G], in_=s4[:, 0:G], func=AF.Sqrt, bias=EPS, scale=1.0 / n), reads=[s4_k], writes=[s4_k])
            K.op("dve", lambda e: e.reciprocal(out=s4[:, 8:8 + G], in_=s4[:, 4:4 + G]), reads=[s4_k], writes=[s4_k])
            K.op("dve", lambda e: e.tensor_tensor(out=dst3, in0=src3, in1=s4[:, 8:8 + G].unsqueeze(2).to_broadcast([128, G, n]), op=ALU.mult),
                 reads=[src_k, s4_k], writes=[dst_k])

        def headnorm_rope_block(P, f, f_k, G, gain, rp, rp_k, out, out_k):
            f3 = f[:, 0:G, :].rearrange("p j (h d) -> p (j h) d", h=NH)
            f4 = f[:, 0:G, :].rearrange("p j (h d) -> p j h d", h=NH)
            o4 = out[:, 0:G, :].rearrange("p j (h d) -> p j h d", h=NH)
            sq, sq_k = P["junkb"].next()
            st, st_k = P["stb"].next()
            sq3 = sq[:].rearrange("p j c -> p (j c)")[:, 0:G * 768].rearrange("p (a d) -> p a d", d=DQK)
            K.op("dve", lambda e: e.tensor_tensor(out=sq3, in0=f3, in1=f3, op=ALU.mult), reads=[f_k], writes=[sq_k])
            K.op("dve", lambda e: e.reduce_sum(out=st[:, 0:G * NH], in_=sq3, axis=AX.X), reads=[sq_k], writes=[st_k])
            K.op("act", lambda e: e.activation(out=st[:, 32:32 + G * NH], in_=st[:, 0:G * NH], func=AF.Sqrt, bias=EPS, scale=1.0 / DQK),
                 reads=[st_k], writes=[st_k])
            K.op("dve", lambda e: e.reciprocal(out=st[:, 64:64 + G * NH], in_=st[:, 32:32 + G * NH]), reads=[st_k], writes=[st_k])
            K.op("dve", lambda e: e.tensor_tensor(out=f3, in0=f3, in1=st[:, 64:64 + G * NH].unsqueeze(2).to_broadcast([128, G * NH, DQK]), op=ALU.mult),
                 reads=[f_k, st_k], writes=[f_k])
            K.op("dve", lambda e: e.tensor_tensor(out=f3, in0=f3, in1=gain[0][:].unsqueeze(1).to_broadcast([128, G * NH, DQK]), op=ALU.mult),
                 reads=[f_k, gain[1]], writes=[f_k])
            tt, tt_k = P["ttb"].next()
            t5 = tt[:].rearrange("p (a j h d) -> p a j h d", a=4, j=4, h=NH)
            x1 = f4[:, :, :, 64:80]
            x2 = f4[:, :, :, 80:96]
            cs = rp[:, 0:G, 0:16].unsqueeze(2).to_broadcast([128, G, NH, 16])
            sn = rp[:, 0:G, 16:32].unsqueeze(2).to_broadcast([128, G, NH, 16])
            K.op("dve", lambda e: e.tensor_tensor(out=t5[:, 0, 0:G], in0=x1, in1=cs, op=ALU.mult), reads=[f_k, rp_k], writes=[tt_k])
            K.op("dve", lambda e: e.tensor_tensor(out=t5[:, 1, 0:G], in0=x2, in1=sn, op=ALU.mult), reads=[f_k, rp_k], writes=[tt_k])
            K.op("dve", lambda e: e.tensor_tensor(out=t5[:, 2, 0:G], in0=x1, in1=sn, op=ALU.mult), reads=[f_k, rp_k], writes=[tt_k])
            K.op("dve", lambda e: e.tensor_tensor(out=t5[:, 3, 0:G], in0=x2, in1=cs, op=ALU.mult), reads=[f_k, rp_k], writes=[tt_k])
            K.op("dve", lambda e: e.tensor_tensor(out=o4[:, :, :, 64:80], in0=t5[:, 0, 0:G], in1=t5[:, 1, 0:G], op=ALU.subtract), reads=[tt_k], writes=[out_k])
            K.op("dve", lambda e: e.tensor_tensor(out=o4[:, :, :, 80:96], in0=t5[:, 2, 0:G], in1=t5[:, 3, 0:G], op=ALU.add), reads=[tt_k], writes=[out_k])
            K.op("pool", lambda e: e.tensor_copy(out=o4[:, :, :, 0:64], in_=f4[:, :, :, 0:64]), reads=[f_k], writes=[out_k])

        def headnorm_rope(P, f, f_k, gain, rp, rp_k, out, out_k):
            f3 = f[:].rearrange("p (h d) -> p h d", h=NH)
            o3 = out[:].rearrange("p (h d) -> p h d", h=NH)
            sq, sq_k = P["sq"].next()
            st, st_k = P["st"].next()
            K.op("dve", lambda e: e.tensor_tensor(out=sq[:], in0=f[:], in1=f[:], op=ALU.mult), reads=[f_k], writes=[sq_k])
            K.op("dve", lambda e: e.reduce_sum(out=st[:, 0:8], in_=sq[:].rearrange("p (h d) -> p h d", h=NH), axis=AX.X),
                 reads=[sq_k], writes=[st_k])
            K.op("act", lambda e: e.activation(out=st[:, 8:16], in_=st[:, 0:8], func=AF.Sqrt, bias=EPS, scale=1.0 / DQK),
                 reads=[st_k], writes=[st_k])
            K.op("dve", lambda e: e.reciprocal(out=st[:, 16:24], in_=st[:, 8:16]), reads=[st_k], writes=[st_k])
            K.op("dve", lambda e: e.tensor_tensor(out=f3, in0=f3, in1=st[:, 16:24].unsqueeze(2).to_broadcast([128, NH, DQK]), op=ALU.mult),
                 reads=[f_k, st_k], writes=[f_k])
            K.op("dve", lambda e: e.tensor_tensor(out=f3, in0=f3, in1=gain[0][:].unsqueeze(1).to_broadcast([128, NH, DQK]), op=ALU.mult),
                 reads=[f_k, gain[1]], writes=[f_k])
            tt, tt_k = P["tt"].next()
            t4 = tt[:].rearrange("p (a h d) -> p a h d", a=4, h=NH)
            x1 = f3[:, :, 64:80]
            x2 = f3[:, :, 80:96]
            cs = rp[:, 0:16].unsqueeze(1).to_broadcast([128, NH, 16])
            sn = rp[:, 16:32].unsqueeze(1).to_broadcast([128, NH, 16])
            K.op("dve", lambda e: e.tensor_tensor(out=t4[:, 0], in0=x1, in1=cs, op=ALU.mult), reads=[f_k, rp_k], writes=[tt_k])
            K.op("dve", lambda e: e.tensor_tensor(out=t4[:, 1], in0=x2, in1=sn, op=ALU.mult), reads=[f_k, rp_k], writes=[tt_k])
            K.op("dve", lambda e: e.tensor_tensor(out=t4[:, 2], in0=x1, in1=sn, op=ALU.mult), reads=[f_k, rp_k], writes=[tt_k])
            K.op("dve", lambda e: e.tensor_tensor(out=t4[:, 3], in0=x2, in1=cs, op=ALU.mult), reads=[f_k, rp_k], writes=[tt_k])
            K.op("dve", lambda e: e.tensor_tensor(out=o3[:, :, 64:80], in0=t4[:, 0], in1=t4[:, 1], op=ALU.subtract), reads=[tt_k], writes=[out_k])
            K.op("dve", lambda e: e.tensor_tensor(out=o3[:, :, 80:96], in0=t4[:, 2], in1=t4[:, 3], op=ALU.add), reads=[tt_k], writes=[out_k])
            K.op("pool", lambda e: e.tensor_copy(out=o3[:, :, 0:64], in_=f3[:, :, 0:64]), reads=[f_k], writes=[out_k])

        import os as _os
        _stop = _os.environ.get("KSTOP", "")
        _phc = [0]

        class _Stop(Exception):
            pass

        def chk(l):
            _phc[0] += 1
            if _stop and _stop == "%d,%d" % (l, _phc[0]):
                K.skip = True
                print('STOPPED at', _stop)

        try:
          for l in range(n_layers):
              _phc[0] = 0
              Wl = {k: W[(l, k)] for k in wshapes}
              segs = all_segs[l]
              if l == 1:
                  with ExitStack() as ph:
                      selt, selt_k = K.sb(ph, "selt", [128, PS // PCH], F32, dma=True)
                      K.dma("sp", selt[:], psel, selt_k, writes=[selt_k])
                      accr_ = K.ring(ph, "xacc", 2, [128, D], F32, dma=True)
                      ldr_ = K.ring(ph, "xld", 4, [128, D], F32, dma=True)
                      for j in range(PSEG // 128):
                          ac, ac_k = accr_.next()
                          K.op("dve", lambda e: e.memset(ac[:], 0.0), writes=[ac_k])
                          for r_ in range(PS // PCH):
                              row = r_ * PCH - HALO + j * 128
                              if row < 0 or row + 128 > PS:
                                  continue
                              ld, ld_k = ldr_.next()
                              K.dma("sp", ld[:], x1full[row:row + 128, :], ld_k, writes=[ld_k], dram_reads=[x1full_k])
                              K.op("dve", lambda e, r_=r_: e.scalar_tensor_tensor(out=ac[:], in0=ld[:], scalar=selt[:, r_:r_ + 1], in1=ac[:],
                                                                                  op0=ALU.mult, op1=ALU.add),
                                   reads=[ld_k, selt_k, ac_k], writes=[ac_k])
                          K.dma("sp", x1seg[j * 128:(j + 1) * 128, :], ac[:], ac_k, reads=[ac_k], acc=[x1seg_k])
                      K.end_phase()
                      chk(l)
              with ExitStack() as ph:
                  PSR_HI[0] = 8
                  nseg = len(segs)
                  cT, cT_k = K.sb(ph, "cT", [128, nseg * KT], F32, dma=True)
                  crep, crep_k = K.sb(ph, "crep", [128, nseg * KT, 128], BF16)
                  bb, bb_k = K.sb(ph, "bb", [1, 6 * D], BF16, dma=True)
                  n1b, n1b_k = K.sb(ph, "n1b", [128, D], F32, dma=True)
                  n2b, n2b_k = K.sb(ph, "n2b", [128, D], F32, dma=True)
                  wr = K.ring(ph, "wada", 2, [128, KT, 512], BF16, dma=True)
                  modt = [K.sb(ph, "modt%d" % s, [128, 6 * D], F32, dma=True) for s in range(nseg)]
                  K.dma("sp", cT[:], cvT, cT_k, writes=[cT_k])
                  K.op("act", lambda e: e.activation(out=cT[:], in_=cT[:], func=AF.Silu), reads=[cT_k], writes=[cT_k])
                  K.op("dve", lambda e: e.tensor_copy(out=crep[:], in_=cT[:].unsqueeze(2).to_broadcast([128, nseg * KT, 128])),
                       reads=[cT_k], writes=[crep_k])
                  K.dma("pool", bb[:], Wl["b_ada"], bb_k, writes=[bb_k])
                  K.dma("sp", n1b[:], Wl["norm1"][0, :].partition_broadcast(128), n1b_k, writes=[n1b_k])
                  K.dma("sp", n2b[:], Wl["norm2"][0, :].partition_broadcast(128), n2b_k, writes=[n2b_k])
                  wav = wview(Wl["w_ada"])
                  for c in range(12):
                      wt, wt_k = wr.next()
                      K.dma("pool", wt[:], wav[:, :, c * 512:(c + 1) * 512], wt_k, writes=[wt_k])
                      for s in range(nseg):
                          ps, ps_k = psr()
                          for kt in range(KT):
                              K.op("pe", lambda e, kt=kt: e.matmul(ps[:], lhsT=crep[:, s * KT + kt, :], rhs=wt[:, kt, :],
                                                                   start=(kt == 0), stop=False),
                                   reads=[crep_k, wt_k], writes=[ps_k], inc=False)
                          K.op("pe", lambda e: e.matmul(ps[:], lhsT=ones_bf[0:1, :], rhs=bb[0:1, c * 512:(c + 1) * 512],
                                                        start=False, stop=True),
                               reads=[ones_k, bb_k], writes=[ps_k])
                          mt, mt_k = modt[s]
                          K.op("act", lambda e: e.activation(out=mt[:, c * 512:(c + 1) * 512], in_=ps[:], func=AF.Copy),
                               reads=[ps_k], writes=[mt_k])
                  for s in range(nseg):
                      mt, mt_k = modt[s]
                      K.op("dve", lambda e: e.scalar_tensor_tensor(out=mt[:, D:2 * D], in0=mt[:, D:2 * D], scalar=1.0, in1=n1b[:],
                                                                   op0=ALU.add, op1=ALU.mult),
                           reads=[mt_k, n1b_k], writes=[mt_k])
                      K.op("dve", lambda e: e.scalar_tensor_tensor(out=mt[:, 4 * D:5 * D], in0=mt[:, 4 * D:5 * D], scalar=1.0, in1=n2b[:],
                                                                   op0=ALU.add, op1=ALU.mult),
                           reads=[mt_k, n2b_k], writes=[mt_k])
                      K.dma("sp", segs[s]["mod"], mt[:], mt_k, reads=[mt_k], acc=[segs[s]["mod_k"]])
                  K.end_phase()
                  chk(l)

              with ExitStack() as ph:
                  wlat, wlat_k = K.sb(ph, "wlat", [128, KT, 416], BF16, dma=True)
                  wuq, wuq_k = K.sb(ph, "wuq", [128, 2, 768], BF16, dma=True)
                  wukv, wukv_k = K.sb(ph, "wukv", [128, 1024], BF16, dma=True)
                  qan, qan_k = K.sb(ph, "qan", [128, 2], F32, dma=True)
                  kvan, kvan_k = K.sb(ph, "kvan", [128, 1], F32, dma=True)
                  gq, gq_k = K.sb(ph, "gq", [128, DQK], F32, dma=True)
                  gk, gk_k = K.sb(ph, "gk", [128, DQK], F32, dma=True)
                  K.dma("pool", wlat[:], wview(Wl["w_in"])[:, :, 0:416], wlat_k, writes=[wlat_k])
                  K.dma("pool", wuq[:], wview(Wl["w_uq"]), wuq_k, writes=[wuq_k])
                  K.dma("pool", wukv[:], Wl["w_ukv"], wukv_k, writes=[wukv_k])
                  K.dma("sp", qan[:], Wl["qan"], qan_k, writes=[qan_k])
                  K.dma("sp", kvan[:], Wl["kvan"], kvan_k, writes=[kvan_k])
                  K.dma("sp", gq[:], Wl["qhn"][0, :].partition_broadcast(128), gq_k, writes=[gq_k])
                  K.dma("sp", gk[:], Wl["khn"][0, :].partition_broadcast(128), gk_k, writes=[gk_k])
                  P = dict(junkb=K.ring(ph, "junkb", 1, [128, 4, D], F32), ssb=K.ring(ph, "ssb", 3, [128, 16], F32),
                           hbb=K.ring(ph, "hbb", 2, [128, 4, D], BF16), stb=K.ring(ph, "stb", 2, [128, 96], F32),
                           ttb=K.ring(ph, "ttb", 1, [128, 4 * 4 * NH * 16], F32))
                  xbr = K.ring(ph, "xb", 2, [128, 4, D], F32, dma=True)
                  rpbr = K.ring(ph, "rpb", 2, [128, 4, 32], F32, dma=True)
                  hst = K.ring(ph, "hst", 2, [128, KT, 512], BF16, dma=True)
                  qst = K.ring(ph, "qst", 1, [128, NH, 512], BF16, dma=True)
                  kst = K.ring(ph, "kst", 1, [128, NH, 512], BF16, dma=True)
                  vst = K.ring(ph, "vst", 2, [128, NH, 4, 65], BF16, dma=True)
                  for vt, vk in vst.items:
                      K.op("dve", lambda e, vt=vt: e.memset(vt[:], 1.0), writes=[vk])
                  gamt, gam_k = K.sb(ph, "gam1", [128, D], F32, dma=True)
                  sht, sh_k = K.sb(ph, "sh1", [128, D], F32, dma=True)
                  lat_r = K.ring(ph, "latb", 2, [128, 4, 416], F32)
                  cqn_r = K.ring(ph, "cqnb", 2, [128, 4, 256], BF16)
                  cqT_r = K.ring(ph, "cqTb", 2, [128, 2, 512], BF16)
                  ckn_r = K.ring(ph, "cknb", 2, [128, 4, 128], BF16)
                  ckT_r = K.ring(ph, "ckTb", 2, [128, 512], BF16)
                  qf_r = K.ring(ph, "qfb", 2, [128, 4, 768], F32)
                  qb_r = K.ring(ph, "qbb", 2, [128, 4, 768], BF16)

                  def passes_of(sg):
                      if sg["prompt"]:
                          return [(sg["ctx"], sg["s"], False, True, sg["rk"], sg["ctx_k"]),
                                  (sg["x"], sg["n"], True, False, sg["rq"], sg["x_k"])]
                      return [(sg["x"], sg["n"], True, True, sg["rq"], sg["x_k"])]

                  items = [(p_[0], p_[5], p_[4], b0, bw) for sg in segs for p_ in passes_of(sg) for (b0, bw) in blocks_of(p_[1])]
                  loaded = {}
                  ctr = [0]

                  def issue(i):
                      if i >= len(items) or i in loaded:
                          return
                      xsrc_, xsrc_k_, rtab_, b0_, bw_ = items[i]
                      G_ = bw_ // 128
                      xb_, xb_k_ = xbr.next()
                      K.dma("sp", xb_[:, 0:G_, :], xsrc_[b0_:b0_ + bw_, :].rearrange("(j p) d -> p j d", p=128), xb_k_, writes=[xb_k_], dram_reads=[xsrc_k_])
                      rp_, rp_k_ = rpbr.next()
                      K.dma("sp", rp_[:, 0:G_, :], rtab_[b0_:b0_ + bw_, :].rearrange("(j p) d -> p j d", p=128), rp_k_, writes=[rp_k_])
                      loaded[i] = (xb_, xb_k_, rp_, rp_k_)

                  for sg in segs:
                      K.dma("sp", sht[:], sg["mod"][:, 0:D], sh_k, writes=[sh_k], dram_reads=[sg["mod_k"]])
                      K.dma("sp", gamt[:], sg["mod"][:, D:2 * D], gam_k, writes=[gam_k], dram_reads=[sg["mod_k"]])
                      for (xsrc, ntok, want_q, want_k, rtab, xsrc_k) in passes_of(sg):
                          for (b0, bw) in blocks_of(ntok):
                              G = bw // 128
                              hs, hs_k = hst.next()
                              i_ = ctr[0]
                              ctr[0] += 1
                              issue(i_)
                              issue(i_ + 1)
                              xb, xb_k, rp, rp_k = loaded.pop(i_)
                              hbb, hbb_k = norm_block(P, xb, xb_k, G, (gamt, gam_k), (sht, sh_k))
                              lat, lat_k = lat_r.next()
                              for j in range(G):
                                  transpose_to(P, hbb[:, j, :], hbb_k, hs[:, :, j * 128:(j + 1) * 128], hs_k)
                              for j in range(G):
                                  pl, pl_k = psr()
                                  for kt in range(KT):
                                      K.op("pe", lambda e, kt=kt, j=j: e.matmul(pl[:, 0:416], lhsT=hs[:, kt, j * 128:(j + 1) * 128], rhs=wlat[:, kt, :],
                                                                               start=(kt == 0), stop=(kt == KT - 1)),
                                           reads=[hs_k, wlat_k], writes=[pl_k], inc=(kt == KT - 1))
                                  K.op("act", lambda e, j=j: e.activation(out=lat[:, j, :], in_=pl[:, 0:416], func=AF.Copy), reads=[pl_k], writes=[lat_k])
                              if want_q:
                                  qs, qs_k = qst.next()
                                  cqn, cqn_k = cqn_r.next()
                                  rms_block(P, lat[:, 0:G, 0:256], lat_k, G, 256, cqn[:, 0:G, :], cqn_k)
                                  cqT, cqT_k = cqT_r.next()
                                  p2, p2_k = psr()
                                  p2b = p2[:].bitcast(BF16)
                                  for j in range(G):
                                      for i in range(2):
                                          K.op("pe", lambda e, i=i, j=j: e.transpose(p2b[:, (j * 2 + i) * 128:(j * 2 + i + 1) * 128], cqn[:, j, i * 128:(i + 1) * 128], ident[:]),
                                               reads=[cqn_k, ident_k], writes=[p2_k], inc=(j == G - 1 and i == 1))
                                  for i in range(2):
                                      K.op("act", lambda e, i=i: e.activation(out=cqT[:, i, 0:G * 128].rearrange("p (j c) -> p j c", j=G),
                                                                              in_=p2b[:, 0:G * 256].rearrange("p (j i c) -> p j i c", j=G, i=2)[:, :, i, :],
                                                                              func=AF.Copy, scale=qan[:, i:i + 1]),
                                           reads=[p2_k, qan_k], writes=[cqT_k])
                                  qf, qf_k = qf_r.next()
                                  for j in range(G):
                                      pq0, pq0_k = psr()
                                      pq1, pq1_k = psr()
                                      for i in range(2):
                                          K.op("pe", lambda e, i=i, j=j: e.matmul(pq0[:, 0:480], lhsT=cqT[:, i, j * 128:(j + 1) * 128], rhs=wuq[:, i, 0:480], start=(i == 0), stop=(i == 1)),
                                               reads=[cqT_k, wuq_k], writes=[pq0_k], inc=(i == 1))
                                      for i in range(2):
                                          K.op("pe", lambda e, i=i, j=j: e.matmul(pq1[:, 0:288], lhsT=cqT[:, i, j * 128:(j + 1) * 128], rhs=wuq[:, i, 480:768], start=(i == 0), stop=(i == 1)),
                                               reads=[cqT_k, wuq_k], writes=[pq1_k], inc=(i == 1))
                                      K.op("act", lambda e, j=j: e.activation(out=qf[:, j, 0:480], in_=pq0[:, 0:480], func=AF.Copy), reads=[pq0_k], writes=[qf_k])
                                      K.op("dve", lambda e, j=j: e.tensor_copy(out=qf[:, j, 480:768], in_=pq1[:, 0:288]), reads=[pq1_k], writes=[qf_k])
                                  qb, qb_k = qb_r.next()
                                  headnorm_rope_block(P, qf, qf_k, G, (gq, gq_k), rp, rp_k, qb, qb_k)
                                  for j in range(G):
                                      transpose_to(P, qb[:, j, :], qb_k, qs[0:DQK, :, j * 128:(j + 1) * 128], qs_k, ncols_src=768, chunk=DQK)
                              if want_k:
                                  ks, ks_k = kst.next()
                                  vs, vs_k = vst.next()
                                  ckn, ckn_k = ckn_r.next()
                                  rms_block(P, lat[:, 0:G, 256:384], lat_k, G, 128, ckn[:, 0:G, :], ckn_k)
                                  ckT, ckT_k = ckT_r.next()
                                  p3, p3_k = psr()
                                  p3b = p3[:].bitcast(BF16)
                                  for j in range(G):
                                      K.op("pe", lambda e, j=j: e.transpose(p3b[:, j * 128:(j + 1) * 128], ckn[:, j, :], ident[:]),
                                           reads=[ckn_k, ident_k], writes=[p3_k], inc=(j == G - 1))
                                  K.op("act", lambda e: e.activation(out=ckT[:, 0:G * 128], in_=p3b[:, 0:G * 128], func=AF.Copy, scale=kvan[:, 0:1]),
                                       reads=[p3_k, kvan_k], writes=[ckT_k])
                                  kf, kf_k = qf_r.next()
                                  kf4 = kf[:].rearrange("p j (h d) -> p j h d", h=NH)
                                  for j in range(G):
                                      pk, pk_k = psr()
                                      pv, pv_k = psr()
                                      K.op("pe", lambda e, j=j: e.matmul(pk[:], lhsT=ckT[:, j * 128:(j + 1) * 128], rhs=wukv[:, 0:512], start=True, stop=True),
                                           reads=[ckT_k, wukv_k], writes=[pk_k])
                                      K.op("pe", lambda e, j=j: e.matmul(pv[:], lhsT=ckT[:, j * 128:(j + 1) * 128], rhs=wukv[:, 512:1024], start=True, stop=True),
                                           reads=[ckT_k, wukv_k], writes=[pv_k])
                                      K.op("act", lambda e, j=j: e.activation(out=kf4[:, j, :, 0:64], in_=pk[:].rearrange("p (h d) -> p h d", h=NH), func=AF.Copy),
                                           reads=[pk_k], writes=[kf_k])
                                      K.op("act", lambda e, j=j: e.activation(out=vs[:, :, j, 0:64], in_=pv[:].rearrange("p (h d) -> p h d", h=NH), func=AF.Copy),
                                           reads=[pv_k], writes=[vs_k])
                                  K.op("dve", lambda e: e.tensor_copy(out=kf4[:, 0:G, :, 64:96], in_=lat[:, 0:G, 384:416].unsqueeze(2).to_broadcast([128, G, NH, 32])),
                                       reads=[lat_k], writes=[kf_k])
                                  kb, kb_k = qb_r.next()
                                  headnorm_rope_block(P, kf, kf_k, G, (gk, gk_k), rp, rp_k, kb, kb_k)
                                  for j in range(G):
                                      transpose_to(P, kb[:, j, :], kb_k, ks[0:DQK, :, j * 128:(j + 1) * 128], ks_k, ncols_src=768, chunk=DQK)
                              nj = G
                              if want_q:
                                  K.dma("sp", sg["hT"][:, :, b0:b0 + bw], hs[:, :, 0:bw], hs_k, reads=[hs_k], acc=[sg["hT_k"]])
                                  K.dma("sp", sg["qT"][:, :, b0:b0 + bw], qs[0:DQK, :, 0:bw], qs_k, reads=[qs_k], acc=[sg["qT_k"]])
                              if want_k:
                                  K.dma("sp", sg["kT"].rearrange("h d s -> d h s")[:, :, b0:b0 + bw], ks[0:DQK, :, 0:bw], ks_k, reads=[ks_k], acc=[sg["kT_k"]])
                                  K.dma("sp", sg["v"].rearrange("h p k c -> p h k c")[:, :, b0 // 128:b0 // 128 + nj, :], vs[:, :, 0:nj, :], vs_k,
                                        reads=[vs_k], acc=[sg["v_k"]])
                  K.end_phase()
                  chk(l)

              with ExitStack() as ph:
                  PSR_HI[0] = 4
                  NW = D_IN - 416
                  wbig, wbig_k = K.sb(ph, "wbig", [128, KT, NW], BF16, dma=True)
                  wv = wview(Wl["w_in"])
                  for c in range(0, NW, 640):
                      K.dma("pool", wbig[:, :, c:c + 640], wv[:, :, 416 + c:416 + c + 640], wbig_k, acc=[wbig_k])
                  wsT, wsT_k = K.sb(ph, "wsT", [128, 4, 128], BF16, dma=True)
                  K.dma("pool", wsT[:], Wl["sgwT"].rearrange("p (g q) -> p g q", g=4), wsT_k, writes=[wsT_k])
                  lng, lng_k = K.sb(ph, "lng", [128, 512], F32, dma=True)
                  lnb, lnb_k = K.sb(ph, "lnb", [128, 512], F32, dma=True)
                  bsb, bsb_k = K.sb(ph, "bsb", [128, 4, 4, 128], F32, dma=True)
                  K.dma("sp", lng[:], Wl["sglng"][0, :].partition_broadcast(128), lng_k, writes=[lng_k])
                  K.dma("sp", lnb[:], Wl["sglnb"][0, :].partition_broadcast(128), lnb_k, writes=[lnb_k])
                  for j in range(4):
                      K.dma("sp", bsb[:, :, j, :], Wl["sgb"][0, :].partition_broadcast(128).rearrange("p (g q) -> p g q", g=4), bsb_k, acc=[bsb_k])
                  hbr = K.ring(ph, "hTb", 2, [128, KT, 512], BF16, dma=True)
                  sgt_r = K.ring(ph, "sgt", 2, [128, 512], F32)
                  glub_r = K.ring(ph, "glub", 2, [128, 4, 512], BF16, dma=True)
                  ug_r = K.ring(ph, "ug", 2, [128, 4, 512], BF16)
                  vg_r = K.ring(ph, "vg", 2, [128, 512], F32)
                  jk_r = K.ring(ph, "jk2", 1, [128, 512], F32)
                  s8_r = K.ring(ph, "s8", 3, [128, 8], F32)
                  vnb_r = K.ring(ph, "vnb", 4, [128, 512], BF16)
                  tq_r = K.ring(ph, "tq", 2, [128, 512], F32)
                  usb_r = K.ring(ph, "usb", 2, [128, 4, 512], BF16, dma=True)
                  gst_r = K.ring(ph, "gst", 2, [128, 24, 512], BF16, dma=True)
                  items = [(sg, b0, bw) for sg in segs for (b0, bw) in blocks_of(sg["n"])]
                  loaded = {}
                  ctr = [0]

                  def issue(i):
                      if i >= len(items) or i in loaded:
                          return
                      sg_, b0_, bw_ = items[i]
                      hT_, hT_k_ = hbr.next()
                      K.dma("sp", hT_[:, :, 0:bw_], sg_["hT"][:, :, b0_:b0_ + bw_], hT_k_, writes=[hT_k_], dram_reads=[sg_["hT_k"]])
                      loaded[i] = (hT_, hT_k_)

                  for sg in segs:
                      n = sg["n"]
                      blks = blocks_of(n)
                      for bi, (b0, bw) in enumerate(blks):
                          nj = bw // 128
                          i_ = ctr[0]
                          ctr[0] += 1
                          issue(i_)
                          issue(i_ + 1)
                          hT, hT_k = loaded.pop(i_)

                          def fm(col0, ps, ps_k):
                              for kt in range(KT):
                                  K.op("pe", lambda e, kt=kt: e.matmul(ps[:, 0:bw], lhsT=wbig[:, kt, col0:col0 + 128], rhs=hT[:, kt, 0:bw],
                                                                       start=(kt == 0), stop=(kt == KT - 1)),
                                       reads=[wbig_k, hT_k], writes=[ps_k], inc=(kt == KT - 1))
                          glub, glub_k = glub_r.next()
                          for c in range(4):
                              pa, pa_k = psr()
                              pg, pg_k = psr()
                              fm(c * 128, pa, pa_k)
                              fm(512 + c * 128, pg, pg_k)
                              sgt, sgt_k = sgt_r.next()
                              K.op("act", lambda e: e.activation(out=sgt[:, 0:bw], in_=pg[:, 0:bw], func=AF.Sigmoid), reads=[pg_k], writes=[sgt_k])
                              K.op("dve", lambda e, c=c: e.tensor_tensor(out=glub[:, c, 0:bw], in0=pa[:, 0:bw], in1=sgt[:, 0:bw], op=ALU.mult),
                                   reads=[pa_k, sgt_k], writes=[glub_k])
                          if sg["prompt"]:
                              if bi == 0:
                                  K.op("dve", lambda e: e.tensor_scalar(out=glub[:, :, 0:128], in0=glub[:, :, 0:128], scalar1=mk[:, 0:1], scalar2=None, op0=ALU.mult),
                                       reads=[glub_k, mk_k], writes=[glub_k])
                              if bi == len(blks) - 1:
                                  K.op("dve", lambda e: e.tensor_scalar(out=glub[:, :, bw - 128:bw], in0=glub[:, :, bw - 128:bw], scalar1=mk[:, 1:2], scalar2=None, op0=ALU.mult),
                                       reads=[glub_k, mk_k], writes=[glub_k])
                          K.dma("sp", sg["glu"][:, :, 15 + b0:15 + b0 + bw], glub[:, :, 0:bw], glub_k, reads=[glub_k], acc=[sg["glu_k"]])
                          ug, ug_k = ug_r.next()
                          for g in range(4):
                              pu, pu_k = psr()
                              fm(1024 + g * 128, pu, pu_k)
                              K.op("act", lambda e, g=g: e.activation(out=ug[:, g, 0:bw], in_=pu[:, 0:bw], func=AF.Gelu_apprx_tanh), reads=[pu_k], writes=[ug_k])
                          pss = [PSB[4 + g] for g in range(4)]
                          vnbs = []
                          for j in range(nj):
                              pv, pv_k = psr()
                              for kt in range(KT):
                                  K.op("pe", lambda e, kt=kt: e.matmul(pv[:], lhsT=hT[:, kt, j * 128:(j + 1) * 128], rhs=wbig[:, kt, 1536:2048],
                                                                       start=(kt == 0), stop=(kt == KT - 1)),
                                       reads=[hT_k, wbig_k], writes=[pv_k], inc=(kt == KT - 1))
                              vg, vg_k = vg_r.next()
                              K.op("act", lambda e: e.activation(out=vg[:], in_=pv[:], func=AF.Gelu_apprx_tanh), reads=[pv_k], writes=[vg_k])
                              s8, s8_k = s8_r.next()
                              jk, jk_k = jk_r.next()
                              K.op("dve", lambda e: e.memset(s8[:], 0.0), writes=[s8_k])
                              K.op("act", lambda e: e.activation(out=jk[:], in_=vg[:], func=AF.Square, accum_out=s8[:, 1:2]), reads=[vg_k, s8_k], writes=[jk_k, s8_k])
                              K.op("dve", lambda e: e.reduce_sum(out=s8[:, 0:1], in_=vg[:], axis=AX.X), reads=[vg_k, s8_k], writes=[s8_k])
                              K.op("dve", lambda e: e.tensor_scalar(out=s8[:, 2:3], in0=s8[:, 0:1], scalar1=1.0 / 512, scalar2=None, op0=ALU.mult), reads=[s8_k], writes=[s8_k])
                              K.op("dve", lambda e: e.tensor_tensor(out=s8[:, 3:4], in0=s8[:, 2:3], in1=s8[:, 2:3], op=ALU.mult), reads=[s8_k], writes=[s8_k])
                              K.op("dve", lambda e: e.scalar_tensor_tensor(out=s8[:, 4:5], in0=s8[:, 1:2], scalar=1.0 / 512, in1=s8[:, 3:4], op0=ALU.mult, op1=ALU.subtract),
                                   reads=[s8_k], writes=[s8_k])
                              K.op("act", lambda e: e.activation(out=s8[:, 5:6], in_=s8[:, 4:5], func=AF.Sqrt, bias=EPS, scale=1.0), reads=[s8_k], writes=[s8_k])
                              K.op("dve", lambda e: e.reciprocal(out=s8[:, 6:7], in_=s8[:, 5:6]), reads=[s8_k], writes=[s8_k])
                              K.op("dve", lambda e: e.tensor_scalar(out=vg[:], in0=vg[:], scalar1=s8[:, 2:3], scalar2=s8[:, 6:7], op0=ALU.subtract, op1=ALU.mult),
                                   reads=[vg_k, s8_k], writes=[vg_k])
                              K.op("dve", lambda e: e.tensor_tensor(out=vg[:], in0=vg[:], in1=lng[:], op=ALU.mult), reads=[vg_k, lng_k], writes=[vg_k])
                              vnb, vnb_k = vnb_r.next()
                              K.op("pool", lambda e: e.tensor_tensor(out=vnb[:], in0=vg[:], in1=lnb[:], op=ALU.add), reads=[vg_k, lnb_k], writes=[vnb_k])
                              vnbs.append((vnb, vnb_k))
                          gst, gst_k = gst_r.next()
                          for m in range(24):
                              pg, pg_k = psr()
                              fm(2048 + m * 128, pg, pg_k)
                              K.op("act", lambda e, m=m: e.activation(out=gst[:, m, 0:bw], in_=pg[:, 0:bw], func=AF.Sigmoid), reads=[pg_k], writes=[gst_k])
                          K.dma("sp", sg["gat"][:, :, b0:b0 + bw], gst[:, :, 0:bw], gst_k, reads=[gst_k], acc=[sg["gat_k"]])
                          for j in range(nj):
                              vnb, vnb_k = vnbs[j]
                              for g in range(4):
                                  K.op("pe", lambda e, g=g: e.matmul(pss[g][0][:, j * 128:(j + 1) * 128], lhsT=vnb[:, g * 128:(g + 1) * 128], rhs=wsT[:, g, :],
                                                                     start=True, stop=True),
                                       reads=[vnb_k, wsT_k], writes=[pss[g][1]], inc=(g == 3))
                          usb, usb_k = usb_r.next()
                          for g in range(4):
                              tq, tq_k = tq_r.next()
                              K.op("dve", lambda e, g=g: e.tensor_tensor(out=tq[:, 0:bw], in0=pss[g][0][:, 0:bw],
                                                                         in1=bsb[:, g, :, :].rearrange("p j q -> p (j q)")[:, 0:bw], op=ALU.add),
                                   reads=[pss[g][1], bsb_k], writes=[tq_k])
                              K.op("dve", lambda e, g=g: e.tensor_tensor(out=usb[:, g, 0:bw], in0=tq[:, 0:bw], in1=ug[:, g, 0:bw], op=ALU.mult),
                                   reads=[tq_k, ug_k], writes=[usb_k])
                          K.dma("sp", sg["us"][:, :, b0:b0 + bw], usb[:, :, 0:bw], usb_k, reads=[usb_k], acc=[sg["us_k"]])
                  K.end_phase()
                  chk(l)

              with ExitStack() as ph:
                  PSR_HI[0] = 4
                  KSB = 4096
                  qtr = K.ring(ph, "qtb", 2, [128, NH, 512], BF16, dma=True)
                  ktr = K.ring(ph, "ktb", 2, [128, KSB], BF16, dma=True)
                  vtr = K.ring(ph, "vtb", 2, [128, KSB // 128, 65], BF16, dma=True)
                  ptr_ = K.ring(ph, "ptb", 4, [128, 512], BF16)
                  aor = K.ring(ph, "aob", 2, [64, NH, 512], BF16, dma=True)
                  rsr = K.ring(ph, "rsb", 2, [128, 512], F32)
                  rir = K.ring(ph, "rib", 2, [64, 512], F32)
                  scale = float(DQK) ** -0.5
                  items = [(sg, b0, bw) for sg in segs for (b0, bw) in blocks_of(sg["n"])]
                  loaded = {}
                  ctr = [0]

                  def issue(i):
                      if i >= len(items) or i in loaded:
                          return
                      sg_, b0_, bw_ = items[i]
                      qt_, qt_k_ = qtr.next()
                      K.dma("sp", qt_[0:DQK, :, 0:bw_], sg_["qT"][:, :, b0_:b0_ + bw_], qt_k_, writes=[qt_k_], dram_reads=[sg_["qT_k"]])
                      loaded[i] = (qt_, qt_k_)

                  for sg in segs:
                      n, S = sg["n"], sg["s"]
                      for (b0, bw) in blocks_of(n):
                          i_ = ctr[0]
                          ctr[0] += 1
                          issue(i_)
                          issue(i_ + 1)
                          qt, qt_k = loaded.pop(i_)
                          ao, ao_k = aor.next()
                          for h in range(NH):
                              po, po_k = PSB[4 + (h % 2)]
                              first = True
                              for (s0, sw) in blocks_of(S, KSB):
                                  kt_, kt_k = ktr.next()
                                  vt_, vt_k = vtr.next()
                                  K.dma("sp", kt_[0:DQK, 0:sw], sg["kT"][h, :, s0:s0 + sw], kt_k, writes=[kt_k], dram_reads=[sg["kT_k"]])
                                  K.dma("sp", vt_[:, 0:sw // 128, :], sg["v"][h, :, s0 // 128:(s0 + sw) // 128, :], vt_k, writes=[vt_k], dram_reads=[sg["v_k"]])
                                  nk = sw // 128
                                  pend = None
                                  for ki in range(nk + 1):
                                      if ki < nk:
                                          ps, ps_k = psr(0, 4)
                                          K.op("pe", lambda e, ki=ki: e.matmul(ps[:, 0:bw], lhsT=kt_[0:DQK, ki * 128:(ki + 1) * 128], rhs=qt[0:DQK, h, 0:bw],
                                                                               start=True, stop=True),
                                               reads=[kt_k, qt_k], writes=[ps_k])
                                          pt, pt_k = ptr_.next()
                                          K.op("act", lambda e: e.activation(out=pt[:, 0:bw], in_=ps[:, 0:bw], func=AF.Exp, scale=scale), reads=[ps_k], writes=[pt_k])
                                          cur = (pt, pt_k, ki)
                                      else:
                                          cur = None
                                      if pend is not None:
                                          ppt, ppt_k, pki = pend
                                          last = (s0 + sw >= S) and (pki == nk - 1)
                                          K.op("pe", lambda e, pki=pki, ppt=ppt, f=first, last=last: e.matmul(po[0:65, 0:bw], lhsT=vt_[:, pki, 0:65], rhs=ppt[:, 0:bw],
                                                                                                            start=f, stop=last),
                                               reads=[vt_k, ppt_k], writes=[po_k])
                                          first = False
                                      pend = cur
                              rs, rs_k = rsr.next()
                              K.op("dve", lambda e: e.tensor_copy(out=rs[64:65, 0:bw], in_=po[64:65, 0:bw]), reads=[po_k], writes=[rs_k])
                              pb, pb_k = PSB[6 + (h % 2)]
                              K.op("pe", lambda e: e.matmul(pb[0:64, 0:bw], lhsT=ones_f[64:65, 0:64], rhs=rs[64:65, 0:bw], start=True, stop=True),
                                   reads=[onesf_k, rs_k], writes=[pb_k])
                              ri, ri_k = rir.next()
                              K.op("dve", lambda e: e.reciprocal(out=ri[:, 0:bw], in_=pb[0:64, 0:bw]), reads=[pb_k], writes=[ri_k])
                              K.op("dve", lambda e, h=h: e.tensor_tensor(out=ao[:, h, 0:bw], in0=po[0:64, 0:bw], in1=ri[:, 0:bw], op=ALU.mult),
                                   reads=[po_k, ri_k], writes=[ao_k])
                          K.dma("pool", sg["ao"][:, :, b0:b0 + bw], ao[:, :, 0:bw], ao_k, reads=[ao_k], acc=[sg["ao_k"]])
                  K.end_phase()
                  chk(l)

              with ExitStack() as ph:
                  PSR_HI[0] = 8
                  wao, wao_k = K.sb(ph, "wao", [64, NH, D], BF16, dma=True)
                  wco, wco_k = K.sb(ph, "wco", [128, 4, D], BF16, dma=True)
                  wso, wso_k = K.sb(ph, "wso", [128, 4, D], BF16, dma=True)
                  wout, wout_k = K.sb(ph, "wout", [128, KT, D], BF16, dma=True)
                  K.dma("pool", wao[:], Wl["w_ao"].rearrange("p (h n) -> p h n", h=NH), wao_k, writes=[wao_k])
                  K.dma("pool", wco[:], wview(Wl["w_co"]), wco_k, writes=[wco_k])
                  K.dma("pool", wso[:], wview(Wl["w_so"]), wso_k, writes=[wso_k])
                  K.dma("pool", wout[:], wview(Wl["w_out"]), wout_k, writes=[wout_k])
                  cdw, cdw_k = K.sb(ph, "cdw", [128, 4, 31], F32, dma=True)
                  cdwb, cdwb_k = K.sb(ph, "cdwb", [128, 4], F32, dma=True)
                  clng, clng_k = K.sb(ph, "clng", [128, 4], F32, dma=True)
                  clnb, clnb_k = K.sb(ph, "clnb", [128, 4], F32, dma=True)
                  K.dma("sp", cdw[:], Wl["cdw"].rearrange("p (c k) -> p c k", c=4), cdw_k, writes=[cdw_k])
                  K.dma("sp", cdwb[:], Wl["cdwb"], cdwb_k, writes=[cdwb_k])
                  K.dma("sp", clng[:], Wl["clng"], clng_k, writes=[clng_k])
                  K.dma("sp", clnb[:], Wl["clnb"], clnb_k, writes=[clnb_k])
                  dgt, dgt_k = K.sb(ph, "dgt", [128, 4, 31, 128], BF16)
                  for c in range(4):
                      for k in range(31):
                          K.op("dve", lambda e, c=c, k=k: e.tensor_scalar(out=dgt[:, c, k, :], in0=ident[:], scalar1=cdw[:, c, k:k + 1], scalar2=None, op0=ALU.mult),
                               reads=[ident_k, cdw_k], writes=[dgt_k])
                  g1t, g1_k = K.sb(ph, "g1t", [128, D], F32, dma=True)
                  glr = K.ring(ph, "glt", 2, [128, 4, 512 + 30], BF16, dma=True)
                  usr = K.ring(ph, "ust", 2, [128, 4, 512], BF16, dma=True)
                  gtr = K.ring(ph, "gtt", 1, [128, 24, 512], BF16, dma=True)
                  aor = K.ring(ph, "aot", 2, [64, NH, 512], BF16, dma=True)
                  hc, hc_k = K.sb(ph, "hc", [128, 4, 512], F32)
                  hcb, hcb_k = K.sb(ph, "hcb", [128, 4, 512], BF16)
                  sqb, sqb_k = K.sb(ph, "sqb", [128, 4, 512], BF16)
                  mean, mean_k = K.sb(ph, "mean", [128, 512], F32)
                  rstd, rstd_k = K.sb(ph, "rstd", [128, 512], F32)
                  cvn, cvn_k = K.sb(ph, "cvn", [128, 4, 512], BF16)
                  mrg, mrg_k = K.sb(ph, "mrg", [128, KT, 512], BF16)
                  tmr = K.ring(ph, "tm", 4, [128, 512], F32)
                  xr = K.ring(ph, "xr2", 2, [128, D], F32, dma=True)
                  items = [(sg, b0, bw) for sg in segs for (b0, bw) in blocks_of(sg["n"])]
                  loaded = {}
                  ctr = [0]

                  def issue(i):
                      if i >= len(items) or i in loaded:
                          return
                      sg_, b0_, bw_ = items[i]
                      glt_, glt_k_ = glr.next()
                      K.dma("sp", glt_[:, :, 0:bw_ + 30], sg_["glu"][:, :, b0_:b0_ + bw_ + 30], glt_k_, writes=[glt_k_], dram_reads=[sg_["glu_k"]])
                      ust_, ust_k_ = usr.next()
                      K.dma("sp", ust_[:, :, 0:bw_], sg_["us"][:, :, b0_:b0_ + bw_], ust_k_, writes=[ust_k_], dram_reads=[sg_["us_k"]])
                      aot_, aot_k_ = aor.next()
                      K.dma("sp", aot_[:, :, 0:bw_], sg_["ao"][:, :, b0_:b0_ + bw_], aot_k_, writes=[aot_k_], dram_reads=[sg_["ao_k"]])
                      loaded[i] = (glt_, glt_k_, ust_, ust_k_, aot_, aot_k_)

                  for sg in segs:
                      n = sg["n"]
                      K.dma("sp", g1t[:], sg["mod"][:, 2 * D:3 * D], g1_k, writes=[g1_k], dram_reads=[sg["mod_k"]])
                      for (b0, bw) in blocks_of(n):
                          nj = bw // 128
                          i_ = ctr[0]
                          ctr[0] += 1
                          issue(i_)
                          issue(i_ + 1)
                          glt, glt_k, ust, ust_k, aot, aot_k = loaded.pop(i_)
                          gtt, gtt_k = gtr.next()
                          K.dma("sp", gtt[:, :, 0:bw], sg["gat"][:, :, b0:b0 + bw], gtt_k, writes=[gtt_k], dram_reads=[sg["gat_k"]])
                          for c in range(4):
                              ps, ps_k = psr()
                              for k in range(31):
                                  K.op("pe", lambda e, c=c, k=k: e.matmul(ps[:, 0:bw], lhsT=dgt[:, c, k, :], rhs=glt[:, c, k:k + bw], start=(k == 0), stop=(k == 30)),
                                       reads=[dgt_k, glt_k], writes=[ps_k], inc=(k == 30))
                              K.op("act", lambda e, c=c: e.activation(out=hc[:, c, 0:bw], in_=ps[:, 0:bw], func=AF.Identity, bias=cdwb[:, c:c + 1], scale=1.0),
                                   reads=[ps_k, cdwb_k], writes=[hc_k])
                          K.op("pool", lambda e: e.tensor_copy(out=hcb[:, :, 0:bw], in_=hc[:, :, 0:bw]), reads=[hc_k], writes=[hcb_k])
                          K.op("dve", lambda e: e.tensor_tensor(out=sqb[:, :, 0:bw], in0=hc[:, :, 0:bw], in1=hc[:, :, 0:bw], op=ALU.mult), reads=[hc_k], writes=[sqb_k])
                          p1, p1_k = psr()
                          p2, p2_k = psr()
                          for c in range(4):
                              K.op("pe", lambda e, c=c: e.matmul(p1[:, 0:bw], lhsT=ones_bf[:], rhs=hcb[:, c, 0:bw], start=(c == 0), stop=(c == 3)),
                                   reads=[ones_k, hcb_k], writes=[p1_k], inc=(c == 3))
                          for c in range(4):
                              K.op("pe", lambda e, c=c: e.matmul(p2[:, 0:bw], lhsT=ones_bf[:], rhs=sqb[:, c, 0:bw], start=(c == 0), stop=(c == 3)),
                                   reads=[ones_k, sqb_k], writes=[p2_k], inc=(c == 3))
                          K.op("dve", lambda e: e.tensor_scalar(out=mean[:, 0:bw], in0=p1[:, 0:bw], scalar1=1.0 / 512, scalar2=None, op0=ALU.mult), reads=[p1_k], writes=[mean_k])
                          tm, tm_k = tmr.next()
                          K.op("dve", lambda e: e.tensor_tensor(out=tm[:, 0:bw], in0=mean[:, 0:bw], in1=mean[:, 0:bw], op=ALU.mult), reads=[mean_k], writes=[tm_k])
                          K.op("dve", lambda e: e.scalar_tensor_tensor(out=tm[:, 0:bw], in0=p2[:, 0:bw], scalar=1.0 / 512, in1=tm[:, 0:bw], op0=ALU.mult, op1=ALU.subtract),
                               reads=[p2_k, tm_k], writes=[tm_k])
                          K.op("act", lambda e: e.activation(out=tm[:, 0:bw], in_=tm[:, 0:bw], func=AF.Sqrt, bias=EPS, scale=1.0), reads=[tm_k], writes=[tm_k])
                          K.op("dve", lambda e: e.reciprocal(out=rstd[:, 0:bw], in_=tm[:, 0:bw]), reads=[tm_k], writes=[rstd_k])
                          for c in range(4):
                              t2, t2_k = tmr.next()
                              K.op("dve", lambda e, c=c: e.tensor_tensor(out=t2[:, 0:bw], in0=hc[:, c, 0:bw], in1=mean[:, 0:bw], op=ALU.subtract), reads=[hc_k, mean_k], writes=[t2_k])
                              K.op("dve", lambda e: e.tensor_tensor(out=t2[:, 0:bw], in0=t2[:, 0:bw], in1=rstd[:, 0:bw], op=ALU.mult), reads=[t2_k, rstd_k], writes=[t2_k])
                              K.op("act", lambda e, c=c: e.activation(out=cvn[:, c, 0:bw], in_=t2[:, 0:bw], func=AF.Silu, bias=clnb[:, c:c + 1], scale=clng[:, c:c + 1]),
                                   reads=[t2_k, clnb_k, clng_k], writes=[cvn_k])
                          for j in range(8):
                              pa, pa_k = psr()
                              pc, pc_k = psr()
                              pss_, pss_k = psr()
                              for h in range(NH):
                                  K.op("pe", lambda e, h=h: e.matmul(pa[:, 0:bw], lhsT=wao[:, h, j * 128:(j + 1) * 128], rhs=aot[:, h, 0:bw], start=(h == 0), stop=(h == NH - 1)),
                                       reads=[wao_k, aot_k], writes=[pa_k], inc=(h == NH - 1))
                              for c in range(4):
                                  K.op("pe", lambda e, c=c: e.matmul(pc[:, 0:bw], lhsT=wco[:, c, j * 128:(j + 1) * 128], rhs=cvn[:, c, 0:bw], start=(c == 0), stop=(c == 3)),
                                       reads=[wco_k, cvn_k], writes=[pc_k], inc=(c == 3))
                              for c in range(4):
                                  K.op("pe", lambda e, c=c: e.matmul(pss_[:, 0:bw], lhsT=wso[:, c, j * 128:(j + 1) * 128], rhs=ust[:, c, 0:bw], start=(c == 0), stop=(c == 3)),
                                       reads=[wso_k, ust_k], writes=[pss_k], inc=(c == 3))
                              m1, m1_k = tmr.next()
                              m2, m2_k = tmr.next()
                              m3, m3_k = tmr.next()
                              K.op("dve", lambda e: e.tensor_tensor(out=m1[:, 0:bw], in0=pa[:, 0:bw], in1=gtt[:, j, 0:bw], op=ALU.mult), reads=[pa_k, gtt_k], writes=[m1_k])
                              K.op("dve", lambda e: e.tensor_tensor(out=m2[:, 0:bw], in0=pc[:, 0:bw], in1=gtt[:, 8 + j, 0:bw], op=ALU.mult), reads=[pc_k, gtt_k], writes=[m2_k])
                              K.op("dve", lambda e: e.tensor_tensor(out=m3[:, 0:bw], in0=pss_[:, 0:bw], in1=gtt[:, 16 + j, 0:bw], op=ALU.mult), reads=[pss_k, gtt_k], writes=[m3_k])
                              K.op("pool", lambda e: e.tensor_tensor(out=m1[:, 0:bw], in0=m1[:, 0:bw], in1=m2[:, 0:bw], op=ALU.add), reads=[m1_k, m2_k], writes=[m1_k])
                              K.op("pool", lambda e, j=j: e.tensor_tensor(out=mrg[:, j, 0:bw], in0=m1[:, 0:bw], in1=m3[:, 0:bw], op=ALU.add), reads=[m1_k, m3_k], writes=[mrg_k])
                          for t in range(nj):
                              t0 = b0 + t * 128
                              xt, xk = xr.next()
                              K.dma("sp", xt[:], sg["x"][t0:t0 + 128, :], xk, writes=[xk], dram_reads=[sg["x_k"]])
                              for half in range(2):
                                  po, po_k = psr()
                                  for j in range(8):
                                      K.op("pe", lambda e, j=j: e.matmul(po[:], lhsT=mrg[:, j, t * 128:(t + 1) * 128], rhs=wout[:, j, half * 512:(half + 1) * 512],
                                                                         start=(j == 0), stop=(j == 7)),
                                           reads=[mrg_k, wout_k], writes=[po_k], inc=(j == 7))
                                  tm2, tm2_k = tmr.next()
                                  K.op("dve", lambda e: e.tensor_tensor(out=tm2[:], in0=po[:], in1=g1t[:, half * 512:(half + 1) * 512], op=ALU.mult), reads=[po_k, g1_k], writes=[tm2_k])
                                  K.op("pool", lambda e: e.tensor_tensor(out=xt[:, half * 512:(half + 1) * 512], in0=xt[:, half * 512:(half + 1) * 512], in1=tm2[:], op=ALU.add),
                                       reads=[xk, tm2_k], writes=[xk])
                              K.dma("pool", sg["xm"][t0:t0 + 128, :], xt[:], xk, reads=[xk], acc=[sg["xm_k"]])
                  K.end_phase()
                  chk(l)

              with ExitStack() as ph:
                  P = dict(junkb=K.ring(ph, "junkc", 2, [128, 4, D], F32), ssb=K.ring(ph, "ssc", 3, [128, 16], F32),
                           hbb=K.ring(ph, "hbc", 2, [128, 4, D], BF16))
                  xbr = K.ring(ph, "xb3", 3, [128, 4, D], F32, dma=True)
                  hst = K.ring(ph, "hst2", 2, [128, KT, 512], BF16, dma=True)
                  gamt, gam_k = K.sb(ph, "gam2", [128, D], F32, dma=True)
                  sht, sh_k = K.sb(ph, "sh2", [128, D], F32, dma=True)
                  items = [(sg, b0, bw) for sg in segs for (b0, bw) in blocks_of(sg["n"])]
                  loaded = {}
                  ctr = [0]

                  def issue(i):
                      if i >= len(items) or i in loaded:
                          return
                      sg_, b0_, bw_ = items[i]
                      xb_, xb_k_ = xbr.next()
                      K.dma("sp", xb_[:, 0:bw_ // 128, :], sg_["xm"][b0_:b0_ + bw_, :].rearrange("(j p) d -> p j d", p=128), xb_k_, writes=[xb_k_], dram_reads=[sg_["xm_k"]])
                      loaded[i] = (xb_, xb_k_)

                  for sg in segs:
                      n = sg["n"]
                      K.dma("sp", sht[:], sg["mod"][:, 3 * D:4 * D], sh_k, writes=[sh_k], dram_reads=[sg["mod_k"]])
                      K.dma("sp", gamt[:], sg["mod"][:, 4 * D:5 * D], gam_k, writes=[gam_k], dram_reads=[sg["mod_k"]])
                      for (b0, bw) in blocks_of(n):
                          G = bw // 128
                          hs, hs_k = hst.next()
                          i_ = ctr[0]
                          ctr[0] += 1
                          issue(i_)
                          issue(i_ + 1)
                          xb, xb_k = loaded.pop(i_)
                          masks = []
                          if sg["prompt"]:
                              for j in range(G):
                                  if b0 + j * 128 == 0:
                                      masks.append((j, 0))
                                  if b0 + j * 128 == n - 128:
                                      masks.append((j, 1))
                          hbb, hbb_k = norm_block(P, xb, xb_k, G, (gamt, gam_k), (sht, sh_k), masks=masks)
                          for j in range(G):
                              transpose_to(P, hbb[:, j, :], hbb_k, hs[:, :, j * 128:(j + 1) * 128], hs_k)
                          K.dma("sp", sg["h2T"][:, :, 1 + b0:1 + b0 + bw], hs[:, :, 0:bw], hs_k, reads=[hs_k], acc=[sg["h2T_k"]])
                  K.end_phase()
                  chk(l)
              with ExitStack() as ph:
                  wup, wup_k = K.sb(ph, "wup", [128, KT, 2 * D_FF], BF16, dma=True)
                  wdn, wdn_k = K.sb(ph, "wdn", [128, NFT, D], BF16, dma=True)
                  wuv = wview(Wl["w_up"])
                  for c in range(0, 2 * D_FF, 704):
                      K.dma("pool", wup[:, :, c:c + 704], wuv[:, :, c:c + 704], wup_k, acc=[wup_k])
                  K.dma("pool", wdn[:], wview(Wl["w_down"]), wdn_k, writes=[wdn_k])
                  fdw, fdw_k = K.sb(ph, "fdw", [128, 44, 3], F32, dma=True)
                  fdwb, fdwb_k = K.sb(ph, "fdwb", [128, 44], F32, dma=True)
                  K.dma("sp", fdw[:], Wl["fdw"].rearrange("p (c k) -> p c k", c=44), fdw_k, writes=[fdw_k])
                  K.dma("sp", fdwb[:], Wl["fdwb"], fdwb_k, writes=[fdwb_k])
                  g2t, g2_k = K.sb(ph, "g2t", [128, D], F32, dma=True)
                  h2r = K.ring(ph, "h2b", 2, [128, KT, 514], BF16, dma=True)
                  zr = K.ring(ph, "zt_", 3, [128, 514], F32)
                  accr = K.ring(ph, "acc", 4, [128, 512], F32)
                  sgr = K.ring(ph, "sgf", 2, [128, 512], F32)
                  uT, uT_k = K.sb(ph, "uT", [128, NFT, 512], BF16)
                  xr = K.ring(ph, "xr4", 2, [128, D], F32, dma=True)
                  tmr = K.ring(ph, "tm4", 2, [128, 512], F32)
                  items = [(sg, b0, bw) for sg in segs for (b0, bw) in blocks_of(sg["n"])]
                  loaded = {}
                  ctr = [0]

                  def issue(i):
                      if i >= len(items) or i in loaded:
                          return
                      sg_, b0_, bw_ = items[i]
                      h2_, h2_k_ = h2r.next()
                      K.dma("sp", h2_[:, :, 0:bw_ + 2], sg_["h2T"][:, :, b0_:b0_ + bw_ + 2], h2_k_, writes=[h2_k_], dram_reads=[sg_["h2T_k"]])
                      loaded[i] = (h2_, h2_k_)

                  for sg in segs:
                      n = sg["n"]
                      K.dma("sp", g2t[:], sg["mod"][:, 5 * D:6 * D], g2_k, writes=[g2_k], dram_reads=[sg["mod_k"]])
                      for (b0, bw) in blocks_of(n):
                          nj = bw // 128
                          i_ = ctr[0]
                          ctr[0] += 1
                          issue(i_)
                          issue(i_ + 1)
                          h2, h2_k = loaded.pop(i_)
                          half = (bw + 2) // 2
                          for i in range(NFT):
                              accs = []
                              for which in range(2):
                                  ci = which * NFT + i
                                  col0 = ci * 128
                                  z, z_k = zr.next()
                                  acc, acc_k = accr.next()
                                  for (c0, c1) in ((0, half), (half, bw + 2)):
                                      ps, ps_k = psr(0, 8)
                                      for kt in range(KT):
                                          K.op("pe", lambda e, kt=kt: e.matmul(ps[:, 0:c1 - c0], lhsT=wup[:, kt, col0:col0 + 128], rhs=h2[:, kt, c0:c1],
                                                                               start=(kt == 0), stop=(kt == KT - 1)),
                                               reads=[wup_k, h2_k], writes=[ps_k], inc=(kt == KT - 1))
                                      K.op("act", lambda e: e.activation(out=z[:, c0:c1], in_=ps[:, 0:c1 - c0], func=AF.Copy), reads=[ps_k], writes=[z_k])
                                      a1 = min(c1, bw)
                                      K.op("act", lambda e: e.activation(out=acc[:, c0:a1], in_=ps[:, 0:a1 - c0], func=AF.Identity,
                                                                         bias=fdwb[:, ci:ci + 1], scale=fdw[:, ci, 0:1]),
                                           reads=[ps_k, fdw_k, fdwb_k], writes=[acc_k])
                                  K.op("dve", lambda e: e.scalar_tensor_tensor(out=acc[:, 0:bw], in0=z[:, 1:bw + 1], scalar=fdw[:, ci, 1:2], in1=acc[:, 0:bw],
                                                                               op0=ALU.mult, op1=ALU.add),
                                       reads=[z_k, fdw_k, acc_k], writes=[acc_k])
                                  K.op("dve", lambda e: e.scalar_tensor_tensor(out=acc[:, 0:bw], in0=z[:, 2:bw + 2], scalar=fdw[:, ci, 2:3], in1=acc[:, 0:bw],
                                                                               op0=ALU.mult, op1=ALU.add),
                                       reads=[z_k, fdw_k, acc_k], writes=[acc_k])
                                  accs.append((acc, acc_k))
                              sgf, sgf_k = sgr.next()
                              K.op("act", lambda e: e.activation(out=sgf[:, 0:bw], in_=accs[0][0][:, 0:bw], func=AF.Silu), reads=[accs[0][1]], writes=[sgf_k])
                              K.op("pool", lambda e, i=i: e.tensor_tensor(out=uT[:, i, 0:bw], in0=sgf[:, 0:bw], in1=accs[1][0][:, 0:bw], op=ALU.mult),
                                   reads=[sgf_k, accs[1][1]], writes=[uT_k])
                          for t in range(nj):
                              t0 = b0 + t * 128
                              if sg["prompt"] and (t0 < HALO or t0 >= n - HALO):
                                  continue
                              xt, xk = xr.next()
                              K.dma("sp", xt[:], sg["xm"][t0:t0 + 128, :], xk, writes=[xk], dram_reads=[sg["xm_k"]])
                              for hf in range(2):
                                  po, po_k = psr(0, 8)
                                  for i in range(NFT):
                                      K.op("pe", lambda e, i=i: e.matmul(po[:], lhsT=uT[:, i, t * 128:(t + 1) * 128], rhs=wdn[:, i, hf * 512:(hf + 1) * 512],
                                                                         start=(i == 0), stop=(i == NFT - 1)),
                                           reads=[uT_k, wdn_k], writes=[po_k], inc=(i == NFT - 1))
                                  tm2, tm2_k = tmr.next()
                                  K.op("dve", lambda e: e.tensor_tensor(out=tm2[:], in0=po[:], in1=g2t[:, hf * 512:(hf + 1) * 512], op=ALU.mult), reads=[po_k, g2_k], writes=[tm2_k])
                                  K.op("pool", lambda e: e.tensor_tensor(out=xt[:, hf * 512:(hf + 1) * 512], in0=xt[:, hf * 512:(hf + 1) * 512], in1=tm2[:], op=ALU.add),
                                       reads=[xk, tm2_k], writes=[xk])
                              yoff = t0 - HALO if sg["prompt"] else t0
                              K.dma("pool", sg["y"][yoff:yoff + 128, :], xt[:], xk, reads=[xk], acc=[sg["y_k"]])
                  K.end_phase()
                  chk(l)
        except _Stop:
            print('STOPPED at', _stop)
        K.barrier()
        print("instructions emitted:", K.nops)
    return nc


def rope_table(pos):
    inv = np.power(np.float32(10000.0), -np.arange(0, 32, 2, dtype=np.float32) / np.float32(32)).astype(np.float32)
    ang = pos.astype(np.float32)[:, None] * inv[None, :]
    return np.concatenate([np.cos(ang), np.sin(ang)], axis=1).astype(np.float32)


def layer_weights(inp, l):
    f = lambda a: np.ascontiguousarray(a, dtype=np.float32)
    ukv = inp["w_ukv"][l].reshape(128, NH, 128)
    w = dict(
        w_ada=f(inp["w_ada"][l]), b_ada=f(inp["b_ada"][l][None, :]), norm1=f(inp["norm1"][l][None, :]), norm2=f(inp["norm2"][l][None, :]),
        w_in=f(inp["w_in"][l]), w_uq=f(inp["w_uq"][l]),
        w_ukv=f(np.concatenate([ukv[:, :, 0:64].reshape(128, 512), ukv[:, :, 64:128].reshape(128, 512)], axis=1)),
        qan=f(inp["q_a_norm"][l].reshape(2, 128).T), kvan=f(inp["kv_a_norm"][l].reshape(128, 1)),
        qhn=f(inp["q_head_norm"][l][None, :]), khn=f(inp["k_head_norm"][l][None, :]),
        w_ao=f(inp["w_attn_o"][l].reshape(NH, 64, D).transpose(1, 0, 2).reshape(64, NH * D)),
        cdw=f(inp["conv_dw"][l].T.reshape(4, 128, 31).transpose(1, 0, 2).reshape(128, 4 * 31)),
        cdwb=f(inp["conv_dw_b"][l].reshape(4, 128).T), clng=f(inp["conv_ln_g"][l].reshape(4, 128).T), clnb=f(inp["conv_ln_b"][l].reshape(4, 128).T),
        w_co=f(inp["w_conv_o"][l]), sglng=f(inp["sg_ln_g"][l][None, :]), sglnb=f(inp["sg_ln_b"][l][None, :]),
        sgwT=f(inp["sg_w"][l].transpose(2, 0, 1).reshape(128, 4 * 128)),
        sgb=f(inp["sg_b"][l].reshape(1, 512)), w_so=f(inp["w_sg_o"][l]), w_out=f(inp["w_out"][l]), w_up=f(inp["w_up"][l]),
        fdw=f(inp["ffn_dw"][l].T.reshape(44, 128, 3).transpose(1, 0, 2).reshape(128, 44 * 3)),
        fdwb=f(inp["ffn_dw_b"][l].reshape(44, 128).T), w_down=f(inp["w_down"][l]))
    return w


_PROG = {}


def run_model(inp, cfg, n_cores=8):
    NS, SS, PCH, PS = cfg["NS"], cfg["SS"], cfg["PCH"], cfg["PS"]
    PSEG = PCH + 2 * HALO
    key = (NS, SS, PCH, PS)
    L = inp["w_ada"].shape[0]
    assert L == 2
    if key not in _PROG:
        _PROG[key] = build_program(cfg, L)
    nc = _PROG[key]
    xp = np.asarray(inp["x_prompt"], dtype=np.float32)
    xs = np.asarray(inp["x_sample"], dtype=np.float32)
    cp = np.asarray(inp["c_prompt"], dtype=np.float32)
    cs = np.asarray(inp["c_sample"], dtype=np.float32)
    nchunk = PS // PCH
    rope_s = rope_table(np.arange(SS))
    rope_pc = rope_table(np.arange(PS))
    wls = [layer_weights(inp, l) for l in range(L)]
    in_maps = []
    for c in range(n_cores):
        b = c // nchunk
        r = c % nchunk
        lo = r * PCH - HALO
        pm = np.zeros((128, 2), np.float32)
        pm[:, 0] = 1.0 if r > 0 else 0.0
        pm[:, 1] = 1.0 if r < nchunk - 1 else 0.0
        sel = np.zeros((128, nchunk), np.float32)
        sel[:, r] = 1.0
        m = dict(xs=np.ascontiguousarray(xs[c * NS:(c + 1) * NS].reshape(NS * SS, D)),
                 xpctx=np.ascontiguousarray(xp[b]),
                 cvT=np.ascontiguousarray(np.concatenate([cs[c * NS:(c + 1) * NS], cp[b:b + 1]], axis=0).reshape((NS + 1) * KT, 128).T),
                 pmask=pm, psel=sel, rope_s=rope_s, rope_pc=rope_pc, rope_pq=rope_table(np.arange(lo, lo + PSEG)))
        for l in range(L):
            for k, v in wls[l].items():
                m["%s_%d" % (k, l)] = v
        in_maps.append(m)
    res = run_bass_kernel_spmd(nc, in_maps, core_ids=list(range(n_cores)))
    ys = np.stack([np.asarray(r_["ys"]).reshape(NS, SS, D) for r_ in res.results], axis=0).reshape(n_cores * NS, SS, D)
    ypo = np.stack([np.asarray(r_["yp"]) for r_ in res.results], axis=0).reshape(n_cores // nchunk, PS, D)
    return ypo.astype(np.float32), ys.astype(np.float32)


def kernel(**inputs):
    yp, ys = run_model(inputs, CFG, 8)
    return (yp, ys)
```

```python
import numpy as np
from contextlib import ExitStack
import concourse.bass as bass
import concourse.mybir as mybir
from concourse.bass_utils import run_bass_kernel_spmd

F32 = mybir.dt.float32
BF16 = mybir.dt.bfloat16
AF = mybir.ActivationFunctionType
ALU = mybir.AluOpType
AX = mybir.AxisListType

D = 1024
KT = 8
NH = 8
DQK = 96
EPS = 1e-6
D_IN = 5536
D_FF = 2816
NFT = 22
HALO = 128

CFG = dict(NS=4, SS=2048, PCH=2048, PS=8192)


class Trk:
    __slots__ = ("w", "r", "dsem", "dcnt")

    def __init__(self):
        self.w = {}
        self.r = {}
        self.dsem = None
        self.dcnt = 0


class KB:
    def __init__(self, nc, es):
        self.nc = nc
        self.es = es
        self.eng = {"pe": nc.tensor, "act": nc.scalar, "dve": nc.vector, "pool": nc.gpsimd, "sp": nc.sync}
        self.sem = {k: es.enter_context(nc.semaphore("s_" + k)) for k in ["pe", "act", "dve", "pool"]}
        self.cnt = {k: 0 for k in self.sem}
        self.seen = {k: {} for k in self.eng}
        self.pend = {k: [] for k in self.eng}
        self.dma_trks = []
        self.nops = 0
        self.sem_free = []
        self.phase_trks = []
        self.nsem = 0

    def get_dsem(self):
        if self.sem_free:
            return self.sem_free.pop()
        self.nsem += 1
        return (self.es.enter_context(self.nc.semaphore("dq%d" % self.nsem)), 0)

    def release(self, trks):
        for k in trks:
            if k.dsem is not None:
                self.sem_free.append((k.dsem, k.dcnt))
                if k in self.dma_trks:
                    self.dma_trks.remove(k)

    def _wait(self, e, sem, val):
        d = self.seen[e]
        if d.get(sem, 0) >= val:
            return
        self.eng[e].wait_ge(sem, val)
        d[sem] = val

    def _deps(self, e, reads, writes):
        for t in reads:
            for s, (v, ek) in t.w.items():
                self._wait(e, s, v)
        for t in writes:
            for s, (v, ek) in t.w.items():
                if ek != e:
                    self._wait(e, s, v)
            for ek, (s, v) in t.r.items():
                if ek != e:
                    self._wait(e, s, v)

    def op(self, e, fn, reads=(), writes=(), inc=True):
        if getattr(self, 'skip', False):
            return
        self._deps(e, reads, writes)
        ins = fn(self.eng[e])
        self.nops += 1
        if not inc:
            self.pend[e].append((reads, writes))
            return
        self.cnt[e] += 1
        c = self.cnt[e]
        s = self.sem[e]
        ins.then_inc(s, 1)
        self.pend[e].append((reads, writes))
        for rd, wr in self.pend[e]:
            for t in wr:
                t.w = {s: (c, e)}
                t.r = {}
        for rd, wr in self.pend[e]:
            for t in rd:
                t.r[e] = (s, c)
        self.pend[e] = []

    def dma(self, q, out, in_, own, reads=(), writes=(), acc=(), dram_reads=(), slow=False):
        if getattr(self, 'skip', False):
            return
        self._deps(q, list(reads) + list(dram_reads), writes)
        for t in acc:
            for ek, (s, v) in t.r.items():
                self._wait(q, s, v)
        own.dcnt += 16
        if slow:
            self.eng[q].dma_start(out=out, in_=in_, allow_slow_non_contiguous=True).then_inc(own.dsem, 16)
        else:
            self.eng[q].dma_start(out=out, in_=in_).then_inc(own.dsem, 16)
        self.nops += 1
        key = ("dma", own.dsem)
        for t in writes:
            t.w = {own.dsem: (own.dcnt, key)}
            t.r = {}
        for t in acc:
            t.w[own.dsem] = (own.dcnt, key)
        for t in reads:
            t.r[key] = (own.dsem, own.dcnt)

    def end_phase(self):
        self.barrier()
        self.release(list(self.phase_trks))
        self.phase_trks = []

    def barrier(self):
        for e in self.eng:
            for k in self.sem:
                if k != e and self.cnt[k] > 0:
                    self._wait(e, self.sem[k], self.cnt[k])
            for t in self.dma_trks:
                if t.dcnt > 0:
                    self._wait(e, t.dsem, t.dcnt)

    def sb(self, es, name, shape, dt, dma=False):
        self.nalloc = getattr(self, "nalloc", 0) + 1
        t = es.enter_context(self.nc.sbuf_tensor("%s_u%d" % (name, self.nalloc), list(shape), dt))
        k = Trk()
        if dma:
            k.dsem, k.dcnt = self.get_dsem()
            self.dma_trks.append(k)
            self.phase_trks.append(k)
        return t, k

    def chunk_trk(self):
        k = Trk()
        k.dsem, k.dcnt = self.get_dsem()
        self.dma_trks.append(k)
        self.phase_trks.append(k)
        return k

    def ring(self, es, name, n, shape, dt, dma=False):
        return Ring([self.sb(es, "%s%d" % (name, i), shape, dt, dma) for i in range(n)])


class Ring:
    def __init__(self, items):
        self.items = items
        self.i = 0

    def next(self):
        it = self.items[self.i % len(self.items)]
        self.i += 1
        return it


def blocks_of(n, bs=512):
    out = []
    o = 0
    while o < n:
        b = min(bs, n - o)
        out.append((o, b))
        o += b
    return out


def build_program(cfg, n_layers=1):
    NS, SS, PCH, PS = cfg["NS"], cfg["SS"], cfg["PCH"], cfg["PS"]
    PSEG = PCH + 2 * HALO
    nc = bass.Bass("TRN2", target_bir_lowering=False)

    def din(name, shape):
        return nc.dram_tensor(name, list(shape), F32, kind="ExternalInput").ap()

    xs = din("xs", [NS * SS, D])
    psel = din("psel", [128, PS // PCH])
    xpctx = din("xpctx", [PS, D])
    cvT = din("cvT", [128, (NS + 1) * KT])
    pmask = din("pmask", [128, 2])
    rope_s = din("rope_s", [SS, 32])
    rope_pc = din("rope_pc", [PS, 32])
    rope_pq = din("rope_pq", [PSEG, 32])
    W = {}
    wshapes = dict(
        w_ada=[D, 6 * D], b_ada=[1, 6 * D], norm1=[1, D], norm2=[1, D], w_in=[D, D_IN],
        w_uq=[256, 768], w_ukv=[128, 1024], qan=[128, 2], kvan=[128, 1], qhn=[1, 96], khn=[1, 96],
        w_ao=[64, 8 * D], cdw=[128, 4 * 31], cdwb=[128, 4], clng=[128, 4], clnb=[128, 4],
        w_co=[512, D], sglng=[1, 512], sglnb=[1, 512], sgwT=[128, 4 * 128], sgb=[1, 512], w_so=[512, D],
        w_out=[D, D], w_up=[D, 2 * D_FF], fdw=[128, 44 * 3], fdwb=[128, 44], w_down=[D_FF, D])
    for l in range(n_layers):
        for k, shp in wshapes.items():
            W[(l, k)] = din("%s_%d" % (k, l), shp)
    ys = nc.dram_tensor("ys", [NS * SS, D], F32, kind="ExternalOutput").ap()
    yp = nc.dram_tensor("yp", [PCH, D], F32, kind="ExternalOutput").ap()

    x1s = nc.dram_tensor("x1s", [NS * SS, D], F32).ap()
    x1full = nc.dram_tensor("x1full", [PS, D], F32).ap()
    x1seg = nc.dram_tensor("x1seg", [PSEG, D], F32).ap()
    x1s_k = [Trk() for _ in range(NS)]
    x1full_k = Trk()
    x1seg_k = Trk()
    assert n_layers == 2

    def make_segs(l):
        segs = []
        for i in range(NS):
            if l == 0:
                segs.append(dict(n=SS, s=SS, x=xs[i * SS:(i + 1) * SS, :], x_k=Trk(), ctx=None, ctx_k=None, rq=rope_s, rk=rope_s,
                                 y=x1s[i * SS:(i + 1) * SS, :], y_k=x1s_k[i], prompt=False))
            else:
                segs.append(dict(n=SS, s=SS, x=x1s[i * SS:(i + 1) * SS, :], x_k=x1s_k[i], ctx=None, ctx_k=None, rq=rope_s, rk=rope_s,
                                 y=ys[i * SS:(i + 1) * SS, :], y_k=Trk(), prompt=False))
        if l == 0:
            segs.append(dict(n=PS, s=PS, x=xpctx, x_k=Trk(), ctx=None, ctx_k=None, rq=rope_pc, rk=rope_pc,
                             y=x1full, y_k=x1full_k, prompt=False))
        else:
            segs.append(dict(n=PSEG, s=PS, x=x1seg, x_k=x1seg_k, ctx=x1full, ctx_k=x1full_k, rq=rope_pq, rk=rope_pc,
                             y=yp, y_k=Trk(), prompt=True))
        for si, sg in enumerate(segs):
            n, s_ = sg["n"], sg["s"]

            def dsc(nm, shape, dt=BF16):
                return nc.dram_tensor("%s_%d_%d" % (nm, l, si), list(shape), dt).ap()
            sg["hT"] = dsc("hT", [128, KT, n]); sg["hT_k"] = Trk()
            sg["qT"] = dsc("qT", [DQK, NH, n]); sg["qT_k"] = Trk()
            sg["kT"] = dsc("kT", [NH, DQK, s_]); sg["kT_k"] = Trk()
            sg["v"] = dsc("v", [NH, 128, s_ // 128, 65]); sg["v_k"] = Trk()
            sg["gat"] = dsc("gat", [128, 24, n]); sg["gat_k"] = Trk()
            sg["glu"] = dsc("glu", [128, 4, n + 30]); sg["glu_k"] = Trk()
            sg["us"] = dsc("us", [128, 4, n]); sg["us_k"] = Trk()
            sg["ao"] = dsc("ao", [64, NH, n]); sg["ao_k"] = Trk()
            sg["xm"] = dsc("xm", [n, D], F32); sg["xm_k"] = Trk()
            sg["h2T"] = dsc("h2T", [128, KT, n + 2]); sg["h2T_k"] = Trk()
            sg["mod"] = dsc("mod", [128, 6 * D], F32); sg["mod_k"] = Trk()
        return segs

    all_segs = [make_segs(l) for l in range(n_layers)]

    with ExitStack() as es:
        K = KB(nc, es)
        ident, ident_k = K.sb(es, "ident", [128, 128], BF16)
        ones_bf, ones_k = K.sb(es, "ones_bf", [128, 128], BF16)
        ones_f, onesf_k = K.sb(es, "ones_f", [128, 64], F32)
        zt, zt_k = K.sb(es, "zt", [128, 64], BF16, dma=True)
        mk, mk_k = K.sb(es, "mk", [128, 2], F32, dma=True)
        K.op("dve", lambda e: e.memset(ident[:], 1.0), writes=[ident_k])
        K.op("pool", lambda e: e.affine_select(out=ident[:], in_=ident[:], pattern=[[-1, 128]],
                                               compare_op=ALU.is_equal, fill=0.0, base=0, channel_multiplier=1),
             reads=[ident_k], writes=[ident_k])
        K.op("dve", lambda e: e.memset(ones_bf[:], 1.0), writes=[ones_k])
        K.op("dve", lambda e: e.memset(ones_f[:], 1.0), writes=[onesf_k])
        K.op("dve", lambda e: e.memset(zt[:], 0.0), writes=[zt_k])
        K.dma("sp", mk[:], pmask, mk_k, writes=[mk_k])
        K.phase_trks = []
        PSB = []
        for i in range(8):
            t = es.enter_context(nc.psum_tensor("psb%d" % i, [128, 512], F32))
            PSB.append((t, Trk()))
        psr_state = [0]

        PSR_HI = [8]

        def psr(lo=0, hi=None):
            if hi is None:
                hi = PSR_HI[0]
            i = lo + psr_state[0] % (hi - lo)
            psr_state[0] += 1
            return PSB[i]

        for sg in [g for sl in all_segs for g in sl]:
            n = sg["n"]
            K.dma("sp", sg["glu"][:, :, 0:15], zt[:, 0:60].rearrange("p (a b) -> p a b", a=4), zt_k, reads=[zt_k], acc=[sg["glu_k"]])
            K.dma("sp", sg["glu"][:, :, n + 15:n + 30], zt[:, 0:60].rearrange("p (a b) -> p a b", a=4), zt_k, reads=[zt_k], acc=[sg["glu_k"]])
            K.dma("sp", sg["h2T"][:, :, 0:1], zt[:, 0:8].rearrange("p (a b) -> p a b", a=8), zt_k, reads=[zt_k], acc=[sg["h2T_k"]], slow=True)
            K.dma("sp", sg["h2T"][:, :, n + 1:n + 2], zt[:, 0:8].rearrange("p (a b) -> p a b", a=8), zt_k, reads=[zt_k], acc=[sg["h2T_k"]], slow=True)

        def wview(ap_, p=128):
            return ap_.rearrange("(kt p) n -> p kt n", p=p)

        def norm_tile(P, xt, xk, gam, sh, mask_col=None):
            junk, junk_k = P["junk"].next()
            ss, ss_k = P["ss"].next()
            K.op("dve", lambda e: e.memset(ss[:], 0.0), writes=[ss_k])
            K.op("act", lambda e: e.activation(out=junk[:], in_=xt[:], func=AF.Square, accum_out=ss[:, 0:1]),
                 reads=[xk, ss_k], writes=[junk_k, ss_k])
            K.op("act", lambda e: e.activation(out=ss[:, 1:2], in_=ss[:, 0:1], func=AF.Sqrt, bias=EPS, scale=1.0 / D),
                 reads=[ss_k], writes=[ss_k])
            K.op("dve", lambda e: e.reciprocal(out=ss[:, 2:3], in_=ss[:, 1:2]), reads=[ss_k], writes=[ss_k])
            K.op("dve", lambda e: e.scalar_tensor_tensor(out=junk[:], in0=xt[:], scalar=ss[:, 2:3], in1=gam[0][:],
                                                         op0=ALU.mult, op1=ALU.mult),
                 reads=[xk, ss_k, gam[1], junk_k], writes=[junk_k])
            hb, hb_k = P["hb"].next()
            K.op("pool", lambda e: e.tensor_tensor(out=hb[:], in0=junk[:], in1=sh[0][:], op=ALU.add),
                 reads=[junk_k, sh[1]], writes=[hb_k])
            if mask_col is not None:
                K.op("dve", lambda e: e.tensor_scalar(out=hb[:], in0=hb[:], scalar1=mk[:, mask_col:mask_col + 1],
                                                      scalar2=None, op0=ALU.mult),
                     reads=[hb_k, mk_k], writes=[hb_k])
            return hb, hb_k

        def transpose_to(P, hb, hb_k, dst, dst_k, ncols_src=D, rows=128, chunk=128):
            nchunk = ncols_src // chunk
            ps, ps_k = psr()
            psb = ps[:].bitcast(BF16)
            for i in range(nchunk):
                K.op("pe", lambda e, i=i: e.transpose(psb[0:chunk, i * 128:(i + 1) * 128], hb[:, i * chunk:(i + 1) * chunk], ident[:]),
                     reads=[hb_k, ident_k], writes=[ps_k], inc=(i == nchunk - 1))
            K.op("act", lambda e: e.activation(out=dst, in_=psb[0:chunk, 0:nchunk * 128].rearrange("p (a b) -> p a b", a=nchunk),
                                               func=AF.Copy),
                 reads=[ps_k], writes=[dst_k])

        def norm_block(P, xb, xb_k, G, gam, sh, masks=()):
            junk, junk_k = P["junkb"].next()
            ss, ss_k = P["ssb"].next()
            K.op("dve", lambda e: e.memset(ss[:], 0.0), writes=[ss_k])
            for j in range(G):
                K.op("act", lambda e, j=j: e.activation(out=junk[:, j, :], in_=xb[:, j, :], func=AF.Square, accum_out=ss[:, j:j + 1]),
                     reads=[xb_k], writes=[junk_k, ss_k])
            K.op("act", lambda e: e.activation(out=ss[:, 4:4 + G], in_=ss[:, 0:G], func=AF.Sqrt, bias=EPS, scale=1.0 / D),
                 reads=[ss_k], writes=[ss_k])
            K.op("dve", lambda e: e.reciprocal(out=ss[:, 8:8 + G], in_=ss[:, 4:4 + G]), reads=[ss_k], writes=[ss_k])
            for j in range(G):
                K.op("dve", lambda e, j=j: e.scalar_tensor_tensor(out=junk[:, j, :], in0=xb[:, j, :], scalar=ss[:, 8 + j:9 + j], in1=gam[0][:],
                                                                  op0=ALU.mult, op1=ALU.mult),
                     reads=[xb_k, ss_k, gam[1], junk_k] if j == 0 else [xb_k, gam[1]], writes=[junk_k])
            hbb, hbb_k = P["hbb"].next()
            K.op("pool", lambda e: e.tensor_tensor(out=hbb[:, 0:G, :], in0=junk[:, 0:G, :], in1=sh[0][:].unsqueeze(1).to_broadcast([128, G, D]), op=ALU.add),
                 reads=[junk_k, sh[1]], writes=[hbb_k])
            for (j, mc) in masks:
                K.op("dve", lambda e, j=j, mc=mc: e.tensor_scalar(out=hbb[:, j, :], in0=hbb[:, j, :], scalar1=mk[:, mc:mc + 1], scalar2=None, op0=ALU.mult),
                     reads=[hbb_k, mk_k], writes=[hbb_k])
            return hbb, hbb_k

        def rms_block(P, src3, src_k, G, n, dst3, dst_k):
            junk, junk_k = P["junkb"].next()
            s4, s4_k = P["ssb"].next()
            K.op("dve", lambda e: e.memset(s4[:], 0.0), writes=[s4_k])
            for j in range(G):
                K.op("act", lambda e, j=j: e.activation(out=junk[:, j, 0:n], in_=src3[:, j, :], func=AF.Square, accum_out=s4[:, j:j + 1]),
                     reads=[src_k], writes=[junk_k, s4_k])
            K.op("act", lambda e: e.activation(out=s4[:, 4:4 + G], in_=s4[:, 0:G], func=AF.Sqrt, bias=EPS, scale=1.0 / n), reads=[s4_k], writes=[s4_k])
            K.op("dve", lambda e: e.reciprocal(out=s4[:, 8:8 + G], in_=s4[:, 4:4 + G]), reads=[s4_k], writes=[s4_k])
            K.op("dve", lambda e: e.tensor_tensor(out=dst3, in0=src3, in1=s4[:, 8:8 + G].unsqueeze(2).to_broadcast([128, G, n]), op=ALU.mult),
                 reads=[src_k, s4_k], writes=[dst_k])

        def headnorm_rope_block(P, f, f_k, G, gain, rp, rp_k, out, out_k):
            f3 = f[:, 0:G, :].rearrange("p j (h d) -> p (j h) d", h=NH)
            f4 = f[:, 0:G, :].rearrange("p j (h d) -> p j h d", h=NH)
            o4 = out[:, 0:G, :].rearrange("p j (h d) -> p j h d", h=NH)
            sq, sq_k = P["junkb"].next()
            st, st_k = P["stb"].next()
            sq3 = sq[:].rearrange("p j c -> p (j c)")[:, 0:G * 768].rearrange("p (a d) -> p a d", d=DQK)
            K.op("dve", lambda e: e.tensor_tensor(out=sq3, in0=f3, in1=f3, op=ALU.mult), reads=[f_k], writes=[sq_k])
            K.op("dve", lambda e: e.reduce_sum(out=st[:, 0:G * NH], in_=sq3, axis=AX.X), reads=[sq_k], writes=[st_k])
            K.op("act", lambda e: e.activation(out=st[:, 32:32 + G * NH], in_=st[:, 0:G * NH], func=AF.Sqrt, bias=EPS, scale=1.0 / DQK),
                 reads=[st_k], writes=[st_k])
            K.op("dve", lambda e: e.reciprocal(out=st[:, 64:64 + G * NH], in_=st[:, 32:32 + G * NH]), reads=[st_k], writes=[st_k])
            K.op("dve", lambda e: e.tensor_tensor(out=f3, in0=f3, in1=st[:, 64:64 + G * NH].unsqueeze(2).to_broadcast([128, G * NH, DQK]), op=ALU.mult),
                 reads=[f_k, st_k], writes=[f_k])
            K.op("dve", lambda e: e.tensor_tensor(out=f3, in0=f3, in1=gain[0][:].unsqueeze(1).to_broadcast([128, G * NH, DQK]), op=ALU.mult),
                 reads=[f_k, gain[1]], writes=[f_k])
            tt, tt_k = P["ttb"].next()
            t5 = tt[:].rearrange("p (a j h d) -> p a j h d", a=4, j=4, h=NH)
            x1 = f4[:, :, :, 64:80]
            x2 = f4[:, :, :, 80:96]
            cs = rp[:, 0:G, 0:16].unsqueeze(2).to_broadcast([128, G, NH, 16])
            sn = rp[:, 0:G, 16:32].unsqueeze(2).to_broadcast([128, G, NH, 16])
            K.op("dve", lambda e: e.tensor_tensor(out=t5[:, 0, 0:G], in0=x1, in1=cs, op=ALU.mult), reads=[f_k, rp_k], writes=[tt_k])
            K.op("dve", lambda e: e.tensor_tensor(out=t5[:, 1, 0:G], in0=x2, in1=sn, op=ALU.mult), reads=[f_k, rp_k], writes=[tt_k])
            K.op("dve", lambda e: e.tensor_tensor(out=t5[:, 2, 0:G], in0=x1, in1=sn, op=ALU.mult), reads=[f_k, rp_k], writes=[tt_k])
            K.op("dve", lambda e: e.tensor_tensor(out=t5[:, 3, 0:G], in0=x2, in1=cs, op=ALU.mult), reads=[f_k, rp_k], writes=[tt_k])
            K.op("dve", lambda e: e.tensor_tensor(out=o4[:, :, :, 64:80], in0=t5[:, 0, 0:G], in1=t5[:, 1, 0:G], op=ALU.subtract), reads=[tt_k], writes=[out_k])
            K.op("dve", lambda e: e.tensor_tensor(out=o4[:, :, :, 80:96], in0=t5[:, 2, 0:G], in1=t5[:, 3, 0:G], op=ALU.add), reads=[tt_k], writes=[out_k])
            K.op("pool", lambda e: e.tensor_copy(out=o4[:, :, :, 0:64], in_=f4[:, :, :, 0:64]), reads=[f_k], writes=[out_k])

        def headnorm_rope(P, f, f_k, gain, rp, rp_k, out, out_k):
            f3 = f[:].rearrange("p (h d) -> p h d", h=NH)
            o3 = out[:].rearrange("p (h d) -> p h d", h=NH)
            sq, sq_k = P["sq"].next()
            st, st_k = P["st"].next()
            K.op("dve", lambda e: e.tensor_tensor(out=sq[:], in0=f[:], in1=f[:], op=ALU.mult), reads=[f_k], writes=[sq_k])
            K.op("dve", lambda e: e.reduce_sum(out=st[:, 0:8], in_=sq[:].rearrange("p (h d) -> p h d", h=NH), axis=AX.X),
                 reads=[sq_k], writes=[st_k])
            K.op("act", lambda e: e.activation(out=st[:, 8:16], in_=st[:, 0:8], func=AF.Sqrt, bias=EPS, scale=1.0 / DQK),
                 reads=[st_k], writes=[st_k])
            K.op("dve", lambda e: e.reciprocal(out=st[:, 16:24], in_=st[:, 8:16]), reads=[st_k], writes=[st_k])
            K.op("dve", lambda e: e.tensor_tensor(out=f3, in0=f3, in1=st[:, 16:24].unsqueeze(2).to_broadcast([128, NH, DQK]), op=ALU.mult),
                 reads=[f_k, st_k], writes=[f_k])
            K.op("dve", lambda e: e.tensor_tensor(out=f3, in0=f3, in1=gain[0][:].unsqueeze(1).to_broadcast([128, NH, DQK]), op=ALU.mult),
                 reads=[f_k, gain[1]], writes=[f_k])
            tt, tt_k = P["tt"].next()
            t4 = tt[:].rearrange("p (a h d) -> p a h d", a=4, h=NH)
            x1 = f3[:, :, 64:80]
            x2 = f3[:, :, 80:96]
            cs = rp[:, 0:16].unsqueeze(1).to_broadcast([128, NH, 16])
            sn = rp[:, 16:32].unsqueeze(1).to_broadcast([128, NH, 16])
            K.op("dve", lambda e: e.tensor_tensor(out=t4[:, 0], in0=x1, in1=cs, op=ALU.mult), reads=[f_k, rp_k], writes=[tt_k])
            K.op("dve", lambda e: e.tensor_tensor(out=t4[:, 1], in0=x2, in1=sn, op=ALU.mult), reads=[f_k, rp_k], writes=[tt_k])
            K.op("dve", lambda e: e.tensor_tensor(out=t4[:, 2], in0=x1, in1=sn, op=ALU.mult), reads=[f_k, rp_k], writes=[tt_k])
            K.op("dve", lambda e: e.tensor_tensor(out=t4[:, 3], in0=x2, in1=cs, op=ALU.mult), reads=[f_k, rp_k], writes=[tt_k])
            K.op("dve", lambda e: e.tensor_tensor(out=o3[:, :, 64:80], in0=t4[:, 0], in1=t4[:, 1], op=ALU.subtract), reads=[tt_k], writes=[out_k])
            K.op("dve", lambda e: e.tensor_tensor(out=o3[:, :, 80:96], in0=t4[:, 2], in1=t4[:, 3], op=ALU.add), reads=[tt_k], writes=[out_k])
            K.op("pool", lambda e: e.tensor_copy(out=o3[:, :, 0:64], in_=f3[:, :, 0:64]), reads=[f_k], writes=[out_k])

        import os as _os
        _stop = _os.environ.get("KSTOP", "")
        _phc = [0]

        class _Stop(Exception):
            pass

        def chk(l):
            _phc[0] += 1
            if _stop and _stop == "%d,%d" % (l, _phc[0]):
                K.skip = True
                print('STOPPED at', _stop)

        try:
          for l in range(n_layers):
              _phc[0] = 0
              Wl = {k: W[(l, k)] for k in wshapes}
              segs = all_segs[l]
              if l == 1:
                  with ExitStack() as ph:
                      selt, selt_k = K.sb(ph, "selt", [128, PS // PCH], F32, dma=True)
                      K.dma("sp", selt[:], psel, selt_k, writes=[selt_k])
                      accr_ = K.ring(ph, "xacc", 2, [128, D], F32, dma=True)
                      ldr_ = K.ring(ph, "xld", 4, [128, D], F32, dma=True)
                      for j in range(PSEG // 128):
                          ac, ac_k = accr_.next()
                          K.op("dve", lambda e: e.memset(ac[:], 0.0), writes=[ac_k])
                          for r_ in range(PS // PCH):
                              row = r_ * PCH - HALO + j * 128
                              if row < 0 or row + 128 > PS:
                                  continue
                              ld, ld_k = ldr_.next()
                              K.dma("sp", ld[:], x1full[row:row + 128, :], ld_k, writes=[ld_k], dram_reads=[x1full_k])
                              K.op("dve", lambda e, r_=r_: e.scalar_tensor_tensor(out=ac[:], in0=ld[:], scalar=selt[:, r_:r_ + 1], in1=ac[:],
                                                                                  op0=ALU.mult, op1=ALU.add),
                                   reads=[ld_k, selt_k, ac_k], writes=[ac_k])
                          K.dma("sp", x1seg[j * 128:(j + 1) * 128, :], ac[:], ac_k, reads=[ac_k], acc=[x1seg_k])
                      K.end_phase()
                      chk(l)
              with ExitStack() as ph:
                  PSR_HI[0] = 8
                  nseg = len(segs)
                  cT, cT_k = K.sb(ph, "cT", [128, nseg * KT], F32, dma=True)
                  crep, crep_k = K.sb(ph, "crep", [128, nseg * KT, 128], BF16)
                  bb, bb_k = K.sb(ph, "bb", [1, 6 * D], BF16, dma=True)
                  n1b, n1b_k = K.sb(ph, "n1b", [128, D], F32, dma=True)
                  n2b, n2b_k = K.sb(ph, "n2b", [128, D], F32, dma=True)
                  wr = K.ring(ph, "wada", 2, [128, KT, 512], BF16, dma=True)
                  modt = [K.sb(ph, "modt%d" % s, [128, 6 * D], F32, dma=True) for s in range(nseg)]
                  K.dma("sp", cT[:], cvT, cT_k, writes=[cT_k])
                  K.op("act", lambda e: e.activation(out=cT[:], in_=cT[:], func=AF.Silu), reads=[cT_k], writes=[cT_k])
                  K.op("dve", lambda e: e.tensor_copy(out=crep[:], in_=cT[:].unsqueeze(2).to_broadcast([128, nseg * KT, 128])),
                       reads=[cT_k], writes=[crep_k])
                  K.dma("pool", bb[:], Wl["b_ada"], bb_k, writes=[bb_k])
                  K.dma("sp", n1b[:], Wl["norm1"][0, :].partition_broadcast(128), n1b_k, writes=[n1b_k])
                  K.dma("sp", n2b[:], Wl["norm2"][0, :].partition_broadcast(128), n2b_k, writes=[n2b_k])
                  wav = wview(Wl["w_ada"])
                  for c in range(12):
                      wt, wt_k = wr.next()
                      K.dma("pool", wt[:], wav[:, :, c * 512:(c + 1) * 512], wt_k, writes=[wt_k])
                      for s in range(nseg):
                          ps, ps_k = psr()
                          for kt in range(KT):
                              K.op("pe", lambda e, kt=kt: e.matmul(ps[:], lhsT=crep[:, s * KT + kt, :], rhs=wt[:, kt, :],
                                                                   start=(kt == 0), stop=False),
                                   reads=[crep_k, wt_k], writes=[ps_k], inc=False)
                          K.op("pe", lambda e: e.matmul(ps[:], lhsT=ones_bf[0:1, :], rhs=bb[0:1, c * 512:(c + 1) * 512],
                                                        start=False, stop=True),
                               reads=[ones_k, bb_k], writes=[ps_k])
                          mt, mt_k = modt[s]
                          K.op("act", lambda e: e.activation(out=mt[:, c * 512:(c + 1) * 512], in_=ps[:], func=AF.Copy),
                               reads=[ps_k], writes=[mt_k])
                  for s in range(nseg):
                      mt, mt_k = modt[s]
                      K.op("dve", lambda e: e.scalar_tensor_tensor(out=mt[:, D:2 * D], in0=mt[:, D:2 * D], scalar=1.0, in1=n1b[:],
                                                                   op0=ALU.add, op1=ALU.mult),
                           reads=[mt_k, n1b_k], writes=[mt_k])
                      K.op("dve", lambda e: e.scalar_tensor_tensor(out=mt[:, 4 * D:5 * D], in0=mt[:, 4 * D:5 * D], scalar=1.0, in1=n2b[:],
                                                                   op0=ALU.add, op1=ALU.mult),
                           reads=[mt_k, n2b_k], writes=[mt_k])
                      K.dma("sp", segs[s]["mod"], mt[:], mt_k, reads=[mt_k], acc=[segs[s]["mod_k"]])
                  K.end_phase()
                  chk(l)

              with ExitStack() as ph:
                  wlat, wlat_k = K.sb(ph, "wlat", [128, KT, 416], BF16, dma=True)
                  wuq, wuq_k = K.sb(ph, "wuq", [128, 2, 768], BF16, dma=True)
                  wukv, wukv_k = K.sb(ph, "wukv", [128, 1024], BF16, dma=True)
                  qan, qan_k = K.sb(ph, "qan", [128, 2], F32, dma=True)
                  kvan, kvan_k = K.sb(ph, "kvan", [128, 1], F32, dma=True)
                  gq, gq_k = K.sb(ph, "gq", [128, DQK], F32, dma=True)
                  gk, gk_k = K.sb(ph, "gk", [128, DQK], F32, dma=True)
                  K.dma("pool", wlat[:], wview(Wl["w_in"])[:, :, 0:416], wlat_k, writes=[wlat_k])
                  K.dma("pool", wuq[:], wview(Wl["w_uq"]), wuq_k, writes=[wuq_k])
                  K.dma("pool", wukv[:], Wl["w_ukv"], wukv_k, writes=[wukv_k])
                  K.dma("sp", qan[:], Wl["qan"], qan_k, writes=[qan_k])
                  K.dma("sp", kvan[:], Wl["kvan"], kvan_k, writes=[kvan_k])
                  K.dma("sp", gq[:], Wl["qhn"][0, :].partition_broadcast(128), gq_k, writes=[gq_k])
                  K.dma("sp", gk[:], Wl["khn"][0, :].partition_broadcast(128), gk_k, writes=[gk_k])
                  P = dict(junkb=K.ring(ph, "junkb", 1, [128, 4, D], F32), ssb=K.ring(ph, "ssb", 3, [128, 16], F32),
                           hbb=K.ring(ph, "hbb", 2, [128, 4, D], BF16), stb=K.ring(ph, "stb", 2, [128, 96], F32),
                           ttb=K.ring(ph, "ttb", 1, [128, 4 * 4 * NH * 16], F32))
                  xbr = K.ring(ph, "xb", 2, [128, 4, D], F32, dma=True)
                  rpbr = K.ring(ph, "rpb", 2, [128, 4, 32], F32, dma=True)
                  hst = K.ring(ph, "hst", 2, [128, KT, 512], BF16, dma=True)
                  qst = K.ring(ph, "qst", 1, [128, NH, 512], BF16, dma=True)
                  kst = K.ring(ph, "kst", 1, [128, NH, 512], BF16, dma=True)
                  vst = K.ring(ph, "vst", 2, [128, NH, 4, 65], BF16, dma=True)
                  for vt, vk in vst.items:
                      K.op("dve", lambda e, vt=vt: e.memset(vt[:], 1.0), writes=[vk])
                  gamt, gam_k = K.sb(ph, "gam1", [128, D], F32, dma=True)
                  sht, sh_k = K.sb(ph, "sh1", [128, D], F32, dma=True)
                  lat_r = K.ring(ph, "latb", 2, [128, 4, 416], F32)
                  cqn_r = K.ring(ph, "cqnb", 2, [128, 4, 256], BF16)
                  cqT_r = K.ring(ph, "cqTb", 2, [128, 2, 512], BF16)
                  ckn_r = K.ring(ph, "cknb", 2, [128, 4, 128], BF16)
                  ckT_r = K.ring(ph, "ckTb", 2, [128, 512], BF16)
                  qf_r = K.ring(ph, "qfb", 2, [128, 4, 768], F32)
                  qb_r = K.ring(ph, "qbb", 2, [128, 4, 768], BF16)

                  def passes_of(sg):
                      if sg["prompt"]:
                          return [(sg["ctx"], sg["s"], False, True, sg["rk"], sg["ctx_k"]),
                                  (sg["x"], sg["n"], True, False, sg["rq"], sg["x_k"])]
                      return [(sg["x"], sg["n"], True, True, sg["rq"], sg["x_k"])]

                  items = [(p_[0], p_[5], p_[4], b0, bw) for sg in segs for p_ in passes_of(sg) for (b0, bw) in blocks_of(p_[1])]
                  loaded = {}
                  ctr = [0]

                  def issue(i):
                      if i >= len(items) or i in loaded:
                          return
                      xsrc_, xsrc_k_, rtab_, b0_, bw_ = items[i]
                      G_ = bw_ // 128
                      xb_, xb_k_ = xbr.next()
                      K.dma("sp", xb_[:, 0:G_, :], xsrc_[b0_:b0_ + bw_, :].rearrange("(j p) d -> p j d", p=128), xb_k_, writes=[xb_k_], dram_reads=[xsrc_k_])
                      rp_, rp_k_ = rpbr.next()
                      K.dma("sp", rp_[:, 0:G_, :], rtab_[b0_:b0_ + bw_, :].rearrange("(j p) d -> p j d", p=128), rp_k_, writes=[rp_k_])
                      loaded[i] = (xb_, xb_k_, rp_, rp_k_)

                  for sg in segs:
                      K.dma("sp", sht[:], sg["mod"][:, 0:D], sh_k, writes=[sh_k], dram_reads=[sg["mod_k"]])
                      K.dma("sp", gamt[:], sg["mod"][:, D:2 * D], gam_k, writes=[gam_k], dram_reads=[sg["mod_k"]])
                      for (xsrc, ntok, want_q, want_k, rtab, xsrc_k) in passes_of(sg):
                          for (b0, bw) in blocks_of(ntok):
                              G = bw // 128
                              hs, hs_k = hst.next()
                              i_ = ctr[0]
                              ctr[0] += 1
                              issue(i_)
                              issue(i_ + 1)
                              xb, xb_k, rp, rp_k = loaded.pop(i_)
                              hbb, hbb_k = norm_block(P, xb, xb_k, G, (gamt, gam_k), (sht, sh_k))
                              lat, lat_k = lat_r.next()
                              for j in range(G):
                                  transpose_to(P, hbb[:, j, :], hbb_k, hs[:, :, j * 128:(j + 1) * 128], hs_k)
                              for j in range(G):
                                  pl, pl_k = psr()
                                  for kt in range(KT):
                                      K.op("pe", lambda e, kt=kt, j=j: e.matmul(pl[:, 0:416], lhsT=hs[:, kt, j * 128:(j + 1) * 128], rhs=wlat[:, kt, :],
                                                                               start=(kt == 0), stop=(kt == KT - 1)),
                                           reads=[hs_k, wlat_k], writes=[pl_k], inc=(kt == KT - 1))
                                  K.op("act", lambda e, j=j: e.activation(out=lat[:, j, :], in_=pl[:, 0:416], func=AF.Copy), reads=[pl_k], writes=[lat_k])
                              if want_q:
                                  qs, qs_k = qst.next()
                                  cqn, cqn_k = cqn_r.next()
                                  rms_block(P, lat[:, 0:G, 0:256], lat_k, G, 256, cqn[:, 0:G, :], cqn_k)
                                  cqT, cqT_k = cqT_r.next()
                                  p2, p2_k = psr()
                                  p2b = p2[:].bitcast(BF16)
                                  for j in range(G):
                                      for i in range(2):
                                          K.op("pe", lambda e, i=i, j=j: e.transpose(p2b[:, (j * 2 + i) * 128:(j * 2 + i + 1) * 128], cqn[:, j, i * 128:(i + 1) * 128], ident[:]),
                                               reads=[cqn_k, ident_k], writes=[p2_k], inc=(j == G - 1 and i == 1))
                                  for i in range(2):
                                      K.op("act", lambda e, i=i: e.activation(out=cqT[:, i, 0:G * 128].rearrange("p (j c) -> p j c", j=G),
                                                                              in_=p2b[:, 0:G * 256].rearrange("p (j i c) -> p j i c", j=G, i=2)[:, :, i, :],
                                                                              func=AF.Copy, scale=qan[:, i:i + 1]),
                                           reads=[p2_k, qan_k], writes=[cqT_k])
                                  qf, qf_k = qf_r.next()
                                  for j in range(G):
                                      pq0, pq0_k = psr()
                                      pq1, pq1_k = psr()
                                      for i in range(2):
                                          K.op("pe", lambda e, i=i, j=j: e.matmul(pq0[:, 0:480], lhsT=cqT[:, i, j * 128:(j + 1) * 128], rhs=wuq[:, i, 0:480], start=(i == 0), stop=(i == 1)),
                                               reads=[cqT_k, wuq_k], writes=[pq0_k], inc=(i == 1))
                                      for i in range(2):
                                          K.op("pe", lambda e, i=i, j=j: e.matmul(pq1[:, 0:288], lhsT=cqT[:, i, j * 128:(j + 1) * 128], rhs=wuq[:, i, 480:768], start=(i == 0), stop=(i == 1)),
                                               reads=[cqT_k, wuq_k], writes=[pq1_k], inc=(i == 1))
                                      K.op("act", lambda e, j=j: e.activation(out=qf[:, j, 0:480], in_=pq0[:, 0:480], func=AF.Copy), reads=[pq0_k], writes=[qf_k])
                                      K.op("dve", lambda e, j=j: e.tensor_copy(out=qf[:, j, 480:768], in_=pq1[:, 0:288]), reads=[pq1_k], writes=[qf_k])
                                  qb, qb_k = qb_r.next()
                                  headnorm_rope_block(P, qf, qf_k, G, (gq, gq_k), rp, rp_k, qb, qb_k)
                                  for j in range(G):
                                      transpose_to(P, qb[:, j, :], qb_k, qs[0:DQK, :, j * 128:(j + 1) * 128], qs_k, ncols_src=768, chunk=DQK)
                              if want_k:
                                  ks, ks_k = kst.next()
                                  vs, vs_k = vst.next()
                                  ckn, ckn_k = ckn_r.next()
                                  rms_block(P, lat[:, 0:G, 256:384], lat_k, G, 128, ckn[:, 0:G, :], ckn_k)
                                  ckT, ckT_k = ckT_r.next()
                                  p3, p3_k = psr()
                                  p3b = p3[:].bitcast(BF16)
                                  for j in range(G):
                                      K.op("pe", lambda e, j=j: e.transpose(p3b[:, j * 128:(j + 1) * 128], ckn[:, j, :], ident[:]),
                                           reads=[ckn_k, ident_k], writes=[p3_k], inc=(j == G - 1))
                                  K.op("act", lambda e: e.activation(out=ckT[:, 0:G * 128], in_=p3b[:, 0:G * 128], func=AF.Copy, scale=kvan[:, 0:1]),
                                       reads=[p3_k, kvan_k], writes=[ckT_k])
                                  kf, kf_k = qf_r.next()
                                  kf4 = kf[:].rearrange("p j (h d) -> p j h d", h=NH)
                                  for j in range(G):
                                      pk, pk_k = psr()
                                      pv, pv_k = psr()
                                      K.op("pe", lambda e, j=j: e.matmul(pk[:], lhsT=ckT[:, j * 128:(j + 1) * 128], rhs=wukv[:, 0:512], start=True, stop=True),
                                           reads=[ckT_k, wukv_k], writes=[pk_k])
                                      K.op("pe", lambda e, j=j: e.matmul(pv[:], lhsT=ckT[:, j * 128:(j + 1) * 128], rhs=wukv[:, 512:1024], start=True, stop=True),
                                           reads=[ckT_k, wukv_k], writes=[pv_k])
                                      K.op("act", lambda e, j=j: e.activation(out=kf4[:, j, :, 0:64], in_=pk[:].rearrange("p (h d) -> p h d", h=NH), func=AF.Copy),
                                           reads=[pk_k], writes=[kf_k])
                                      K.op("act", lambda e, j=j: e.activation(out=vs[:, :, j, 0:64], in_=pv[:].rearrange("p (h d) -> p h d", h=NH), func=AF.Copy),
                                           reads=[pv_k], writes=[vs_k])
                                  K.op("dve", lambda e: e.tensor_copy(out=kf4[:, 0:G, :, 64:96], in_=lat[:, 0:G, 384:416].unsqueeze(2).to_broadcast([128, G, NH, 32])),
                                       reads=[lat_k], writes=[kf_k])
                                  kb, kb_k = qb_r.next()
                                  headnorm_rope_block(P, kf, kf_k, G, (gk, gk_k), rp, rp_k, kb, kb_k)
                                  for j in range(G):
                                      transpose_to(P, kb[:, j, :], kb_k, ks[0:DQK, :, j * 128:(j + 1) * 128], ks_k, ncols_src=768, chunk=DQK)
                              nj = G
                              if want_q:
                                  K.dma("sp", sg["hT"][:, :, b0:b0 + bw], hs[:, :, 0:bw], hs_k, reads=[hs_k], acc=[sg["hT_k"]])
                                  K.dma("sp", sg["qT"][:, :, b0:b0 + bw], qs[0:DQK, :, 0:bw], qs_k, reads=[qs_k], acc=[sg["qT_k"]])
                              if want_k:
                                  K.dma("sp", sg["kT"].rearrange("h d s -> d h s")[:, :, b0:b0 + bw], ks[0:DQK, :, 0:bw], ks_k, reads=[ks_k], acc=[sg["kT_k"]])
                                  K.dma("sp", sg["v"].rearrange("h p k c -> p h k c")[:, :, b0 // 128:b0 // 128 + nj, :], vs[:, :, 0:nj, :], vs_k,
                                        reads=[vs_k], acc=[sg["v_k"]])
                  K.end_phase()
                  chk(l)

              with ExitStack() as ph:
                  PSR_HI[0] = 4
                  NW = D_IN - 416
                  wbig, wbig_k = K.sb(ph, "wbig", [128, KT, NW], BF16, dma=True)
                  wv = wview(Wl["w_in"])
                  wbig_ks = [K.chunk_trk() for _ in range(NW // 640)]
                  for ci_, c in enumerate(range(0, NW, 640)):
                      K.dma("pool", wbig[:, :, c:c + 640], wv[:, :, 416 + c:416 + c + 640], wbig_ks[ci_], writes=[wbig_ks[ci_]])
                  wsT, wsT_k = K.sb(ph, "wsT", [128, 4, 128], BF16, dma=True)
                  K.dma("pool", wsT[:], Wl["sgwT"].rearrange("p (g q) -> p g q", g=4), wsT_k, writes=[wsT_k])
                  lng, lng_k = K.sb(ph, "lng", [128, 512], F32, dma=True)
                  lnb, lnb_k = K.sb(ph, "lnb", [128, 512], F32, dma=True)
                  bsb, bsb_k = K.sb(ph, "bsb", [128, 4, 4, 128], F32, dma=True)
                  K.dma("sp", lng[:], Wl["sglng"][0, :].partition_broadcast(128), lng_k, writes=[lng_k])
                  K.dma("sp", lnb[:], Wl["sglnb"][0, :].partition_broadcast(128), lnb_k, writes=[lnb_k])
                  for j in range(4):
                      K.dma("sp", bsb[:, :, j, :], Wl["sgb"][0, :].partition_broadcast(128).rearrange("p (g q) -> p g q", g=4), bsb_k, acc=[bsb_k])
                  hbr = K.ring(ph, "hTb", 2, [128, KT, 512], BF16, dma=True)
                  sgt_r = K.ring(ph, "sgt", 2, [128, 512], F32)
                  glub_r = K.ring(ph, "glub", 2, [128, 4, 512], BF16, dma=True)
                  ug_r = K.ring(ph, "ug", 2, [128, 4, 512], BF16)
                  vg_r = K.ring(ph, "vg", 2, [128, 512], F32)
                  jk_r = K.ring(ph, "jk2", 1, [128, 512], F32)
                  s8_r = K.ring(ph, "s8", 3, [128, 8], F32)
                  vnb_r = K.ring(ph, "vnb", 4, [128, 512], BF16)
                  tq_r = K.ring(ph, "tq", 2, [128, 512], F32)
                  usb_r = K.ring(ph, "usb", 2, [128, 4, 512], BF16, dma=True)
                  gst_r = K.ring(ph, "gst", 2, [128, 24, 512], BF16, dma=True)
                  items = [(sg, b0, bw) for sg in segs for (b0, bw) in blocks_of(sg["n"])]
                  loaded = {}
                  ctr = [0]

                  def issue(i):
                      if i >= len(items) or i in loaded:
                          return
                      sg_, b0_, bw_ = items[i]
                      hT_, hT_k_ = hbr.next()
                      K.dma("sp", hT_[:, :, 0:bw_], sg_["hT"][:, :, b0_:b0_ + bw_], hT_k_, writes=[hT_k_], dram_reads=[sg_["hT_k"]])
                      loaded[i] = (hT_, hT_k_)

                  for sg in segs:
                      n = sg["n"]
                      blks = blocks_of(n)
                      for bi, (b0, bw) in enumerate(blks):
                          nj = bw // 128
                          i_ = ctr[0]
                          ctr[0] += 1
                          issue(i_)
                          issue(i_ + 1)
                          hT, hT_k = loaded.pop(i_)

                          def fm(col0, ps, ps_k):
                              for kt in range(KT):
                                  K.op("pe", lambda e, kt=kt: e.matmul(ps[:, 0:bw], lhsT=wbig[:, kt, col0:col0 + 128], rhs=hT[:, kt, 0:bw],
                                                                       start=(kt == 0), stop=(kt == KT - 1)),
                                       reads=[wbig_ks[col0 // 640], hT_k], writes=[ps_k], inc=(kt == KT - 1))
                          glub, glub_k = glub_r.next()
                          for c in range(4):
                              pa, pa_k = psr()
                              pg, pg_k = psr()
                              fm(c * 128, pa, pa_k)
                              fm(512 + c * 128, pg, pg_k)
                              sgt, sgt_k = sgt_r.next()
                              K.op("act", lambda e: e.activation(out=sgt[:, 0:bw], in_=pg[:, 0:bw], func=AF.Sigmoid), reads=[pg_k], writes=[sgt_k])
                              K.op("dve", lambda e, c=c: e.tensor_tensor(out=glub[:, c, 0:bw], in0=pa[:, 0:bw], in1=sgt[:, 0:bw], op=ALU.mult),
                                   reads=[pa_k, sgt_k], writes=[glub_k])
                          if sg["prompt"]:
                              if bi == 0:
                                  K.op("dve", lambda e: e.tensor_scalar(out=glub[:, :, 0:128], in0=glub[:, :, 0:128], scalar1=mk[:, 0:1], scalar2=None, op0=ALU.mult),
                                       reads=[glub_k, mk_k], writes=[glub_k])
                              if bi == len(blks) - 1:
                                  K.op("dve", lambda e: e.tensor_scalar(out=glub[:, :, bw - 128:bw], in0=glub[:, :, bw - 128:bw], scalar1=mk[:, 1:2], scalar2=None, op0=ALU.mult),
                                       reads=[glub_k, mk_k], writes=[glub_k])
                          K.dma("sp", sg["glu"][:, :, 15 + b0:15 + b0 + bw], glub[:, :, 0:bw], glub_k, reads=[glub_k], acc=[sg["glu_k"]])
                          ug, ug_k = ug_r.next()
                          for g in range(4):
                              pu, pu_k = psr()
                              fm(1024 + g * 128, pu, pu_k)
                              K.op("act", lambda e, g=g: e.activation(out=ug[:, g, 0:bw], in_=pu[:, 0:bw], func=AF.Gelu_apprx_tanh), reads=[pu_k], writes=[ug_k])
                          pss = [PSB[4 + g] for g in range(4)]
                          vnbs = []
                          for j in range(nj):
                              pv, pv_k = psr()
                              for kt in range(KT):
                                  K.op("pe", lambda e, kt=kt: e.matmul(pv[:], lhsT=hT[:, kt, j * 128:(j + 1) * 128], rhs=wbig[:, kt, 1536:2048],
                                                                       start=(kt == 0), stop=(kt == KT - 1)),
                                       reads=[hT_k, wbig_ks[2], wbig_ks[3]], writes=[pv_k], inc=(kt == KT - 1))
                              vg, vg_k = vg_r.next()
                              K.op("act", lambda e: e.activation(out=vg[:], in_=pv[:], func=AF.Gelu_apprx_tanh), reads=[pv_k], writes=[vg_k])
                              s8, s8_k = s8_r.next()
                              jk, jk_k = jk_r.next()
                              K.op("dve", lambda e: e.memset(s8[:], 0.0), writes=[s8_k])
                              K.op("act", lambda e: e.activation(out=jk[:], in_=vg[:], func=AF.Square, accum_out=s8[:, 1:2]), reads=[vg_k, s8_k], writes=[jk_k, s8_k])
                              K.op("dve", lambda e: e.reduce_sum(out=s8[:, 0:1], in_=vg[:], axis=AX.X), reads=[vg_k, s8_k], writes=[s8_k])
                              K.op("dve", lambda e: e.tensor_scalar(out=s8[:, 2:3], in0=s8[:, 0:1], scalar1=1.0 / 512, scalar2=None, op0=ALU.mult), reads=[s8_k], writes=[s8_k])
                              K.op("dve", lambda e: e.tensor_tensor(out=s8[:, 3:4], in0=s8[:, 2:3], in1=s8[:, 2:3], op=ALU.mult), reads=[s8_k], writes=[s8_k])
                              K.op("dve", lambda e: e.scalar_tensor_tensor(out=s8[:, 4:5], in0=s8[:, 1:2], scalar=1.0 / 512, in1=s8[:, 3:4], op0=ALU.mult, op1=ALU.subtract),
                                   reads=[s8_k], writes=[s8_k])
                              K.op("act", lambda e: e.activation(out=s8[:, 5:6], in_=s8[:, 4:5], func=AF.Sqrt, bias=EPS, scale=1.0), reads=[s8_k], writes=[s8_k])
                              K.op("dve", lambda e: e.reciprocal(out=s8[:, 6:7], in_=s8[:, 5:6]), reads=[s8_k], writes=[s8_k])
                              K.op("dve", lambda e: e.tensor_scalar(out=vg[:], in0=vg[:], scalar1=s8[:, 2:3], scalar2=s8[:, 6:7], op0=ALU.subtract, op1=ALU.mult),
                                   reads=[vg_k, s8_k], writes=[vg_k])
                              K.op("dve", lambda e: e.tensor_tensor(out=vg[:], in0=vg[:], in1=lng[:], op=ALU.mult), reads=[vg_k, lng_k], writes=[vg_k])
                              vnb, vnb_k = vnb_r.next()
                              K.op("pool", lambda e: e.tensor_tensor(out=vnb[:], in0=vg[:], in1=lnb[:], op=ALU.add), reads=[vg_k, lnb_k], writes=[vnb_k])
                              vnbs.append((vnb, vnb_k))
                          gst, gst_k = gst_r.next()
                          for m in range(24):
                              pg, pg_k = psr()
                              fm(2048 + m * 128, pg, pg_k)
                              K.op("act", lambda e, m=m: e.activation(out=gst[:, m, 0:bw], in_=pg[:, 0:bw], func=AF.Sigmoid), reads=[pg_k], writes=[gst_k])
                          K.dma("sp", sg["gat"][:, :, b0:b0 + bw], gst[:, :, 0:bw], gst_k, reads=[gst_k], acc=[sg["gat_k"]])
                          for j in range(nj):
                              vnb, vnb_k = vnbs[j]
                              for g in range(4):
                                  K.op("pe", lambda e, g=g: e.matmul(pss[g][0][:, j * 128:(j + 1) * 128], lhsT=vnb[:, g * 128:(g + 1) * 128], rhs=wsT[:, g, :],
                                                                     start=True, stop=True),
                                       reads=[vnb_k, wsT_k], writes=[pss[g][1]], inc=(g == 3))
                          usb, usb_k = usb_r.next()
                          for g in range(4):
                              tq, tq_k = tq_r.next()
                              K.op("dve", lambda e, g=g: e.tensor_tensor(out=tq[:, 0:bw], in0=pss[g][0][:, 0:bw],
                                                                         in1=bsb[:, g, :, :].rearrange("p j q -> p (j q)")[:, 0:bw], op=ALU.add),
                                   reads=[pss[g][1], bsb_k], writes=[tq_k])
                              K.op("dve", lambda e, g=g: e.tensor_tensor(out=usb[:, g, 0:bw], in0=tq[:, 0:bw], in1=ug[:, g, 0:bw], op=ALU.mult),
                                   reads=[tq_k, ug_k], writes=[usb_k])
                          K.dma("sp", sg["us"][:, :, b0:b0 + bw], usb[:, :, 0:bw], usb_k, reads=[usb_k], acc=[sg["us_k"]])
                  K.end_phase()
                  chk(l)

              with ExitStack() as ph:
                  PSR_HI[0] = 4
                  KSB = 4096
                  qtr = K.ring(ph, "qtb", 2, [128, NH, 512], BF16, dma=True)
                  ktr = K.ring(ph, "ktb", 2, [128, KSB], BF16, dma=True)
                  vtr = K.ring(ph, "vtb", 2, [128, KSB // 128, 65], BF16, dma=True)
                  ptr_ = K.ring(ph, "ptb", 4, [128, 512], BF16)
                  aor = K.ring(ph, "aob", 2, [64, NH, 512], BF16, dma=True)
                  rsr = K.ring(ph, "rsb", 2, [128, 512], F32)
                  rir = K.ring(ph, "rib", 2, [64, 512], F32)
                  scale = float(DQK) ** -0.5
                  items = [(sg, b0, bw) for sg in segs for (b0, bw) in blocks_of(sg["n"])]
                  loaded = {}
                  ctr = [0]

                  def issue(i):
                      if i >= len(items) or i in loaded:
                          return
                      sg_, b0_, bw_ = items[i]
                      qt_, qt_k_ = qtr.next()
                      K.dma("sp", qt_[0:DQK, :, 0:bw_], sg_["qT"][:, :, b0_:b0_ + bw_], qt_k_, writes=[qt_k_], dram_reads=[sg_["qT_k"]])
                      loaded[i] = (qt_, qt_k_)

                  for sg in segs:
                      n, S = sg["n"], sg["s"]
                      for (b0, bw) in blocks_of(n):
                          i_ = ctr[0]
                          ctr[0] += 1
                          issue(i_)
                          issue(i_ + 1)
                          qt, qt_k = loaded.pop(i_)
                          ao, ao_k = aor.next()
                          for h in range(NH):
                              po, po_k = PSB[4 + (h % 2)]
                              first = True
                              for (s0, sw) in blocks_of(S, KSB):
                                  kt_, kt_k = ktr.next()
                                  vt_, vt_k = vtr.next()
                                  K.dma("sp", kt_[0:DQK, 0:sw], sg["kT"][h, :, s0:s0 + sw], kt_k, writes=[kt_k], dram_reads=[sg["kT_k"]])
                                  K.dma("sp", vt_[:, 0:sw // 128, :], sg["v"][h, :, s0 // 128:(s0 + sw) // 128, :], vt_k, writes=[vt_k], dram_reads=[sg["v_k"]])
                                  nk = sw // 128
                                  pend = None
                                  for ki in range(nk + 1):
                                      if ki < nk:
                                          ps, ps_k = psr(0, 4)
                                          K.op("pe", lambda e, ki=ki: e.matmul(ps[:, 0:bw], lhsT=kt_[0:DQK, ki * 128:(ki + 1) * 128], rhs=qt[0:DQK, h, 0:bw],
                                                                               start=True, stop=True),
                                               reads=[kt_k, qt_k], writes=[ps_k])
                                          pt, pt_k = ptr_.next()
                                          K.op("act", lambda e: e.activation(out=pt[:, 0:bw], in_=ps[:, 0:bw], func=AF.Exp, scale=scale), reads=[ps_k], writes=[pt_k])
                                          cur = (pt, pt_k, ki)
                                      else:
                                          cur = None
                                      if pend is not None:
                                          ppt, ppt_k, pki = pend
                                          last = (s0 + sw >= S) and (pki == nk - 1)
                                          K.op("pe", lambda e, pki=pki, ppt=ppt, f=first, last=last: e.matmul(po[0:65, 0:bw], lhsT=vt_[:, pki, 0:65], rhs=ppt[:, 0:bw],
                                                                                                            start=f, stop=last),
                                               reads=[vt_k, ppt_k], writes=[po_k])
                                          first = False
                                      pend = cur
                              rs, rs_k = rsr.next()
                              K.op("dve", lambda e: e.tensor_copy(out=rs[64:65, 0:bw], in_=po[64:65, 0:bw]), reads=[po_k], writes=[rs_k])
                              pb, pb_k = PSB[6 + (h % 2)]
                              K.op("pe", lambda e: e.matmul(pb[0:64, 0:bw], lhsT=ones_f[64:65, 0:64], rhs=rs[64:65, 0:bw], start=True, stop=True),
                                   reads=[onesf_k, rs_k], writes=[pb_k])
                              ri, ri_k = rir.next()
                              K.op("dve", lambda e: e.reciprocal(out=ri[:, 0:bw], in_=pb[0:64, 0:bw]), reads=[pb_k], writes=[ri_k])
                              K.op("dve", lambda e, h=h: e.tensor_tensor(out=ao[:, h, 0:bw], in0=po[0:64, 0:bw], in1=ri[:, 0:bw], op=ALU.mult),
                                   reads=[po_k, ri_k], writes=[ao_k])
                          K.dma("pool", sg["ao"][:, :, b0:b0 + bw], ao[:, :, 0:bw], ao_k, reads=[ao_k], acc=[sg["ao_k"]])
                  K.end_phase()
                  chk(l)

              with ExitStack() as ph:
                  PSR_HI[0] = 8
                  wao, wao_k = K.sb(ph, "wao", [64, NH, D], BF16, dma=True)
                  wco, wco_k = K.sb(ph, "wco", [128, 4, D], BF16, dma=True)
                  wso, wso_k = K.sb(ph, "wso", [128, 4, D], BF16, dma=True)
                  wout, wout_k = K.sb(ph, "wout", [128, KT, D], BF16, dma=True)
                  K.dma("pool", wao[:], Wl["w_ao"].rearrange("p (h n) -> p h n", h=NH), wao_k, writes=[wao_k])
                  K.dma("pool", wco[:], wview(Wl["w_co"]), wco_k, writes=[wco_k])
                  K.dma("pool", wso[:], wview(Wl["w_so"]), wso_k, writes=[wso_k])
                  K.dma("pool", wout[:], wview(Wl["w_out"]), wout_k, writes=[wout_k])
                  cdw, cdw_k = K.sb(ph, "cdw", [128, 4, 31], F32, dma=True)
                  cdwb, cdwb_k = K.sb(ph, "cdwb", [128, 4], F32, dma=True)
                  clng, clng_k = K.sb(ph, "clng", [128, 4], F32, dma=True)
                  clnb, clnb_k = K.sb(ph, "clnb", [128, 4], F32, dma=True)
                  K.dma("sp", cdw[:], Wl["cdw"].rearrange("p (c k) -> p c k", c=4), cdw_k, writes=[cdw_k])
                  K.dma("sp", cdwb[:], Wl["cdwb"], cdwb_k, writes=[cdwb_k])
                  K.dma("sp", clng[:], Wl["clng"], clng_k, writes=[clng_k])
                  K.dma("sp", clnb[:], Wl["clnb"], clnb_k, writes=[clnb_k])
                  dgt, dgt_k = K.sb(ph, "dgt", [128, 4, 31, 128], BF16)
                  for c in range(4):
                      for k in range(31):
                          K.op("dve", lambda e, c=c, k=k: e.tensor_scalar(out=dgt[:, c, k, :], in0=ident[:], scalar1=cdw[:, c, k:k + 1], scalar2=None, op0=ALU.mult),
                               reads=[ident_k, cdw_k], writes=[dgt_k])
                  g1t, g1_k = K.sb(ph, "g1t", [128, D], F32, dma=True)
                  glr = K.ring(ph, "glt", 2, [128, 4, 512 + 30], BF16, dma=True)
                  usr = K.ring(ph, "ust", 2, [128, 4, 512], BF16, dma=True)
                  gtr = K.ring(ph, "gtt", 1, [128, 24, 512], BF16, dma=True)
                  aor = K.ring(ph, "aot", 2, [64, NH, 512], BF16, dma=True)
                  hc, hc_k = K.sb(ph, "hc", [128, 4, 512], F32)
                  hcb, hcb_k = K.sb(ph, "hcb", [128, 4, 512], BF16)
                  sqb, sqb_k = K.sb(ph, "sqb", [128, 4, 512], BF16)
                  mean, mean_k = K.sb(ph, "mean", [128, 512], F32)
                  rstd, rstd_k = K.sb(ph, "rstd", [128, 512], F32)
                  cvn, cvn_k = K.sb(ph, "cvn", [128, 4, 512], BF16)
                  mrg, mrg_k = K.sb(ph, "mrg", [128, KT, 512], BF16)
                  tmr = K.ring(ph, "tm", 4, [128, 512], F32)
                  xr = K.ring(ph, "xr2", 2, [128, D], F32, dma=True)
                  items = [(sg, b0, bw) for sg in segs for (b0, bw) in blocks_of(sg["n"])]
                  loaded = {}
                  ctr = [0]

                  def issue(i):
                      if i >= len(items) or i in loaded:
                          return
                      sg_, b0_, bw_ = items[i]
                      glt_, glt_k_ = glr.next()
                      K.dma("sp", glt_[:, :, 0:bw_ + 30], sg_["glu"][:, :, b0_:b0_ + bw_ + 30], glt_k_, writes=[glt_k_], dram_reads=[sg_["glu_k"]])
                      ust_, ust_k_ = usr.next()
                      K.dma("sp", ust_[:, :, 0:bw_], sg_["us"][:, :, b0_:b0_ + bw_], ust_k_, writes=[ust_k_], dram_reads=[sg_["us_k"]])
                      aot_, aot_k_ = aor.next()
                      K.dma("sp", aot_[:, :, 0:bw_], sg_["ao"][:, :, b0_:b0_ + bw_], aot_k_, writes=[aot_k_], dram_reads=[sg_["ao_k"]])
                      loaded[i] = (glt_, glt_k_, ust_, ust_k_, aot_, aot_k_)

                  for sg in segs:
                      n = sg["n"]
                      K.dma("sp", g1t[:], sg["mod"][:, 2 * D:3 * D], g1_k, writes=[g1_k], dram_reads=[sg["mod_k"]])
                      for (b0, bw) in blocks_of(n):
                          nj = bw // 128
                          i_ = ctr[0]
                          ctr[0] += 1
                          issue(i_)
                          issue(i_ + 1)
                          glt, glt_k, ust, ust_k, aot, aot_k = loaded.pop(i_)
                          gtt, gtt_k = gtr.next()
                          K.dma("sp", gtt[:, :, 0:bw], sg["gat"][:, :, b0:b0 + bw], gtt_k, writes=[gtt_k], dram_reads=[sg["gat_k"]])
                          for c in range(4):
                              ps, ps_k = psr()
                              for k in range(31):
                                  K.op("pe", lambda e, c=c, k=k: e.matmul(ps[:, 0:bw], lhsT=dgt[:, c, k, :], rhs=glt[:, c, k:k + bw], start=(k == 0), stop=(k == 30)),
                                       reads=[dgt_k, glt_k], writes=[ps_k], inc=(k == 30))
                              K.op("act", lambda e, c=c: e.activation(out=hc[:, c, 0:bw], in_=ps[:, 0:bw], func=AF.Identity, bias=cdwb[:, c:c + 1], scale=1.0),
                                   reads=[ps_k, cdwb_k], writes=[hc_k])
                          K.op("pool", lambda e: e.tensor_copy(out=hcb[:, :, 0:bw], in_=hc[:, :, 0:bw]), reads=[hc_k], writes=[hcb_k])
                          K.op("dve", lambda e: e.tensor_tensor(out=sqb[:, :, 0:bw], in0=hc[:, :, 0:bw], in1=hc[:, :, 0:bw], op=ALU.mult), reads=[hc_k], writes=[sqb_k])
                          p1, p1_k = psr()
                          p2, p2_k = psr()
                          for c in range(4):
                              K.op("pe", lambda e, c=c: e.matmul(p1[:, 0:bw], lhsT=ones_bf[:], rhs=hcb[:, c, 0:bw], start=(c == 0), stop=(c == 3)),
                                   reads=[ones_k, hcb_k], writes=[p1_k], inc=(c == 3))
                          for c in range(4):
                              K.op("pe", lambda e, c=c: e.matmul(p2[:, 0:bw], lhsT=ones_bf[:], rhs=sqb[:, c, 0:bw], start=(c == 0), stop=(c == 3)),
                                   reads=[ones_k, sqb_k], writes=[p2_k], inc=(c == 3))
                          K.op("dve", lambda e: e.tensor_scalar(out=mean[:, 0:bw], in0=p1[:, 0:bw], scalar1=1.0 / 512, scalar2=None, op0=ALU.mult), reads=[p1_k], writes=[mean_k])
                          tm, tm_k = tmr.next()
                          K.op("dve", lambda e: e.tensor_tensor(out=tm[:, 0:bw], in0=mean[:, 0:bw], in1=mean[:, 0:bw], op=ALU.mult), reads=[mean_k], writes=[tm_k])
                          K.op("dve", lambda e: e.scalar_tensor_tensor(out=tm[:, 0:bw], in0=p2[:, 0:bw], scalar=1.0 / 512, in1=tm[:, 0:bw], op0=ALU.mult, op1=ALU.subtract),
                               reads=[p2_k, tm_k], writes=[tm_k])
                          K.op("act", lambda e: e.activation(out=tm[:, 0:bw], in_=tm[:, 0:bw], func=AF.Sqrt, bias=EPS, scale=1.0), reads=[tm_k], writes=[tm_k])
                          K.op("dve", lambda e: e.reciprocal(out=rstd[:, 0:bw], in_=tm[:, 0:bw]), reads=[tm_k], writes=[rstd_k])
                          for c in range(4):
                              t2, t2_k = tmr.next()
                              K.op("dve", lambda e, c=c: e.tensor_tensor(out=t2[:, 0:bw], in0=hc[:, c, 0:bw], in1=mean[:, 0:bw], op=ALU.subtract), reads=[hc_k, mean_k], writes=[t2_k])
                              K.op("dve", lambda e: e.tensor_tensor(out=t2[:, 0:bw], in0=t2[:, 0:bw], in1=rstd[:, 0:bw], op=ALU.mult), reads=[t2_k, rstd_k], writes=[t2_k])
                              K.op("act", lambda e, c=c: e.activation(out=cvn[:, c, 0:bw], in_=t2[:, 0:bw], func=AF.Silu, bias=clnb[:, c:c + 1], scale=clng[:, c:c + 1]),
                                   reads=[t2_k, clnb_k, clng_k], writes=[cvn_k])
                          for j in range(8):
                              pa, pa_k = psr()
                              pc, pc_k = psr()
                              pss_, pss_k = psr()
                              for h in range(NH):
                                  K.op("pe", lambda e, h=h: e.matmul(pa[:, 0:bw], lhsT=wao[:, h, j * 128:(j + 1) * 128], rhs=aot[:, h, 0:bw], start=(h == 0), stop=(h == NH - 1)),
                                       reads=[wao_k, aot_k], writes=[pa_k], inc=(h == NH - 1))
                              for c in range(4):
                                  K.op("pe", lambda e, c=c: e.matmul(pc[:, 0:bw], lhsT=wco[:, c, j * 128:(j + 1) * 128], rhs=cvn[:, c, 0:bw], start=(c == 0), stop=(c == 3)),
                                       reads=[wco_k, cvn_k], writes=[pc_k], inc=(c == 3))
                              for c in range(4):
                                  K.op("pe", lambda e, c=c: e.matmul(pss_[:, 0:bw], lhsT=wso[:, c, j * 128:(j + 1) * 128], rhs=ust[:, c, 0:bw], start=(c == 0), stop=(c == 3)),
                                       reads=[wso_k, ust_k], writes=[pss_k], inc=(c == 3))
                              m1, m1_k = tmr.next()
                              m2, m2_k = tmr.next()
                              m3, m3_k = tmr.next()
                              K.op("dve", lambda e: e.tensor_tensor(out=m1[:, 0:bw], in0=pa[:, 0:bw], in1=gtt[:, j, 0:bw], op=ALU.mult), reads=[pa_k, gtt_k], writes=[m1_k])
                              K.op("dve", lambda e: e.tensor_tensor(out=m2[:, 0:bw], in0=pc[:, 0:bw], in1=gtt[:, 8 + j, 0:bw], op=ALU.mult), reads=[pc_k, gtt_k], writes=[m2_k])
                              K.op("dve", lambda e: e.tensor_tensor(out=m3[:, 0:bw], in0=pss_[:, 0:bw], in1=gtt[:, 16 + j, 0:bw], op=ALU.mult), reads=[pss_k, gtt_k], writes=[m3_k])
                              K.op("pool", lambda e: e.tensor_tensor(out=m1[:, 0:bw], in0=m1[:, 0:bw], in1=m2[:, 0:bw], op=ALU.add), reads=[m1_k, m2_k], writes=[m1_k])
                              K.op("pool", lambda e, j=j: e.tensor_tensor(out=mrg[:, j, 0:bw], in0=m1[:, 0:bw], in1=m3[:, 0:bw], op=ALU.add), reads=[m1_k, m3_k], writes=[mrg_k])
                          for t in range(nj):
                              t0 = b0 + t * 128
                              xt, xk = xr.next()
                              K.dma("sp", xt[:], sg["x"][t0:t0 + 128, :], xk, writes=[xk], dram_reads=[sg["x_k"]])
                              for half in range(2):
                                  po, po_k = psr()
                                  for j in range(8):
                                      K.op("pe", lambda e, j=j: e.matmul(po[:], lhsT=mrg[:, j, t * 128:(t + 1) * 128], rhs=wout[:, j, half * 512:(half + 1) * 512],
                                                                         start=(j == 0), stop=(j == 7)),
                                           reads=[mrg_k, wout_k], writes=[po_k], inc=(j == 7))
                                  tm2, tm2_k = tmr.next()
                                  K.op("dve", lambda e: e.tensor_tensor(out=tm2[:], in0=po[:], in1=g1t[:, half * 512:(half + 1) * 512], op=ALU.mult), reads=[po_k, g1_k], writes=[tm2_k])
                                  K.op("pool", lambda e: e.tensor_tensor(out=xt[:, half * 512:(half + 1) * 512], in0=xt[:, half * 512:(half + 1) * 512], in1=tm2[:], op=ALU.add),
                                       reads=[xk, tm2_k], writes=[xk])
                              K.dma("pool", sg["xm"][t0:t0 + 128, :], xt[:], xk, reads=[xk], acc=[sg["xm_k"]])
                  K.end_phase()
                  chk(l)

              with ExitStack() as ph:
                  P = dict(junkb=K.ring(ph, "junkc", 2, [128, 4, D], F32), ssb=K.ring(ph, "ssc", 3, [128, 16], F32),
                           hbb=K.ring(ph, "hbc", 2, [128, 4, D], BF16))
                  xbr = K.ring(ph, "xb3", 3, [128, 4, D], F32, dma=True)
                  hst = K.ring(ph, "hst2", 2, [128, KT, 512], BF16, dma=True)
                  gamt, gam_k = K.sb(ph, "gam2", [128, D], F32, dma=True)
                  sht, sh_k = K.sb(ph, "sh2", [128, D], F32, dma=True)
                  items = [(sg, b0, bw) for sg in segs for (b0, bw) in blocks_of(sg["n"])]
                  loaded = {}
                  ctr = [0]

                  def issue(i):
                      if i >= len(items) or i in loaded:
                          return
                      sg_, b0_, bw_ = items[i]
                      xb_, xb_k_ = xbr.next()
                      K.dma("sp", xb_[:, 0:bw_ // 128, :], sg_["xm"][b0_:b0_ + bw_, :].rearrange("(j p) d -> p j d", p=128), xb_k_, writes=[xb_k_], dram_reads=[sg_["xm_k"]])
                      loaded[i] = (xb_, xb_k_)

                  for sg in segs:
                      n = sg["n"]
                      K.dma("sp", sht[:], sg["mod"][:, 3 * D:4 * D], sh_k, writes=[sh_k], dram_reads=[sg["mod_k"]])
                      K.dma("sp", gamt[:], sg["mod"][:, 4 * D:5 * D], gam_k, writes=[gam_k], dram_reads=[sg["mod_k"]])
                      for (b0, bw) in blocks_of(n):
                          G = bw // 128
                          hs, hs_k = hst.next()
                          i_ = ctr[0]
                          ctr[0] += 1
                          issue(i_)
                          issue(i_ + 1)
                          xb, xb_k = loaded.pop(i_)
                          masks = []
                          if sg["prompt"]:
                              for j in range(G):
                                  if b0 + j * 128 == 0:
                                      masks.append((j, 0))
                                  if b0 + j * 128 == n - 128:
                                      masks.append((j, 1))
                          hbb, hbb_k = norm_block(P, xb, xb_k, G, (gamt, gam_k), (sht, sh_k), masks=masks)
                          for j in range(G):
                              transpose_to(P, hbb[:, j, :], hbb_k, hs[:, :, j * 128:(j + 1) * 128], hs_k)
                          K.dma("sp", sg["h2T"][:, :, 1 + b0:1 + b0 + bw], hs[:, :, 0:bw], hs_k, reads=[hs_k], acc=[sg["h2T_k"]])
                  K.end_phase()
                  chk(l)
              with ExitStack() as ph:
                  wup, wup_k = K.sb(ph, "wup", [128, KT, 2 * D_FF], BF16, dma=True)
                  wdn, wdn_k = K.sb(ph, "wdn", [128, NFT, D], BF16, dma=True)
                  wuv = wview(Wl["w_up"])
                  wup_ks = [K.chunk_trk() for _ in range(11)]
                  for c in [0, 5, 1, 6, 2, 7, 3, 8, 4, 9, 10]:
                      K.dma("pool", wup[:, :, c * 512:(c + 1) * 512], wuv[:, :, c * 512:(c + 1) * 512], wup_ks[c], writes=[wup_ks[c]])
                  K.dma("pool", wdn[:], wview(Wl["w_down"]), wdn_k, writes=[wdn_k])
                  fdw, fdw_k = K.sb(ph, "fdw", [128, 44, 3], F32, dma=True)
                  fdwb, fdwb_k = K.sb(ph, "fdwb", [128, 44], F32, dma=True)
                  K.dma("sp", fdw[:], Wl["fdw"].rearrange("p (c k) -> p c k", c=44), fdw_k, writes=[fdw_k])
                  K.dma("sp", fdwb[:], Wl["fdwb"], fdwb_k, writes=[fdwb_k])
                  g2t, g2_k = K.sb(ph, "g2t", [128, D], F32, dma=True)
                  h2r = K.ring(ph, "h2b", 2, [128, KT, 514], BF16, dma=True)
                  zr = K.ring(ph, "zt_", 3, [128, 514], F32)
                  accr = K.ring(ph, "acc", 4, [128, 512], F32)
                  sgr = K.ring(ph, "sgf", 2, [128, 512], F32)
                  uT, uT_k = K.sb(ph, "uT", [128, NFT, 512], BF16)
                  xr = K.ring(ph, "xr4", 2, [128, D], F32, dma=True)
                  tmr = K.ring(ph, "tm4", 2, [128, 512], F32)
                  items = [(sg, b0, bw) for sg in segs for (b0, bw) in blocks_of(sg["n"])]
                  loaded = {}
                  ctr = [0]

                  def issue(i):
                      if i >= len(items) or i in loaded:
                          return
                      sg_, b0_, bw_ = items[i]
                      h2_, h2_k_ = h2r.next()
                      K.dma("sp", h2_[:, :, 0:bw_ + 2], sg_["h2T"][:, :, b0_:b0_ + bw_ + 2], h2_k_, writes=[h2_k_], dram_reads=[sg_["h2T_k"]])
                      loaded[i] = (h2_, h2_k_)

                  for sg in segs:
                      n = sg["n"]
                      K.dma("sp", g2t[:], sg["mod"][:, 5 * D:6 * D], g2_k, writes=[g2_k], dram_reads=[sg["mod_k"]])
                      for (b0, bw) in blocks_of(n):
                          nj = bw // 128
                          i_ = ctr[0]
                          ctr[0] += 1
                          issue(i_)
                          issue(i_ + 1)
                          h2, h2_k = loaded.pop(i_)
                          half = (bw + 2) // 2
                          for i in range(NFT):
                              accs = []
                              for which in range(2):
                                  ci = which * NFT + i
                                  col0 = ci * 128
                                  z, z_k = zr.next()
                                  acc, acc_k = accr.next()
                                  for (c0, c1) in ((0, half), (half, bw + 2)):
                                      ps, ps_k = psr(0, 8)
                                      for kt in range(KT):
                                          K.op("pe", lambda e, kt=kt: e.matmul(ps[:, 0:c1 - c0], lhsT=wup[:, kt, col0:col0 + 128], rhs=h2[:, kt, c0:c1],
                                                                               start=(kt == 0), stop=(kt == KT - 1)),
                                               reads=[wup_ks[col0 // 512], h2_k], writes=[ps_k], inc=(kt == KT - 1))
                                      K.op("act", lambda e: e.activation(out=z[:, c0:c1], in_=ps[:, 0:c1 - c0], func=AF.Copy), reads=[ps_k], writes=[z_k])
                                      a1 = min(c1, bw)
                                      K.op("act", lambda e: e.activation(out=acc[:, c0:a1], in_=ps[:, 0:a1 - c0], func=AF.Identity,
                                                                         bias=fdwb[:, ci:ci + 1], scale=fdw[:, ci, 0:1]),
                                           reads=[ps_k, fdw_k, fdwb_k], writes=[acc_k])
                                  K.op("dve", lambda e: e.scalar_tensor_tensor(out=acc[:, 0:bw], in0=z[:, 1:bw + 1], scalar=fdw[:, ci, 1:2], in1=acc[:, 0:bw],
                                                                               op0=ALU.mult, op1=ALU.add),
                                       reads=[z_k, fdw_k, acc_k], writes=[acc_k])
                                  K.op("dve", lambda e: e.scalar_tensor_tensor(out=acc[:, 0:bw], in0=z[:, 2:bw + 2], scalar=fdw[:, ci, 2:3], in1=acc[:, 0:bw],
                                                                               op0=ALU.mult, op1=ALU.add),
                                       reads=[z_k, fdw_k, acc_k], writes=[acc_k])
                                  accs.append((acc, acc_k))
                              sgf, sgf_k = sgr.next()
                              K.op("act", lambda e: e.activation(out=sgf[:, 0:bw], in_=accs[0][0][:, 0:bw], func=AF.Silu), reads=[accs[0][1]], writes=[sgf_k])
                              K.op("pool", lambda e, i=i: e.tensor_tensor(out=uT[:, i, 0:bw], in0=sgf[:, 0:bw], in1=accs[1][0][:, 0:bw], op=ALU.mult),
                                   reads=[sgf_k, accs[1][1]], writes=[uT_k])
                          for t in range(nj):
                              t0 = b0 + t * 128
                              if sg["prompt"] and (t0 < HALO or t0 >= n - HALO):
                                  continue
                              xt, xk = xr.next()
                              K.dma("sp", xt[:], sg["xm"][t0:t0 + 128, :], xk, writes=[xk], dram_reads=[sg["xm_k"]])
                              for hf in range(2):
                                  po, po_k = psr(0, 8)
                                  for i in range(NFT):
                                      K.op("pe", lambda e, i=i: e.matmul(po[:], lhsT=uT[:, i, t * 128:(t + 1) * 128], rhs=wdn[:, i, hf * 512:(hf + 1) * 512],
                                                                         start=(i == 0), stop=(i == NFT - 1)),
                                           reads=[uT_k, wdn_k], writes=[po_k], inc=(i == NFT - 1))
                                  tm2, tm2_k = tmr.next()
                                  K.op("dve", lambda e: e.tensor_tensor(out=tm2[:], in0=po[:], in1=g2t[:, hf * 512:(hf + 1) * 512], op=ALU.mult), reads=[po_k, g2_k], writes=[tm2_k])
                                  K.op("pool", lambda e: e.tensor_tensor(out=xt[:, hf * 512:(hf + 1) * 512], in0=xt[:, hf * 512:(hf + 1) * 512], in1=tm2[:], op=ALU.add),
                                       reads=[xk, tm2_k], writes=[xk])
                              yoff = t0 - HALO if sg["prompt"] else t0
                              K.dma("pool", sg["y"][yoff:yoff + 128, :], xt[:], xk, reads=[xk], acc=[sg["y_k"]])
                  K.end_phase()
                  chk(l)
        except _Stop:
            print('STOPPED at', _stop)
        K.barrier()
        print("instructions emitted:", K.nops)
    return nc


def rope_table(pos):
    inv = np.power(np.float32(10000.0), -np.arange(0, 32, 2, dtype=np.float32) / np.float32(32)).astype(np.float32)
    ang = pos.astype(np.float32)[:, None] * inv[None, :]
    return np.concatenate([np.cos(ang), np.sin(ang)], axis=1).astype(np.float32)


def layer_weights(inp, l):
    f = lambda a: np.ascontiguousarray(a, dtype=np.float32)
    ukv = inp["w_ukv"][l].reshape(128, NH, 128)
    w = dict(
        w_ada=f(inp["w_ada"][l]), b_ada=f(inp["b_ada"][l][None, :]), norm1=f(inp["norm1"][l][None, :]), norm2=f(inp["norm2"][l][None, :]),
        w_in=f(inp["w_in"][l]), w_uq=f(inp["w_uq"][l]),
        w_ukv=f(np.concatenate([ukv[:, :, 0:64].reshape(128, 512), ukv[:, :, 64:128].reshape(128, 512)], axis=1)),
        qan=f(inp["q_a_norm"][l].reshape(2, 128).T), kvan=f(inp["kv_a_norm"][l].reshape(128, 1)),
        qhn=f(inp["q_head_norm"][l][None, :]), khn=f(inp["k_head_norm"][l][None, :]),
        w_ao=f(inp["w_attn_o"][l].reshape(NH, 64, D).transpose(1, 0, 2).reshape(64, NH * D)),
        cdw=f(inp["conv_dw"][l].T.reshape(4, 128, 31).transpose(1, 0, 2).reshape(128, 4 * 31)),
        cdwb=f(inp["conv_dw_b"][l].reshape(4, 128).T), clng=f(inp["conv_ln_g"][l].reshape(4, 128).T), clnb=f(inp["conv_ln_b"][l].reshape(4, 128).T),
        w_co=f(inp["w_conv_o"][l]), sglng=f(inp["sg_ln_g"][l][None, :]), sglnb=f(inp["sg_ln_b"][l][None, :]),
        sgwT=f(inp["sg_w"][l].transpose(2, 0, 1).reshape(128, 4 * 128)),
        sgb=f(inp["sg_b"][l].reshape(1, 512)), w_so=f(inp["w_sg_o"][l]), w_out=f(inp["w_out"][l]), w_up=f(inp["w_up"][l]),
        fdw=f(inp["ffn_dw"][l].T.reshape(44, 128, 3).transpose(1, 0, 2).reshape(128, 44 * 3)),
        fdwb=f(inp["ffn_dw_b"][l].reshape(44, 128).T), w_down=f(inp["w_down"][l]))
    return w


_PROG = {}


def run_model(inp, cfg, n_cores=8):
    NS, SS, PCH, PS = cfg["NS"], cfg["SS"], cfg["PCH"], cfg["PS"]
    PSEG = PCH + 2 * HALO
    key = (NS, SS, PCH, PS)
    L = inp["w_ada"].shape[0]
    assert L == 2
    if key not in _PROG:
        _PROG[key] = build_program(cfg, L)
    nc = _PROG[key]
    xp = np.asarray(inp["x_prompt"], dtype=np.float32)
    xs = np.asarray(inp["x_sample"], dtype=np.float32)
    cp = np.asarray(inp["c_prompt"], dtype=np.float32)
    cs = np.asarray(inp["c_sample"], dtype=np.float32)
    nchunk = PS // PCH
    rope_s = rope_table(np.arange(SS))
    rope_pc = rope_table(np.arange(PS))
    wls = [layer_weights(inp, l) for l in range(L)]
    in_maps = []
    for c in range(n_cores):
        b = c // nchunk
        r = c % nchunk
        lo = r * PCH - HALO
        pm = np.zeros((128, 2), np.float32)
        pm[:, 0] = 1.0 if r > 0 else 0.0
        pm[:, 1] = 1.0 if r < nchunk - 1 else 0.0
        sel = np.zeros((128, nchunk), np.float32)
        sel[:, r] = 1.0
        m = dict(xs=np.ascontiguousarray(xs[c * NS:(c + 1) * NS].reshape(NS * SS, D)),
                 xpctx=np.ascontiguousarray(xp[b]),
                 cvT=np.ascontiguousarray(np.concatenate([cs[c * NS:(c + 1) * NS], cp[b:b + 1]], axis=0).reshape((NS + 1) * KT, 128).T),
                 pmask=pm, psel=sel, rope_s=rope_s, rope_pc=rope_pc, rope_pq=rope_table(np.arange(lo, lo + PSEG)))
        for l in range(L):
            for k, v in wls[l].items():
                m["%s_%d" % (k, l)] = v
        in_maps.append(m)
    res = run_bass_kernel_spmd(nc, in_maps, core_ids=list(range(n_cores)))
    ys = np.stack([np.asarray(r_["ys"]).reshape(NS, SS, D) for r_ in res.results], axis=0).reshape(n_cores * NS, SS, D)
    ypo = np.stack([np.asarray(r_["yp"]) for r_ in res.results], axis=0).reshape(n_cores // nchunk, PS, D)
    return ypo.astype(np.float32), ys.astype(np.float32)


def kernel(**inputs):
    yp, ys = run_model(inputs, CFG, 8)
    return (yp, ys)
```
